# Optimizing a Trainium2 kernel written in Bass

```python
import jax, jax.numpy as jnp
from jax import lax
import numpy as np

D_MODEL = 2048
BATCH = 4
SEQ = 4096
DEPTH = 4

N_HEADS = 16
HEAD_DIM = 128
NSA_KV_HEADS = 4
CMP_BLOCK = 32
CMP_STRIDE = 16
CMP_HIDDEN = 256
SEL_BLOCK = 64
SEL_TOPK = 16
WINDOW = 512
NSA_Q_CHUNK = 32
SB_Q_BLOCK = 128
D_FF = 5632
ROPE_THETA = 10000.0
EPS = 1e-6
N_A_LAYERS = DEPTH // 2
N_B_LAYERS = DEPTH - N_A_LAYERS
NSA_IN_DIM = N_HEADS * HEAD_DIM + 6 * NSA_KV_HEADS * HEAD_DIM + 3 * N_HEADS
NEG = -1e30
FORCE = 1e9

kernel_name = "yoco_nsa_stickbreaking_hybrid"


def rms_norm(x, g):
    xf = x.astype(jnp.float32)
    y = xf * lax.rsqrt(jnp.mean(xf * xf, axis=-1, keepdims=True) + EPS)
    return (y * g.astype(jnp.float32)).astype(x.dtype)


def modulate(h, shift, scale):
    return h * (1 + scale[:, None, :]) + shift[:, None, :]


def rope(x, pos):
    half = HEAD_DIM // 2
    inv = ROPE_THETA ** (-jnp.arange(half, dtype=jnp.float32) / half)
    ang = pos.astype(jnp.float32)[:, None] * inv[None, :]
    cos = jnp.cos(ang)[None, :, None, :]
    sin = jnp.sin(ang)[None, :, None, :]
    xf = x.astype(jnp.float32)
    x1, x2 = xf[..., :half], xf[..., half:]
    return jnp.concatenate([x1 * cos - x2 * sin, x2 * cos + x1 * sin], axis=-1).astype(x.dtype)


def compress(kv, pe, w1, w2):
    B, S, G, Dh = kv.shape
    nc = (S - CMP_BLOCK) // CMP_STRIDE + 1
    idx = np.arange(nc)[:, None] * CMP_STRIDE + np.arange(CMP_BLOCK)[None, :]
    blocks = kv[:, idx] + pe[None, None, :, None, :]
    flat = blocks.transpose(0, 1, 3, 2, 4).reshape(B, nc, G, CMP_BLOCK * Dh)
    return jax.nn.gelu(flat @ w1) @ w2


def nsa_attention(q, kc, vc, ks, vs, kw, vw, gates):
    B, S, H, Dh = q.shape
    G = NSA_KV_HEADS
    R = H // G
    nc = kc.shape[1]
    nb = S // SEL_BLOCK
    n_sel = min(SEL_TOPK, nb)
    QC = NSA_Q_CHUNK
    scale = HEAD_DIM ** -0.5
    qg = q.reshape(B, S, G, R, Dh)
    gg = gates.reshape(B, S, G, R, 3)
    cmp_end = jnp.arange(nc) * CMP_STRIDE + CMP_BLOCK - 1
    c_start = np.arange(nc)[:, None] * CMP_STRIDE
    s_start = np.arange(nb)[None, :] * SEL_BLOCK
    overlap = jnp.asarray(((c_start < s_start + SEL_BLOCK) & (c_start + CMP_BLOCK > s_start)).astype(np.float32))
    blk_ids = jnp.arange(nb)
    ks_blk = ks.reshape(B, nb, SEL_BLOCK, G, Dh).transpose(0, 3, 1, 2, 4)
    vs_blk = vs.reshape(B, nb, SEL_BLOCK, G, Dh).transpose(0, 3, 1, 2, 4)
    kw_pad = jnp.pad(kw, ((0, 0), (WINDOW, 0), (0, 0), (0, 0)))
    vw_pad = jnp.pad(vw, ((0, 0), (WINDOW, 0), (0, 0), (0, 0)))
    vc32 = vc.astype(jnp.float32)
    bi = jnp.arange(B)[:, None, None, None]
    gi = jnp.arange(G)[None, :, None, None]

    def chunk(ci):
        start = ci * QC
        qc = lax.dynamic_slice_in_dim(qg, start, QC, axis=1)
        gc = lax.dynamic_slice_in_dim(gg, start, QC, axis=1).astype(jnp.float32)
        t = start + jnp.arange(QC)
        s = jnp.einsum('bqgrd,bngd->bgrqn', qc, kc).astype(jnp.float32) * scale
        valid = cmp_end[None, :] <= t[:, None]
        p_c = jax.nn.softmax(jnp.where(valid, s, NEG), axis=-1) * jnp.any(valid, axis=-1)[:, None].astype(jnp.float32)
        o_c = jnp.einsum('bgrqn,bngd->bqgrd', p_c, vc32)
        imp = jnp.einsum('bgrqn,nm->bgqm', p_c, overlap)
        cur = t // SEL_BLOCK
        forced = (blk_ids[None, :] == 0) | (blk_ids[None, :] == cur[:, None]) | (blk_ids[None, :] == cur[:, None] - 1)
        avail = blk_ids[None, :] * SEL_BLOCK <= t[:, None]
        imp = jnp.where(forced, FORCE, jnp.where(avail, imp, -FORCE))
        _, sel = lax.top_k(imp, n_sel)
        k_sel = ks_blk[bi, gi, sel]
        v_sel = vs_blk[bi, gi, sel].astype(jnp.float32)
        tok = sel[..., None] * SEL_BLOCK + jnp.arange(SEL_BLOCK)
        m_s = tok <= t[None, None, :, None, None]
        s = jnp.einsum('bqgrd,bgqnld->bgrqnl', qc, k_sel).astype(jnp.float32) * scale
        s = jnp.where(m_s[:, :, None], s, NEG).reshape(B, G, R, QC, n_sel * SEL_BLOCK)
        p_s = jax.nn.softmax(s, axis=-1).reshape(B, G, R, QC, n_sel, SEL_BLOCK)
        o_s = jnp.einsum('bgrqnl,bgqnld->bqgrd', p_s, v_sel)
        kwin = lax.dynamic_slice_in_dim(kw_pad, start, WINDOW + QC, axis=1)
        vwin = lax.dynamic_slice_in_dim(vw_pad, start, WINDOW + QC, axis=1).astype(jnp.float32)
        kpos = start - WINDOW + jnp.arange(WINDOW + QC)
        m_w = (kpos[None, :] <= t[:, None]) & (kpos[None, :] > t[:, None] - WINDOW) & (kpos[None, :] >= 0)
        s = jnp.einsum('bqgrd,bkgd->bgrqk', qc, kwin).astype(jnp.float32) * scale
        p_w = jax.nn.softmax(jnp.where(m_w, s, NEG), axis=-1)
        o_w = jnp.einsum('bgrqk,bkgd->bqgrd', p_w, vwin)
        return gc[..., 0:1] * o_c + gc[..., 1:2] * o_s + gc[..., 2:3] * o_w

    out = lax.map(chunk, jnp.arange(S // QC))
    return out.transpose(1, 0, 2, 3, 4, 5).reshape(B, S, H, Dh)


def nsa_mixer(h, pos, w_in, gate_b, cmp_pe, cmp_w1, cmp_w2, w_out):
    B, S, _ = h.shape
    proj = h @ w_in
    sizes = [N_HEADS * HEAD_DIM] + [NSA_KV_HEADS * HEAD_DIM] * 6
    q, kc, vc, ks, vs, kw, vw, g = jnp.split(proj, np.cumsum(sizes).tolist(), axis=-1)
    kv_shape = (B, S, NSA_KV_HEADS, HEAD_DIM)
    q = rope(q.reshape(B, S, N_HEADS, HEAD_DIM), pos)
    kc = rope(kc.reshape(kv_shape), pos)
    ks = rope(ks.reshape(kv_shape), pos)
    kw = rope(kw.reshape(kv_shape), pos)
    vc, vs, vw = vc.reshape(kv_shape), vs.reshape(kv_shape), vw.reshape(kv_shape)
    gates = jax.nn.sigmoid((g + gate_b).astype(jnp.float32)).reshape(B, S, N_HEADS, 3)
    kc = compress(kc, cmp_pe[0], cmp_w1[0], cmp_w2[0])
    vc = compress(vc, cmp_pe[1], cmp_w1[1], cmp_w2[1])
    o = nsa_attention(q, kc, vc, ks, vs, kw, vw, gates)
    return o.reshape(B, S, N_HEADS * HEAD_DIM).astype(h.dtype) @ w_out


def stick_breaking_attention(q, k, v):
    B, S, H, Dh = q.shape
    scale = HEAD_DIM ** -0.5
    kpos = jnp.arange(S)
    v32 = v.astype(jnp.float32)

    def block(bi):
        start = bi * SB_Q_BLOCK
        qb = lax.dynamic_slice_in_dim(q, start, SB_Q_BLOCK, axis=1)
        t = start + jnp.arange(SB_Q_BLOCK)
        z = jnp.einsum('bqhd,bkhd->bhqk', qb, k).astype(jnp.float32) * scale
        causal = kpos[None, :] < t[:, None]
        sp = jnp.where(causal, jax.nn.softplus(z), 0.0)
        after = lax.cumsum(sp, axis=3, reverse=True) - sp
        log_a = jax.nn.log_sigmoid(z) - after
        a = jnp.where(causal, jnp.exp(log_a), 0.0)
        return jnp.einsum('bhqk,bkhd->bqhd', a, v32)

    out = lax.map(block, jnp.arange(S // SB_Q_BLOCK))
    return out.transpose(1, 0, 2, 3, 4).reshape(B, S, H, Dh)


def sb_mixer(h, k, v, w_q, w_out):
    B, S, _ = h.shape
    q = (h @ w_q).reshape(B, S, N_HEADS, HEAD_DIM)
    o = stick_breaking_attention(q, k, v)
    return o.reshape(B, S, N_HEADS * HEAD_DIM).astype(h.dtype) @ w_out


def swiglu(h, w_in, w_out):
    gate, up = jnp.split(h @ w_in, 2, axis=-1)
    return (jax.nn.silu(gate) * up) @ w_out


def setup_inputs(seed: int = 0) -> dict:
    key = jax.random.key(seed)
    ks = jax.random.split(key, 19)
    D = D_MODEL
    f32 = jnp.float32

    def nrm(k, shape, fan_in, gain=1.0):
        return jax.random.normal(k, shape, f32) * (gain * fan_in ** -0.5)

    return {
        "x": jax.random.normal(ks[0], (BATCH, SEQ, D), f32),
        "c": jax.random.normal(ks[1], (BATCH, D), f32),
        "mod_w": nrm(ks[2], (DEPTH, D, 6 * D), D, 0.5),
        "mod_b": 0.02 * jax.random.normal(ks[3], (DEPTH, 6 * D), f32),
        "norm_g": 1.0 + 0.05 * jax.random.normal(ks[4], (DEPTH, 4, D), f32),
        "ffn_w_in": nrm(ks[5], (DEPTH, D, 2 * D_FF), D),
        "ffn_w_out": nrm(ks[6], (DEPTH, D_FF, D), D_FF),
        "a_w_in": nrm(ks[7], (N_A_LAYERS, D, NSA_IN_DIM), D),
        "a_gate_b": 0.02 * jax.random.normal(ks[8], (N_A_LAYERS, 3 * N_HEADS), f32),
        "a_cmp_pe": 0.5 * jax.random.normal(ks[9], (N_A_LAYERS, 2, CMP_BLOCK, HEAD_DIM), f32),
        "a_cmp_w1": nrm(ks[10], (N_A_LAYERS, 2, CMP_BLOCK * HEAD_DIM, CMP_HIDDEN), CMP_BLOCK * HEAD_DIM),
        "a_cmp_w2": nrm(ks[11], (N_A_LAYERS, 2, CMP_HIDDEN, HEAD_DIM), CMP_HIDDEN),
        "a_w_out": nrm(ks[12], (N_A_LAYERS, N_HEADS * HEAD_DIM, D), N_HEADS * HEAD_DIM),
        "b_w_q": nrm(ks[13], (N_B_LAYERS, D, N_HEADS * HEAD_DIM), D),
        "b_w_out": nrm(ks[14], (N_B_LAYERS, N_HEADS * HEAD_DIM, D), N_HEADS * HEAD_DIM),
        "kv_norm_g": 1.0 + 0.05 * jax.random.normal(ks[15], (D,), f32),
        "kv_mod_w": nrm(ks[16], (D, 2 * D), D, 0.5),
        "kv_mod_b": 0.02 * jax.random.normal(ks[17], (2 * D,), f32),
        "kv_w": nrm(ks[18], (D, 2 * N_HEADS * HEAD_DIM), D),
    }


def reference(x, c, mod_w, mod_b, norm_g, ffn_w_in, ffn_w_out, a_w_in, a_gate_b, a_cmp_pe, a_cmp_w1, a_cmp_w2, a_w_out, b_w_q, b_w_out, kv_norm_g, kv_mod_w, kv_mod_b, kv_w):
    B, S, _ = x.shape
    pos = jnp.arange(S)
    c_act = jax.nn.silu(c)
    k_sh = None
    v_sh = None
    for layer in range(DEPTH):
        mod = c_act @ mod_w[layer] + mod_b[layer]
        sh1, sc1, g1, sh2, sc2, g2 = jnp.split(mod, 6, axis=-1)
        h = modulate(rms_norm(x, norm_g[layer, 0]), sh1, sc1)
        if layer < N_A_LAYERS:
            y = nsa_mixer(h, pos, a_w_in[layer], a_gate_b[layer], a_cmp_pe[layer], a_cmp_w1[layer], a_cmp_w2[layer], a_w_out[layer])
        else:
            if layer == N_A_LAYERS:
                kv_sh, kv_sc = jnp.split(c_act @ kv_mod_w + kv_mod_b, 2, axis=-1)
                hk = modulate(rms_norm(x, kv_norm_g), kv_sh, kv_sc)
                k_sh, v_sh = jnp.split(hk @ kv_w, 2, axis=-1)
                k_sh = k_sh.reshape(B, S, N_HEADS, HEAD_DIM)
                v_sh = v_sh.reshape(B, S, N_HEADS, HEAD_DIM)
            j = layer - N_A_LAYERS
            y = sb_mixer(h, k_sh, v_sh, b_w_q[j], b_w_out[j])
        x = x + g1[:, None, :] * rms_norm(y, norm_g[layer, 1])
        h = modulate(rms_norm(x, norm_g[layer, 2]), sh2, sc2)
        y = swiglu(h, ffn_w_in[layer], ffn_w_out[layer])
        x = x + g2[:, None, :] * rms_norm(y, norm_g[layer, 3])
    return x
```

```python
from contextlib import ExitStack
import numpy as np
import ml_dtypes
import concourse.bass as bass
import concourse.mybir as mybir
from concourse.bass_utils import run_bass_kernel_spmd

F32 = mybir.dt.float32
BF16 = mybir.dt.bfloat16
AF = mybir.ActivationFunctionType
ALU = mybir.AluOpType
AX = mybir.AxisListType

SAME_ENGINE_SYNC = True

D = 2048
T = 4096
NT = T // 128
KC = 16
DFF = 5632
NH = 16
G = 4
EPS = 1e-6
SCALE = 128 ** -0.5
NCMP = 255
NAIN = 5168


class Buf:
    __slots__ = ("name", "w", "r", "ps")

    def __init__(self, name="", ps=False):
        self.name = name
        self.w = None
        self.r = []
        self.ps = ps


class Eng:
    def __init__(self, nc, e, name, pe=False, ndma=0):
        self.e = e
        self.name = name
        self.sem = nc.alloc_semaphore("s_" + name)
        self.cnt = 0
        self.seen = {}
        self.pe = pe
        self.dsems = [[nc.alloc_semaphore("d_%s%d" % (name, i)), 0] for i in range(ndma)]
        self.dnext = 0


class K:
    def __init__(self):
        nc = bass.Bass("TRN2", target_bir_lowering=False)
        self.nc = nc
        self.pe = Eng(nc, nc.tensor, "pe", pe=True)
        self.act = Eng(nc, nc.scalar, "act")
        self.dve = Eng(nc, nc.vector, "dve")
        self.pool = Eng(nc, nc.gpsimd, "pool", ndma=8)
        self.sp = Eng(nc, nc.sync, "sp", ndma=16)
        self.engs = [self.pe, self.act, self.dve, self.pool, self.sp]
        self.nins = 0
        self.nwait = 0

    def _wait(self, eng, tok):
        sem, val = tok
        if sem is eng.sem and (eng.pe or not SAME_ENGINE_SYNC):
            return
        key = id(sem)
        if eng.seen.get(key, 0) >= val:
            return
        eng.e.wait_ge(sem, val)
        eng.seen[key] = val
        self.nwait += 1

    def _deps(self, eng, R, W):
        for b in R:
            if b.w is not None:
                self._wait(eng, b.w)
        for b in W:
            if b.w is not None:
                self._wait(eng, b.w)
            for t in b.r:
                self._wait(eng, t)

    def _commit(self, tok, R, W):
        for b in W:
            b.w = tok
            b.r = []
        for b in R:
            b.r.append(tok)
            if len(b.r) > 16:
                d = {}
                for s, v in b.r:
                    kk = id(s)
                    if kk not in d or d[kk][1] < v:
                        d[kk] = (s, v)
                b.r = list(d.values())

    def op(self, eng, fn, R=(), W=()):
        px = [b for b in R if b.ps and b not in W]
        if px:
            W = list(W) + px
            R = [b for b in R if not b.ps]
        self._deps(eng, R, W)
        ins = fn(eng.e)
        eng.cnt += 1
        ins.then_inc(eng.sem, 1)
        self._commit((eng.sem, eng.cnt), R, W)
        self.nins += 1
        return ins

    def dma(self, q, out, in_, R=(), W=(), **kw):
        self._deps(q, R, W)
        slot = q.dsems[q.dnext]
        q.dnext = (q.dnext + 1) % len(q.dsems)
        if slot[1] > 0:
            self._wait(q, (slot[0], slot[1]))
        ins = q.e.dma_start(out=out, in_=in_, **kw)
        slot[1] += 16
        ins.then_inc(slot[0], 16)
        self._commit((slot[0], slot[1]), R, W)
        self.nins += 1
        return ins

    def fence(self):
        for e in self.engs:
            for o in self.engs:
                if o is not e and o.cnt > 0:
                    self._wait(e, (o.sem, o.cnt))
                for s in o.dsems:
                    if s[1] > 0:
                        self._wait(e, (s[0], s[1]))


def _bf(a):
    return np.ascontiguousarray(a).astype(ml_dtypes.bfloat16)


def make_consts():
    c = {}
    p = np.arange(128)
    ident = np.eye(128, dtype=np.float32)
    rot = np.zeros((128, 128), np.float32)
    for dp in range(64):
        rot[dp + 64, dp] = -1.0
        rot[dp, dp + 64] = 1.0
    ones = np.ones((128, 128), np.float32)
    ltri = (p[:, None] > p[None, :]).astype(np.float32)
    causal = (p[:, None] <= p[None, :]).astype(np.float32)
    anti = (p[:, None] > p[None, :]).astype(np.float32)
    c["cst_bf"] = _bf(np.stack([ident, rot, ones, ltri, causal, anti], axis=1))
    t = np.arange(T, dtype=np.float64)
    inv = 10000.0 ** (-np.arange(64, dtype=np.float64) / 64)
    ang = (t[None, :].astype(np.float32) * inv.astype(np.float32)[:, None]).astype(np.float32)
    cs = np.stack([np.concatenate([np.cos(ang), np.cos(ang)], 0), np.concatenate([np.sin(ang), np.sin(ang)], 0)], axis=1)
    c["cossin"] = np.ascontiguousarray(cs.astype(np.float32))
    n = np.arange(256)
    cmp_end = n * 16 + 31
    mc = ((cmp_end[:, None] <= np.arange(T)[None, :]) & (n[:, None] < NCMP)).astype(np.float32)
    c["maskc"] = _bf(mc.reshape(2, 128, T).transpose(1, 0, 2))
    c_start = n[:, None] * 16
    s_start = np.arange(64)[None, :] * 64
    ov = ((c_start < s_start + 64) & (c_start + 32 > s_start) & (n[:, None] < NCMP)).astype(np.float32)
    ext = np.zeros((256, 65), np.float32)
    ext[:NCMP, 0] = 1.0
    ext[:, 1:] = ov
    c["vext"] = _bf(ext.reshape(2, 128, 65).transpose(1, 0, 2))
    tt = np.arange(T)
    blk = np.arange(64)
    cur = tt // 64
    forced = (blk[None, :] == 0) | (blk[None, :] == cur[:, None]) | (blk[None, :] == cur[:, None] - 1)
    avail = blk[None, :] * 64 <= tt[:, None]
    availm = (avail & ~forced).astype(np.float32)
    forcem = np.where(forced, 1e9 * (1.0 + (blk[None, :] == 0) + 2.0 * (blk[None, :] == cur[:, None])), np.where(avail, 0.0, -1e9)).astype(np.float32)
    c["selm"] = np.ascontiguousarray(np.stack([availm, forcem], 0).reshape(2, NT, 128, 64).transpose(2, 0, 1, 3))
    kk = np.arange(128)[:, None]
    qq = np.arange(512)[None, :]
    msb = np.stack([((a * 128 + kk) < qq) for a in range(4)], axis=1).astype(np.float32)
    c["msb_f"] = np.ascontiguousarray(msb)
    c["msb_b"] = _bf(msb)
    return c


class Prog:
    def __init__(self, nlayers=4, dbg=(), stop=None):
        self.stop = stop
        self.k = K()
        k = self.k
        nc = k.nc
        self.nc = nc
        self.dbg = set(dbg)
        self.nlayers = nlayers
        dt = nc.dram_tensor
        self.inp = {}

        def din(name, shape, dtype=F32):
            self.inp[name] = dt(name, list(shape), dtype, kind="ExternalInput")
            return self.inp[name]

        self.x = din("x", [T, D])
        self.cT = din("cT", [128, KC])
        self.mod_w = din("mod_w", [4, D, 6 * D])
        self.mod_b = din("mod_b", [4, 6 * D])
        self.norm_g = din("norm_g", [4, 4, D])
        self.ffn_w_in = din("ffn_w_in", [4, D, 2 * DFF])
        self.ffn_w_out = din("ffn_w_out", [4, DFF, D])
        self.a_w_in = din("a_w_in", [2, D, NAIN])
        self.a_gate_b = din("a_gate_b", [2, 48])
        self.a_cmp_peT = din("a_cmp_peT", [2, 2, 128, 32])
        self.a_cmp_w1 = din("a_cmp_w1", [2, 2, 4096, 256])
        self.a_cmp_w2 = din("a_cmp_w2", [2, 2, 256, 128])
        self.a_w_out = din("a_w_out", [2, D, D])
        self.b_w_q = din("b_w_q", [2, D, D])
        self.b_w_out = din("b_w_out", [2, D, D])
        self.kv_norm_g = din("kv_norm_g", [1, D])
        self.kv_mod_w = din("kv_mod_w", [D, 2 * D])
        self.kv_mod_b = din("kv_mod_b", [1, 2 * D])
        self.kv_w = din("kv_w", [D, 2 * D])
        self.c_cst = din("cst_bf", [128, 6, 128], BF16)
        self.c_cossin = din("cossin", [128, 2, T])
        self.c_maskc = din("maskc", [128, 2, T], BF16)
        self.c_vext = din("vext", [128, 2, 65], BF16)
        self.c_selm = din("selm", [128, 2, NT, 64])
        self.c_msb_f = din("msb_f", [128, 4, 512])
        self.c_msb_b = din("msb_b", [128, 4, 512], BF16)
        self.out = dt("out", [T, D], F32, kind="ExternalOutput")

        def scr(name, shape, dtype):
            kind = "ExternalOutput" if name in self.dbg else "Internal"
            return dt(name, list(shape), dtype, kind=kind)

        self.xs1 = scr("xs1", [T, D], F32)
        self.xs2 = scr("xs2", [T, D], F32)
        self.hT = [scr("hT%d" % i, [128, KC, T], BF16) for i in range(3)]
        self.qT = scr("qT", [G, 128, NT, 4, 128], BF16)
        self.kcT = scr("kcT", [G, 128, T], BF16)
        self.vcT = scr("vcT", [G, 128, T], BF16)
        self.ksT = scr("ksT", [G, 128, T], BF16)
        self.kwT = scr("kwT", [G, 128, T], BF16)
        self.vs = scr("vs", [T, 512], BF16)
        self.vw = scr("vw", [T, 512], BF16)
        self.opart = scr("opart", [T, D], F32)
        self.selT = scr("selT", [G, 64, T], BF16)
        self.kTsh = scr("kTsh", [NH, 128, T], BF16)
        self.vsh = scr("vsh", [T, D], BF16)
        self.qsb = scr("qsb", [NH, 128, T], BF16)
        self.dbufs = {}

        self.ps = [nc.alloc_psum_tensor("ps%d" % i, [128, 512], F32) for i in range(8)]
        self.psB = [Buf("ps%d" % i, ps=True) for i in range(8)]
        self.cst = nc.alloc_sbuf_tensor("cst", [128, 6, 128], BF16)
        self.cstB = Buf("cst")
        self.crep = nc.alloc_sbuf_tensor("crep", [128, KC, 128], BF16)
        self.crepB = Buf("crep")
        self.NW = 2
        self.wt = [nc.alloc_sbuf_tensor("wt%d" % i, [128, KC, 512], BF16) for i in range(self.NW)]
        self.wB = [Buf("wt%d" % i) for i in range(self.NW)]
        self.wi = 0
        self.vec = [nc.alloc_sbuf_tensor("vec%d" % i, [128, D], F32) for i in range(4)]
        self.vecB = [Buf("vec%d" % i) for i in range(4)]
        self.gate_sb = None
        self.gateB = Buf("gate")
        self.uid = 0

    def sbt(self, name, shape, dtype):
        self.uid += 1
        return self.nc.sbuf_tensor("%s_u%d" % (name, self.uid), shape, dtype)

    def db(self, name, i=0):
        key = (name, i)
        b = self.dbufs.get(key)
        if b is None:
            b = self.dbufs[key] = Buf("%s_%s" % key)
        return b

    def E(self, eng, meth, *a, R=(), W=(), **kw):
        return self.k.op(eng, lambda e: getattr(e, meth)(*a, **kw), R, W)

    def ident(self):
        return self.cst[:, 0, :]

    def bcast_row(self, handle, off, n):
        return bass.AP(handle, off, [[0, 128], [1, n]])

    def wload(self, src2d, kc=16, ncols=512):
        i = self.wi
        self.wi = (i + 1) % self.NW
        self.k.dma(self.k.pool, self.wt[i][:, 0:kc, 0:ncols], src2d.rearrange("(k p) n -> p k n", p=128), W=[self.wB[i]])
        return self.wt[i], self.wB[i]

    def psbf(self, i):
        return self.ps[i][:, :].bitcast(BF16)

    def setup(self):
        k = self.k
        nc = self.nc
        k.dma(k.sp, self.cst[:, :, :], self.c_cst[:, :, :], W=[self.cstB])
        with ExitStack() as es:
            cf = es.enter_context(self.sbt("cf", [128, KC], F32))
            ca = es.enter_context(self.sbt("ca", [128, KC], F32))
            cfB = Buf()
            caB = Buf()
            k.dma(k.sp, cf[:, :], self.cT[:, :], W=[cfB])
            self.E(k.act, "activation", ca[:, :], cf[:, :], AF.Silu, R=[cfB], W=[caB])
            self.E(k.dve, "tensor_copy", self.crep[:, :, :], ca[:, :].unsqueeze(2).to_broadcast([128, KC, 128]), R=[caB], W=[self.crepB])
            k.fence()

    def modvec(self, vi, w2d, col0, bias_handle, bias_off):
        k = self.k
        dst = self.vec[vi]
        dB = self.vecB[vi]
        k.dma(k.sp, dst[:, :], self.bcast_row(bias_handle, bias_off, D), W=[dB])
        for cb in range(4):
            wt, wB = self.wload(w2d[:, col0 + cb * 512: col0 + (cb + 1) * 512])
            pi = cb % 2
            for kk in range(KC):
                self.E(k.pe, "matmul", self.ps[pi][:, :], self.crep[:, kk, :], wt[:, kk, :], start=(kk == 0), stop=(kk == KC - 1),
                       R=[self.crepB, wB], W=[self.psB[pi]])
            self.E(k.dve, "tensor_tensor", dst[:, cb * 512:(cb + 1) * 512], self.ps[pi][:, :], dst[:, cb * 512:(cb + 1) * 512], ALU.add,
                   R=[self.psB[pi], dB], W=[dB])

    def load_gain(self, vi, handle, off):
        self.k.dma(self.k.sp, self.vec[vi][:, :], self.bcast_row(handle, off, D), W=[self.vecB[vi]])

    def mod_AB(self, l, which, gi, va, vb, vtmp):
        k = self.k
        w2d = self.mod_w.ap()[l]
        base = 0 if which == 0 else 3 * D
        self.modvec(vb, w2d, base, self.mod_b, l * 6 * D + base)
        self.modvec(va, w2d, base + D, self.mod_b, l * 6 * D + base + D)
        self.load_gain(vtmp, self.norm_g, (l * 4 + gi) * D)
        self.E(k.dve, "scalar_tensor_tensor", self.vec[va][:, :], self.vec[va][:, :], 1.0, self.vec[vtmp][:, :], ALU.add, ALU.mult,
               R=[self.vecB[va], self.vecB[vtmp]], W=[self.vecB[va]])

    def mod_C(self, l, which, gi, vc, vtmp):
        k = self.k
        w2d = self.mod_w.ap()[l]
        base = 2 * D if which == 0 else 5 * D
        self.modvec(vc, w2d, base, self.mod_b, l * 6 * D + base)
        self.load_gain(vtmp, self.norm_g, (l * 4 + gi) * D)
        self.E(k.dve, "tensor_tensor", self.vec[vc][:, :], self.vec[vc][:, :], self.vec[vtmp][:, :], ALU.mult,
               R=[self.vecB[vc], self.vecB[vtmp]], W=[self.vecB[vc]])

    def alloc_norm_tiles(self, es):
        nc = self.nc
        w = {}
        w["ss"] = [es.enter_context(self.sbt("ss%d" % i, [128, 8], F32)) for i in range(2)]
        w["ssB"] = [Buf() for _ in range(2)]
        w["hn"] = es.enter_context(self.sbt("hn", [128, D], F32))
        w["hnB"] = Buf()
        w["junk"] = w["hn"]
        w["junkB"] = w["hnB"]
        w["hb"] = [es.enter_context(self.sbt("hb%d" % i, [128, D], BF16)) for i in range(1)]
        w["hbB"] = [Buf() for _ in range(1)]
        w["hTt"] = [es.enter_context(self.sbt("hTt%d" % i, [128, KC, 128], BF16)) for i in range(1)]
        w["hTtB"] = [Buf() for _ in range(1)]
        w["i"] = 0
        return w

    def rstd(self, w, src, srcB, n):
        k = self.k
        i = w["i"] % 2
        w["i"] += 1
        ss = w["ss"][i]
        sB = w["ssB"][i]
        self.E(k.pool, "memset", ss[:, 0:1], 0.0, W=[sB])
        self.E(k.act, "activation", w["junk"][:, 0:n], src, AF.Square, accum_out=ss[:, 0:1], R=[srcB], W=[w["junkB"], sB])
        self.E(k.dve, "tensor_scalar", ss[:, 1:2], ss[:, 0:1], 1.0 / n, EPS, ALU.mult, ALU.add, R=[sB], W=[sB])
        self.E(k.act, "activation", ss[:, 2:3], ss[:, 1:2], AF.Sqrt, R=[sB], W=[sB])
        self.E(k.dve, "reciprocal", ss[:, 3:4], ss[:, 2:3], R=[sB], W=[sB])
        return ss[:, 3:4], sB

    def prenorm_tile(self, w, xt, xB, va, vb, hTi, tt, pbanks=(0, 1)):
        k = self.k
        r, rB = self.rstd(w, xt, xB, D)
        j = tt % len(w["hb"])
        self.E(k.dve, "scalar_tensor_tensor", w["hn"][:, :], xt, r, self.vec[va][:, :], ALU.mult, ALU.mult,
               R=[xB, rB, self.vecB[va]], W=[w["hnB"]])
        self.E(k.pool, "tensor_tensor", w["hb"][j][:, :], w["hn"][:, :], self.vec[vb][:, :], ALU.add,
               R=[w["hnB"], self.vecB[vb]], W=[w["hbB"][j]])
        self.transpose_to_hT(w, w["hb"][j], w["hbB"][j], hTi, tt, pbanks)

    def transpose_to_hT(self, w, src, srcB, hTi, tt, pbanks=(0, 1), nch=KC, ch0=0):
        k = self.k
        j = tt % len(w["hTt"])
        hTt = w["hTt"][j]
        hB = w["hTtB"][j]
        for half in range((nch + 7) // 8):
            pi = pbanks[half % len(pbanks)]
            n8 = min(8, nch - half * 8)
            pb = self.psbf(pi)
            for c in range(n8):
                cc = half * 8 + c
                self.E(k.pe, "transpose", pb[:, c * 128:(c + 1) * 128], src[:, cc * 128:(cc + 1) * 128], self.ident(),
                       R=[srcB, self.cstB], W=[self.psB[pi]])
            eng = k.act if half % 2 == 0 else k.dve
            meth = "copy" if eng is k.act else "tensor_copy"
            self.E(eng, meth, hTt[:, half * 8: half * 8 + n8, :], pb[:, 0:n8 * 128].rearrange("p (k n) -> p k n", k=n8),
                   R=[self.psB[pi]], W=[hB])
        k.dma(k.sp, self.hT[hTi][:, ch0:ch0 + nch, tt * 128:(tt + 1) * 128], hTt[:, 0:nch, :], R=[hB], W=[self.db("hT%d" % hTi, tt)])

    def norm_phase(self, xsrc, xname, va, vb, hTi):
        k = self.k
        nc = self.nc
        with ExitStack() as es:
            w = self.alloc_norm_tiles(es)
            xt = [es.enter_context(self.sbt("xt%d" % i, [128, D], F32)) for i in range(2)]
            xB = [Buf() for _ in range(2)]
            for tt in range(NT):
                j = tt % 2
                k.dma(k.sp, xt[j][:, :], xsrc[tt * 128:(tt + 1) * 128, :], R=[self.db(xname, tt)], W=[xB[j]])
                self.prenorm_tile(w, xt[j][:, :], xB[j], va, vb, hTi, tt)
            k.fence()

    def residual_tile(self, w, y, yB, xt, xB, xsrc, xsname, xdst, xdname, vc, tt):
        k = self.k
        k.dma(k.sp, xt, xsrc[tt * 128:(tt + 1) * 128, :], R=[self.db(xsname, tt)], W=[xB])
        r, rB = self.rstd(w, y, yB, D)
        self.E(k.dve, "scalar_tensor_tensor", w["hn"][:, :], y, r, self.vec[vc][:, :], ALU.mult, ALU.mult,
               R=[yB, rB, self.vecB[vc]], W=[w["hnB"]])
        self.E(k.pool, "tensor_tensor", xt, xt, w["hn"][:, :], ALU.add, R=[xB, w["hnB"]], W=[xB])
        k.dma(k.sp, xdst[tt * 128:(tt + 1) * 128, :], xt, R=[xB], W=[self.db(xdname, tt)])

    def proj(self, hTi, w2d, blocks, fF=None, fT=None, pF=(0, 1), pT=(2, 3)):
        k = self.k
        nc = self.nc
        with ExitStack() as es:
            hb = [es.enter_context(self.sbt("phT%d" % i, [128, KC, 512], BF16)) for i in range(2)]
            hB = [Buf() for _ in range(2)]
            cnt = 0
            for tb in range(T // 512):
                j = tb % 2
                k.dma(k.sp, hb[j][:, :, :], self.hT[hTi][:, :, tb * 512:(tb + 1) * 512],
                      R=[self.db("hT%d" % hTi, 4 * tb + s) for s in range(4)], W=[hB[j]])
                for (col0, ncols, mode, tag) in blocks:
                    wt, wB = self.wload(w2d[:, col0:col0 + ncols], ncols=ncols)
                    if mode == "F":
                        for jj in range(ncols // 128):
                            pi = pF[cnt % len(pF)]
                            cnt += 1
                            for kk in range(KC):
                                self.E(k.pe, "matmul", self.ps[pi][:, :], wt[:, kk, jj * 128:(jj + 1) * 128], hb[j][:, kk, :],
                                       start=(kk == 0), stop=(kk == KC - 1), R=[wB, hB[j]], W=[self.psB[pi]])
                            fF(tag, jj, tb, pi)
                    else:
                        for s in range(4):
                            pi = pT[cnt % len(pT)]
                            cnt += 1
                            for kk in range(KC):
                                self.E(k.pe, "matmul", self.ps[pi][:, 0:ncols], hb[j][:, kk, s * 128:(s + 1) * 128], wt[:, kk, 0:ncols],
                                       start=(kk == 0), stop=(kk == KC - 1), R=[wB, hB[j]], W=[self.psB[pi]])
                            fT(tag, s, tb, pi)
            k.fence()

    def nsa_proj(self, l):
        k = self.k
        nc = self.nc
        with ExitStack() as es:
            cs = [es.enter_context(self.sbt("cs%d" % i, [128, 2, 512], F32)) for i in range(2)]
            csB = [Buf() for _ in range(2)]
            raw = [es.enter_context(self.sbt("raw%d" % i, [128, 512], BF16)) for i in range(2)]
            rawB = [Buf() for _ in range(2)]
            t1 = [es.enter_context(self.sbt("t1_%d" % i, [128, 512], F32)) for i in range(2)]
            t1B = [Buf() for _ in range(2)]
            t2 = [es.enter_context(self.sbt("t2_%d" % i, [128, 512], F32)) for i in range(2)]
            t2B = [Buf() for _ in range(2)]
            ob = [es.enter_context(self.sbt("ob%d" % i, [128, 512], BF16)) for i in range(3)]
            obB = [Buf() for _ in range(3)]
            gb = es.enter_context(self.sbt("gb", [128, 48], F32))
            gbB = Buf()
            gt = es.enter_context(self.sbt("gt", [128, 48], F32))
            gtB = Buf()
            k.dma(k.sp, gb[:, :], self.bcast_row(self.a_gate_b, l * 48, 48), W=[gbB])
            st = {"n": 0, "tb": -1}
            featdst = {4: self.kcT, 5: self.vcT, 6: self.ksT, 8: self.kwT}
            featname = {4: "kcT", 5: "vcT", 6: "ksT", 8: "kwT"}

            def fF(tag, jj, tb, pi):
                n = st["n"]
                st["n"] += 1
                i2 = n % 2
                i3 = n % 3
                if st["tb"] != tb:
                    st["tb"] = tb
                    k.dma(k.sp, cs[tb % 2][:, :, :], self.c_cossin[:, :, tb * 512:(tb + 1) * 512], W=[csB[tb % 2]])
                c_ = cs[tb % 2]
                cB = csB[tb % 2]
                P = self.ps[pi]
                PB = self.psB[pi]
                if tag == 5:
                    self.E(k.act, "copy", ob[i3][:, :], P[:, :], R=[PB], W=[obB[i3]])
                else:
                    self.E(k.act, "copy", raw[i2][:, :], P[:, :], R=[PB], W=[rawB[i2]])
                    p2 = 4 + i2
                    self.E(k.pe, "matmul", self.ps[p2][:, :], self.cst[:, 1, :], raw[i2][:, :], start=True, stop=True,
                           R=[self.cstB, rawB[i2]], W=[self.psB[p2]])
                    self.E(k.dve, "tensor_tensor", t1[i2][:, :], P[:, :], c_[:, 0, :], ALU.mult, R=[PB, cB], W=[t1B[i2]])
                    self.E(k.dve, "tensor_tensor", t2[i2][:, :], self.ps[p2][:, :], c_[:, 1, :], ALU.mult, R=[self.psB[p2], cB], W=[t2B[i2]])
                    self.E(k.pool, "tensor_tensor", ob[i3][:, :], t1[i2][:, :], t2[i2][:, :], ALU.add, R=[t1B[i2], t2B[i2]], W=[obB[i3]])
                if tag < 4:
                    dst = self.qT[tag, :, 4 * tb:4 * tb + 4, jj, :]
                    k.dma(k.sp, dst, ob[i3][:, :].rearrange("p (a t) -> p a t", a=4), R=[obB[i3]], W=[self.db("qT", (tag, tb))])
                else:
                    k.dma(k.sp, featdst[tag][jj, :, tb * 512:(tb + 1) * 512], ob[i3][:, :], R=[obB[i3]], W=[self.db(featname[tag], (jj, tb))])

            def fT(tag, s, tb, pi):
                n = st["n"]
                st["n"] += 1
                i3 = n % 3
                P = self.ps[pi]
                PB = self.psB[pi]
                tt = 4 * tb + s
                if tag == 10:
                    self.E(k.dve, "tensor_tensor", gt[:, :], P[:, 0:48], gb[:, :], ALU.add, R=[PB, gbB], W=[gtB])
                    self.E(k.act, "activation", self.gate_sb[:, tt, :], gt[:, :], AF.Sigmoid, R=[gtB], W=[self.gateB])
                else:
                    dst = self.vs if tag == 7 else self.vw
                    self.E(k.act, "copy", ob[i3][:, :], P[:, :], R=[PB], W=[obB[i3]])
                    k.dma(k.sp, dst[tt * 128:(tt + 1) * 128, :], ob[i3][:, :], R=[obB[i3]], W=[self.db("vs" if tag == 7 else "vw", tt)])

            blocks = [(wb * 512, 512, "F", wb) for wb in (0, 1, 2, 3, 4, 5, 6, 8)]
            blocks += [(7 * 512, 512, "T", 7), (9 * 512, 512, "T", 9), (5120, 48, "T", 10)]
            import os
            if os.environ.get("TB"):
                keep = [int(x) for x in os.environ["TB"].split(",")]
                blocks = [b for b in blocks if b[3] in keep]
            self.proj(0, self.a_w_in.ap()[l], blocks, fF, fT)

    def nsa_compress(self, l, es_outer):
        k = self.k
        nc = self.nc
        kcmpT = es_outer.enter_context(self.sbt("kcmpT", [128, G, 256], BF16))
        kcB = Buf()
        vext = es_outer.enter_context(self.sbt("vext", [128, G, 2, 193], BF16))
        vxB = Buf()
        self.E(k.pool, "memset", kcmpT[:, :, :], 0.0, W=[kcB])
        self.E(k.pool, "memset", vext[:, :, :, :], 0.0, W=[vxB])
        for g in range(G):
            k.dma(k.sp, vext[:, g, :, 128:193], self.c_vext[:, :, :], W=[vxB])
        with ExitStack() as es:
            src = [es.enter_context(self.sbt("csrc%d" % i, [128, T], BF16)) for i in range(2)]
            srcB = [Buf() for _ in range(2)]
            w1 = es.enter_context(self.sbt("cw1", [128, 32, 256], BF16))
            w1B = Buf()
            w2 = es.enter_context(self.sbt("cw2", [128, 2, 128], BF16))
            w2B = Buf()
            pef = es.enter_context(self.sbt("pef", [128, 32], F32))
            pefB = Buf()
            perep = es.enter_context(self.sbt("perep", [128, 32, 128], BF16))
            peB = Buf()
            xs = es.enter_context(self.sbt("gxs", [128, 256], F32))
            xsB = Buf()
            u = es.enter_context(self.sbt("gu", [128, 256], F32))
            uB = Buf()
            sg = es.enter_context(self.sbt("gsg", [128, 256], F32))
            sgB = Buf()
            hid = es.enter_context(self.sbt("ghid", [128, 256], BF16))
            hidB = Buf()
            hidT = es.enter_context(self.sbt("ghidT", [128, 2, 128], BF16))
            hidTB = Buf()
            n = 0
            for kv in range(2):
                k.dma(k.pool, w1[:, :, :], self.a_cmp_w1.ap()[l, kv].rearrange("(c p) n -> p c n", p=128), W=[w1B])
                k.dma(k.pool, w2[:, :, :], self.a_cmp_w2.ap()[l, kv].rearrange("(c p) n -> p c n", p=128), W=[w2B])
                k.dma(k.sp, pef[:, :], self.a_cmp_peT.ap()[l, kv], W=[pefB])
                self.E(k.dve, "tensor_copy", perep[:, :, :], pef[:, :].unsqueeze(2).to_broadcast([128, 32, 128]), R=[pefB], W=[peB])
                srcd = self.kcT if kv == 0 else self.vcT
                sname = "kcT" if kv == 0 else "vcT"
                for g in range(G):
                    sj = n % 2
                    n += 1
                    k.dma(k.sp, src[sj][:, :], srcd[g, :, :], R=[self.db(sname, (g, tb)) for tb in range(8)], W=[srcB[sj]])
                    for half in range(2):
                        nr = 128 if half == 0 else 127
                        n0 = half * 128
                        pi = 0
                        P = self.ps[pi]
                        PB = self.psB[pi]
                        for li in range(32):
                            self.E(k.pe, "matmul", P[0:nr, 0:256], perep[:, li, 0:nr], w1[:, li, :], start=(li == 0), stop=False,
                                   R=[peB, w1B], W=[PB])
                        for li in range(32):
                            a0 = li + 16 * n0
                            self.E(k.pe, "matmul", P[0:nr, 0:256], src[sj][:, a0:a0 + 16 * (nr - 1) + 1:16], w1[:, li, :], start=False, stop=(li == 31),
                                   R=[srcB[sj], w1B], W=[PB])
                        self.E(k.act, "copy", xs[0:nr, :], P[0:nr, 0:256], R=[PB], W=[xsB])
                        self.E(k.dve, "tensor_tensor", u[0:nr, :], xs[0:nr, :], xs[0:nr, :], ALU.mult, R=[xsB], W=[uB])
                        self.E(k.dve, "tensor_scalar", u[0:nr, :], u[0:nr, :], 0.044715, 1.0, ALU.mult, ALU.add, R=[uB], W=[uB])
                        self.E(k.dve, "tensor_tensor", u[0:nr, :], u[0:nr, :], xs[0:nr, :], ALU.mult, R=[uB, xsB], W=[uB])
                        self.E(k.act, "activation", sg[0:nr, :], u[0:nr, :], AF.Sigmoid, scale=1.5957691216057308, R=[uB], W=[sgB])
                        if nr < 128:
                            self.E(k.pool, "memset", hid[:, :], 0.0, W=[hidB])
                        self.E(k.dve, "tensor_tensor", hid[0:nr, :], xs[0:nr, :], sg[0:nr, :], ALU.mult, R=[xsB, sgB], W=[hidB])
                        pb = self.psbf(1)
                        for c in range(2):
                            self.E(k.pe, "transpose", pb[:, c * 128:(c + 1) * 128], hid[:, c * 128:(c + 1) * 128], self.ident(),
                                   R=[hidB, self.cstB], W=[self.psB[1]])
                        self.E(k.act, "copy", hidT[:, :, :], pb[:, 0:256].rearrange("p (c n) -> p c n", c=2), R=[self.psB[1]], W=[hidTB])
                        P2 = self.ps[2]
                        if kv == 0:
                            for c in range(2):
                                self.E(k.pe, "matmul", P2[:, 0:nr], w2[:, c, :], hidT[:, c, 0:nr], start=(c == 0), stop=(c == 1),
                                       R=[w2B, hidTB], W=[self.psB[2]])
                            self.E(k.act, "copy", kcmpT[:, g, n0:n0 + nr], P2[:, 0:nr], R=[self.psB[2]], W=[kcB])
                        else:
                            for c in range(2):
                                self.E(k.pe, "matmul", P2[0:nr, 0:128], hidT[:, c, 0:nr], w2[:, c, :], start=(c == 0), stop=(c == 1),
                                       R=[w2B, hidTB], W=[self.psB[2]])
                            self.E(k.act, "copy", vext[0:nr, g, half, 0:128], P2[0:nr, 0:128], R=[self.psB[2]], W=[vxB])
            k.fence()
        return kcmpT, kcB, vext, vxB

    def attn_pair(self, kT, kB, q, qB, vx, vxB_, nv, mask, mB, acc, first, last, pt, ptB, spi, nrows=128):
        k = self.k
        P = self.ps[spi]
        PB = self.psB[spi]
        self.E(k.pe, "matmul", P[0:nrows, :], kT, q, start=True, stop=True, R=[kB, qB], W=[PB])
        self.E(k.act, "activation", pt[0:nrows, :], P[0:nrows, :], AF.Exp, scale=SCALE, R=[PB], W=[ptB])
        if mask is not None:
            self.E(k.dve, "tensor_tensor", pt[0:nrows, :].rearrange("p (r t) -> p r t", r=4), pt[0:nrows, :].rearrange("p (r t) -> p r t", r=4),
                   mask.unsqueeze(1).to_broadcast([nrows, 4, 128]), ALU.mult, R=[ptB, mB], W=[ptB])
        for r in range(4):
            self.E(k.pe, "matmul", self.ps[acc[r]][:, 0:nv], pt[0:nrows, r * 128:(r + 1) * 128], vx, start=first, stop=last,
                   R=[ptB, vxB_], W=[self.psB[acc[r]]])

    def nsa_attn(self, l):
        k = self.k
        nc = self.nc
        with ExitStack() as es0:
            kcmpT, kcB, vext, vxB = self.nsa_compress(l, es0)
            with ExitStack() as es:
                sb = lambda name, shape, dtp: es.enter_context(self.sbt(name, shape, dtp))
                maskc_r = [sb("maskc%d" % i, [128, 2, 128], BF16) for i in range(2)]
                mcB_r = [Buf() for _ in range(2)]
                selm_r = [sb("selm%d" % i, [128, 2, 64], F32) for i in range(2)]
                smB_r = [Buf() for _ in range(2)]
                qg = sb("qg", [128, NT, 512], BF16)
                qgB = Buf()
                ksg = sb("ksg", [128, T], BF16)
                ksB = Buf()
                kwg = sb("kwg", [128, T], BF16)
                kwB = Buf()
                vsg = sb("vsg", [128, NT, 129], BF16)
                vsB = Buf()
                vwg = sb("vwg", [128, NT, 129], BF16)
                vwB = Buf()
                self.E(k.pool, "memset", vsg[:, :, 128:129], 1.0, W=[vsB])
                self.E(k.pool, "memset", vwg[:, :, 128:129], 1.0, W=[vwB])
                pt = [sb("pt%d" % i, [128, 512], BF16) for i in range(3)]
                ptB = [Buf() for _ in range(3)]
                msk = [sb("msk%d" % i, [128, NT, 128], BF16) for i in range(2)]
                mskB = [Buf() for _ in range(2)]
                ot = [sb("ot%d" % i, [128, 512], F32) for i in range(2)]
                otB = [Buf() for _ in range(2)]
                otb = [sb("otb%d" % i, [128, 512], BF16) for i in range(2)]
                otbB = [Buf() for _ in range(2)]
                st = [sb("ast%d" % i, [128, 16], F32) for i in range(2)]
                stB = [Buf() for _ in range(2)]
                imp = sb("imp", [128, 64], F32)
                impB = Buf()
                vv = sb("vv", [128, 64], F32)
                vvB = Buf()
                cmpt = sb("cmpt", [128, 64, 64], BF16)
                cmpB = Buf()
                cnt = sb("cnt", [128, 64], F32)
                cntB = Buf()
                selb = sb("selb", [128, 64], BF16)
                selbB = Buf()
                selT_t = [sb("selTt%d" % i, [64, 128], BF16) for i in range(2)]
                selTB = [Buf() for _ in range(2)]
                w = {"hTt": [sb("ohTt%d" % i, [128, 4, 128], BF16) for i in range(2)], "hTtB": [Buf() for _ in range(2)]}
                acc = (4, 5, 6, 7)
                npair = 0

                def finish_branch(tt, g, gi, nv, stt, sB_, first_out, o, oB, prev=None, prevB=None):
                    for r in range(4):
                        A = self.ps[acc[r]]
                        AB = self.psB[acc[r]]
                        c0 = r * 3
                        self.E(k.dve, "tensor_scalar", stt[:, c0:c0 + 1], A[:, 128:129], 1e-30, None, ALU.max, R=[AB], W=[sB_])
                        self.E(k.dve, "reciprocal", stt[:, c0 + 1:c0 + 2], stt[:, c0:c0 + 1], R=[sB_], W=[sB_])
                        self.E(k.dve, "tensor_tensor", stt[:, c0 + 2:c0 + 3], stt[:, c0 + 1:c0 + 2],
                               self.gate_sb[:, tt, (g * 4 + r) * 3 + gi:(g * 4 + r) * 3 + gi + 1], ALU.mult, R=[sB_, self.gateB], W=[sB_])
                        if prev is None:
                            self.E(k.dve, "tensor_scalar", o[:, r * 128:(r + 1) * 128], A[:, 0:128], stt[:, c0 + 2:c0 + 3], None, ALU.mult,
                                   R=[AB, sB_], W=[oB])
                        else:
                            self.E(k.dve, "scalar_tensor_tensor", o[:, r * 128:(r + 1) * 128], A[:, 0:128], stt[:, c0 + 2:c0 + 3],
                                   prev[:, r * 128:(r + 1) * 128], ALU.mult, ALU.add, R=[AB, sB_, prevB], W=[oB])

                for g in range(G):
                    k.dma(k.sp, qg[:, :, :], self.qT[g].rearrange("p a r t -> p a (r t)"), R=[self.db("qT", (g, tb)) for tb in range(8)], W=[qgB])
                    k.dma(k.sp, ksg[:, :], self.ksT[g, :, :], R=[self.db("ksT", (g, tb)) for tb in range(8)], W=[ksB])
                    k.dma(k.sp, kwg[:, :], self.kwT[g, :, :], R=[self.db("kwT", (g, tb)) for tb in range(8)], W=[kwB])
                    k.dma(k.sp, vsg[:, :, 0:128], self.vs[:, g * 128:(g + 1) * 128].rearrange("(j p) c -> p j c", p=128),
                          R=[self.db("vs", tt) for tt in range(NT)], W=[vsB])
                    k.dma(k.sp, vwg[:, :, 0:128], self.vw[:, g * 128:(g + 1) * 128].rearrange("(j p) c -> p j c", p=128),
                          R=[self.db("vw", tt) for tt in range(NT)], W=[vwB])
                    for tt in range(NT):
                        q = qg[:, tt, :]
                        maskc = maskc_r[tt % 2]
                        mcB = mcB_r[tt % 2]
                        selm = selm_r[tt % 2]
                        smB = smB_r[tt % 2]
                        k.dma(k.sp, maskc[:, :, :], self.c_maskc[:, :, tt * 128:(tt + 1) * 128], W=[mcB])
                        k.dma(k.sp, selm[:, :, :], self.c_selm[:, :, tt, :], W=[smB])
                        halves = [0] + ([1] if tt >= 16 else [])
                        for hi, half in enumerate(halves):
                            need_mask = not (half == 0 and tt >= 17)
                            m = maskc[:, half, :] if need_mask else None
                            pi = npair % 3
                            self.attn_pair(kcmpT[:, g, half * 128:(half + 1) * 128], kcB, q, qgB, vext[:, g, half, :], vxB, 193,
                                           m, mcB, acc, hi == 0, hi == len(halves) - 1, pt[pi], ptB[pi], npair % 2)
                            npair += 1
                        j = tt % 2
                        stt = st[j]
                        sB_ = stB[j]
                        finish_branch(tt, g, 0, 193, stt, sB_, True, ot[j], otB[j])
                        for r in range(4):
                            A = self.ps[acc[r]]
                            AB = self.psB[acc[r]]
                            if r == 0:
                                self.E(k.dve, "tensor_scalar", imp[:, :], A[:, 129:193], stt[:, 1:2], None, ALU.mult, R=[AB, sB_], W=[impB])
                            else:
                                self.E(k.dve, "scalar_tensor_tensor", imp[:, :], A[:, 129:193], stt[:, r * 3 + 1:r * 3 + 2], imp[:, :],
                                       ALU.mult, ALU.add, R=[AB, sB_, impB], W=[impB])
                        k.dma(k.sp, self.opart[tt * 128:(tt + 1) * 128, g * 512:(g + 1) * 512], ot[j][:, :], R=[otB[j]], W=[self.db("opart", (g, tt))])
                        self.E(k.dve, "tensor_tensor", vv[:, :], imp[:, :], selm[:, 0, :], ALU.mult, R=[impB, smB], W=[vvB])
                        self.E(k.dve, "tensor_tensor", vv[:, :], vv[:, :], selm[:, 1, :], ALU.add, R=[vvB, smB], W=[vvB])
                        self.E(k.dve, "tensor_tensor", cmpt[:, :, :], vv[:, :].unsqueeze(1).to_broadcast([128, 64, 64]),
                               vv[:, :].unsqueeze(2).to_broadcast([128, 64, 64]), ALU.is_gt, R=[vvB], W=[cmpB])
                        self.E(k.dve, "reduce_sum", cnt[:, :], cmpt[:, :, :], AX.X, R=[cmpB], W=[cntB])
                        self.E(k.dve, "tensor_scalar", selb[:, :], cnt[:, :], 15.5, None, ALU.is_lt, R=[cntB], W=[selbB])
                        pb = self.psbf(2)
                        self.E(k.pe, "transpose", pb[0:64, 0:128], selb[:, :], self.ident(), R=[selbB, self.cstB], W=[self.psB[2]])
                        self.E(k.act, "copy", selT_t[j][:, :], pb[0:64, 0:128], R=[self.psB[2]], W=[selTB[j]])
                        k.dma(k.sp, self.selT[g, :, tt * 128:(tt + 1) * 128], selT_t[j][:, :], R=[selTB[j]], W=[self.db("selT", (g, tt))])
                    for tt in range(NT):
                        q = qg[:, tt, :]
                        j = tt % 2
                        mk = msk[j]
                        mkB = mskB[j]
                        for b in range(2):
                            srcap = bass.AP(self.selT, g * 64 * T + b * T + tt * 128, [[0, 64], [2 * T, tt + 1], [1, 128]])
                            k.dma(k.sp, mk[b * 64:(b + 1) * 64, 0:tt + 1, :], srcap, R=[self.db("selT", (g, tt))], W=[mkB])
                        self.E(k.pool, "tensor_tensor", mk[:, tt, :], mk[:, tt, :], self.cst[:, 4, :], ALU.mult, R=[mkB, self.cstB], W=[mkB])
                        k.dma(k.sp, ot[j][:, :], self.opart[tt * 128:(tt + 1) * 128, g * 512:(g + 1) * 512], R=[self.db("opart", (g, tt))], W=[otB[j]])
                        stt = st[j]
                        sB_ = stB[j]
                        js = list(range(max(0, tt - 4), tt + 1))
                        for ji, jk in enumerate(js):
                            m = None
                            if jk == tt:
                                m = self.cst[:, 4, :]
                            elif jk == tt - 4:
                                m = self.cst[:, 5, :]
                            pi = npair % 3
                            self.attn_pair(kwg[:, jk * 128:(jk + 1) * 128], kwB, q, qgB, vwg[:, jk, :], vwB, 129,
                                           m, self.cstB, acc, ji == 0, ji == len(js) - 1, pt[pi], ptB[pi], npair % 2)
                            npair += 1
                        finish_branch(tt, g, 2, 129, stt, sB_, False, ot[j], otB[j], ot[j], otB[j])
                        for jk in range(tt + 1):
                            pi = npair % 3
                            self.attn_pair(ksg[:, jk * 128:(jk + 1) * 128], ksB, q, qgB, vsg[:, jk, :], vsB, 129,
                                           mk[:, jk, :], mkB, acc, jk == 0, jk == tt, pt[pi], ptB[pi], npair % 2)
                            npair += 1
                        finish_branch(tt, g, 1, 129, stt, sB_, False, ot[j], otB[j], ot[j], otB[j])
                        self.E(k.pool, "tensor_copy", otb[j][:, :], ot[j][:, :], R=[otB[j]], W=[otbB[j]])
                        self.transpose_to_hT(w, otb[j], otbB[j], 2, tt, pbanks=(2, 3), nch=4, ch0=g * 4)
                k.fence()

    def outproj(self, w2d, xsrc, xsname, xdst, xdname, vc, va, vb, hTdst):
        k = self.k
        nc = self.nc
        with ExitStack() as es:
            w = self.alloc_norm_tiles(es)
            ysb = [es.enter_context(self.sbt("ysb%d" % i, [128, D], F32)) for i in range(4)]
            yB = [Buf() for _ in range(4)]
            xt = [es.enter_context(self.sbt("xt%d" % i, [128, D], F32)) for i in range(2)]
            xB = [Buf() for _ in range(2)]
            hb = [es.enter_context(self.sbt("phT%d" % i, [128, KC, 512], BF16)) for i in range(2)]
            hB = [Buf() for _ in range(2)]
            cnt = 0
            for tb in range(T // 512):
                j = tb % 2
                k.dma(k.sp, hb[j][:, :, :], self.hT[2][:, :, tb * 512:(tb + 1) * 512],
                      R=[self.db("hT2", 4 * tb + s) for s in range(4)], W=[hB[j]])
                for cb in range(4):
                    wt, wB = self.wload(w2d[:, cb * 512:(cb + 1) * 512])
                    for s in range(4):
                        pi = 4 + cnt % 4
                        cnt += 1
                        for kk in range(KC):
                            self.E(k.pe, "matmul", self.ps[pi][:, :], hb[j][:, kk, s * 128:(s + 1) * 128], wt[:, kk, :],
                                   start=(kk == 0), stop=(kk == KC - 1), R=[wB, hB[j]], W=[self.psB[pi]])
                        self.E(k.act, "copy", ysb[s][:, cb * 512:(cb + 1) * 512], self.ps[pi][:, :], R=[self.psB[pi]], W=[yB[s]])
                for s in range(4):
                    tt = 4 * tb + s
                    self.residual_tile(w, ysb[s][:, :], yB[s], xt[s % 2][:, :], xB[s % 2], xsrc, xsname, xdst, xdname, vc, tt)
                    if hTdst is not None:
                        self.prenorm_tile(w, xt[s % 2][:, :], xB[s % 2], va, vb, hTdst, tt)
            k.fence()

    def ffn(self, l, xsrc, xsname, xdst, xdname, vc, va, vb, hTdst):
        k = self.k
        nc = self.nc
        Win = self.ffn_w_in.ap()[l]
        Wout = self.ffn_w_out.ap()[l]
        NF = DFF // 128
        with ExitStack() as es:
            w = self.alloc_norm_tiles(es)
            ysb = [es.enter_context(self.sbt("ysb%d" % i, [128, D], F32)) for i in range(4)]
            yB = [Buf() for _ in range(4)]
            xt = [es.enter_context(self.sbt("xt%d" % i, [128, D], F32)) for i in range(1)]
            xB = [Buf() for _ in range(1)]
            hb = [es.enter_context(self.sbt("phT%d" % i, [128, KC, 512], BF16)) for i in range(1)]
            hB = [Buf() for _ in range(1)]
            actT = es.enter_context(self.sbt("actT", [128, NF, 512], BF16))
            aB = [Buf() for _ in range(11)]
            sg = [es.enter_context(self.sbt("fsg%d" % i, [128, 512], F32)) for i in range(2)]
            sgB = [Buf() for _ in range(2)]
            n = 0
            for tb in range(T // 512):
                j = 0
                k.dma(k.sp, hb[j][:, :, :], self.hT[1][:, :, tb * 512:(tb + 1) * 512],
                      R=[self.db("hT1", 4 * tb + s) for s in range(4)], W=[hB[j]])
                for fg in range(11):
                    wg, wgB = self.wload(Win[:, fg * 512:(fg + 1) * 512])
                    wu, wuB = self.wload(Win[:, DFF + fg * 512: DFF + (fg + 1) * 512])
                    for jj in range(4):
                        i2 = n % 2
                        n += 1
                        pg = i2
                        pu = 2 + i2
                        for kk in range(KC):
                            self.E(k.pe, "matmul", self.ps[pg][:, :], wg[:, kk, jj * 128:(jj + 1) * 128], hb[j][:, kk, :],
                                   start=(kk == 0), stop=(kk == KC - 1), R=[wgB, hB[j]], W=[self.psB[pg]])
                        for kk in range(KC):
                            self.E(k.pe, "matmul", self.ps[pu][:, :], wu[:, kk, jj * 128:(jj + 1) * 128], hb[j][:, kk, :],
                                   start=(kk == 0), stop=(kk == KC - 1), R=[wuB, hB[j]], W=[self.psB[pu]])
                        self.E(k.act, "activation", sg[i2][:, :], self.ps[pg][:, :], AF.Silu, R=[self.psB[pg]], W=[sgB[i2]])
                        self.E(k.dve, "tensor_tensor", actT[:, fg * 4 + jj, :], sg[i2][:, :], self.ps[pu][:, :], ALU.mult,
                               R=[sgB[i2], self.psB[pu]], W=[aB[fg]])
                for cb in range(4):
                    for kg in range(4):
                        wt, wB = self.wload(Wout[kg * 11 * 128:(kg + 1) * 11 * 128, cb * 512:(cb + 1) * 512], kc=11)
                        for s in range(4):
                            pi = 4 + s
                            for kk in range(11):
                                ch = kg * 11 + kk
                                self.E(k.pe, "matmul", self.ps[pi][:, :], actT[:, ch, s * 128:(s + 1) * 128], wt[:, kk, :],
                                       start=(ch == 0), stop=(ch == NF - 1), R=[wB, aB[ch // 4]], W=[self.psB[pi]])
                    for s in range(4):
                        self.E(k.act, "copy", ysb[s][:, cb * 512:(cb + 1) * 512], self.ps[4 + s][:, :], R=[self.psB[4 + s]], W=[yB[s]])
                for s in range(4):
                    tt = 4 * tb + s
                    self.residual_tile(w, ysb[s][:, :], yB[s], xt[0][:, :], xB[0], xsrc, xsname, xdst, xdname, vc, tt)
                    if hTdst is not None:
                        self.prenorm_tile(w, xt[0][:, :], xB[0], va, vb, hTdst, tt, pbanks=(0, 1))
            k.fence()

    def sb_kv(self, xsrc, xname):
        k = self.k
        self.modvec(1, self.kv_mod_w.ap(), 0, self.kv_mod_b, 0)
        self.modvec(0, self.kv_mod_w.ap(), D, self.kv_mod_b, D)
        self.load_gain(2, self.kv_norm_g, 0)
        self.E(k.dve, "scalar_tensor_tensor", self.vec[0][:, :], self.vec[0][:, :], 1.0, self.vec[2][:, :], ALU.add, ALU.mult,
               R=[self.vecB[0], self.vecB[2]], W=[self.vecB[0]])
        self.norm_phase(xsrc, xname, 0, 1, 2)
        nc = self.nc
        with ExitStack() as es:
            ob = [es.enter_context(self.sbt("ob%d" % i, [128, 512], BF16)) for i in range(3)]
            obB = [Buf() for _ in range(3)]
            st = {"n": 0}

            def fF(tag, jj, tb, pi):
                i3 = st["n"] % 3
                st["n"] += 1
                self.E(k.act, "copy", ob[i3][:, :], self.ps[pi][:, :], R=[self.psB[pi]], W=[obB[i3]])
                k.dma(k.sp, self.kTsh[tag * 4 + jj, :, tb * 512:(tb + 1) * 512], ob[i3][:, :], R=[obB[i3]], W=[self.db("kTsh", (tag * 4 + jj, tb))])

            def fT(tag, s, tb, pi):
                i3 = st["n"] % 3
                st["n"] += 1
                tt = 4 * tb + s
                self.E(k.dve, "tensor_copy", ob[i3][:, :], self.ps[pi][:, :], R=[self.psB[pi]], W=[obB[i3]])
                k.dma(k.sp, self.vsh[tt * 128:(tt + 1) * 128, tag * 512:(tag + 1) * 512], ob[i3][:, :], R=[obB[i3]], W=[self.db("vsh", (tag, tt))])

            blocks = [(c * 512, 512, "F", c) for c in range(4)] + [(D + c * 512, 512, "T", c) for c in range(4)]
            self.proj(2, self.kv_w.ap(), blocks, fF, fT)

    def sb_q(self, l2):
        k = self.k
        nc = self.nc
        with ExitStack() as es:
            ob = [es.enter_context(self.sbt("ob%d" % i, [128, 512], BF16)) for i in range(3)]
            obB = [Buf() for _ in range(3)]
            st = {"n": 0}

            def fF(tag, jj, tb, pi):
                i3 = st["n"] % 3
                st["n"] += 1
                self.E(k.act, "copy", ob[i3][:, :], self.ps[pi][:, :], R=[self.psB[pi]], W=[obB[i3]])
                k.dma(k.sp, self.qsb[tag * 4 + jj, :, tb * 512:(tb + 1) * 512], ob[i3][:, :], R=[obB[i3]], W=[self.db("qsb", (tag * 4 + jj, tb))])

            blocks = [(c * 512, 512, "F", c) for c in range(4)]
            self.proj(0, self.b_w_q.ap()[l2], blocks, fF, None)

    def sb_attn(self):
        k = self.k
        nc = self.nc
        with ExitStack() as es:
            sb = lambda name, shape, dtp: es.enter_context(self.sbt(name, shape, dtp))
            mf = sb("msbf", [128, 4, 512], F32)
            mfB = Buf()
            mb = sb("msbb", [128, 4, 512], BF16)
            mbB = Buf()
            k.dma(k.sp, mf[:, :, :], self.c_msb_f[:, :, :], W=[mfB])
            k.dma(k.sp, mb[:, :, :], self.c_msb_b[:, :, :], W=[mbB])
            kT = [sb("skT%d" % i, [128, T], BF16) for i in range(2)]
            kTB = [Buf() for _ in range(2)]
            qT = [sb("sqT%d" % i, [128, T], BF16) for i in range(2)]
            qTB = [Buf() for _ in range(2)]
            vh = [sb("svh%d" % i, [128, NT, 128], BF16) for i in range(2)]
            vhB = [Buf() for _ in range(2)]
            ee = [sb("see%d" % i, [128, 512], F32) for i in range(2)]
            eeB = [Buf() for _ in range(2)]
            sp = [sb("ssp%d" % i, [128, 512], F32) for i in range(2)]
            spB = [Buf() for _ in range(2)]
            spb = [sb("sspb%d" % i, [128, 512], BF16) for i in range(2)]
            spbB = [Buf() for _ in range(2)]
            t1 = [sb("st1_%d" % i, [128, 512], F32) for i in range(2)]
            t1B = [Buf() for _ in range(2)]
            t2 = [sb("st2_%d" % i, [128, 512], F32) for i in range(2)]
            t2B = [Buf() for _ in range(2)]
            pt = [sb("spt%d" % i, [128, 512], BF16) for i in range(2)]
            ptB = [Buf() for _ in range(2)]
            carry = sb("scarry", [128, 512], F32)
            cB = Buf()
            ob = [sb("sob%d" % i, [128, 512], BF16) for i in range(2)]
            obB = [Buf() for _ in range(2)]
            n = 0
            nq = 0
            for h in range(NH):
                hj = h % 2
                k.dma(k.sp, kT[hj][:, :], self.kTsh[h, :, :], R=[self.db("kTsh", (h, tb)) for tb in range(8)], W=[kTB[hj]])
                k.dma(k.sp, qT[hj][:, :], self.qsb[h, :, :], R=[self.db("qsb", (h, tb)) for tb in range(8)], W=[qTB[hj]])
                k.dma(k.sp, vh[hj][:, :, :], self.vsh[:, h * 128:(h + 1) * 128].rearrange("(j p) c -> p j c", p=128),
                      R=[self.db("vsh", (h // 4, tt)) for tt in range(NT)], W=[vhB[hj]])
                for qb in range(T // 512):
                    q = qT[hj][:, qb * 512:(qb + 1) * 512]
                    accp = 6 + nq % 2
                    nq += 1
                    jtop = 4 * qb + 3
                    for jk in range(jtop, -1, -1):
                        i = n % 2
                        n += 1
                        a = jk - 4 * qb
                        first = (jk == jtop)
                        last = (jk == 0)
                        Pz = self.ps[i]
                        PzB = self.psB[i]
                        Pa = self.ps[2 + i]
                        PaB = self.psB[2 + i]
                        Pt = self.ps[4 + i]
                        PtB = self.psB[4 + i]
                        self.E(k.pe, "matmul", Pz[:, :], kT[hj][:, jk * 128:(jk + 1) * 128], q, start=True, stop=True,
                               R=[kTB[hj], qTB[hj]], W=[PzB])
                        self.E(k.act, "activation", ee[i][:, :], Pz[:, :], AF.Exp, scale=SCALE, R=[PzB], W=[eeB[i]])
                        self.E(k.act, "activation", sp[i][:, :], ee[i][:, :], AF.Ln, bias=1.0, R=[eeB[i]], W=[spB[i]])
                        if a >= 0:
                            self.E(k.pool, "tensor_tensor", sp[i][:, :], sp[i][:, :], mf[:, a, :], ALU.mult, R=[spB[i], mfB], W=[spB[i]])
                        self.E(k.pool, "tensor_copy", spb[i][:, :], sp[i][:, :], R=[spB[i]], W=[spbB[i]])
                        self.E(k.pe, "matmul", Pa[:, :], self.cst[:, 3, :], spb[i][:, :], start=True, stop=True,
                               R=[self.cstB, spbB[i]], W=[PaB])
                        if not last:
                            self.E(k.pe, "matmul", Pt[:, :], self.cst[:, 2, :], spb[i][:, :], start=True, stop=True,
                                   R=[self.cstB, spbB[i]], W=[PtB])
                        self.E(k.dve, "scalar_tensor_tensor", t1[i][:, :], Pz[:, :], SCALE, sp[i][:, :], ALU.mult, ALU.subtract,
                               R=[PzB, spB[i]], W=[t1B[i]])
                        self.E(k.dve, "tensor_tensor", t2[i][:, :], t1[i][:, :], Pa[:, :], ALU.subtract, R=[t1B[i], PaB], W=[t2B[i]])
                        if not first:
                            self.E(k.pool, "tensor_tensor", t2[i][:, :], t2[i][:, :], carry[:, :], ALU.subtract, R=[t2B[i], cB], W=[t2B[i]])
                        self.E(k.act, "activation", pt[i][:, :], t2[i][:, :], AF.Exp, R=[t2B[i]], W=[ptB[i]])
                        if a >= 0:
                            self.E(k.pool, "tensor_tensor", pt[i][:, :], pt[i][:, :], mb[:, a, :], ALU.mult, R=[ptB[i], mbB], W=[ptB[i]])
                        self.E(k.pe, "matmul", self.ps[accp][:, :], vh[hj][:, jk, :], pt[i][:, :], start=first, stop=last,
                               R=[vhB[hj], ptB[i]], W=[self.psB[accp]])
                        if not last:
                            if first:
                                self.E(k.dve, "tensor_copy", carry[:, :], Pt[:, :], R=[PtB], W=[cB])
                            else:
                                self.E(k.dve, "tensor_tensor", carry[:, :], carry[:, :], Pt[:, :], ALU.add, R=[cB, PtB], W=[cB])
                    oj = nq % 2
                    self.E(k.act, "copy", ob[oj][:, :], self.ps[accp][:, :], R=[self.psB[accp]], W=[obB[oj]])
                    k.dma(k.sp, self.hT[2][:, h, qb * 512:(qb + 1) * 512], ob[oj][:, :], R=[obB[oj]],
                          W=[self.db("hT2", 4 * qb + s) for s in range(4)])
            k.fence()

    def build(self):
        k = self.k
        self.setup()
        xin, xin_name = self.x.ap(), "x"
        for l in range(self.nlayers):
            last = (l == self.nlayers - 1)
            if l == 0:
                self.mod_AB(l, 0, 0, 0, 1, 2)
                self.norm_phase(xin, xin_name, 0, 1, 0)
                if self.stop == "norm0":
                    return self.nc
            if l == 2:
                self.sb_kv(xin, xin_name)
            if l < 2:
                with ExitStack() as esg:
                    self.gate_sb = esg.enter_context(self.sbt("gate_sb", [128, NT, 48], F32))
                    self.nsa_proj(l)
                    if self.stop == "nsa_proj":
                        return self.nc
                    self.nsa_attn(l)
                    k.fence()
                if self.stop == "nsa_attn":
                    return self.nc
                wout = self.a_w_out.ap()[l]
            else:
                self.sb_q(l - 2)
                self.sb_attn()
                wout = self.b_w_out.ap()[l - 2]
            self.mod_C(l, 0, 1, 2, 3)
            self.mod_AB(l, 1, 2, 0, 1, 3)
            self.outproj(wout, xin, xin_name, self.xs1.ap(), "xs1", 2, 0, 1, 1)
            if self.stop == "outproj%d" % l:
                return self.nc
            self.mod_C(l, 1, 3, 2, 3)
            if not last:
                self.mod_AB(l + 1, 0, 0, 0, 1, 3)
            xdst, xdname = (self.out.ap(), "out") if last else (self.xs2.ap(), "xs2")
            self.ffn(l, self.xs1.ap(), "xs1", xdst, xdname, 2, 0, 1, None if last else 0)
            xin, xin_name = xdst, xdname
        k.fence()
        return self.nc


_CONSTS = None


def make_in_maps(inputs, ncores=8):
    global _CONSTS
    if _CONSTS is None:
        _CONSTS = make_consts()
    f = lambda a: np.ascontiguousarray(np.asarray(a, dtype=np.float32))
    shared = {
        "mod_w": f(inputs["mod_w"]), "mod_b": f(inputs["mod_b"]), "norm_g": f(inputs["norm_g"]),
        "ffn_w_in": f(inputs["ffn_w_in"]), "ffn_w_out": f(inputs["ffn_w_out"]), "a_w_in": f(inputs["a_w_in"]),
        "a_gate_b": f(inputs["a_gate_b"]),
        "a_cmp_peT": np.ascontiguousarray(np.asarray(inputs["a_cmp_pe"], dtype=np.float32).transpose(0, 1, 3, 2)),
        "a_cmp_w1": f(inputs["a_cmp_w1"]), "a_cmp_w2": f(inputs["a_cmp_w2"]), "a_w_out": f(inputs["a_w_out"]),
        "b_w_q": f(inputs["b_w_q"]), "b_w_out": f(inputs["b_w_out"]),
        "kv_norm_g": f(inputs["kv_norm_g"]).reshape(1, D), "kv_mod_w": f(inputs["kv_mod_w"]),
        "kv_mod_b": f(inputs["kv_mod_b"]).reshape(1, 2 * D), "kv_w": f(inputs["kv_w"]),
    }
    shared.update(_CONSTS)
    x = np.asarray(inputs["x"], dtype=np.float32)
    c = np.asarray(inputs["c"], dtype=np.float32)
    maps = []
    for core in range(ncores):
        b = core % 4
        m = dict(shared)
        m["x"] = np.ascontiguousarray(x[b])
        m["cT"] = np.ascontiguousarray(c[b].reshape(KC, 128).T)
        maps.append(m)
    return maps


def kernel(**inputs):
    prog = Prog()
    nc = prog.build()
    maps = make_in_maps(inputs)
    res = run_bass_kernel_spmd(nc, maps, core_ids=list(range(8)))
    out = np.stack([np.asarray(res.results[b]["out"], dtype=np.float32) for b in range(4)], axis=0)
    return out
```

```python
from contextlib import ExitStack
import numpy as np
import ml_dtypes
import concourse.bass as bass
import concourse.mybir as mybir
from concourse.bass_utils import run_bass_kernel_spmd

F32 = mybir.dt.float32
BF16 = mybir.dt.bfloat16
AF = mybir.ActivationFunctionType
ALU = mybir.AluOpType
AX = mybir.AxisListType

SAME_ENGINE_SYNC = True

D = 2048
T = 4096
NT = T // 128
KC = 16
DFF = 5632
NH = 16
G = 4
EPS = 1e-6
SCALE = 128 ** -0.5
NCMP = 255
NAIN = 5168


class Buf:
    __slots__ = ("name", "w", "r", "ps")

    def __init__(self, name="", ps=False):
        self.name = name
        self.w = None
        self.r = []
        self.ps = ps


class Eng:
    def __init__(self, nc, e, name, pe=False, ndma=0):
        self.e = e
        self.name = name
        self.sem = nc.alloc_semaphore("s_" + name)
        self.cnt = 0
        self.seen = {}
        self.pe = pe
        self.dsems = [[nc.alloc_semaphore("d_%s%d" % (name, i)), 0] for i in range(ndma)]
        self.dnext = 0


class K:
    def __init__(self):
        nc = bass.Bass("TRN2", target_bir_lowering=False)
        self.nc = nc
        self.pe = Eng(nc, nc.tensor, "pe", pe=True)
        self.act = Eng(nc, nc.scalar, "act")
        self.dve = Eng(nc, nc.vector, "dve")
        self.pool = Eng(nc, nc.gpsimd, "pool", ndma=8)
        self.sp = Eng(nc, nc.sync, "sp", ndma=16)
        self.engs = [self.pe, self.act, self.dve, self.pool, self.sp]
        self.nins = 0
        self.nwait = 0

    def _wait(self, eng, tok):
        sem, val = tok
        if sem is eng.sem and (eng.pe or not SAME_ENGINE_SYNC):
            return
        key = id(sem)
        if eng.seen.get(key, 0) >= val:
            return
        eng.e.wait_ge(sem, val)
        eng.seen[key] = val
        self.nwait += 1

    def _deps(self, eng, R, W):
        for b in R:
            if b.w is not None:
                self._wait(eng, b.w)
        for b in W:
            if b.w is not None:
                self._wait(eng, b.w)
            for t in b.r:
                self._wait(eng, t)

    def _commit(self, tok, R, W):
        for b in W:
            b.w = tok
            b.r = []
        for b in R:
            b.r.append(tok)
            if len(b.r) > 16:
                d = {}
                for s, v in b.r:
                    kk = id(s)
                    if kk not in d or d[kk][1] < v:
                        d[kk] = (s, v)
                b.r = list(d.values())

    def op(self, eng, fn, R=(), W=()):
        px = [b for b in R if b.ps and b not in W]
        if px:
            W = list(W) + px
            R = [b for b in R if not b.ps]
        self._deps(eng, R, W)
        ins = fn(eng.e)
        eng.cnt += 1
        ins.then_inc(eng.sem, 1)
        self._commit((eng.sem, eng.cnt), R, W)
        self.nins += 1
        return ins

    def dma(self, q, out, in_, R=(), W=(), **kw):
        self._deps(q, R, W)
        slot = q.dsems[q.dnext]
        q.dnext = (q.dnext + 1) % len(q.dsems)
        if slot[1] > 0:
            self._wait(q, (slot[0], slot[1]))
        ins = q.e.dma_start(out=out, in_=in_, **kw)
        slot[1] += 16
        ins.then_inc(slot[0], 16)
        self._commit((slot[0], slot[1]), R, W)
        self.nins += 1
        return ins

    def fence(self):
        for e in self.engs:
            for o in self.engs:
                if o is not e and o.cnt > 0:
                    self._wait(e, (o.sem, o.cnt))
                for s in o.dsems:
                    if s[1] > 0:
                        self._wait(e, (s[0], s[1]))


def _bf(a):
    return np.ascontiguousarray(a).astype(ml_dtypes.bfloat16)


def make_consts():
    c = {}
    p = np.arange(128)
    ident = np.eye(128, dtype=np.float32)
    rot = np.zeros((128, 128), np.float32)
    for dp in range(64):
        rot[dp + 64, dp] = -1.0
        rot[dp, dp + 64] = 1.0
    ones = np.ones((128, 128), np.float32)
    ltri = (p[:, None] > p[None, :]).astype(np.float32)
    causal = (p[:, None] <= p[None, :]).astype(np.float32)
    anti = (p[:, None] > p[None, :]).astype(np.float32)
    trin = -(ltri + ident) / np.float32(SCALE)
    c["cst_bf"] = _bf(np.stack([ident, rot, ones, ltri, causal, anti, trin], axis=1))
    t = np.arange(T, dtype=np.float64)
    inv = 10000.0 ** (-np.arange(64, dtype=np.float64) / 64)
    ang = (t[None, :].astype(np.float32) * inv.astype(np.float32)[:, None]).astype(np.float32)
    cs = np.stack([np.concatenate([np.cos(ang), np.cos(ang)], 0), np.concatenate([np.sin(ang), np.sin(ang)], 0)], axis=1)
    c["cossin"] = np.ascontiguousarray(cs.astype(np.float32))
    n = np.arange(256)
    cmp_end = n * 16 + 31
    mc = ((cmp_end[:, None] <= np.arange(T)[None, :]) & (n[:, None] < NCMP)).astype(np.float32)
    c["maskc"] = _bf(mc.reshape(2, 128, T).transpose(1, 0, 2))
    c_start = n[:, None] * 16
    s_start = np.arange(64)[None, :] * 64
    ov = ((c_start < s_start + 64) & (c_start + 32 > s_start) & (n[:, None] < NCMP)).astype(np.float32)
    ext = np.zeros((256, 65), np.float32)
    ext[:NCMP, 0] = 1.0
    ext[:, 1:] = ov
    c["vext"] = _bf(ext.reshape(2, 128, 65).transpose(1, 0, 2))
    tt = np.arange(T)
    blk = np.arange(64)
    cur = tt // 64
    forced = (blk[None, :] == 0) | (blk[None, :] == cur[:, None]) | (blk[None, :] == cur[:, None] - 1)
    avail = blk[None, :] * 64 <= tt[:, None]
    availm = (avail & ~forced).astype(np.float32)
    forcem = np.where(forced, 1e9 * (1.0 + (blk[None, :] == 0) + 2.0 * (blk[None, :] == cur[:, None])), np.where(avail, 0.0, -1e9)).astype(np.float32)
    c["selm"] = np.ascontiguousarray(np.stack([availm, forcem], 0).reshape(2, NT, 128, 64).transpose(2, 0, 1, 3))
    kk = np.arange(128)[:, None]
    qq = np.arange(512)[None, :]
    msb = np.stack([((a * 128 + kk) < qq) for a in range(4)], axis=1).astype(np.float32)
    c["msb_f"] = np.ascontiguousarray(msb)
    c["msb_b"] = _bf(msb)
    return c


class Prog:
    def __init__(self, nlayers=4, dbg=(), stop=None):
        self.stop = stop
        self.k = K()
        k = self.k
        nc = k.nc
        self.nc = nc
        self.dbg = set(dbg)
        self.nlayers = nlayers
        dt = nc.dram_tensor
        self.inp = {}

        def din(name, shape, dtype=F32):
            self.inp[name] = dt(name, list(shape), dtype, kind="ExternalInput")
            return self.inp[name]

        self.x = din("x", [T, D])
        self.cT = din("cT", [128, KC])
        self.mod_w = din("mod_w", [4, D, 6 * D])
        self.mod_b = din("mod_b", [4, 6 * D])
        self.norm_g = din("norm_g", [4, 4, D])
        self.ffn_w_in = din("ffn_w_in", [4, D, 2 * DFF])
        self.ffn_w_out = din("ffn_w_out", [4, DFF, D])
        self.a_w_in = din("a_w_in", [2, D, NAIN])
        self.a_gate_b = din("a_gate_b", [2, 48])
        self.a_cmp_peT = din("a_cmp_peT", [2, 2, 128, 32])
        self.a_cmp_w1 = din("a_cmp_w1", [2, 2, 4096, 256])
        self.a_cmp_w2 = din("a_cmp_w2", [2, 2, 256, 128])
        self.a_w_out = din("a_w_out", [2, D, D])
        self.b_w_q = din("b_w_q", [2, D, D])
        self.b_w_out = din("b_w_out", [2, D, D])
        self.kv_norm_g = din("kv_norm_g", [1, D])
        self.kv_mod_w = din("kv_mod_w", [D, 2 * D])
        self.kv_mod_b = din("kv_mod_b", [1, 2 * D])
        self.kv_w = din("kv_w", [D, 2 * D])
        self.c_cst = din("cst_bf", [128, 7, 128], BF16)
        self.c_cossin = din("cossin", [128, 2, T])
        self.c_maskc = din("maskc", [128, 2, T], BF16)
        self.c_vext = din("vext", [128, 2, 65], BF16)
        self.c_selm = din("selm", [128, 2, NT, 64])
        self.c_msb_f = din("msb_f", [128, 4, 512])
        self.c_msb_b = din("msb_b", [128, 4, 512], BF16)
        self.out = dt("out", [T, D], F32, kind="ExternalOutput")

        def scr(name, shape, dtype):
            kind = "ExternalOutput" if name in self.dbg else "Internal"
            return dt(name, list(shape), dtype, kind=kind)

        self.xs1 = scr("xs1", [T, D], F32)
        self.xs2 = scr("xs2", [T, D], F32)
        self.hT = [scr("hT%d" % i, [128, KC, T], BF16) for i in range(3)]
        self.qT = scr("qT", [G, 128, NT, 4, 128], BF16)
        self.kcT = scr("kcT", [G, 128, T], BF16)
        self.vcT = scr("vcT", [G, 128, T], BF16)
        self.ksT = scr("ksT", [G, 128, T], BF16)
        self.kwT = scr("kwT", [G, 128, T], BF16)
        self.vs = scr("vs", [T, 512], BF16)
        self.vw = scr("vw", [T, 512], BF16)
        self.opart = scr("opart", [T, D], F32)
        self.selT = scr("selT", [G, 64, T], BF16)
        self.kTsh = scr("kTsh", [NH, 128, T], BF16)
        self.vsh = scr("vsh", [T, D], BF16)
        self.qsb = scr("qsb", [NH, 128, T], BF16)
        self.dbufs = {}

        self.ps = [nc.alloc_psum_tensor("ps%d" % i, [128, 512], F32) for i in range(8)]
        self.psB = [Buf("ps%d" % i, ps=True) for i in range(8)]
        self.cst = nc.alloc_sbuf_tensor("cst", [128, 7, 128], BF16)
        self.cstB = Buf("cst")
        self.crep = nc.alloc_sbuf_tensor("crep", [128, KC, 128], BF16)
        self.crepB = Buf("crep")
        self.NW = 2
        self.wt = [nc.alloc_sbuf_tensor("wt%d" % i, [128, KC, 512], BF16) for i in range(self.NW)]
        self.wB = [Buf("wt%d" % i) for i in range(self.NW)]
        self.wi = 0
        self.vec = [nc.alloc_sbuf_tensor("vec%d" % i, [128, D], F32) for i in range(4)]
        self.vecB = [Buf("vec%d" % i) for i in range(4)]
        self.gate_sb = None
        self.gateB = Buf("gate")
        self.uid = 0

    def sbt(self, name, shape, dtype):
        self.uid += 1
        return self.nc.sbuf_tensor("%s_u%d" % (name, self.uid), shape, dtype)

    def db(self, name, i=0):
        key = (name, i)
        b = self.dbufs.get(key)
        if b is None:
            b = self.dbufs[key] = Buf("%s_%s" % key)
        return b

    def E(self, eng, meth, *a, R=(), W=(), **kw):
        return self.k.op(eng, lambda e: getattr(e, meth)(*a, **kw), R, W)

    def ident(self):
        return self.cst[:, 0, :]

    def bcast_row(self, handle, off, n):
        return bass.AP(handle, off, [[0, 128], [1, n]])

    def wload(self, src2d, kc=16, ncols=512):
        i = self.wi
        self.wi = (i + 1) % self.NW
        self.k.dma(self.k.pool, self.wt[i][:, 0:kc, 0:ncols], src2d.rearrange("(k p) n -> p k n", p=128), W=[self.wB[i]])
        return self.wt[i], self.wB[i]

    def psbf(self, i):
        return self.ps[i][:, :].bitcast(BF16)

    def setup(self):
        k = self.k
        nc = self.nc
        k.dma(k.sp, self.cst[:, :, :], self.c_cst[:, :, :], W=[self.cstB])
        with ExitStack() as es:
            cf = es.enter_context(self.sbt("cf", [128, KC], F32))
            ca = es.enter_context(self.sbt("ca", [128, KC], F32))
            cfB = Buf()
            caB = Buf()
            k.dma(k.sp, cf[:, :], self.cT[:, :], W=[cfB])
            self.E(k.act, "activation", ca[:, :], cf[:, :], AF.Silu, R=[cfB], W=[caB])
            self.E(k.dve, "tensor_copy", self.crep[:, :, :], ca[:, :].unsqueeze(2).to_broadcast([128, KC, 128]), R=[caB], W=[self.crepB])
            k.fence()

    def modvec(self, vi, w2d, col0, bias_handle, bias_off):
        k = self.k
        dst = self.vec[vi]
        dB = self.vecB[vi]
        k.dma(k.sp, dst[:, :], self.bcast_row(bias_handle, bias_off, D), W=[dB])
        for cb in range(4):
            wt, wB = self.wload(w2d[:, col0 + cb * 512: col0 + (cb + 1) * 512])
            pi = cb % 2
            for kk in range(KC):
                self.E(k.pe, "matmul", self.ps[pi][:, :], self.crep[:, kk, :], wt[:, kk, :], start=(kk == 0), stop=(kk == KC - 1),
                       R=[self.crepB, wB], W=[self.psB[pi]])
            self.E(k.dve, "tensor_tensor", dst[:, cb * 512:(cb + 1) * 512], self.ps[pi][:, :], dst[:, cb * 512:(cb + 1) * 512], ALU.add,
                   R=[self.psB[pi], dB], W=[dB])

    def load_gain(self, vi, handle, off):
        self.k.dma(self.k.sp, self.vec[vi][:, :], self.bcast_row(handle, off, D), W=[self.vecB[vi]])

    def mod_AB(self, l, which, gi, va, vb, vtmp):
        k = self.k
        w2d = self.mod_w.ap()[l]
        base = 0 if which == 0 else 3 * D
        self.modvec(vb, w2d, base, self.mod_b, l * 6 * D + base)
        self.modvec(va, w2d, base + D, self.mod_b, l * 6 * D + base + D)
        self.load_gain(vtmp, self.norm_g, (l * 4 + gi) * D)
        self.E(k.dve, "scalar_tensor_tensor", self.vec[va][:, :], self.vec[va][:, :], 1.0, self.vec[vtmp][:, :], ALU.add, ALU.mult,
               R=[self.vecB[va], self.vecB[vtmp]], W=[self.vecB[va]])

    def mod_C(self, l, which, gi, vc, vtmp):
        k = self.k
        w2d = self.mod_w.ap()[l]
        base = 2 * D if which == 0 else 5 * D
        self.modvec(vc, w2d, base, self.mod_b, l * 6 * D + base)
        self.load_gain(vtmp, self.norm_g, (l * 4 + gi) * D)
        self.E(k.dve, "tensor_tensor", self.vec[vc][:, :], self.vec[vc][:, :], self.vec[vtmp][:, :], ALU.mult,
               R=[self.vecB[vc], self.vecB[vtmp]], W=[self.vecB[vc]])

    def alloc_norm_tiles(self, es):
        nc = self.nc
        w = {}
        w["ss"] = [es.enter_context(self.sbt("ss%d" % i, [128, 8], F32)) for i in range(2)]
        w["ssB"] = [Buf() for _ in range(2)]
        w["hn"] = es.enter_context(self.sbt("hn", [128, D], F32))
        w["hnB"] = Buf()
        w["junk"] = w["hn"]
        w["junkB"] = w["hnB"]
        w["hb"] = [es.enter_context(self.sbt("hb%d" % i, [128, D], BF16)) for i in range(1)]
        w["hbB"] = [Buf() for _ in range(1)]
        w["hTt"] = [es.enter_context(self.sbt("hTt%d" % i, [128, KC, 128], BF16)) for i in range(1)]
        w["hTtB"] = [Buf() for _ in range(1)]
        w["i"] = 0
        return w

    def rstd(self, w, src, srcB, n):
        k = self.k
        i = w["i"] % 2
        w["i"] += 1
        ss = w["ss"][i]
        sB = w["ssB"][i]
        self.E(k.pool, "memset", ss[:, 0:1], 0.0, W=[sB])
        self.E(k.act, "activation", w["junk"][:, 0:n], src, AF.Square, accum_out=ss[:, 0:1], R=[srcB], W=[w["junkB"], sB])
        self.E(k.dve, "tensor_scalar", ss[:, 1:2], ss[:, 0:1], 1.0 / n, EPS, ALU.mult, ALU.add, R=[sB], W=[sB])
        self.E(k.act, "activation", ss[:, 2:3], ss[:, 1:2], AF.Sqrt, R=[sB], W=[sB])
        self.E(k.dve, "reciprocal", ss[:, 3:4], ss[:, 2:3], R=[sB], W=[sB])
        return ss[:, 3:4], sB

    def prenorm_tile(self, w, xt, xB, va, vb, hTi, tt, pbanks=(0, 1)):
        k = self.k
        r, rB = self.rstd(w, xt, xB, D)
        j = tt % len(w["hb"])
        self.E(k.dve, "scalar_tensor_tensor", w["hn"][:, :], xt, r, self.vec[va][:, :], ALU.mult, ALU.mult,
               R=[xB, rB, self.vecB[va]], W=[w["hnB"]])
        self.E(k.pool, "tensor_tensor", w["hb"][j][:, :], w["hn"][:, :], self.vec[vb][:, :], ALU.add,
               R=[w["hnB"], self.vecB[vb]], W=[w["hbB"][j]])
        self.transpose_to_hT(w, w["hb"][j], w["hbB"][j], hTi, tt, pbanks)

    def transpose_to_hT(self, w, src, srcB, hTi, tt, pbanks=(0, 1), nch=KC, ch0=0):
        k = self.k
        j = tt % len(w["hTt"])
        hTt = w["hTt"][j]
        hB = w["hTtB"][j]
        for half in range((nch + 7) // 8):
            pi = pbanks[half % len(pbanks)]
            n8 = min(8, nch - half * 8)
            pb = self.psbf(pi)
            for c in range(n8):
                cc = half * 8 + c
                self.E(k.pe, "transpose", pb[:, c * 128:(c + 1) * 128], src[:, cc * 128:(cc + 1) * 128], self.ident(),
                       R=[srcB, self.cstB], W=[self.psB[pi]])
            eng = k.act if half % 2 == 0 else k.dve
            meth = "copy" if eng is k.act else "tensor_copy"
            self.E(eng, meth, hTt[:, half * 8: half * 8 + n8, :], pb[:, 0:n8 * 128].rearrange("p (k n) -> p k n", k=n8),
                   R=[self.psB[pi]], W=[hB])
        k.dma(k.sp, self.hT[hTi][:, ch0:ch0 + nch, tt * 128:(tt + 1) * 128], hTt[:, 0:nch, :], R=[hB], W=[self.db("hT%d" % hTi, tt)])

    def norm_phase(self, xsrc, xname, va, vb, hTi):
        k = self.k
        nc = self.nc
        with ExitStack() as es:
            w = self.alloc_norm_tiles(es)
            xt = [es.enter_context(self.sbt("xt%d" % i, [128, D], F32)) for i in range(2)]
            xB = [Buf() for _ in range(2)]
            for tt in range(NT):
                j = tt % 2
                k.dma(k.sp, xt[j][:, :], xsrc[tt * 128:(tt + 1) * 128, :], R=[self.db(xname, tt)], W=[xB[j]])
                self.prenorm_tile(w, xt[j][:, :], xB[j], va, vb, hTi, tt)
            k.fence()

    def residual_tile(self, w, y, yB, xt, xB, xsrc, xsname, xdst, xdname, vc, tt):
        k = self.k
        k.dma(k.sp, xt, xsrc[tt * 128:(tt + 1) * 128, :], R=[self.db(xsname, tt)], W=[xB])
        r, rB = self.rstd(w, y, yB, D)
        self.E(k.dve, "scalar_tensor_tensor", w["hn"][:, :], y, r, self.vec[vc][:, :], ALU.mult, ALU.mult,
               R=[yB, rB, self.vecB[vc]], W=[w["hnB"]])
        self.E(k.pool, "tensor_tensor", xt, xt, w["hn"][:, :], ALU.add, R=[xB, w["hnB"]], W=[xB])
        k.dma(k.sp, xdst[tt * 128:(tt + 1) * 128, :], xt, R=[xB], W=[self.db(xdname, tt)])

    def proj(self, hTi, w2d, blocks, fF=None, fT=None, pF=(0, 1), pT=(2, 3)):
        k = self.k
        nc = self.nc
        with ExitStack() as es:
            hb = [es.enter_context(self.sbt("phT%d" % i, [128, KC, 512], BF16)) for i in range(2)]
            hB = [Buf() for _ in range(2)]
            cnt = 0
            for tb in range(T // 512):
                j = tb % 2
                k.dma(k.sp, hb[j][:, :, :], self.hT[hTi][:, :, tb * 512:(tb + 1) * 512],
                      R=[self.db("hT%d" % hTi, 4 * tb + s) for s in range(4)], W=[hB[j]])
                for (col0, ncols, mode, tag) in blocks:
                    wt, wB = self.wload(w2d[:, col0:col0 + ncols], ncols=ncols)
                    if mode == "F":
                        for jj in range(ncols // 128):
                            pi = pF[cnt % len(pF)]
                            cnt += 1
                            for kk in range(KC):
                                self.E(k.pe, "matmul", self.ps[pi][:, :], wt[:, kk, jj * 128:(jj + 1) * 128], hb[j][:, kk, :],
                                       start=(kk == 0), stop=(kk == KC - 1), R=[wB, hB[j]], W=[self.psB[pi]])
                            fF(tag, jj, tb, pi)
                    else:
                        for s in range(4):
                            pi = pT[cnt % len(pT)]
                            cnt += 1
                            for kk in range(KC):
                                self.E(k.pe, "matmul", self.ps[pi][:, 0:ncols], hb[j][:, kk, s * 128:(s + 1) * 128], wt[:, kk, 0:ncols],
                                       start=(kk == 0), stop=(kk == KC - 1), R=[wB, hB[j]], W=[self.psB[pi]])
                            fT(tag, s, tb, pi)
            k.fence()

    def nsa_proj(self, l):
        k = self.k
        nc = self.nc
        with ExitStack() as es:
            cs = [es.enter_context(self.sbt("cs%d" % i, [128, 2, 512], F32)) for i in range(2)]
            csB = [Buf() for _ in range(2)]
            raw = [es.enter_context(self.sbt("raw%d" % i, [128, 512], BF16)) for i in range(2)]
            rawB = [Buf() for _ in range(2)]
            t1 = [es.enter_context(self.sbt("t1_%d" % i, [128, 512], F32)) for i in range(2)]
            t1B = [Buf() for _ in range(2)]
            t2 = [es.enter_context(self.sbt("t2_%d" % i, [128, 512], F32)) for i in range(2)]
            t2B = [Buf() for _ in range(2)]
            ob = [es.enter_context(self.sbt("ob%d" % i, [128, 512], BF16)) for i in range(3)]
            obB = [Buf() for _ in range(3)]
            gb = es.enter_context(self.sbt("gb", [128, 48], F32))
            gbB = Buf()
            gt = es.enter_context(self.sbt("gt", [128, 48], F32))
            gtB = Buf()
            k.dma(k.sp, gb[:, :], self.bcast_row(self.a_gate_b, l * 48, 48), W=[gbB])
            st = {"n": 0, "tb": -1}
            featdst = {4: self.kcT, 5: self.vcT, 6: self.ksT, 8: self.kwT}
            featname = {4: "kcT", 5: "vcT", 6: "ksT", 8: "kwT"}

            def fF(tag, jj, tb, pi):
                n = st["n"]
                st["n"] += 1
                i2 = n % 2
                i3 = n % 3
                if st["tb"] != tb:
                    st["tb"] = tb
                    k.dma(k.sp, cs[tb % 2][:, :, :], self.c_cossin[:, :, tb * 512:(tb + 1) * 512], W=[csB[tb % 2]])
                c_ = cs[tb % 2]
                cB = csB[tb % 2]
                P = self.ps[pi]
                PB = self.psB[pi]
                if tag == 5:
                    self.E(k.act, "copy", ob[i3][:, :], P[:, :], R=[PB], W=[obB[i3]])
                else:
                    self.E(k.act, "copy", raw[i2][:, :], P[:, :], R=[PB], W=[rawB[i2]])
                    p2 = 4 + i2
                    self.E(k.pe, "matmul", self.ps[p2][:, :], self.cst[:, 1, :], raw[i2][:, :], start=True, stop=True,
                           R=[self.cstB, rawB[i2]], W=[self.psB[p2]])
                    self.E(k.dve, "tensor_tensor", t1[i2][:, :], P[:, :], c_[:, 0, :], ALU.mult, R=[PB, cB], W=[t1B[i2]])
                    self.E(k.dve, "tensor_tensor", t2[i2][:, :], self.ps[p2][:, :], c_[:, 1, :], ALU.mult, R=[self.psB[p2], cB], W=[t2B[i2]])
                    self.E(k.pool, "tensor_tensor", ob[i3][:, :], t1[i2][:, :], t2[i2][:, :], ALU.add, R=[t1B[i2], t2B[i2]], W=[obB[i3]])
                if tag < 4:
                    dst = self.qT[tag, :, 4 * tb:4 * tb + 4, jj, :]
                    k.dma(k.sp, dst, ob[i3][:, :].rearrange("p (a t) -> p a t", a=4), R=[obB[i3]], W=[self.db("qT", (tag, tb))])
                else:
                    k.dma(k.sp, featdst[tag][jj, :, tb * 512:(tb + 1) * 512], ob[i3][:, :], R=[obB[i3]], W=[self.db(featname[tag], (jj, tb))])

            def fT(tag, s, tb, pi):
                n = st["n"]
                st["n"] += 1
                i3 = n % 3
                P = self.ps[pi]
                PB = self.psB[pi]
                tt = 4 * tb + s
                if tag == 10:
                    self.E(k.dve, "tensor_tensor", gt[:, :], P[:, 0:48], gb[:, :], ALU.add, R=[PB, gbB], W=[gtB])
                    self.E(k.act, "activation", self.gate_sb[:, tt, :], gt[:, :], AF.Sigmoid, R=[gtB], W=[self.gateB])
                else:
                    dst = self.vs if tag == 7 else self.vw
                    self.E(k.act, "copy", ob[i3][:, :], P[:, :], R=[PB], W=[obB[i3]])
                    k.dma(k.sp, dst[tt * 128:(tt + 1) * 128, :], ob[i3][:, :], R=[obB[i3]], W=[self.db("vs" if tag == 7 else "vw", tt)])

            blocks = [(wb * 512, 512, "F", wb) for wb in (0, 1, 2, 3, 4, 5, 6, 8)]
            blocks += [(7 * 512, 512, "T", 7), (9 * 512, 512, "T", 9), (5120, 48, "T", 10)]
            import os
            if os.environ.get("TB"):
                keep = [int(x) for x in os.environ["TB"].split(",")]
                blocks = [b for b in blocks if b[3] in keep]
            self.proj(0, self.a_w_in.ap()[l], blocks, fF, fT)

    def nsa_compress(self, l, es_outer):
        k = self.k
        nc = self.nc
        kcmpT = es_outer.enter_context(self.sbt("kcmpT", [128, G, 256], BF16))
        kcB = Buf()
        vext = es_outer.enter_context(self.sbt("vext", [128, G, 2, 193], BF16))
        vxB = Buf()
        self.E(k.pool, "memset", kcmpT[:, :, :], 0.0, W=[kcB])
        self.E(k.pool, "memset", vext[:, :, :, :], 0.0, W=[vxB])
        for g in range(G):
            k.dma(k.sp, vext[:, g, :, 128:193], self.c_vext[:, :, :], W=[vxB])
        with ExitStack() as es:
            src = [es.enter_context(self.sbt("csrc%d" % i, [128, T], BF16)) for i in range(2)]
            srcB = [Buf() for _ in range(2)]
            w1 = es.enter_context(self.sbt("cw1", [128, 32, 256], BF16))
            w1B = Buf()
            w2 = es.enter_context(self.sbt("cw2", [128, 2, 128], BF16))
            w2B = Buf()
            pef = es.enter_context(self.sbt("pef", [128, 32], F32))
            pefB = Buf()
            perep = es.enter_context(self.sbt("perep", [128, 32, 128], BF16))
            peB = Buf()
            xs = es.enter_context(self.sbt("gxs", [128, 256], F32))
            xsB = Buf()
            u = es.enter_context(self.sbt("gu", [128, 256], F32))
            uB = Buf()
            sg = es.enter_context(self.sbt("gsg", [128, 256], F32))
            sgB = Buf()
            hid = es.enter_context(self.sbt("ghid", [128, 256], BF16))
            hidB = Buf()
            hidT = es.enter_context(self.sbt("ghidT", [128, 2, 128], BF16))
            hidTB = Buf()
            n = 0
            for kv in range(2):
                k.dma(k.pool, w1[:, :, :], self.a_cmp_w1.ap()[l, kv].rearrange("(c p) n -> p c n", p=128), W=[w1B])
                k.dma(k.pool, w2[:, :, :], self.a_cmp_w2.ap()[l, kv].rearrange("(c p) n -> p c n", p=128), W=[w2B])
                k.dma(k.sp, pef[:, :], self.a_cmp_peT.ap()[l, kv], W=[pefB])
                self.E(k.dve, "tensor_copy", perep[:, :, :], pef[:, :].unsqueeze(2).to_broadcast([128, 32, 128]), R=[pefB], W=[peB])
                srcd = self.kcT if kv == 0 else self.vcT
                sname = "kcT" if kv == 0 else "vcT"
                for g in range(G):
                    sj = n % 2
                    n += 1
                    k.dma(k.sp, src[sj][:, :], srcd[g, :, :], R=[self.db(sname, (g, tb)) for tb in range(8)], W=[srcB[sj]])
                    for half in range(2):
                        nr = 128 if half == 0 else 127
                        n0 = half * 128
                        pi = 0
                        P = self.ps[pi]
                        PB = self.psB[pi]
                        for li in range(32):
                            self.E(k.pe, "matmul", P[0:nr, 0:256], perep[:, li, 0:nr], w1[:, li, :], start=(li == 0), stop=False,
                                   R=[peB, w1B], W=[PB])
                        for li in range(32):
                            a0 = li + 16 * n0
                            self.E(k.pe, "matmul", P[0:nr, 0:256], src[sj][:, a0:a0 + 16 * (nr - 1) + 1:16], w1[:, li, :], start=False, stop=(li == 31),
                                   R=[srcB[sj], w1B], W=[PB])
                        self.E(k.act, "copy", xs[0:nr, :], P[0:nr, 0:256], R=[PB], W=[xsB])
                        self.E(k.dve, "tensor_tensor", u[0:nr, :], xs[0:nr, :], xs[0:nr, :], ALU.mult, R=[xsB], W=[uB])
                        self.E(k.dve, "tensor_scalar", u[0:nr, :], u[0:nr, :], 0.044715, 1.0, ALU.mult, ALU.add, R=[uB], W=[uB])
                        self.E(k.dve, "tensor_tensor", u[0:nr, :], u[0:nr, :], xs[0:nr, :], ALU.mult, R=[uB, xsB], W=[uB])
                        self.E(k.act, "activation", sg[0:nr, :], u[0:nr, :], AF.Sigmoid, scale=1.5957691216057308, R=[uB], W=[sgB])
                        if nr < 128:
                            self.E(k.pool, "memset", hid[:, :], 0.0, W=[hidB])
                        self.E(k.dve, "tensor_tensor", hid[0:nr, :], xs[0:nr, :], sg[0:nr, :], ALU.mult, R=[xsB, sgB], W=[hidB])
                        pb = self.psbf(1)
                        for c in range(2):
                            self.E(k.pe, "transpose", pb[:, c * 128:(c + 1) * 128], hid[:, c * 128:(c + 1) * 128], self.ident(),
                                   R=[hidB, self.cstB], W=[self.psB[1]])
                        self.E(k.act, "copy", hidT[:, :, :], pb[:, 0:256].rearrange("p (c n) -> p c n", c=2), R=[self.psB[1]], W=[hidTB])
                        P2 = self.ps[2]
                        if kv == 0:
                            for c in range(2):
                                self.E(k.pe, "matmul", P2[:, 0:nr], w2[:, c, :], hidT[:, c, 0:nr], start=(c == 0), stop=(c == 1),
                                       R=[w2B, hidTB], W=[self.psB[2]])
                            self.E(k.act, "copy", kcmpT[:, g, n0:n0 + nr], P2[:, 0:nr], R=[self.psB[2]], W=[kcB])
                        else:
                            for c in range(2):
                                self.E(k.pe, "matmul", P2[0:nr, 0:128], hidT[:, c, 0:nr], w2[:, c, :], start=(c == 0), stop=(c == 1),
                                       R=[w2B, hidTB], W=[self.psB[2]])
                            self.E(k.act, "copy", vext[0:nr, g, half, 0:128], P2[0:nr, 0:128], R=[self.psB[2]], W=[vxB])
            k.fence()
        return kcmpT, kcB, vext, vxB

    def attn_pair(self, kT, kB, q, qB, vx, vxB_, nv, mask, mB, acc, first, last, pt, ptB, spi, nrows=128):
        k = self.k
        P = self.ps[spi]
        PB = self.psB[spi]
        self.E(k.pe, "matmul", P[0:nrows, :], kT, q, start=True, stop=True, R=[kB, qB], W=[PB])

        def rest():
            self.E(k.act, "activation", pt[0:nrows, :], P[0:nrows, :], AF.Exp, scale=SCALE, R=[PB], W=[ptB])
            if mask is not None:
                self.E(k.dve, "tensor_tensor", pt[0:nrows, :].rearrange("p (r t) -> p r t", r=4), pt[0:nrows, :].rearrange("p (r t) -> p r t", r=4),
                       mask.unsqueeze(1).to_broadcast([nrows, 4, 128]), ALU.mult, R=[ptB, mB], W=[ptB])
            for r in range(4):
                self.E(k.pe, "matmul", self.ps[acc[r]][:, 0:nv], pt[0:nrows, r * 128:(r + 1) * 128], vx, start=first, stop=last,
                       R=[ptB, vxB_], W=[self.psB[acc[r]]])

        prev = getattr(self, "_pend", None)
        self._pend = rest
        if prev is not None:
            prev()

    def attn_flush(self):
        prev = getattr(self, "_pend", None)
        self._pend = None
        if prev is not None:
            prev()

    def nsa_attn(self, l):
        k = self.k
        nc = self.nc
        with ExitStack() as es0:
            kcmpT, kcB, vext, vxB = self.nsa_compress(l, es0)
            with ExitStack() as es:
                sb = lambda name, shape, dtp: es.enter_context(self.sbt(name, shape, dtp))
                maskc_r = [sb("maskc%d" % i, [128, 2, 128], BF16) for i in range(2)]
                mcB_r = [Buf() for _ in range(2)]
                selm_r = [sb("selm%d" % i, [128, 2, 64], F32) for i in range(2)]
                smB_r = [Buf() for _ in range(2)]
                qg = sb("qg", [128, NT, 512], BF16)
                qgB = Buf()
                ksg = sb("ksg", [128, T], BF16)
                ksB = Buf()
                kwg = sb("kwg", [128, T], BF16)
                kwB = Buf()
                vsg = sb("vsg", [128, NT, 129], BF16)
                vsB = Buf()
                vwg = sb("vwg", [128, NT, 129], BF16)
                vwB = Buf()
                self.E(k.pool, "memset", vsg[:, :, 128:129], 1.0, W=[vsB])
                self.E(k.pool, "memset", vwg[:, :, 128:129], 1.0, W=[vwB])
                pt = [sb("pt%d" % i, [128, 512], BF16) for i in range(3)]
                ptB = [Buf() for _ in range(3)]
                msk = [sb("msk%d" % i, [128, NT, 128], BF16) for i in range(2)]
                mskB = [Buf() for _ in range(2)]
                ot = [sb("ot%d" % i, [128, 512], F32) for i in range(2)]
                otB = [Buf() for _ in range(2)]
                otb = [sb("otb%d" % i, [128, 512], BF16) for i in range(2)]
                otbB = [Buf() for _ in range(2)]
                st = [sb("ast%d" % i, [128, 16], F32) for i in range(2)]
                stB = [Buf() for _ in range(2)]
                imp = sb("imp", [128, 64], F32)
                impB = Buf()
                vv = sb("vv", [128, 64], F32)
                vvB = Buf()
                cmpt = sb("cmpt", [128, 64, 64], BF16)
                cmpB = Buf()
                cnt = sb("cnt", [128, 64], F32)
                cntB = Buf()
                selb = sb("selb", [128, 64], BF16)
                selbB = Buf()
                selT_t = [sb("selTt%d" % i, [64, 128], BF16) for i in range(2)]
                selTB = [Buf() for _ in range(2)]
                w = {"hTt": [sb("ohTt%d" % i, [128, 4, 128], BF16) for i in range(2)], "hTtB": [Buf() for _ in range(2)]}
                acc = (4, 5, 6, 7)
                npair = 0

                def finish_branch(tt, g, gi, nv, stt, sB_, first_out, o, oB, prev=None, prevB=None):
                    for r in range(4):
                        A = self.ps[acc[r]]
                        AB = self.psB[acc[r]]
                        c0 = r * 3
                        self.E(k.dve, "tensor_scalar", stt[:, c0:c0 + 1], A[:, 128:129], 1e-30, None, ALU.max, R=[AB], W=[sB_])
                        self.E(k.dve, "reciprocal", stt[:, c0 + 1:c0 + 2], stt[:, c0:c0 + 1], R=[sB_], W=[sB_])
                        self.E(k.dve, "tensor_tensor", stt[:, c0 + 2:c0 + 3], stt[:, c0 + 1:c0 + 2],
                               self.gate_sb[:, tt, (g * 4 + r) * 3 + gi:(g * 4 + r) * 3 + gi + 1], ALU.mult, R=[sB_, self.gateB], W=[sB_])
                        if prev is None:
                            self.E(k.dve, "tensor_scalar", o[:, r * 128:(r + 1) * 128], A[:, 0:128], stt[:, c0 + 2:c0 + 3], None, ALU.mult,
                                   R=[AB, sB_], W=[oB])
                        else:
                            self.E(k.dve, "scalar_tensor_tensor", o[:, r * 128:(r + 1) * 128], A[:, 0:128], stt[:, c0 + 2:c0 + 3],
                                   prev[:, r * 128:(r + 1) * 128], ALU.mult, ALU.add, R=[AB, sB_, prevB], W=[oB])

                for g in range(G):
                    k.dma(k.sp, qg[:, :, :], self.qT[g].rearrange("p a r t -> p a (r t)"), R=[self.db("qT", (g, tb)) for tb in range(8)], W=[qgB])
                    k.dma(k.sp, ksg[:, :], self.ksT[g, :, :], R=[self.db("ksT", (g, tb)) for tb in range(8)], W=[ksB])
                    k.dma(k.sp, kwg[:, :], self.kwT[g, :, :], R=[self.db("kwT", (g, tb)) for tb in range(8)], W=[kwB])
                    k.dma(k.sp, vsg[:, :, 0:128], self.vs[:, g * 128:(g + 1) * 128].rearrange("(j p) c -> p j c", p=128),
                          R=[self.db("vs", tt) for tt in range(NT)], W=[vsB])
                    k.dma(k.sp, vwg[:, :, 0:128], self.vw[:, g * 128:(g + 1) * 128].rearrange("(j p) c -> p j c", p=128),
                          R=[self.db("vw", tt) for tt in range(NT)], W=[vwB])
                    for tt in range(NT):
                        q = qg[:, tt, :]
                        maskc = maskc_r[tt % 2]
                        mcB = mcB_r[tt % 2]
                        selm = selm_r[tt % 2]
                        smB = smB_r[tt % 2]
                        k.dma(k.sp, maskc[:, :, :], self.c_maskc[:, :, tt * 128:(tt + 1) * 128], W=[mcB])
                        k.dma(k.sp, selm[:, :, :], self.c_selm[:, :, tt, :], W=[smB])
                        halves = [0] + ([1] if tt >= 16 else [])
                        for hi, half in enumerate(halves):
                            need_mask = not (half == 0 and tt >= 17)
                            m = maskc[:, half, :] if need_mask else None
                            pi = npair % 3
                            self.attn_pair(kcmpT[:, g, half * 128:(half + 1) * 128], kcB, q, qgB, vext[:, g, half, :], vxB, 193,
                                           m, mcB, acc, hi == 0, hi == len(halves) - 1, pt[pi], ptB[pi], npair % 3)
                            npair += 1
                        j = tt % 2
                        stt = st[j]
                        sB_ = stB[j]
                        self.attn_flush()
                        finish_branch(tt, g, 0, 193, stt, sB_, True, ot[j], otB[j])
                        for r in range(4):
                            A = self.ps[acc[r]]
                            AB = self.psB[acc[r]]
                            if r == 0:
                                self.E(k.dve, "tensor_scalar", imp[:, :], A[:, 129:193], stt[:, 1:2], None, ALU.mult, R=[AB, sB_], W=[impB])
                            else:
                                self.E(k.dve, "scalar_tensor_tensor", imp[:, :], A[:, 129:193], stt[:, r * 3 + 1:r * 3 + 2], imp[:, :],
                                       ALU.mult, ALU.add, R=[AB, sB_, impB], W=[impB])
                        k.dma(k.sp, self.opart[tt * 128:(tt + 1) * 128, g * 512:(g + 1) * 512], ot[j][:, :], R=[otB[j]], W=[self.db("opart", (g, tt))])
                        self.E(k.dve, "tensor_tensor", vv[:, :], imp[:, :], selm[:, 0, :], ALU.mult, R=[impB, smB], W=[vvB])
                        self.E(k.dve, "tensor_tensor", vv[:, :], vv[:, :], selm[:, 1, :], ALU.add, R=[vvB, smB], W=[vvB])
                        self.E(k.dve, "tensor_tensor", cmpt[:, :, :], vv[:, :].unsqueeze(1).to_broadcast([128, 64, 64]),
                               vv[:, :].unsqueeze(2).to_broadcast([128, 64, 64]), ALU.is_gt, R=[vvB], W=[cmpB])
                        self.E(k.dve, "reduce_sum", cnt[:, :], cmpt[:, :, :], AX.X, R=[cmpB], W=[cntB])
                        self.E(k.dve, "tensor_scalar", selb[:, :], cnt[:, :], 15.5, None, ALU.is_lt, R=[cntB], W=[selbB])
                        pb = self.psbf(3)
                        self.E(k.pe, "transpose", pb[0:64, 0:128], selb[:, :], self.ident(), R=[selbB, self.cstB], W=[self.psB[3]])
                        self.E(k.act, "copy", selT_t[j][:, :], pb[0:64, 0:128], R=[self.psB[3]], W=[selTB[j]])
                        k.dma(k.sp, self.selT[g, :, tt * 128:(tt + 1) * 128], selT_t[j][:, :], R=[selTB[j]], W=[self.db("selT", (g, tt))])
                    for tt in range(NT):
                        q = qg[:, tt, :]
                        j = tt % 2
                        mk = msk[j]
                        mkB = mskB[j]
                        for b in range(2):
                            srcap = bass.AP(self.selT, g * 64 * T + b * T + tt * 128, [[0, 64], [2 * T, tt + 1], [1, 128]])
                            k.dma(k.sp, mk[b * 64:(b + 1) * 64, 0:tt + 1, :], srcap, R=[self.db("selT", (g, tt))], W=[mkB])
                        self.E(k.pool, "tensor_tensor", mk[:, tt, :], mk[:, tt, :], self.cst[:, 4, :], ALU.mult, R=[mkB, self.cstB], W=[mkB])
                        k.dma(k.sp, ot[j][:, :], self.opart[tt * 128:(tt + 1) * 128, g * 512:(g + 1) * 512], R=[self.db("opart", (g, tt))], W=[otB[j]])
                        stt = st[j]
                        sB_ = stB[j]
                        js = list(range(max(0, tt - 4), tt + 1))
                        for ji, jk in enumerate(js):
                            m = None
                            if jk == tt:
                                m = self.cst[:, 4, :]
                            elif jk == tt - 4:
                                m = self.cst[:, 5, :]
                            pi = npair % 3
                            self.attn_pair(kwg[:, jk * 128:(jk + 1) * 128], kwB, q, qgB, vwg[:, jk, :], vwB, 129,
                                           m, self.cstB, acc, ji == 0, ji == len(js) - 1, pt[pi], ptB[pi], npair % 3)
                            npair += 1
                        self.attn_flush()
                        finish_branch(tt, g, 2, 129, stt, sB_, False, ot[j], otB[j], ot[j], otB[j])
                        for jk in range(tt + 1):
                            pi = npair % 3
                            self.attn_pair(ksg[:, jk * 128:(jk + 1) * 128], ksB, q, qgB, vsg[:, jk, :], vsB, 129,
                                           mk[:, jk, :], mkB, acc, jk == 0, jk == tt, pt[pi], ptB[pi], npair % 3)
                            npair += 1
                        self.attn_flush()
                        finish_branch(tt, g, 1, 129, stt, sB_, False, ot[j], otB[j], ot[j], otB[j])
                        self.E(k.pool, "tensor_copy", otb[j][:, :], ot[j][:, :], R=[otB[j]], W=[otbB[j]])
                        self.transpose_to_hT(w, otb[j], otbB[j], 2, tt, pbanks=(3,), nch=4, ch0=g * 4)
                k.fence()

    def outproj(self, w2d, xsrc, xsname, xdst, xdname, vc, va, vb, hTdst):
        k = self.k
        nc = self.nc
        with ExitStack() as es:
            w = self.alloc_norm_tiles(es)
            ysb = [es.enter_context(self.sbt("ysb%d" % i, [128, D], F32)) for i in range(4)]
            yB = [Buf() for _ in range(4)]
            xt = [es.enter_context(self.sbt("xt%d" % i, [128, D], F32)) for i in range(2)]
            xB = [Buf() for _ in range(2)]
            hb = [es.enter_context(self.sbt("phT%d" % i, [128, KC, 512], BF16)) for i in range(2)]
            hB = [Buf() for _ in range(2)]
            cnt = 0
            for tb in range(T // 512):
                j = tb % 2
                k.dma(k.sp, hb[j][:, :, :], self.hT[2][:, :, tb * 512:(tb + 1) * 512],
                      R=[self.db("hT2", 4 * tb + s) for s in range(4)], W=[hB[j]])
                for cb in range(4):
                    wt, wB = self.wload(w2d[:, cb * 512:(cb + 1) * 512])
                    for s in range(4):
                        pi = 4 + cnt % 4
                        cnt += 1
                        for kk in range(KC):
                            self.E(k.pe, "matmul", self.ps[pi][:, :], hb[j][:, kk, s * 128:(s + 1) * 128], wt[:, kk, :],
                                   start=(kk == 0), stop=(kk == KC - 1), R=[wB, hB[j]], W=[self.psB[pi]])
                        self.E(k.act, "copy", ysb[s][:, cb * 512:(cb + 1) * 512], self.ps[pi][:, :], R=[self.psB[pi]], W=[yB[s]])
                for s in range(4):
                    tt = 4 * tb + s
                    self.residual_tile(w, ysb[s][:, :], yB[s], xt[s % 2][:, :], xB[s % 2], xsrc, xsname, xdst, xdname, vc, tt)
                    if hTdst is not None:
                        self.prenorm_tile(w, xt[s % 2][:, :], xB[s % 2], va, vb, hTdst, tt)
            k.fence()

    def ffn(self, l, xsrc, xsname, xdst, xdname, vc, va, vb, hTdst):
        k = self.k
        nc = self.nc
        Win = self.ffn_w_in.ap()[l]
        Wout = self.ffn_w_out.ap()[l]
        NF = DFF // 128
        with ExitStack() as es:
            w = self.alloc_norm_tiles(es)
            ysb = [es.enter_context(self.sbt("ysb%d" % i, [128, D], F32)) for i in range(4)]
            yB = [Buf() for _ in range(4)]
            xt = [es.enter_context(self.sbt("xt%d" % i, [128, D], F32)) for i in range(1)]
            xB = [Buf() for _ in range(1)]
            hb = [es.enter_context(self.sbt("phT%d" % i, [128, KC, 512], BF16)) for i in range(1)]
            hB = [Buf() for _ in range(1)]
            actT = es.enter_context(self.sbt("actT", [128, NF, 512], BF16))
            aB = [Buf() for _ in range(11)]
            sg = [es.enter_context(self.sbt("fsg%d" % i, [128, 512], F32)) for i in range(2)]
            sgB = [Buf() for _ in range(2)]
            n = 0
            for tb in range(T // 512):
                j = 0
                k.dma(k.sp, hb[j][:, :, :], self.hT[1][:, :, tb * 512:(tb + 1) * 512],
                      R=[self.db("hT1", 4 * tb + s) for s in range(4)], W=[hB[j]])
                for fg in range(11):
                    wg, wgB = self.wload(Win[:, fg * 512:(fg + 1) * 512])
                    wu, wuB = self.wload(Win[:, DFF + fg * 512: DFF + (fg + 1) * 512])
                    for jj in range(4):
                        i2 = n % 2
                        n += 1
                        pg = i2
                        pu = 2 + i2
                        for kk in range(KC):
                            self.E(k.pe, "matmul", self.ps[pg][:, :], wg[:, kk, jj * 128:(jj + 1) * 128], hb[j][:, kk, :],
                                   start=(kk == 0), stop=(kk == KC - 1), R=[wgB, hB[j]], W=[self.psB[pg]])
                        for kk in range(KC):
                            self.E(k.pe, "matmul", self.ps[pu][:, :], wu[:, kk, jj * 128:(jj + 1) * 128], hb[j][:, kk, :],
                                   start=(kk == 0), stop=(kk == KC - 1), R=[wuB, hB[j]], W=[self.psB[pu]])
                        self.E(k.act, "activation", sg[i2][:, :], self.ps[pg][:, :], AF.Silu, R=[self.psB[pg]], W=[sgB[i2]])
                        self.E(k.dve, "tensor_tensor", actT[:, fg * 4 + jj, :], sg[i2][:, :], self.ps[pu][:, :], ALU.mult,
                               R=[sgB[i2], self.psB[pu]], W=[aB[fg]])
                for cb in range(4):
                    for kg in range(4):
                        wt, wB = self.wload(Wout[kg * 11 * 128:(kg + 1) * 11 * 128, cb * 512:(cb + 1) * 512], kc=11)
                        for s in range(4):
                            pi = 4 + s
                            for kk in range(11):
                                ch = kg * 11 + kk
                                self.E(k.pe, "matmul", self.ps[pi][:, :], actT[:, ch, s * 128:(s + 1) * 128], wt[:, kk, :],
                                       start=(ch == 0), stop=(ch == NF - 1), R=[wB, aB[ch // 4]], W=[self.psB[pi]])
                    for s in range(4):
                        self.E(k.act, "copy", ysb[s][:, cb * 512:(cb + 1) * 512], self.ps[4 + s][:, :], R=[self.psB[4 + s]], W=[yB[s]])
                for s in range(4):
                    tt = 4 * tb + s
                    self.residual_tile(w, ysb[s][:, :], yB[s], xt[0][:, :], xB[0], xsrc, xsname, xdst, xdname, vc, tt)
                    if hTdst is not None:
                        self.prenorm_tile(w, xt[0][:, :], xB[0], va, vb, hTdst, tt, pbanks=(0, 1))
            k.fence()

    def sb_kv(self, xsrc, xname):
        k = self.k
        self.modvec(1, self.kv_mod_w.ap(), 0, self.kv_mod_b, 0)
        self.modvec(0, self.kv_mod_w.ap(), D, self.kv_mod_b, D)
        self.load_gain(2, self.kv_norm_g, 0)
        self.E(k.dve, "scalar_tensor_tensor", self.vec[0][:, :], self.vec[0][:, :], 1.0, self.vec[2][:, :], ALU.add, ALU.mult,
               R=[self.vecB[0], self.vecB[2]], W=[self.vecB[0]])
        self.norm_phase(xsrc, xname, 0, 1, 2)
        nc = self.nc
        with ExitStack() as es:
            ob = [es.enter_context(self.sbt("ob%d" % i, [128, 512], BF16)) for i in range(3)]
            obB = [Buf() for _ in range(3)]
            st = {"n": 0}

            def fF(tag, jj, tb, pi):
                i3 = st["n"] % 3
                st["n"] += 1
                self.E(k.act, "copy", ob[i3][:, :], self.ps[pi][:, :], R=[self.psB[pi]], W=[obB[i3]])
                k.dma(k.sp, self.kTsh[tag * 4 + jj, :, tb * 512:(tb + 1) * 512], ob[i3][:, :], R=[obB[i3]], W=[self.db("kTsh", (tag * 4 + jj, tb))])

            def fT(tag, s, tb, pi):
                i3 = st["n"] % 3
                st["n"] += 1
                tt = 4 * tb + s
                self.E(k.dve, "tensor_copy", ob[i3][:, :], self.ps[pi][:, :], R=[self.psB[pi]], W=[obB[i3]])
                k.dma(k.sp, self.vsh[tt * 128:(tt + 1) * 128, tag * 512:(tag + 1) * 512], ob[i3][:, :], R=[obB[i3]], W=[self.db("vsh", (tag, tt))])

            blocks = [(c * 512, 512, "F", c) for c in range(4)] + [(D + c * 512, 512, "T", c) for c in range(4)]
            self.proj(2, self.kv_w.ap(), blocks, fF, fT)

    def sb_q(self, l2):
        k = self.k
        nc = self.nc
        with ExitStack() as es:
            ob = [es.enter_context(self.sbt("ob%d" % i, [128, 512], BF16)) for i in range(3)]
            obB = [Buf() for _ in range(3)]
            st = {"n": 0}

            def fF(tag, jj, tb, pi):
                i3 = st["n"] % 3
                st["n"] += 1
                self.E(k.act, "copy", ob[i3][:, :], self.ps[pi][:, :], R=[self.psB[pi]], W=[obB[i3]])
                k.dma(k.sp, self.qsb[tag * 4 + jj, :, tb * 512:(tb + 1) * 512], ob[i3][:, :], R=[obB[i3]], W=[self.db("qsb", (tag * 4 + jj, tb))])

            blocks = [(c * 512, 512, "F", c) for c in range(4)]
            self.proj(0, self.b_w_q.ap()[l2], blocks, fF, None)

    def sb_attn(self):
        k = self.k
        nc = self.nc
        with ExitStack() as es:
            sb = lambda name, shape, dtp: es.enter_context(self.sbt(name, shape, dtp))
            mb = sb("msbb", [128, 4, 512], BF16)
            mbB = Buf()
            k.dma(k.sp, mb[:, :, :], self.c_msb_b[:, :, :], W=[mbB])
            kT = [sb("skT%d" % i, [128, T], BF16) for i in range(2)]
            kTB = [Buf() for _ in range(2)]
            qT = [sb("sqT%d" % i, [128, T], BF16) for i in range(2)]
            qTB = [Buf() for _ in range(2)]
            vh = [sb("svh%d" % i, [128, NT, 128], BF16) for i in range(2)]
            vhB = [Buf() for _ in range(2)]
            NR = 3
            ee = [sb("see%d" % i, [128, 512], F32) for i in range(NR)]
            eeB = [Buf() for _ in range(NR)]
            spb = [sb("sspb%d" % i, [128, 512], BF16) for i in range(NR)]
            spbB = [Buf() for _ in range(NR)]
            t3 = [sb("st3_%d" % i, [128, 512], F32) for i in range(NR)]
            t3B = [Buf() for _ in range(NR)]
            pt = [sb("spt%d" % i, [128, 512], BF16) for i in range(NR)]
            ptB = [Buf() for _ in range(NR)]
            carry = sb("scarry", [128, 512], F32)
            cB = Buf()
            ob = [sb("sob%d" % i, [128, 512], BF16) for i in range(2)]
            obB = [Buf() for _ in range(2)]
            st = {"n": 0, "nq": 0}

            def stageA(p):
                Pz = self.ps[p["zi"]]
                self.E(k.pe, "matmul", Pz[:, :], p["kT"], p["q"], start=True, stop=True, R=[p["kB"], p["qB"]], W=[self.psB[p["zi"]]])

            def stageB(p):
                i = p["i"]
                Pz = self.ps[p["zi"]]
                PzB = self.psB[p["zi"]]
                self.E(k.act, "activation", ee[i][:, :], Pz[:, :], AF.Exp, scale=SCALE, R=[PzB], W=[eeB[i]])
                self.E(k.act, "activation", spb[i][:, :], ee[i][:, :], AF.Ln, bias=1.0, R=[eeB[i]], W=[spbB[i]])
                if p["a"] >= 0:
                    self.E(k.pool, "tensor_tensor", spb[i][:, :], spb[i][:, :], mb[:, p["a"], :], ALU.mult, R=[spbB[i], mbB], W=[spbB[i]])
                self.E(k.pe, "matmul", Pz[:, :], self.cst[:, 6, :], spb[i][:, :], start=False, stop=True, skip_group_check=True,
                       R=[self.cstB, spbB[i]], W=[PzB])
                if not p["last"]:
                    ti = p["ti"]
                    self.E(k.pe, "matmul", self.ps[ti][:, :], self.cst[:, 2, :], spb[i][:, :], start=True, stop=True,
                           R=[self.cstB, spbB[i]], W=[self.psB[ti]])

            def stageC(p):
                i = p["i"]
                Pz = self.ps[p["zi"]]
                PzB = self.psB[p["zi"]]
                if p["first"]:
                    self.E(k.act, "activation", pt[i][:, :], Pz[:, :], AF.Exp, scale=SCALE, R=[PzB], W=[ptB[i]])
                else:
                    self.E(k.dve, "scalar_tensor_tensor", t3[i][:, :], Pz[:, :], SCALE, carry[:, :], ALU.mult, ALU.subtract,
                           R=[PzB, cB], W=[t3B[i]])
                    self.E(k.act, "activation", pt[i][:, :], t3[i][:, :], AF.Exp, R=[t3B[i]], W=[ptB[i]])
                if p["a"] >= 0:
                    self.E(k.pool, "tensor_tensor", pt[i][:, :], pt[i][:, :], mb[:, p["a"], :], ALU.mult, R=[ptB[i], mbB], W=[ptB[i]])
                self.E(k.pe, "matmul", self.ps[p["acc"]][:, :], p["v"], pt[i][:, :], start=p["first"], stop=p["last"],
                       R=[p["vB"], ptB[i]], W=[self.psB[p["acc"]]])
                if not p["last"]:
                    ti = p["ti"]
                    if p["first"]:
                        self.E(k.dve, "tensor_copy", carry[:, :], self.ps[ti][:, :], R=[self.psB[ti]], W=[cB])
                    else:
                        self.E(k.dve, "tensor_tensor", carry[:, :], carry[:, :], self.ps[ti][:, :], ALU.add, R=[cB, self.psB[ti]], W=[cB])
                if p["last"]:
                    oj = st["nq"] % 2
                    st["nq"] += 1
                    self.E(k.act, "copy", ob[oj][:, :], self.ps[p["acc"]][:, :], R=[self.psB[p["acc"]]], W=[obB[oj]])
                    k.dma(k.sp, self.hT[2][:, p["h"], p["qb"] * 512:(p["qb"] + 1) * 512], ob[oj][:, :], R=[obB[oj]],
                          W=[self.db("hT2", 4 * p["qb"] + s) for s in range(4)])

            pairs = []
            for h in range(NH):
                hj = h % 2
                for qb in range(T // 512):
                    jtop = 4 * qb + 3
                    for jk in range(jtop, -1, -1):
                        pairs.append(dict(h=h, hj=hj, qb=qb, jk=jk, a=jk - 4 * qb, first=(jk == jtop), last=(jk == 0)))
            loaded = set()
            nacc = 0
            for n, p in enumerate(pairs):
                p["i"] = n % NR
                p["zi"] = n % 3
                p["ti"] = 3 + n % 2
                if p["first"]:
                    nacc += 1
                p["acc"] = 6 + nacc % 2
            def ensure_loaded(h):
                if h in loaded or h >= NH:
                    return
                loaded.add(h)
                hj = h % 2
                k.dma(k.sp, kT[hj][:, :], self.kTsh[h, :, :], R=[self.db("kTsh", (h, tb)) for tb in range(8)], W=[kTB[hj]])
                k.dma(k.sp, qT[hj][:, :], self.qsb[h, :, :], R=[self.db("qsb", (h, tb)) for tb in range(8)], W=[qTB[hj]])
                k.dma(k.sp, vh[hj][:, :, :], self.vsh[:, h * 128:(h + 1) * 128].rearrange("(j p) c -> p j c", p=128),
                      R=[self.db("vsh", (h // 4, tt)) for tt in range(NT)], W=[vhB[hj]])
            def prep(p):
                ensure_loaded(p["h"])
                hj = p["hj"]
                jk = p["jk"]
                p["kT"] = kT[hj][:, jk * 128:(jk + 1) * 128]
                p["kB"] = kTB[hj]
                p["q"] = qT[hj][:, p["qb"] * 512:(p["qb"] + 1) * 512]
                p["qB"] = qTB[hj]
                p["v"] = vh[hj][:, jk, :]
                p["vB"] = vhB[hj]
            N = len(pairs)
            for n in range(N + 2):
                if n < N:
                    prep(pairs[n])
                    stageA(pairs[n])
                if 0 <= n - 1 < N:
                    stageB(pairs[n - 1])
                if 0 <= n - 2 < N:
                    stageC(pairs[n - 2])
            k.fence()

    def build(self):
        k = self.k
        self.setup()
        xin, xin_name = self.x.ap(), "x"
        for l in range(self.nlayers):
            last = (l == self.nlayers - 1)
            if l == 0:
                self.mod_AB(l, 0, 0, 0, 1, 2)
                self.norm_phase(xin, xin_name, 0, 1, 0)
                if self.stop == "norm0":
                    return self.nc
            if l == 2:
                self.sb_kv(xin, xin_name)
            if l < 2:
                with ExitStack() as esg:
                    self.gate_sb = esg.enter_context(self.sbt("gate_sb", [128, NT, 48], F32))
                    self.nsa_proj(l)
                    if self.stop == "nsa_proj":
                        return self.nc
                    self.nsa_attn(l)
                    k.fence()
                if self.stop == "nsa_attn":
                    return self.nc
                wout = self.a_w_out.ap()[l]
            else:
                self.sb_q(l - 2)
                self.sb_attn()
                wout = self.b_w_out.ap()[l - 2]
            self.mod_C(l, 0, 1, 2, 3)
            self.mod_AB(l, 1, 2, 0, 1, 3)
            self.outproj(wout, xin, xin_name, self.xs1.ap(), "xs1", 2, 0, 1, 1)
            if self.stop == "outproj%d" % l:
                return self.nc
            self.mod_C(l, 1, 3, 2, 3)
            if not last:
                self.mod_AB(l + 1, 0, 0, 0, 1, 3)
            xdst, xdname = (self.out.ap(), "out") if last else (self.xs2.ap(), "xs2")
            self.ffn(l, self.xs1.ap(), "xs1", xdst, xdname, 2, 0, 1, None if last else 0)
            xin, xin_name = xdst, xdname
        k.fence()
        return self.nc


_CONSTS = None


def make_in_maps(inputs, ncores=8):
    global _CONSTS
    if _CONSTS is None:
        _CONSTS = make_consts()
    f = lambda a: np.ascontiguousarray(np.asarray(a, dtype=np.float32))
    shared = {
        "mod_w": f(inputs["mod_w"]), "mod_b": f(inputs["mod_b"]), "norm_g": f(inputs["norm_g"]),
        "ffn_w_in": f(inputs["ffn_w_in"]), "ffn_w_out": f(inputs["ffn_w_out"]), "a_w_in": f(inputs["a_w_in"]),
        "a_gate_b": f(inputs["a_gate_b"]),
        "a_cmp_peT": np.ascontiguousarray(np.asarray(inputs["a_cmp_pe"], dtype=np.float32).transpose(0, 1, 3, 2)),
        "a_cmp_w1": f(inputs["a_cmp_w1"]), "a_cmp_w2": f(inputs["a_cmp_w2"]), "a_w_out": f(inputs["a_w_out"]),
        "b_w_q": f(inputs["b_w_q"]), "b_w_out": f(inputs["b_w_out"]),
        "kv_norm_g": f(inputs["kv_norm_g"]).reshape(1, D), "kv_mod_w": f(inputs["kv_mod_w"]),
        "kv_mod_b": f(inputs["kv_mod_b"]).reshape(1, 2 * D), "kv_w": f(inputs["kv_w"]),
    }
    shared.update(_CONSTS)
    x = np.asarray(inputs["x"], dtype=np.float32)
    c = np.asarray(inputs["c"], dtype=np.float32)
    maps = []
    for core in range(ncores):
        b = core % 4
        m = dict(shared)
        m["x"] = np.ascontiguousarray(x[b])
        m["cT"] = np.ascontiguousarray(c[b].reshape(KC, 128).T)
        maps.append(m)
    return maps


def kernel(**inputs):
    prog = Prog()
    nc = prog.build()
    maps = make_in_maps(inputs)
    res = run_bass_kernel_spmd(nc, maps, core_ids=list(range(8)))
    out = np.stack([np.asarray(res.results[b]["out"], dtype=np.float32) for b in range(4)], axis=0)
    return out
```

```python
from contextlib import ExitStack
import numpy as np
import ml_dtypes
import concourse.bass as bass
import concourse.mybir as mybir
from concourse.bass_utils import run_bass_kernel_spmd

F32 = mybir.dt.float32
BF16 = mybir.dt.bfloat16
AF = mybir.ActivationFunctionType
ALU = mybir.AluOpType
AX = mybir.AxisListType

SAME_ENGINE_SYNC = True

D = 2048
T = 4096
NT = T // 128
KC = 16
DFF = 5632
NH = 16
G = 4
EPS = 1e-6
SCALE = 128 ** -0.5
NCMP = 255
NAIN = 5168


class Buf:
    __slots__ = ("name", "w", "r", "ps")

    def __init__(self, name="", ps=False):
        self.name = name
        self.w = None
        self.r = []
        self.ps = ps


class Eng:
    def __init__(self, nc, e, name, pe=False, ndma=0):
        self.e = e
        self.name = name
        self.sem = nc.alloc_semaphore("s_" + name)
        self.cnt = 0
        self.seen = {}
        self.pe = pe
        self.dsems = [[nc.alloc_semaphore("d_%s%d" % (name, i)), 0] for i in range(ndma)]
        self.dnext = 0


class K:
    def __init__(self):
        nc = bass.Bass("TRN2", target_bir_lowering=False)
        self.nc = nc
        self.pe = Eng(nc, nc.tensor, "pe", pe=True)
        self.act = Eng(nc, nc.scalar, "act")
        self.dve = Eng(nc, nc.vector, "dve")
        self.pool = Eng(nc, nc.gpsimd, "pool", ndma=8)
        self.sp = Eng(nc, nc.sync, "sp", ndma=16)
        self.engs = [self.pe, self.act, self.dve, self.pool, self.sp]
        self.nins = 0
        self.nwait = 0

    def _wait(self, eng, tok):
        sem, val = tok
        if sem is eng.sem and (eng.pe or not SAME_ENGINE_SYNC):
            return
        key = id(sem)
        if eng.seen.get(key, 0) >= val:
            return
        eng.e.wait_ge(sem, val)
        eng.seen[key] = val
        self.nwait += 1

    def _deps(self, eng, R, W):
        for b in R:
            if b.w is not None:
                self._wait(eng, b.w)
        for b in W:
            if b.w is not None:
                self._wait(eng, b.w)
            for t in b.r:
                self._wait(eng, t)

    def _commit(self, tok, R, W):
        for b in W:
            b.w = tok
            b.r = []
        for b in R:
            b.r.append(tok)
            if len(b.r) > 16:
                d = {}
                for s, v in b.r:
                    kk = id(s)
                    if kk not in d or d[kk][1] < v:
                        d[kk] = (s, v)
                b.r = list(d.values())

    def op(self, eng, fn, R=(), W=()):
        px = [b for b in R if b.ps and b not in W]
        if px:
            W = list(W) + px
            R = [b for b in R if not b.ps]
        self._deps(eng, R, W)
        ins = fn(eng.e)
        eng.cnt += 1
        ins.then_inc(eng.sem, 1)
        self._commit((eng.sem, eng.cnt), R, W)
        self.nins += 1
        return ins

    def dma(self, q, out, in_, R=(), W=(), **kw):
        self._deps(q, R, W)
        slot = q.dsems[q.dnext]
        q.dnext = (q.dnext + 1) % len(q.dsems)
        if slot[1] > 0:
            self._wait(q, (slot[0], slot[1]))
        ins = q.e.dma_start(out=out, in_=in_, **kw)
        slot[1] += 16
        ins.then_inc(slot[0], 16)
        self._commit((slot[0], slot[1]), R, W)
        self.nins += 1
        return ins

    def fence(self):
        for e in self.engs:
            for o in self.engs:
                if o is not e and o.cnt > 0:
                    self._wait(e, (o.sem, o.cnt))
                for s in o.dsems:
                    if s[1] > 0:
                        self._wait(e, (s[0], s[1]))


def _bf(a):
    return np.ascontiguousarray(a).astype(ml_dtypes.bfloat16)


def make_consts():
    c = {}
    p = np.arange(128)
    ident = np.eye(128, dtype=np.float32)
    rot = np.zeros((128, 128), np.float32)
    for dp in range(64):
        rot[dp + 64, dp] = -1.0
        rot[dp, dp + 64] = 1.0
    ones = np.ones((128, 128), np.float32)
    ltri = (p[:, None] > p[None, :]).astype(np.float32)
    causal = (p[:, None] <= p[None, :]).astype(np.float32)
    anti = (p[:, None] > p[None, :]).astype(np.float32)
    trin = -(ltri + ident) / np.float32(SCALE)
    c["cst_bf"] = _bf(np.stack([ident, rot, ones, ltri, causal, anti, trin], axis=1))
    t = np.arange(T, dtype=np.float64)
    inv = 10000.0 ** (-np.arange(64, dtype=np.float64) / 64)
    ang = (t[None, :].astype(np.float32) * inv.astype(np.float32)[:, None]).astype(np.float32)
    cs = np.stack([np.concatenate([np.cos(ang), np.cos(ang)], 0), np.concatenate([np.sin(ang), np.sin(ang)], 0)], axis=1)
    c["cossin"] = np.ascontiguousarray(cs.astype(np.float32))
    n = np.arange(256)
    cmp_end = n * 16 + 31
    mc = ((cmp_end[:, None] <= np.arange(T)[None, :]) & (n[:, None] < NCMP)).astype(np.float32)
    c["maskc"] = _bf(mc.reshape(2, 128, T).transpose(1, 0, 2))
    c_start = n[:, None] * 16
    s_start = np.arange(64)[None, :] * 64
    ov = ((c_start < s_start + 64) & (c_start + 32 > s_start) & (n[:, None] < NCMP)).astype(np.float32)
    ext = np.zeros((256, 65), np.float32)
    ext[:NCMP, 0] = 1.0
    ext[:, 1:] = ov
    c["vext"] = _bf(ext.reshape(2, 128, 65).transpose(1, 0, 2))
    tt = np.arange(T)
    blk = np.arange(64)
    cur = tt // 64
    forced = (blk[None, :] == 0) | (blk[None, :] == cur[:, None]) | (blk[None, :] == cur[:, None] - 1)
    avail = blk[None, :] * 64 <= tt[:, None]
    availm = (avail & ~forced).astype(np.float32)
    forcem = np.where(forced, 1e9 * (1.0 + (blk[None, :] == 0) + 2.0 * (blk[None, :] == cur[:, None])), np.where(avail, 0.0, -1e9)).astype(np.float32)
    c["selm"] = np.ascontiguousarray(np.stack([availm, forcem], 0).reshape(2, NT, 128, 64).transpose(2, 0, 1, 3))
    kk = np.arange(128)[:, None]
    qq = np.arange(512)[None, :]
    msb = np.stack([((a * 128 + kk) < qq) for a in range(4)], axis=1).astype(np.float32)
    c["msb_f"] = np.ascontiguousarray(msb)
    c["msb_b"] = _bf(msb)
    return c


class Prog:
    def __init__(self, nlayers=4, dbg=(), stop=None):
        self.stop = stop
        self.k = K()
        k = self.k
        nc = k.nc
        self.nc = nc
        self.dbg = set(dbg)
        self.nlayers = nlayers
        dt = nc.dram_tensor
        self.inp = {}

        def din(name, shape, dtype=F32):
            self.inp[name] = dt(name, list(shape), dtype, kind="ExternalInput")
            return self.inp[name]

        self.x = din("x", [T, D])
        self.cT = din("cT", [128, KC])
        self.mod_w = din("mod_w", [4, D, 6 * D])
        self.mod_b = din("mod_b", [4, 6 * D])
        self.norm_g = din("norm_g", [4, 4, D])
        self.ffn_w_in = din("ffn_w_in", [4, D, 2 * DFF])
        self.ffn_w_out = din("ffn_w_out", [4, DFF, D])
        self.a_w_in = din("a_w_in", [2, D, NAIN])
        self.a_gate_b = din("a_gate_b", [2, 48])
        self.a_cmp_peT = din("a_cmp_peT", [2, 2, 128, 32])
        self.a_cmp_w1 = din("a_cmp_w1", [2, 2, 4096, 256])
        self.a_cmp_w2 = din("a_cmp_w2", [2, 2, 256, 128])
        self.a_w_out = din("a_w_out", [2, D, D])
        self.b_w_q = din("b_w_q", [2, D, D])
        self.b_w_out = din("b_w_out", [2, D, D])
        self.kv_norm_g = din("kv_norm_g", [1, D])
        self.kv_mod_w = din("kv_mod_w", [D, 2 * D])
        self.kv_mod_b = din("kv_mod_b", [1, 2 * D])
        self.kv_w = din("kv_w", [D, 2 * D])
        self.c_cst = din("cst_bf", [128, 7, 128], BF16)
        self.c_cossin = din("cossin", [128, 2, T])
        self.c_maskc = din("maskc", [128, 2, T], BF16)
        self.c_vext = din("vext", [128, 2, 65], BF16)
        self.c_selm = din("selm", [128, 2, NT, 64])
        self.c_msb_f = din("msb_f", [128, 4, 512])
        self.c_msb_b = din("msb_b", [128, 4, 512], BF16)
        self.out = dt("out", [T, D], F32, kind="ExternalOutput")

        def scr(name, shape, dtype):
            kind = "ExternalOutput" if name in self.dbg else "Internal"
            return dt(name, list(shape), dtype, kind=kind)

        self.xs1 = scr("xs1", [T, D], F32)
        self.xs2 = scr("xs2", [T, D], F32)
        self.hT = [scr("hT%d" % i, [128, KC, T], BF16) for i in range(3)]
        self.qT = scr("qT", [G, 128, NT, 4, 128], BF16)
        self.kcT = scr("kcT", [G, 128, T], BF16)
        self.vcT = scr("vcT", [G, 128, T], BF16)
        self.ksT = scr("ksT", [G, 128, T], BF16)
        self.kwT = scr("kwT", [G, 128, T], BF16)
        self.vs = scr("vs", [T, 512], BF16)
        self.vw = scr("vw", [T, 512], BF16)
        self.opart = scr("opart", [T, D], F32)
        self.selT = scr("selT", [G, 64, T], BF16)
        self.kTsh = scr("kTsh", [NH, 128, T], BF16)
        self.vsh = scr("vsh", [T, D], BF16)
        self.qsb = scr("qsb", [NH, 128, T], BF16)
        self.wsc_in = scr("wsc_in", [22, 128, KC * 512], BF16)
        self.wsc_out = scr("wsc_out", [16, 128, 11 * 512], BF16)
        self.wsc_p = scr("wsc_p", [11, 128, KC * 512], BF16)
        self.wsc_o = scr("wsc_o", [4, 128, KC * 512], BF16)
        self.wsc_kv = scr("wsc_kv", [8, 128, KC * 512], BF16)
        self.dbufs = {}

        self.ps = [nc.alloc_psum_tensor("ps%d" % i, [128, 512], F32) for i in range(8)]
        self.psB = [Buf("ps%d" % i, ps=True) for i in range(8)]
        self.cst = nc.alloc_sbuf_tensor("cst", [128, 7, 128], BF16)
        self.cstB = Buf("cst")
        self.crep = nc.alloc_sbuf_tensor("crep", [128, KC, 128], BF16)
        self.crepB = Buf("crep")
        self.NW = 3
        self.wt = [nc.alloc_sbuf_tensor("wt%d" % i, [128, KC, 512], BF16) for i in range(self.NW)]
        self.wB = [Buf("wt%d" % i) for i in range(self.NW)]
        self.wi = 0
        self.vec = [nc.alloc_sbuf_tensor("vec%d" % i, [128, D], F32) for i in range(4)]
        self.vecB = [Buf("vec%d" % i) for i in range(4)]
        self.gate_sb = None
        self.gateB = Buf("gate")
        self.uid = 0

    def sbt(self, name, shape, dtype):
        self.uid += 1
        return self.nc.sbuf_tensor("%s_u%d" % (name, self.uid), shape, dtype)

    def db(self, name, i=0):
        key = (name, i)
        b = self.dbufs.get(key)
        if b is None:
            b = self.dbufs[key] = Buf("%s_%s" % key)
        return b

    def E(self, eng, meth, *a, R=(), W=(), **kw):
        return self.k.op(eng, lambda e: getattr(e, meth)(*a, **kw), R, W)

    def ident(self):
        return self.cst[:, 0, :]

    def bcast_row(self, handle, off, n):
        return bass.AP(handle, off, [[0, 128], [1, n]])

    def wload(self, src2d, kc=16, ncols=512):
        i = self.wi
        self.wi = (i + 1) % self.NW
        self.k.dma(self.k.pool, self.wt[i][:, 0:kc, 0:ncols], src2d.rearrange("(k p) n -> p k n", p=128), W=[self.wB[i]])
        return self.wt[i], self.wB[i]

    def precast(self, scr_t, name, tile, src2d, kc=KC, ncols=512):
        self.k.dma(self.k.pool, scr_t[tile].rearrange("p (k n) -> p k n", k=kc)[:, :, 0:ncols],
                   src2d.rearrange("(k p) n -> p k n", p=128), W=[self.db(name, tile)])

    def wload_bf(self, scr_t, name, tile, kc=KC, ncols=512):
        i = self.wi
        self.wi = (i + 1) % self.NW
        self.k.dma(self.k.sp, self.wt[i][:, 0:kc, 0:ncols], scr_t[tile].rearrange("p (k n) -> p k n", k=kc)[:, :, 0:ncols],
                   R=[self.db(name, tile)], W=[self.wB[i]])
        return self.wt[i], self.wB[i]

    def precast_ffn(self, l):
        Win = self.ffn_w_in.ap()[l]
        Wout = self.ffn_w_out.ap()[l]
        for fg in range(11):
            self.precast(self.wsc_in, "wsc_in", 2 * fg, Win[:, fg * 512:(fg + 1) * 512])
            self.precast(self.wsc_in, "wsc_in", 2 * fg + 1, Win[:, DFF + fg * 512: DFF + (fg + 1) * 512])
        for cb in range(4):
            for kg in range(4):
                self.precast(self.wsc_out, "wsc_out", cb * 4 + kg, Wout[kg * 11 * 128:(kg + 1) * 11 * 128, cb * 512:(cb + 1) * 512], kc=11)

    def precast_sq(self, scr_t, name, w2d, ncols_total):
        nb = (ncols_total + 511) // 512
        for b in range(nb):
            nc_ = min(512, ncols_total - b * 512)
            self.precast(scr_t, name, b, w2d[:, b * 512:b * 512 + nc_], ncols=nc_)

    def psbf(self, i):
        return self.ps[i][:, :].bitcast(BF16)

    def setup(self):
        k = self.k
        nc = self.nc
        k.dma(k.sp, self.cst[:, :, :], self.c_cst[:, :, :], W=[self.cstB])
        with ExitStack() as es:
            cf = es.enter_context(self.sbt("cf", [128, KC], F32))
            ca = es.enter_context(self.sbt("ca", [128, KC], F32))
            cfB = Buf()
            caB = Buf()
            k.dma(k.sp, cf[:, :], self.cT[:, :], W=[cfB])
            self.E(k.act, "activation", ca[:, :], cf[:, :], AF.Silu, R=[cfB], W=[caB])
            self.E(k.dve, "tensor_copy", self.crep[:, :, :], ca[:, :].unsqueeze(2).to_broadcast([128, KC, 128]), R=[caB], W=[self.crepB])
            k.fence()

    def modvec(self, vi, w2d, col0, bias_handle, bias_off):
        k = self.k
        dst = self.vec[vi]
        dB = self.vecB[vi]
        k.dma(k.sp, dst[:, :], self.bcast_row(bias_handle, bias_off, D), W=[dB])
        for cb in range(4):
            wt, wB = self.wload(w2d[:, col0 + cb * 512: col0 + (cb + 1) * 512])
            pi = cb % 2
            for kk in range(KC):
                self.E(k.pe, "matmul", self.ps[pi][:, :], self.crep[:, kk, :], wt[:, kk, :], start=(kk == 0), stop=(kk == KC - 1),
                       R=[self.crepB, wB], W=[self.psB[pi]])
            self.E(k.dve, "tensor_tensor", dst[:, cb * 512:(cb + 1) * 512], self.ps[pi][:, :], dst[:, cb * 512:(cb + 1) * 512], ALU.add,
                   R=[self.psB[pi], dB], W=[dB])

    def load_gain(self, vi, handle, off):
        self.k.dma(self.k.sp, self.vec[vi][:, :], self.bcast_row(handle, off, D), W=[self.vecB[vi]])

    def mod_AB(self, l, which, gi, va, vb, vtmp):
        k = self.k
        w2d = self.mod_w.ap()[l]
        base = 0 if which == 0 else 3 * D
        self.modvec(vb, w2d, base, self.mod_b, l * 6 * D + base)
        self.modvec(va, w2d, base + D, self.mod_b, l * 6 * D + base + D)
        self.load_gain(vtmp, self.norm_g, (l * 4 + gi) * D)
        self.E(k.dve, "scalar_tensor_tensor", self.vec[va][:, :], self.vec[va][:, :], 1.0, self.vec[vtmp][:, :], ALU.add, ALU.mult,
               R=[self.vecB[va], self.vecB[vtmp]], W=[self.vecB[va]])

    def mod_C(self, l, which, gi, vc, vtmp):
        k = self.k
        w2d = self.mod_w.ap()[l]
        base = 2 * D if which == 0 else 5 * D
        self.modvec(vc, w2d, base, self.mod_b, l * 6 * D + base)
        self.load_gain(vtmp, self.norm_g, (l * 4 + gi) * D)
        self.E(k.dve, "tensor_tensor", self.vec[vc][:, :], self.vec[vc][:, :], self.vec[vtmp][:, :], ALU.mult,
               R=[self.vecB[vc], self.vecB[vtmp]], W=[self.vecB[vc]])

    def alloc_norm_tiles(self, es):
        nc = self.nc
        w = {}
        w["ss"] = [es.enter_context(self.sbt("ss%d" % i, [128, 8], F32)) for i in range(2)]
        w["ssB"] = [Buf() for _ in range(2)]
        w["hn"] = es.enter_context(self.sbt("hn", [128, D], F32))
        w["hnB"] = Buf()
        w["junk"] = w["hn"]
        w["junkB"] = w["hnB"]
        w["hb"] = [es.enter_context(self.sbt("hb%d" % i, [128, D], BF16)) for i in range(1)]
        w["hbB"] = [Buf() for _ in range(1)]
        w["hTt"] = [es.enter_context(self.sbt("hTt%d" % i, [128, KC, 128], BF16)) for i in range(1)]
        w["hTtB"] = [Buf() for _ in range(1)]
        w["i"] = 0
        return w

    def rstd(self, w, src, srcB, n):
        k = self.k
        i = w["i"] % 2
        w["i"] += 1
        ss = w["ss"][i]
        sB = w["ssB"][i]
        self.E(k.pool, "memset", ss[:, 0:1], 0.0, W=[sB])
        self.E(k.act, "activation", w["junk"][:, 0:n], src, AF.Square, accum_out=ss[:, 0:1], R=[srcB], W=[w["junkB"], sB])
        self.E(k.dve, "tensor_scalar", ss[:, 1:2], ss[:, 0:1], 1.0 / n, EPS, ALU.mult, ALU.add, R=[sB], W=[sB])
        self.E(k.act, "activation", ss[:, 2:3], ss[:, 1:2], AF.Sqrt, R=[sB], W=[sB])
        self.E(k.dve, "reciprocal", ss[:, 3:4], ss[:, 2:3], R=[sB], W=[sB])
        return ss[:, 3:4], sB

    def prenorm_tile(self, w, xt, xB, va, vb, hTi, tt, pbanks=(0, 1)):
        k = self.k
        r, rB = self.rstd(w, xt, xB, D)
        j = tt % len(w["hb"])
        self.E(k.dve, "scalar_tensor_tensor", w["hn"][:, :], xt, r, self.vec[va][:, :], ALU.mult, ALU.mult,
               R=[xB, rB, self.vecB[va]], W=[w["hnB"]])
        self.E(k.pool, "tensor_tensor", w["hb"][j][:, :], w["hn"][:, :], self.vec[vb][:, :], ALU.add,
               R=[w["hnB"], self.vecB[vb]], W=[w["hbB"][j]])
        self.transpose_to_hT(w, w["hb"][j], w["hbB"][j], hTi, tt, pbanks)

    def transpose_to_hT(self, w, src, srcB, hTi, tt, pbanks=(0, 1), nch=KC, ch0=0):
        k = self.k
        j = tt % len(w["hTt"])
        hTt = w["hTt"][j]
        hB = w["hTtB"][j]
        for half in range((nch + 7) // 8):
            pi = pbanks[half % len(pbanks)]
            n8 = min(8, nch - half * 8)
            pb = self.psbf(pi)
            for c in range(n8):
                cc = half * 8 + c
                self.E(k.pe, "transpose", pb[:, c * 128:(c + 1) * 128], src[:, cc * 128:(cc + 1) * 128], self.ident(),
                       R=[srcB, self.cstB], W=[self.psB[pi]])
            eng = k.act if half % 2 == 0 else k.dve
            meth = "copy" if eng is k.act else "tensor_copy"
            self.E(eng, meth, hTt[:, half * 8: half * 8 + n8, :], pb[:, 0:n8 * 128].rearrange("p (k n) -> p k n", k=n8),
                   R=[self.psB[pi]], W=[hB])
        k.dma(k.sp, self.hT[hTi][:, ch0:ch0 + nch, tt * 128:(tt + 1) * 128], hTt[:, 0:nch, :], R=[hB], W=[self.db("hT%d" % hTi, tt)])

    def norm_phase(self, xsrc, xname, va, vb, hTi):
        k = self.k
        nc = self.nc
        with ExitStack() as es:
            w = self.alloc_norm_tiles(es)
            xt = [es.enter_context(self.sbt("xt%d" % i, [128, D], F32)) for i in range(2)]
            xB = [Buf() for _ in range(2)]
            for tt in range(NT):
                j = tt % 2
                k.dma(k.sp, xt[j][:, :], xsrc[tt * 128:(tt + 1) * 128, :], R=[self.db(xname, tt)], W=[xB[j]])
                self.prenorm_tile(w, xt[j][:, :], xB[j], va, vb, hTi, tt)
            k.fence()

    def residual_tile(self, w, y, yB, xt, xB, xsrc, xsname, xdst, xdname, vc, tt):
        k = self.k
        k.dma(k.sp, xt, xsrc[tt * 128:(tt + 1) * 128, :], R=[self.db(xsname, tt)], W=[xB])
        r, rB = self.rstd(w, y, yB, D)
        self.E(k.dve, "scalar_tensor_tensor", w["hn"][:, :], y, r, self.vec[vc][:, :], ALU.mult, ALU.mult,
               R=[yB, rB, self.vecB[vc]], W=[w["hnB"]])
        self.E(k.pool, "tensor_tensor", xt, xt, w["hn"][:, :], ALU.add, R=[xB, w["hnB"]], W=[xB])
        k.dma(k.sp, xdst[tt * 128:(tt + 1) * 128, :], xt, R=[xB], W=[self.db(xdname, tt)])

    def proj(self, hTi, wsrc, blocks, fF=None, fT=None, pF=(0, 1), pT=(2, 3)):
        k = self.k
        nc = self.nc
        with ExitStack() as es:
            hb = [es.enter_context(self.sbt("phT%d" % i, [128, KC, 512], BF16)) for i in range(2)]
            hB = [Buf() for _ in range(2)]
            cnt = 0
            for tb in range(T // 512):
                j = tb % 2
                k.dma(k.sp, hb[j][:, :, :], self.hT[hTi][:, :, tb * 512:(tb + 1) * 512],
                      R=[self.db("hT%d" % hTi, 4 * tb + s) for s in range(4)], W=[hB[j]])
                for (col0, ncols, mode, tag) in blocks:
                    wt, wB = self.wload_bf(wsrc[0], wsrc[1], col0 // 512, ncols=ncols)
                    if mode == "F":
                        for jj in range(ncols // 128):
                            pi = pF[cnt % len(pF)]
                            cnt += 1
                            for kk in range(KC):
                                self.E(k.pe, "matmul", self.ps[pi][:, :], wt[:, kk, jj * 128:(jj + 1) * 128], hb[j][:, kk, :],
                                       start=(kk == 0), stop=(kk == KC - 1), R=[wB, hB[j]], W=[self.psB[pi]])
                            fF(tag, jj, tb, pi)
                    else:
                        for s in range(4):
                            pi = pT[cnt % len(pT)]
                            cnt += 1
                            for kk in range(KC):
                                self.E(k.pe, "matmul", self.ps[pi][:, 0:ncols], hb[j][:, kk, s * 128:(s + 1) * 128], wt[:, kk, 0:ncols],
                                       start=(kk == 0), stop=(kk == KC - 1), R=[wB, hB[j]], W=[self.psB[pi]])
                            fT(tag, s, tb, pi)
            k.fence()

    def nsa_proj(self, l):
        k = self.k
        nc = self.nc
        with ExitStack() as es:
            cs = [es.enter_context(self.sbt("cs%d" % i, [128, 2, 512], F32)) for i in range(2)]
            csB = [Buf() for _ in range(2)]
            raw = [es.enter_context(self.sbt("raw%d" % i, [128, 512], BF16)) for i in range(2)]
            rawB = [Buf() for _ in range(2)]
            t1 = [es.enter_context(self.sbt("t1_%d" % i, [128, 512], F32)) for i in range(2)]
            t1B = [Buf() for _ in range(2)]
            t2 = [es.enter_context(self.sbt("t2_%d" % i, [128, 512], F32)) for i in range(2)]
            t2B = [Buf() for _ in range(2)]
            ob = [es.enter_context(self.sbt("ob%d" % i, [128, 512], BF16)) for i in range(3)]
            obB = [Buf() for _ in range(3)]
            gb = es.enter_context(self.sbt("gb", [128, 48], F32))
            gbB = Buf()
            gt = es.enter_context(self.sbt("gt", [128, 48], F32))
            gtB = Buf()
            k.dma(k.sp, gb[:, :], self.bcast_row(self.a_gate_b, l * 48, 48), W=[gbB])
            st = {"n": 0, "tb": -1}
            featdst = {4: self.kcT, 5: self.vcT, 6: self.ksT, 8: self.kwT}
            featname = {4: "kcT", 5: "vcT", 6: "ksT", 8: "kwT"}

            def fF(tag, jj, tb, pi):
                n = st["n"]
                st["n"] += 1
                i2 = n % 2
                i3 = n % 3
                if st["tb"] != tb:
                    st["tb"] = tb
                    k.dma(k.sp, cs[tb % 2][:, :, :], self.c_cossin[:, :, tb * 512:(tb + 1) * 512], W=[csB[tb % 2]])
                c_ = cs[tb % 2]
                cB = csB[tb % 2]
                P = self.ps[pi]
                PB = self.psB[pi]
                if tag == 5:
                    self.E(k.act, "copy", ob[i3][:, :], P[:, :], R=[PB], W=[obB[i3]])
                else:
                    self.E(k.act, "copy", raw[i2][:, :], P[:, :], R=[PB], W=[rawB[i2]])
                    p2 = 4 + i2
                    self.E(k.pe, "matmul", self.ps[p2][:, :], self.cst[:, 1, :], raw[i2][:, :], start=True, stop=True,
                           R=[self.cstB, rawB[i2]], W=[self.psB[p2]])
                    self.E(k.dve, "tensor_tensor", t1[i2][:, :], P[:, :], c_[:, 0, :], ALU.mult, R=[PB, cB], W=[t1B[i2]])
                    self.E(k.dve, "tensor_tensor", t2[i2][:, :], self.ps[p2][:, :], c_[:, 1, :], ALU.mult, R=[self.psB[p2], cB], W=[t2B[i2]])
                    self.E(k.pool, "tensor_tensor", ob[i3][:, :], t1[i2][:, :], t2[i2][:, :], ALU.add, R=[t1B[i2], t2B[i2]], W=[obB[i3]])
                if tag < 4:
                    dst = self.qT[tag, :, 4 * tb:4 * tb + 4, jj, :]
                    k.dma(k.sp, dst, ob[i3][:, :].rearrange("p (a t) -> p a t", a=4), R=[obB[i3]], W=[self.db("qT", (tag, tb))])
                else:
                    k.dma(k.sp, featdst[tag][jj, :, tb * 512:(tb + 1) * 512], ob[i3][:, :], R=[obB[i3]], W=[self.db(featname[tag], (jj, tb))])

            def fT(tag, s, tb, pi):
                n = st["n"]
                st["n"] += 1
                i3 = n % 3
                P = self.ps[pi]
                PB = self.psB[pi]
                tt = 4 * tb + s
                if tag == 10:
                    self.E(k.dve, "tensor_tensor", gt[:, :], P[:, 0:48], gb[:, :], ALU.add, R=[PB, gbB], W=[gtB])
                    self.E(k.act, "activation", self.gate_sb[:, tt, :], gt[:, :], AF.Sigmoid, R=[gtB], W=[self.gateB])
                else:
                    dst = self.vs if tag == 7 else self.vw
                    self.E(k.act, "copy", ob[i3][:, :], P[:, :], R=[PB], W=[obB[i3]])
                    k.dma(k.sp, dst[tt * 128:(tt + 1) * 128, :], ob[i3][:, :], R=[obB[i3]], W=[self.db("vs" if tag == 7 else "vw", tt)])

            blocks = [(wb * 512, 512, "F", wb) for wb in (0, 1, 2, 3, 4, 5, 6, 8)]
            blocks += [(7 * 512, 512, "T", 7), (9 * 512, 512, "T", 9), (5120, 48, "T", 10)]
            import os
            if os.environ.get("TB"):
                keep = [int(x) for x in os.environ["TB"].split(",")]
                blocks = [b for b in blocks if b[3] in keep]
            self.proj(0, (self.wsc_p, "wsc_p"), blocks, fF, fT)

    def nsa_compress(self, l, es_outer):
        k = self.k
        nc = self.nc
        kcmpT = es_outer.enter_context(self.sbt("kcmpT", [128, G, 256], BF16))
        kcB = Buf()
        vext = es_outer.enter_context(self.sbt("vext", [128, G, 2, 193], BF16))
        vxB = Buf()
        self.E(k.pool, "memset", kcmpT[:, :, :], 0.0, W=[kcB])
        self.E(k.pool, "memset", vext[:, :, :, :], 0.0, W=[vxB])
        for g in range(G):
            k.dma(k.sp, vext[:, g, :, 128:193], self.c_vext[:, :, :], W=[vxB])
        with ExitStack() as es:
            src = [es.enter_context(self.sbt("csrc%d" % i, [128, T], BF16)) for i in range(2)]
            srcB = [Buf() for _ in range(2)]
            w1 = es.enter_context(self.sbt("cw1", [128, 32, 256], BF16))
            w1B = Buf()
            w2 = es.enter_context(self.sbt("cw2", [128, 2, 128], BF16))
            w2B = Buf()
            pef = es.enter_context(self.sbt("pef", [128, 32], F32))
            pefB = Buf()
            perep = es.enter_context(self.sbt("perep", [128, 32, 128], BF16))
            peB = Buf()
            xs = es.enter_context(self.sbt("gxs", [128, 256], F32))
            xsB = Buf()
            u = es.enter_context(self.sbt("gu", [128, 256], F32))
            uB = Buf()
            sg = es.enter_context(self.sbt("gsg", [128, 256], F32))
            sgB = Buf()
            hid = es.enter_context(self.sbt("ghid", [128, 256], BF16))
            hidB = Buf()
            hidT = es.enter_context(self.sbt("ghidT", [128, 2, 128], BF16))
            hidTB = Buf()
            n = 0
            for kv in range(2):
                k.dma(k.pool, w1[:, :, :], self.a_cmp_w1.ap()[l, kv].rearrange("(c p) n -> p c n", p=128), W=[w1B])
                k.dma(k.pool, w2[:, :, :], self.a_cmp_w2.ap()[l, kv].rearrange("(c p) n -> p c n", p=128), W=[w2B])
                k.dma(k.sp, pef[:, :], self.a_cmp_peT.ap()[l, kv], W=[pefB])
                self.E(k.dve, "tensor_copy", perep[:, :, :], pef[:, :].unsqueeze(2).to_broadcast([128, 32, 128]), R=[pefB], W=[peB])
                srcd = self.kcT if kv == 0 else self.vcT
                sname = "kcT" if kv == 0 else "vcT"
                for g in range(G):
                    sj = n % 2
                    n += 1
                    k.dma(k.sp, src[sj][:, :], srcd[g, :, :], R=[self.db(sname, (g, tb)) for tb in range(8)], W=[srcB[sj]])
                    for half in range(2):
                        nr = 128 if half == 0 else 127
                        n0 = half * 128
                        pi = 0
                        P = self.ps[pi]
                        PB = self.psB[pi]
                        for li in range(32):
                            self.E(k.pe, "matmul", P[0:nr, 0:256], perep[:, li, 0:nr], w1[:, li, :], start=(li == 0), stop=False,
                                   R=[peB, w1B], W=[PB])
                        for li in range(32):
                            a0 = li + 16 * n0
                            self.E(k.pe, "matmul", P[0:nr, 0:256], src[sj][:, a0:a0 + 16 * (nr - 1) + 1:16], w1[:, li, :], start=False, stop=(li == 31),
                                   R=[srcB[sj], w1B], W=[PB])
                        self.E(k.act, "copy", xs[0:nr, :], P[0:nr, 0:256], R=[PB], W=[xsB])
                        self.E(k.dve, "tensor_tensor", u[0:nr, :], xs[0:nr, :], xs[0:nr, :], ALU.mult, R=[xsB], W=[uB])
                        self.E(k.dve, "tensor_scalar", u[0:nr, :], u[0:nr, :], 0.044715, 1.0, ALU.mult, ALU.add, R=[uB], W=[uB])
                        self.E(k.dve, "tensor_tensor", u[0:nr, :], u[0:nr, :], xs[0:nr, :], ALU.mult, R=[uB, xsB], W=[uB])
                        self.E(k.act, "activation", sg[0:nr, :], u[0:nr, :], AF.Sigmoid, scale=1.5957691216057308, R=[uB], W=[sgB])
                        if nr < 128:
                            self.E(k.pool, "memset", hid[:, :], 0.0, W=[hidB])
                        self.E(k.dve, "tensor_tensor", hid[0:nr, :], xs[0:nr, :], sg[0:nr, :], ALU.mult, R=[xsB, sgB], W=[hidB])
                        pb = self.psbf(1)
                        for c in range(2):
                            self.E(k.pe, "transpose", pb[:, c * 128:(c + 1) * 128], hid[:, c * 128:(c + 1) * 128], self.ident(),
                                   R=[hidB, self.cstB], W=[self.psB[1]])
                        self.E(k.act, "copy", hidT[:, :, :], pb[:, 0:256].rearrange("p (c n) -> p c n", c=2), R=[self.psB[1]], W=[hidTB])
                        P2 = self.ps[2]
                        if kv == 0:
                            for c in range(2):
                                self.E(k.pe, "matmul", P2[:, 0:nr], w2[:, c, :], hidT[:, c, 0:nr], start=(c == 0), stop=(c == 1),
                                       R=[w2B, hidTB], W=[self.psB[2]])
                            self.E(k.act, "copy", kcmpT[:, g, n0:n0 + nr], P2[:, 0:nr], R=[self.psB[2]], W=[kcB])
                        else:
                            for c in range(2):
                                self.E(k.pe, "matmul", P2[0:nr, 0:128], hidT[:, c, 0:nr], w2[:, c, :], start=(c == 0), stop=(c == 1),
                                       R=[w2B, hidTB], W=[self.psB[2]])
                            self.E(k.act, "copy", vext[0:nr, g, half, 0:128], P2[0:nr, 0:128], R=[self.psB[2]], W=[vxB])
            k.fence()
        return kcmpT, kcB, vext, vxB

    def attn_pair(self, kT, kB, q, qB, vx, vxB_, nv, mask, mB, acc, first, last, pt, ptB, spi, nrows=128):
        k = self.k
        P = self.ps[spi]
        PB = self.psB[spi]
        self.E(k.pe, "matmul", P[0:nrows, :], kT, q, start=True, stop=True, R=[kB, qB], W=[PB])

        def rest():
            self.E(k.act, "activation", pt[0:nrows, :], P[0:nrows, :], AF.Exp, scale=SCALE, R=[PB], W=[ptB])
            if mask is not None:
                self.E(k.dve, "tensor_tensor", pt[0:nrows, :].rearrange("p (r t) -> p r t", r=4), pt[0:nrows, :].rearrange("p (r t) -> p r t", r=4),
                       mask.unsqueeze(1).to_broadcast([nrows, 4, 128]), ALU.mult, R=[ptB, mB], W=[ptB])
            for r in range(4):
                self.E(k.pe, "matmul", self.ps[acc[r]][:, 0:nv], pt[0:nrows, r * 128:(r + 1) * 128], vx, start=first, stop=last,
                       R=[ptB, vxB_], W=[self.psB[acc[r]]])

        prev = getattr(self, "_pend", None)
        self._pend = rest
        if prev is not None:
            prev()

    def attn_flush(self):
        prev = getattr(self, "_pend", None)
        self._pend = None
        if prev is not None:
            prev()

    def nsa_attn(self, l):
        k = self.k
        nc = self.nc
        with ExitStack() as es0:
            kcmpT, kcB, vext, vxB = self.nsa_compress(l, es0)
            with ExitStack() as es:
                sb = lambda name, shape, dtp: es.enter_context(self.sbt(name, shape, dtp))
                maskc_r = [sb("maskc%d" % i, [128, 2, 128], BF16) for i in range(2)]
                mcB_r = [Buf() for _ in range(2)]
                selm_r = [sb("selm%d" % i, [128, 2, 64], F32) for i in range(2)]
                smB_r = [Buf() for _ in range(2)]
                qg = sb("qg", [128, NT, 512], BF16)
                qgB = Buf()
                ksg = sb("ksg", [128, T], BF16)
                ksB = Buf()
                kwg = sb("kwg", [128, T], BF16)
                kwB = Buf()
                vsg = sb("vsg", [128, NT, 129], BF16)
                vsB = Buf()
                vwg = sb("vwg", [128, NT, 129], BF16)
                vwB = Buf()
                self.E(k.pool, "memset", vsg[:, :, 128:129], 1.0, W=[vsB])
                self.E(k.pool, "memset", vwg[:, :, 128:129], 1.0, W=[vwB])
                pt = [sb("pt%d" % i, [128, 512], BF16) for i in range(3)]
                ptB = [Buf() for _ in range(3)]
                msk = [sb("msk%d" % i, [128, NT, 128], BF16) for i in range(2)]
                mskB = [Buf() for _ in range(2)]
                ot = [sb("ot%d" % i, [128, 512], F32) for i in range(2)]
                otB = [Buf() for _ in range(2)]
                otb = [sb("otb%d" % i, [128, 512], BF16) for i in range(2)]
                otbB = [Buf() for _ in range(2)]
                st = [sb("ast%d" % i, [128, 16], F32) for i in range(2)]
                stB = [Buf() for _ in range(2)]
                imp = sb("imp", [128, 64], F32)
                impB = Buf()
                vv = sb("vv", [128, 64], F32)
                vvB = Buf()
                cmpt = sb("cmpt", [128, 64, 64], BF16)
                cmpB = Buf()
                cnt = sb("cnt", [128, 64], F32)
                cntB = Buf()
                selb = sb("selb", [128, 64], BF16)
                selbB = Buf()
                selT_t = [sb("selTt%d" % i, [64, 128], BF16) for i in range(2)]
                selTB = [Buf() for _ in range(2)]
                w = {"hTt": [sb("ohTt%d" % i, [128, 4, 128], BF16) for i in range(2)], "hTtB": [Buf() for _ in range(2)]}
                acc = (4, 5, 6, 7)
                npair = 0

                def finish_branch(tt, g, gi, nv, stt, sB_, first_out, o, oB, prev=None, prevB=None):
                    for r in range(4):
                        A = self.ps[acc[r]]
                        AB = self.psB[acc[r]]
                        c0 = r * 3
                        self.E(k.dve, "tensor_scalar", stt[:, c0:c0 + 1], A[:, 128:129], 1e-30, None, ALU.max, R=[AB], W=[sB_])
                        self.E(k.dve, "reciprocal", stt[:, c0 + 1:c0 + 2], stt[:, c0:c0 + 1], R=[sB_], W=[sB_])
                        self.E(k.dve, "tensor_tensor", stt[:, c0 + 2:c0 + 3], stt[:, c0 + 1:c0 + 2],
                               self.gate_sb[:, tt, (g * 4 + r) * 3 + gi:(g * 4 + r) * 3 + gi + 1], ALU.mult, R=[sB_, self.gateB], W=[sB_])
                        if prev is None:
                            self.E(k.dve, "tensor_scalar", o[:, r * 128:(r + 1) * 128], A[:, 0:128], stt[:, c0 + 2:c0 + 3], None, ALU.mult,
                                   R=[AB, sB_], W=[oB])
                        else:
                            self.E(k.dve, "scalar_tensor_tensor", o[:, r * 128:(r + 1) * 128], A[:, 0:128], stt[:, c0 + 2:c0 + 3],
                                   prev[:, r * 128:(r + 1) * 128], ALU.mult, ALU.add, R=[AB, sB_, prevB], W=[oB])

                for g in range(G):
                    k.dma(k.sp, qg[:, :, :], self.qT[g].rearrange("p a r t -> p a (r t)"), R=[self.db("qT", (g, tb)) for tb in range(8)], W=[qgB])
                    k.dma(k.sp, ksg[:, :], self.ksT[g, :, :], R=[self.db("ksT", (g, tb)) for tb in range(8)], W=[ksB])
                    k.dma(k.sp, kwg[:, :], self.kwT[g, :, :], R=[self.db("kwT", (g, tb)) for tb in range(8)], W=[kwB])
                    k.dma(k.sp, vsg[:, :, 0:128], self.vs[:, g * 128:(g + 1) * 128].rearrange("(j p) c -> p j c", p=128),
                          R=[self.db("vs", tt) for tt in range(NT)], W=[vsB])
                    k.dma(k.sp, vwg[:, :, 0:128], self.vw[:, g * 128:(g + 1) * 128].rearrange("(j p) c -> p j c", p=128),
                          R=[self.db("vw", tt) for tt in range(NT)], W=[vwB])
                    for tt in range(NT):
                        q = qg[:, tt, :]
                        maskc = maskc_r[tt % 2]
                        mcB = mcB_r[tt % 2]
                        selm = selm_r[tt % 2]
                        smB = smB_r[tt % 2]
                        k.dma(k.sp, maskc[:, :, :], self.c_maskc[:, :, tt * 128:(tt + 1) * 128], W=[mcB])
                        k.dma(k.sp, selm[:, :, :], self.c_selm[:, :, tt, :], W=[smB])
                        halves = [0] + ([1] if tt >= 16 else [])
                        for hi, half in enumerate(halves):
                            need_mask = not (half == 0 and tt >= 17)
                            m = maskc[:, half, :] if need_mask else None
                            pi = npair % 3
                            self.attn_pair(kcmpT[:, g, half * 128:(half + 1) * 128], kcB, q, qgB, vext[:, g, half, :], vxB, 193,
                                           m, mcB, acc, hi == 0, hi == len(halves) - 1, pt[pi], ptB[pi], npair % 3)
                            npair += 1
                        j = tt % 2
                        stt = st[j]
                        sB_ = stB[j]
                        self.attn_flush()
                        finish_branch(tt, g, 0, 193, stt, sB_, True, ot[j], otB[j])
                        for r in range(4):
                            A = self.ps[acc[r]]
                            AB = self.psB[acc[r]]
                            if r == 0:
                                self.E(k.dve, "tensor_scalar", imp[:, :], A[:, 129:193], stt[:, 1:2], None, ALU.mult, R=[AB, sB_], W=[impB])
                            else:
                                self.E(k.dve, "scalar_tensor_tensor", imp[:, :], A[:, 129:193], stt[:, r * 3 + 1:r * 3 + 2], imp[:, :],
                                       ALU.mult, ALU.add, R=[AB, sB_, impB], W=[impB])
                        k.dma(k.sp, self.opart[tt * 128:(tt + 1) * 128, g * 512:(g + 1) * 512], ot[j][:, :], R=[otB[j]], W=[self.db("opart", (g, tt))])
                        self.E(k.dve, "tensor_tensor", vv[:, :], imp[:, :], selm[:, 0, :], ALU.mult, R=[impB, smB], W=[vvB])
                        self.E(k.dve, "tensor_tensor", vv[:, :], vv[:, :], selm[:, 1, :], ALU.add, R=[vvB, smB], W=[vvB])
                        self.E(k.dve, "tensor_tensor", cmpt[:, :, :], vv[:, :].unsqueeze(1).to_broadcast([128, 64, 64]),
                               vv[:, :].unsqueeze(2).to_broadcast([128, 64, 64]), ALU.is_gt, R=[vvB], W=[cmpB])
                        self.E(k.dve, "reduce_sum", cnt[:, :], cmpt[:, :, :], AX.X, R=[cmpB], W=[cntB])
                        self.E(k.dve, "tensor_scalar", selb[:, :], cnt[:, :], 15.5, None, ALU.is_lt, R=[cntB], W=[selbB])
                        pb = self.psbf(3)
                        self.E(k.pe, "transpose", pb[0:64, 0:128], selb[:, :], self.ident(), R=[selbB, self.cstB], W=[self.psB[3]])
                        self.E(k.act, "copy", selT_t[j][:, :], pb[0:64, 0:128], R=[self.psB[3]], W=[selTB[j]])
                        k.dma(k.sp, self.selT[g, :, tt * 128:(tt + 1) * 128], selT_t[j][:, :], R=[selTB[j]], W=[self.db("selT", (g, tt))])
                    for tt in range(NT):
                        q = qg[:, tt, :]
                        j = tt % 2
                        mk = msk[j]
                        mkB = mskB[j]
                        for b in range(2):
                            srcap = bass.AP(self.selT, g * 64 * T + b * T + tt * 128, [[0, 64], [2 * T, tt + 1], [1, 128]])
                            k.dma(k.sp, mk[b * 64:(b + 1) * 64, 0:tt + 1, :], srcap, R=[self.db("selT", (g, tt))], W=[mkB])
                        self.E(k.pool, "tensor_tensor", mk[:, tt, :], mk[:, tt, :], self.cst[:, 4, :], ALU.mult, R=[mkB, self.cstB], W=[mkB])
                        k.dma(k.sp, ot[j][:, :], self.opart[tt * 128:(tt + 1) * 128, g * 512:(g + 1) * 512], R=[self.db("opart", (g, tt))], W=[otB[j]])
                        stt = st[j]
                        sB_ = stB[j]
                        js = list(range(max(0, tt - 4), tt + 1))
                        for ji, jk in enumerate(js):
                            m = None
                            if jk == tt:
                                m = self.cst[:, 4, :]
                            elif jk == tt - 4:
                                m = self.cst[:, 5, :]
                            pi = npair % 3
                            self.attn_pair(kwg[:, jk * 128:(jk + 1) * 128], kwB, q, qgB, vwg[:, jk, :], vwB, 129,
                                           m, self.cstB, acc, ji == 0, ji == len(js) - 1, pt[pi], ptB[pi], npair % 3)
                            npair += 1
                        self.attn_flush()
                        finish_branch(tt, g, 2, 129, stt, sB_, False, ot[j], otB[j], ot[j], otB[j])
                        for jk in range(tt + 1):
                            pi = npair % 3
                            self.attn_pair(ksg[:, jk * 128:(jk + 1) * 128], ksB, q, qgB, vsg[:, jk, :], vsB, 129,
                                           mk[:, jk, :], mkB, acc, jk == 0, jk == tt, pt[pi], ptB[pi], npair % 3)
                            npair += 1
                        self.attn_flush()
                        finish_branch(tt, g, 1, 129, stt, sB_, False, ot[j], otB[j], ot[j], otB[j])
                        self.E(k.pool, "tensor_copy", otb[j][:, :], ot[j][:, :], R=[otB[j]], W=[otbB[j]])
                        self.transpose_to_hT(w, otb[j], otbB[j], 2, tt, pbanks=(3,), nch=4, ch0=g * 4)
                k.fence()

    def outproj(self, xsrc, xsname, xdst, xdname, vc, va, vb, hTdst):
        k = self.k
        nc = self.nc
        with ExitStack() as es:
            w = self.alloc_norm_tiles(es)
            ysb = [es.enter_context(self.sbt("ysb%d" % i, [128, D], F32)) for i in range(4)]
            yB = [Buf() for _ in range(4)]
            xt = [es.enter_context(self.sbt("xt%d" % i, [128, D], F32)) for i in range(2)]
            xB = [Buf() for _ in range(2)]
            hb = [es.enter_context(self.sbt("phT%d" % i, [128, KC, 512], BF16)) for i in range(2)]
            hB = [Buf() for _ in range(2)]
            cnt = 0
            for tb in range(T // 512):
                j = tb % 2
                k.dma(k.sp, hb[j][:, :, :], self.hT[2][:, :, tb * 512:(tb + 1) * 512],
                      R=[self.db("hT2", 4 * tb + s) for s in range(4)], W=[hB[j]])
                for cb in range(4):
                    wt, wB = self.wload_bf(self.wsc_o, "wsc_o", cb)
                    for s in range(4):
                        pi = 4 + cnt % 4
                        cnt += 1
                        for kk in range(KC):
                            self.E(k.pe, "matmul", self.ps[pi][:, :], hb[j][:, kk, s * 128:(s + 1) * 128], wt[:, kk, :],
                                   start=(kk == 0), stop=(kk == KC - 1), R=[wB, hB[j]], W=[self.psB[pi]])
                        self.E(k.act, "copy", ysb[s][:, cb * 512:(cb + 1) * 512], self.ps[pi][:, :], R=[self.psB[pi]], W=[yB[s]])
                for s in range(4):
                    tt = 4 * tb + s
                    self.residual_tile(w, ysb[s][:, :], yB[s], xt[s % 2][:, :], xB[s % 2], xsrc, xsname, xdst, xdname, vc, tt)
                    if hTdst is not None:
                        self.prenorm_tile(w, xt[s % 2][:, :], xB[s % 2], va, vb, hTdst, tt)
            k.fence()

    def ffn(self, l, xsrc, xsname, xdst, xdname, vc, va, vb, hTdst):
        k = self.k
        nc = self.nc
        Win = self.ffn_w_in.ap()[l]
        Wout = self.ffn_w_out.ap()[l]
        NF = DFF // 128
        with ExitStack() as es:
            w = self.alloc_norm_tiles(es)
            ysb = [es.enter_context(self.sbt("ysb%d" % i, [128, D], F32)) for i in range(4)]
            yB = [Buf() for _ in range(4)]
            xt = [es.enter_context(self.sbt("xt%d" % i, [128, D], F32)) for i in range(1)]
            xB = [Buf() for _ in range(1)]
            hb = [es.enter_context(self.sbt("phT%d" % i, [128, KC, 512], BF16)) for i in range(1)]
            hB = [Buf() for _ in range(1)]
            actT = es.enter_context(self.sbt("actT", [128, NF, 512], BF16))
            aB = [Buf() for _ in range(11)]
            sg = [es.enter_context(self.sbt("fsg%d" % i, [128, 512], F32)) for i in range(2)]
            sgB = [Buf() for _ in range(2)]
            n = 0
            for tb in range(T // 512):
                j = 0
                k.dma(k.sp, hb[j][:, :, :], self.hT[1][:, :, tb * 512:(tb + 1) * 512],
                      R=[self.db("hT1", 4 * tb + s) for s in range(4)], W=[hB[j]])
                for fg in range(11):
                    wg, wgB = self.wload_bf(self.wsc_in, "wsc_in", 2 * fg)
                    wu, wuB = self.wload_bf(self.wsc_in, "wsc_in", 2 * fg + 1)
                    for jj in range(4):
                        i2 = n % 2
                        n += 1
                        pg = i2
                        pu = 2 + i2
                        for kk in range(KC):
                            self.E(k.pe, "matmul", self.ps[pg][:, :], wg[:, kk, jj * 128:(jj + 1) * 128], hb[j][:, kk, :],
                                   start=(kk == 0), stop=(kk == KC - 1), R=[wgB, hB[j]], W=[self.psB[pg]])
                        for kk in range(KC):
                            self.E(k.pe, "matmul", self.ps[pu][:, :], wu[:, kk, jj * 128:(jj + 1) * 128], hb[j][:, kk, :],
                                   start=(kk == 0), stop=(kk == KC - 1), R=[wuB, hB[j]], W=[self.psB[pu]])
                        self.E(k.act, "activation", sg[i2][:, :], self.ps[pg][:, :], AF.Silu, R=[self.psB[pg]], W=[sgB[i2]])
                        self.E(k.dve, "tensor_tensor", actT[:, fg * 4 + jj, :], sg[i2][:, :], self.ps[pu][:, :], ALU.mult,
                               R=[sgB[i2], self.psB[pu]], W=[aB[fg]])
                for cb in range(4):
                    for kg in range(4):
                        wt, wB = self.wload_bf(self.wsc_out, "wsc_out", cb * 4 + kg, kc=11)
                        for s in range(4):
                            pi = 4 + s
                            for kk in range(11):
                                ch = kg * 11 + kk
                                self.E(k.pe, "matmul", self.ps[pi][:, :], actT[:, ch, s * 128:(s + 1) * 128], wt[:, kk, :],
                                       start=(ch == 0), stop=(ch == NF - 1), R=[wB, aB[ch // 4]], W=[self.psB[pi]])
                    for s in range(4):
                        self.E(k.act, "copy", ysb[s][:, cb * 512:(cb + 1) * 512], self.ps[4 + s][:, :], R=[self.psB[4 + s]], W=[yB[s]])
                for s in range(4):
                    tt = 4 * tb + s
                    self.residual_tile(w, ysb[s][:, :], yB[s], xt[0][:, :], xB[0], xsrc, xsname, xdst, xdname, vc, tt)
                    if hTdst is not None:
                        self.prenorm_tile(w, xt[0][:, :], xB[0], va, vb, hTdst, tt, pbanks=(0, 1))
            k.fence()

    def sb_kv(self, xsrc, xname):
        k = self.k
        self.modvec(1, self.kv_mod_w.ap(), 0, self.kv_mod_b, 0)
        self.modvec(0, self.kv_mod_w.ap(), D, self.kv_mod_b, D)
        self.load_gain(2, self.kv_norm_g, 0)
        self.E(k.dve, "scalar_tensor_tensor", self.vec[0][:, :], self.vec[0][:, :], 1.0, self.vec[2][:, :], ALU.add, ALU.mult,
               R=[self.vecB[0], self.vecB[2]], W=[self.vecB[0]])
        self.norm_phase(xsrc, xname, 0, 1, 2)
        nc = self.nc
        with ExitStack() as es:
            ob = [es.enter_context(self.sbt("ob%d" % i, [128, 512], BF16)) for i in range(3)]
            obB = [Buf() for _ in range(3)]
            st = {"n": 0}

            def fF(tag, jj, tb, pi):
                i3 = st["n"] % 3
                st["n"] += 1
                self.E(k.act, "copy", ob[i3][:, :], self.ps[pi][:, :], R=[self.psB[pi]], W=[obB[i3]])
                k.dma(k.sp, self.kTsh[tag * 4 + jj, :, tb * 512:(tb + 1) * 512], ob[i3][:, :], R=[obB[i3]], W=[self.db("kTsh", (tag * 4 + jj, tb))])

            def fT(tag, s, tb, pi):
                i3 = st["n"] % 3
                st["n"] += 1
                tt = 4 * tb + s
                self.E(k.dve, "tensor_copy", ob[i3][:, :], self.ps[pi][:, :], R=[self.psB[pi]], W=[obB[i3]])
                k.dma(k.sp, self.vsh[tt * 128:(tt + 1) * 128, tag * 512:(tag + 1) * 512], ob[i3][:, :], R=[obB[i3]], W=[self.db("vsh", (tag, tt))])

            blocks = [(c * 512, 512, "F", c) for c in range(4)] + [(D + c * 512, 512, "T", c) for c in range(4)]
            self.proj(2, (self.wsc_kv, "wsc_kv"), blocks, fF, fT)

    def sb_q(self, l2):
        k = self.k
        nc = self.nc
        with ExitStack() as es:
            ob = [es.enter_context(self.sbt("ob%d" % i, [128, 512], BF16)) for i in range(3)]
            obB = [Buf() for _ in range(3)]
            st = {"n": 0}

            def fF(tag, jj, tb, pi):
                i3 = st["n"] % 3
                st["n"] += 1
                self.E(k.act, "copy", ob[i3][:, :], self.ps[pi][:, :], R=[self.psB[pi]], W=[obB[i3]])
                k.dma(k.sp, self.qsb[tag * 4 + jj, :, tb * 512:(tb + 1) * 512], ob[i3][:, :], R=[obB[i3]], W=[self.db("qsb", (tag * 4 + jj, tb))])

            blocks = [(c * 512, 512, "F", c) for c in range(4)]
            self.proj(0, (self.wsc_p, "wsc_p"), blocks, fF, None)

    def sb_attn(self):
        k = self.k
        nc = self.nc
        with ExitStack() as es:
            sb = lambda name, shape, dtp: es.enter_context(self.sbt(name, shape, dtp))
            mb = sb("msbb", [128, 4, 512], BF16)
            mbB = Buf()
            k.dma(k.sp, mb[:, :, :], self.c_msb_b[:, :, :], W=[mbB])
            kT = [sb("skT%d" % i, [128, T], BF16) for i in range(2)]
            kTB = [Buf() for _ in range(2)]
            qT = [sb("sqT%d" % i, [128, T], BF16) for i in range(2)]
            qTB = [Buf() for _ in range(2)]
            vh = [sb("svh%d" % i, [128, NT, 128], BF16) for i in range(2)]
            vhB = [Buf() for _ in range(2)]
            NR = 3
            ee = [sb("see%d" % i, [128, 512], F32) for i in range(NR)]
            eeB = [Buf() for _ in range(NR)]
            spb = [sb("sspb%d" % i, [128, 512], BF16) for i in range(NR)]
            spbB = [Buf() for _ in range(NR)]
            t3 = [sb("st3_%d" % i, [128, 512], F32) for i in range(NR)]
            t3B = [Buf() for _ in range(NR)]
            pt = [sb("spt%d" % i, [128, 512], BF16) for i in range(NR)]
            ptB = [Buf() for _ in range(NR)]
            carry = sb("scarry", [128, 512], F32)
            cB = Buf()
            ob = [sb("sob%d" % i, [128, 512], BF16) for i in range(2)]
            obB = [Buf() for _ in range(2)]
            st = {"n": 0, "nq": 0}

            def stageA(p):
                Pz = self.ps[p["zi"]]
                self.E(k.pe, "matmul", Pz[:, :], p["kT"], p["q"], start=True, stop=True, R=[p["kB"], p["qB"]], W=[self.psB[p["zi"]]])

            def stageB(p):
                i = p["i"]
                Pz = self.ps[p["zi"]]
                PzB = self.psB[p["zi"]]
                self.E(k.act, "activation", ee[i][:, :], Pz[:, :], AF.Exp, scale=SCALE, R=[PzB], W=[eeB[i]])
                self.E(k.act, "activation", spb[i][:, :], ee[i][:, :], AF.Ln, bias=1.0, R=[eeB[i]], W=[spbB[i]])
                if p["a"] >= 0:
                    self.E(k.pool, "tensor_tensor", spb[i][:, :], spb[i][:, :], mb[:, p["a"], :], ALU.mult, R=[spbB[i], mbB], W=[spbB[i]])
                self.E(k.pe, "matmul", Pz[:, :], self.cst[:, 6, :], spb[i][:, :], start=False, stop=True, skip_group_check=True,
                       R=[self.cstB, spbB[i]], W=[PzB])
                if not p["last"]:
                    ti = p["ti"]
                    self.E(k.pe, "matmul", self.ps[ti][:, :], self.cst[:, 2, :], spb[i][:, :], start=True, stop=True,
                           R=[self.cstB, spbB[i]], W=[self.psB[ti]])

            def stageC(p):
                i = p["i"]
                Pz = self.ps[p["zi"]]
                PzB = self.psB[p["zi"]]
                if p["first"]:
                    self.E(k.act, "activation", pt[i][:, :], Pz[:, :], AF.Exp, scale=SCALE, R=[PzB], W=[ptB[i]])
                else:
                    self.E(k.dve, "scalar_tensor_tensor", t3[i][:, :], Pz[:, :], SCALE, carry[:, :], ALU.mult, ALU.subtract,
                           R=[PzB, cB], W=[t3B[i]])
                    self.E(k.act, "activation", pt[i][:, :], t3[i][:, :], AF.Exp, R=[t3B[i]], W=[ptB[i]])
                if p["a"] >= 0:
                    self.E(k.pool, "tensor_tensor", pt[i][:, :], pt[i][:, :], mb[:, p["a"], :], ALU.mult, R=[ptB[i], mbB], W=[ptB[i]])
                self.E(k.pe, "matmul", self.ps[p["acc"]][:, :], p["v"], pt[i][:, :], start=p["first"], stop=p["last"],
                       R=[p["vB"], ptB[i]], W=[self.psB[p["acc"]]])
                if not p["last"]:
                    ti = p["ti"]
                    if p["first"]:
                        self.E(k.dve, "tensor_copy", carry[:, :], self.ps[ti][:, :], R=[self.psB[ti]], W=[cB])
                    else:
                        self.E(k.dve, "tensor_tensor", carry[:, :], carry[:, :], self.ps[ti][:, :], ALU.add, R=[cB, self.psB[ti]], W=[cB])
                if p["last"]:
                    oj = st["nq"] % 2
                    st["nq"] += 1
                    self.E(k.act, "copy", ob[oj][:, :], self.ps[p["acc"]][:, :], R=[self.psB[p["acc"]]], W=[obB[oj]])
                    k.dma(k.sp, self.hT[2][:, p["h"], p["qb"] * 512:(p["qb"] + 1) * 512], ob[oj][:, :], R=[obB[oj]],
                          W=[self.db("hT2", 4 * p["qb"] + s) for s in range(4)])

            pairs = []
            for h in range(NH):
                hj = h % 2
                for qb in range(T // 512):
                    jtop = 4 * qb + 3
                    for jk in range(jtop, -1, -1):
                        pairs.append(dict(h=h, hj=hj, qb=qb, jk=jk, a=jk - 4 * qb, first=(jk == jtop), last=(jk == 0)))
            loaded = set()
            nacc = 0
            for n, p in enumerate(pairs):
                p["i"] = n % NR
                p["zi"] = n % 3
                p["ti"] = 3 + n % 2
                if p["first"]:
                    nacc += 1
                p["acc"] = 6 + nacc % 2
            def ensure_loaded(h):
                if h in loaded or h >= NH:
                    return
                loaded.add(h)
                hj = h % 2
                k.dma(k.sp, kT[hj][:, :], self.kTsh[h, :, :], R=[self.db("kTsh", (h, tb)) for tb in range(8)], W=[kTB[hj]])
                k.dma(k.sp, qT[hj][:, :], self.qsb[h, :, :], R=[self.db("qsb", (h, tb)) for tb in range(8)], W=[qTB[hj]])
                k.dma(k.sp, vh[hj][:, :, :], self.vsh[:, h * 128:(h + 1) * 128].rearrange("(j p) c -> p j c", p=128),
                      R=[self.db("vsh", (h // 4, tt)) for tt in range(NT)], W=[vhB[hj]])
            def prep(p):
                ensure_loaded(p["h"])
                hj = p["hj"]
                jk = p["jk"]
                p["kT"] = kT[hj][:, jk * 128:(jk + 1) * 128]
                p["kB"] = kTB[hj]
                p["q"] = qT[hj][:, p["qb"] * 512:(p["qb"] + 1) * 512]
                p["qB"] = qTB[hj]
                p["v"] = vh[hj][:, jk, :]
                p["vB"] = vhB[hj]
            N = len(pairs)
            for n in range(N + 2):
                if n < N:
                    prep(pairs[n])
                    stageA(pairs[n])
                if 0 <= n - 1 < N:
                    stageB(pairs[n - 1])
                if 0 <= n - 2 < N:
                    stageC(pairs[n - 2])
            k.fence()

    def build(self):
        k = self.k
        self.setup()
        xin, xin_name = self.x.ap(), "x"
        self.precast_sq(self.wsc_p, "wsc_p", self.a_w_in.ap()[0], NAIN)
        for l in range(self.nlayers):
            last = (l == self.nlayers - 1)
            if l == 0:
                self.mod_AB(l, 0, 0, 0, 1, 2)
                self.norm_phase(xin, xin_name, 0, 1, 0)
                if self.stop == "norm0":
                    return self.nc
            if l == 2:
                self.sb_kv(xin, xin_name)
            self.precast_sq(self.wsc_o, "wsc_o", (self.a_w_out.ap()[l] if l < 2 else self.b_w_out.ap()[l - 2]), D)
            self.precast_ffn(l)
            if l < 2:
                with ExitStack() as esg:
                    self.gate_sb = esg.enter_context(self.sbt("gate_sb", [128, NT, 48], F32))
                    self.nsa_proj(l)
                    if self.stop == "nsa_proj":
                        return self.nc
                    self.nsa_attn(l)
                    k.fence()
                if self.stop == "nsa_attn":
                    return self.nc
                wout = self.a_w_out.ap()[l]
            else:
                self.sb_q(l - 2)
                self.sb_attn()
                wout = self.b_w_out.ap()[l - 2]
            self.mod_C(l, 0, 1, 2, 3)
            self.mod_AB(l, 1, 2, 0, 1, 3)
            self.outproj(xin, xin_name, self.xs1.ap(), "xs1", 2, 0, 1, 1)
            if self.stop == "outproj%d" % l:
                return self.nc
            self.mod_C(l, 1, 3, 2, 3)
            if not last:
                self.mod_AB(l + 1, 0, 0, 0, 1, 3)
            xdst, xdname = (self.out.ap(), "out") if last else (self.xs2.ap(), "xs2")
            if not last:
                if l + 1 < 2:
                    self.precast_sq(self.wsc_p, "wsc_p", self.a_w_in.ap()[l + 1], NAIN)
                else:
                    self.precast_sq(self.wsc_p, "wsc_p", self.b_w_q.ap()[l + 1 - 2], D)
                if l + 1 == 2:
                    self.precast_sq(self.wsc_kv, "wsc_kv", self.kv_w.ap(), 2 * D)
            self.ffn(l, self.xs1.ap(), "xs1", xdst, xdname, 2, 0, 1, None if last else 0)
            xin, xin_name = xdst, xdname
        k.fence()
        return self.nc


_CONSTS = None


def make_in_maps(inputs, ncores=8):
    global _CONSTS
    if _CONSTS is None:
        _CONSTS = make_consts()
    f = lambda a: np.ascontiguousarray(np.asarray(a, dtype=np.float32))
    shared = {
        "mod_w": f(inputs["mod_w"]), "mod_b": f(inputs["mod_b"]), "norm_g": f(inputs["norm_g"]),
        "ffn_w_in": f(inputs["ffn_w_in"]), "ffn_w_out": f(inputs["ffn_w_out"]), "a_w_in": f(inputs["a_w_in"]),
        "a_gate_b": f(inputs["a_gate_b"]),
        "a_cmp_peT": np.ascontiguousarray(np.asarray(inputs["a_cmp_pe"], dtype=np.float32).transpose(0, 1, 3, 2)),
        "a_cmp_w1": f(inputs["a_cmp_w1"]), "a_cmp_w2": f(inputs["a_cmp_w2"]), "a_w_out": f(inputs["a_w_out"]),
        "b_w_q": f(inputs["b_w_q"]), "b_w_out": f(inputs["b_w_out"]),
        "kv_norm_g": f(inputs["kv_norm_g"]).reshape(1, D), "kv_mod_w": f(inputs["kv_mod_w"]),
        "kv_mod_b": f(inputs["kv_mod_b"]).reshape(1, 2 * D), "kv_w": f(inputs["kv_w"]),
    }
    shared.update(_CONSTS)
    x = np.asarray(inputs["x"], dtype=np.float32)
    c = np.asarray(inputs["c"], dtype=np.float32)
    maps = []
    for core in range(ncores):
        b = core % 4
        m = dict(shared)
        m["x"] = np.ascontiguousarray(x[b])
        m["cT"] = np.ascontiguousarray(c[b].reshape(KC, 128).T)
        maps.append(m)
    return maps


def kernel(**inputs):
    prog = Prog()
    nc = prog.build()
    maps = make_in_maps(inputs)
    res = run_bass_kernel_spmd(nc, maps, core_ids=list(range(8)))
    out = np.stack([np.asarray(res.results[b]["out"], dtype=np.float32) for b in range(4)], axis=0)
    return out
```

```python
from contextlib import ExitStack
import numpy as np
import ml_dtypes
import concourse.bass as bass
import concourse.mybir as mybir
from concourse.bass_utils import run_bass_kernel_spmd

F32 = mybir.dt.float32
BF16 = mybir.dt.bfloat16
AF = mybir.ActivationFunctionType
ALU = mybir.AluOpType
AX = mybir.AxisListType

SAME_ENGINE_SYNC = True

D = 2048
T = 4096
NT = T // 128
KC = 16
DFF = 5632
NH = 16
G = 4
EPS = 1e-6
SCALE = 128 ** -0.5
NCMP = 255
NAIN = 5168


class Buf:
    __slots__ = ("name", "w", "r", "ps")

    def __init__(self, name="", ps=False):
        self.name = name
        self.w = None
        self.r = []
        self.ps = ps


class Eng:
    def __init__(self, nc, e, name, pe=False, ndma=0):
        self.e = e
        self.name = name
        self.sem = nc.alloc_semaphore("s_" + name)
        self.cnt = 0
        self.seen = {}
        self.pe = pe
        self.dsems = [[nc.alloc_semaphore("d_%s%d" % (name, i)), 0] for i in range(ndma)]
        self.dnext = 0


class K:
    def __init__(self):
        nc = bass.Bass("TRN2", target_bir_lowering=False)
        self.nc = nc
        self.pe = Eng(nc, nc.tensor, "pe", pe=True)
        self.act = Eng(nc, nc.scalar, "act")
        self.dve = Eng(nc, nc.vector, "dve")
        self.pool = Eng(nc, nc.gpsimd, "pool", ndma=8)
        self.sp = Eng(nc, nc.sync, "sp", ndma=16)
        self.engs = [self.pe, self.act, self.dve, self.pool, self.sp]
        self.nins = 0
        self.nwait = 0

    def _wait(self, eng, tok):
        sem, val = tok
        if sem is eng.sem and (eng.pe or not SAME_ENGINE_SYNC):
            return
        key = id(sem)
        if eng.seen.get(key, 0) >= val:
            return
        eng.e.wait_ge(sem, val)
        eng.seen[key] = val
        self.nwait += 1

    def _deps(self, eng, R, W):
        for b in R:
            if b.w is not None:
                self._wait(eng, b.w)
        for b in W:
            if b.w is not None:
                self._wait(eng, b.w)
            for t in b.r:
                self._wait(eng, t)

    def _commit(self, tok, R, W):
        for b in W:
            b.w = tok
            b.r = []
        for b in R:
            b.r.append(tok)
            if len(b.r) > 16:
                d = {}
                for s, v in b.r:
                    kk = id(s)
                    if kk not in d or d[kk][1] < v:
                        d[kk] = (s, v)
                b.r = list(d.values())

    def op(self, eng, fn, R=(), W=()):
        px = [b for b in R if b.ps and b not in W]
        if px:
            W = list(W) + px
            R = [b for b in R if not b.ps]
        self._deps(eng, R, W)
        ins = fn(eng.e)
        eng.cnt += 1
        ins.then_inc(eng.sem, 1)
        self._commit((eng.sem, eng.cnt), R, W)
        self.nins += 1
        return ins

    def dma(self, q, out, in_, R=(), W=(), **kw):
        self._deps(q, R, W)
        slot = q.dsems[q.dnext]
        q.dnext = (q.dnext + 1) % len(q.dsems)
        if slot[1] > 0:
            self._wait(q, (slot[0], slot[1]))
        ins = q.e.dma_start(out=out, in_=in_, **kw)
        slot[1] += 16
        ins.then_inc(slot[0], 16)
        self._commit((slot[0], slot[1]), R, W)
        self.nins += 1
        return ins

    def fence(self):
        for e in self.engs:
            for o in self.engs:
                if o is not e and o.cnt > 0:
                    self._wait(e, (o.sem, o.cnt))
                for s in o.dsems:
                    if s[1] > 0:
                        self._wait(e, (s[0], s[1]))


def _bf(a):
    return np.ascontiguousarray(a).astype(ml_dtypes.bfloat16)


def make_consts():
    c = {}
    p = np.arange(128)
    ident = np.eye(128, dtype=np.float32)
    rot = np.zeros((128, 128), np.float32)
    for dp in range(64):
        rot[dp + 64, dp] = -1.0
        rot[dp, dp + 64] = 1.0
    ones = np.ones((128, 128), np.float32)
    ltri = (p[:, None] > p[None, :]).astype(np.float32)
    causal = (p[:, None] <= p[None, :]).astype(np.float32)
    anti = (p[:, None] > p[None, :]).astype(np.float32)
    trin = -(ltri + ident) / np.float32(SCALE)
    c["cst_bf"] = _bf(np.stack([ident, rot, ones, ltri, causal, anti, trin], axis=1))
    t = np.arange(T, dtype=np.float64)
    inv = 10000.0 ** (-np.arange(64, dtype=np.float64) / 64)
    ang = (t[None, :].astype(np.float32) * inv.astype(np.float32)[:, None]).astype(np.float32)
    cs = np.stack([np.concatenate([np.cos(ang), np.cos(ang)], 0), np.concatenate([np.sin(ang), np.sin(ang)], 0)], axis=1)
    c["cossin"] = np.ascontiguousarray(cs.astype(np.float32))
    n = np.arange(256)
    cmp_end = n * 16 + 31
    mc = ((cmp_end[:, None] <= np.arange(T)[None, :]) & (n[:, None] < NCMP)).astype(np.float32)
    c["maskc"] = _bf(mc.reshape(2, 128, T).transpose(1, 0, 2))
    c_start = n[:, None] * 16
    s_start = np.arange(64)[None, :] * 64
    ov = ((c_start < s_start + 64) & (c_start + 32 > s_start) & (n[:, None] < NCMP)).astype(np.float32)
    ext = np.zeros((256, 65), np.float32)
    ext[:NCMP, 0] = 1.0
    ext[:, 1:] = ov
    c["vext"] = _bf(ext.reshape(2, 128, 65).transpose(1, 0, 2))
    tt = np.arange(T)
    blk = np.arange(64)
    cur = tt // 64
    forced = (blk[None, :] == 0) | (blk[None, :] == cur[:, None]) | (blk[None, :] == cur[:, None] - 1)
    avail = blk[None, :] * 64 <= tt[:, None]
    availm = (avail & ~forced).astype(np.float32)
    forcem = np.where(forced, 1e9 * (1.0 + (blk[None, :] == 0) + 2.0 * (blk[None, :] == cur[:, None])), np.where(avail, 0.0, -1e9)).astype(np.float32)
    c["selm"] = np.ascontiguousarray(np.stack([availm, forcem], 0).reshape(2, NT, 128, 64).transpose(2, 0, 1, 3))
    kk = np.arange(128)[:, None]
    qq = np.arange(512)[None, :]
    msb = np.stack([((a * 128 + kk) < qq) for a in range(4)], axis=1).astype(np.float32)
    c["msb_f"] = np.ascontiguousarray(msb)
    c["msb_b"] = _bf(msb)
    return c


class Prog:
    def __init__(self, nlayers=4, dbg=(), stop=None):
        self.stop = stop
        self.k = K()
        k = self.k
        nc = k.nc
        self.nc = nc
        self.dbg = set(dbg)
        self.nlayers = nlayers
        dt = nc.dram_tensor
        self.inp = {}

        def din(name, shape, dtype=F32):
            self.inp[name] = dt(name, list(shape), dtype, kind="ExternalInput")
            return self.inp[name]

        self.x = din("x", [T, D])
        self.cT = din("cT", [128, KC])
        self.mod_w = din("mod_w", [4, D, 6 * D])
        self.mod_b = din("mod_b", [4, 6 * D])
        self.norm_g = din("norm_g", [4, 4, D])
        self.ffn_w_in = din("ffn_w_in", [4, D, 2 * DFF])
        self.ffn_w_out = din("ffn_w_out", [4, DFF, D])
        self.a_w_in = din("a_w_in", [2, D, NAIN])
        self.a_gate_b = din("a_gate_b", [2, 48])
        self.a_cmp_peT = din("a_cmp_peT", [2, 2, 128, 32])
        self.a_cmp_w1 = din("a_cmp_w1", [2, 2, 4096, 256])
        self.a_cmp_w2 = din("a_cmp_w2", [2, 2, 256, 128])
        self.a_w_out = din("a_w_out", [2, D, D])
        self.b_w_q = din("b_w_q", [2, D, D])
        self.b_w_out = din("b_w_out", [2, D, D])
        self.kv_norm_g = din("kv_norm_g", [1, D])
        self.kv_mod_w = din("kv_mod_w", [D, 2 * D])
        self.kv_mod_b = din("kv_mod_b", [1, 2 * D])
        self.kv_w = din("kv_w", [D, 2 * D])
        self.c_cst = din("cst_bf", [128, 7, 128], BF16)
        self.c_cossin = din("cossin", [128, 2, T])
        self.c_maskc = din("maskc", [128, 2, T], BF16)
        self.c_vext = din("vext", [128, 2, 65], BF16)
        self.c_selm = din("selm", [128, 2, NT, 64])
        self.c_msb_f = din("msb_f", [128, 4, 512])
        self.c_msb_b = din("msb_b", [128, 4, 512], BF16)
        self.out = dt("out", [T, D], F32, kind="ExternalOutput")

        def scr(name, shape, dtype):
            kind = "ExternalOutput" if name in self.dbg else "Internal"
            return dt(name, list(shape), dtype, kind=kind)

        self.xs1 = scr("xs1", [T, D], F32)
        self.xs2 = scr("xs2", [T, D], F32)
        self.hT = [scr("hT%d" % i, [128, KC, T], BF16) for i in range(3)]
        self.qT = scr("qT", [G, 128, NT, 4, 128], BF16)
        self.kcT = scr("kcT", [G, 128, T], BF16)
        self.vcT = scr("vcT", [G, 128, T], BF16)
        self.ksT = scr("ksT", [G, 128, T], BF16)
        self.kwT = scr("kwT", [G, 128, T], BF16)
        self.vs = scr("vs", [T, 512], BF16)
        self.vw = scr("vw", [T, 512], BF16)
        self.opart = scr("opart", [T, D], F32)
        self.selT = scr("selT", [G, 64, T], BF16)
        self.kTsh = scr("kTsh", [NH, 128, T], BF16)
        self.vsh = scr("vsh", [T, D], BF16)
        self.qsb = scr("qsb", [NH, 128, T], BF16)
        self.wsc_in = scr("wsc_in", [22, 128, KC * 512], BF16)
        self.wsc_out = scr("wsc_out", [16, 128, 11 * 512], BF16)
        self.wsc_p = scr("wsc_p", [11, 128, KC * 512], BF16)
        self.wsc_o = scr("wsc_o", [4, 128, KC * 512], BF16)
        self.wsc_kv = scr("wsc_kv", [8, 128, KC * 512], BF16)
        self.dbufs = {}

        self.ps = [nc.alloc_psum_tensor("ps%d" % i, [128, 512], F32) for i in range(8)]
        self.psB = [Buf("ps%d" % i, ps=True) for i in range(8)]
        self.cst = nc.alloc_sbuf_tensor("cst", [128, 7, 128], BF16)
        self.cstB = Buf("cst")
        self.crep = nc.alloc_sbuf_tensor("crep", [128, KC, 128], BF16)
        self.crepB = Buf("crep")
        self.NW = 3
        self.wt = [nc.alloc_sbuf_tensor("wt%d" % i, [128, KC, 512], BF16) for i in range(self.NW)]
        self.wB = [Buf("wt%d" % i) for i in range(self.NW)]
        self.wi = 0
        self.vec = [nc.alloc_sbuf_tensor("vec%d" % i, [128, D], F32) for i in range(4)]
        self.vecB = [Buf("vec%d" % i) for i in range(4)]
        self.gate_sb = None
        self.gateB = Buf("gate")
        self.uid = 0

    def sbt(self, name, shape, dtype):
        self.uid += 1
        return self.nc.sbuf_tensor("%s_u%d" % (name, self.uid), shape, dtype)

    def db(self, name, i=0):
        key = (name, i)
        b = self.dbufs.get(key)
        if b is None:
            b = self.dbufs[key] = Buf("%s_%s" % key)
        return b

    def E(self, eng, meth, *a, R=(), W=(), **kw):
        return self.k.op(eng, lambda e: getattr(e, meth)(*a, **kw), R, W)

    def ident(self):
        return self.cst[:, 0, :]

    def bcast_row(self, handle, off, n):
        return bass.AP(handle, off, [[0, 128], [1, n]])

    def wload(self, src2d, kc=16, ncols=512):
        i = self.wi
        self.wi = (i + 1) % self.NW
        self.k.dma(self.k.pool, self.wt[i][:, 0:kc, 0:ncols], src2d.rearrange("(k p) n -> p k n", p=128), W=[self.wB[i]])
        return self.wt[i], self.wB[i]

    def precast(self, scr_t, name, tile, src2d, kc=KC, ncols=512):
        self.k.dma(self.k.pool, scr_t[tile].rearrange("p (k n) -> p k n", k=kc)[:, :, 0:ncols],
                   src2d.rearrange("(k p) n -> p k n", p=128), W=[self.db(name, tile)])

    def wload_bf(self, scr_t, name, tile, kc=KC, ncols=512):
        i = self.wi
        self.wi = (i + 1) % self.NW
        self.k.dma(self.k.sp, self.wt[i][:, 0:kc, 0:ncols], scr_t[tile].rearrange("p (k n) -> p k n", k=kc)[:, :, 0:ncols],
                   R=[self.db(name, tile)], W=[self.wB[i]])
        return self.wt[i], self.wB[i]

    def precast_ffn(self, l):
        Win = self.ffn_w_in.ap()[l]
        Wout = self.ffn_w_out.ap()[l]
        for fg in range(11):
            self.precast(self.wsc_in, "wsc_in", 2 * fg, Win[:, fg * 512:(fg + 1) * 512])
            self.precast(self.wsc_in, "wsc_in", 2 * fg + 1, Win[:, DFF + fg * 512: DFF + (fg + 1) * 512])
        for cb in range(4):
            for kg in range(4):
                self.precast(self.wsc_out, "wsc_out", cb * 4 + kg, Wout[kg * 11 * 128:(kg + 1) * 11 * 128, cb * 512:(cb + 1) * 512], kc=11)

    def precast_sq(self, scr_t, name, w2d, ncols_total):
        nb = (ncols_total + 511) // 512
        for b in range(nb):
            nc_ = min(512, ncols_total - b * 512)
            self.precast(scr_t, name, b, w2d[:, b * 512:b * 512 + nc_], ncols=nc_)

    def psbf(self, i):
        return self.ps[i][:, :].bitcast(BF16)

    def setup(self):
        k = self.k
        nc = self.nc
        k.dma(k.sp, self.cst[:, :, :], self.c_cst[:, :, :], W=[self.cstB])
        with ExitStack() as es:
            cf = es.enter_context(self.sbt("cf", [128, KC], F32))
            ca = es.enter_context(self.sbt("ca", [128, KC], F32))
            cfB = Buf()
            caB = Buf()
            k.dma(k.sp, cf[:, :], self.cT[:, :], W=[cfB])
            self.E(k.act, "activation", ca[:, :], cf[:, :], AF.Silu, R=[cfB], W=[caB])
            self.E(k.dve, "tensor_copy", self.crep[:, :, :], ca[:, :].unsqueeze(2).to_broadcast([128, KC, 128]), R=[caB], W=[self.crepB])
            k.fence()

    def modvec(self, vi, w2d, col0, bias_handle, bias_off):
        k = self.k
        dst = self.vec[vi]
        dB = self.vecB[vi]
        k.dma(k.sp, dst[:, :], self.bcast_row(bias_handle, bias_off, D), W=[dB])
        for cb in range(4):
            wt, wB = self.wload(w2d[:, col0 + cb * 512: col0 + (cb + 1) * 512])
            pi = cb % 2
            for kk in range(KC):
                self.E(k.pe, "matmul", self.ps[pi][:, :], self.crep[:, kk, :], wt[:, kk, :], start=(kk == 0), stop=(kk == KC - 1),
                       R=[self.crepB, wB], W=[self.psB[pi]])
            self.E(k.dve, "tensor_tensor", dst[:, cb * 512:(cb + 1) * 512], self.ps[pi][:, :], dst[:, cb * 512:(cb + 1) * 512], ALU.add,
                   R=[self.psB[pi], dB], W=[dB])

    def load_gain(self, vi, handle, off):
        self.k.dma(self.k.sp, self.vec[vi][:, :], self.bcast_row(handle, off, D), W=[self.vecB[vi]])

    def mod_AB(self, l, which, gi, va, vb, vtmp):
        k = self.k
        w2d = self.mod_w.ap()[l]
        base = 0 if which == 0 else 3 * D
        self.modvec(vb, w2d, base, self.mod_b, l * 6 * D + base)
        self.modvec(va, w2d, base + D, self.mod_b, l * 6 * D + base + D)
        self.load_gain(vtmp, self.norm_g, (l * 4 + gi) * D)
        self.E(k.dve, "scalar_tensor_tensor", self.vec[va][:, :], self.vec[va][:, :], 1.0, self.vec[vtmp][:, :], ALU.add, ALU.mult,
               R=[self.vecB[va], self.vecB[vtmp]], W=[self.vecB[va]])

    def mod_C(self, l, which, gi, vc, vtmp):
        k = self.k
        w2d = self.mod_w.ap()[l]
        base = 2 * D if which == 0 else 5 * D
        self.modvec(vc, w2d, base, self.mod_b, l * 6 * D + base)
        self.load_gain(vtmp, self.norm_g, (l * 4 + gi) * D)
        self.E(k.dve, "tensor_tensor", self.vec[vc][:, :], self.vec[vc][:, :], self.vec[vtmp][:, :], ALU.mult,
               R=[self.vecB[vc], self.vecB[vtmp]], W=[self.vecB[vc]])

    def alloc_norm_tiles(self, es):
        nc = self.nc
        w = {}
        w["ss"] = [es.enter_context(self.sbt("ss%d" % i, [128, 8], F32)) for i in range(2)]
        w["ssB"] = [Buf() for _ in range(2)]
        w["hn"] = es.enter_context(self.sbt("hn", [128, D], F32))
        w["hnB"] = Buf()
        w["junk"] = w["hn"]
        w["junkB"] = w["hnB"]
        w["hb"] = [es.enter_context(self.sbt("hb%d" % i, [128, D], BF16)) for i in range(1)]
        w["hbB"] = [Buf() for _ in range(1)]
        w["hTt"] = [es.enter_context(self.sbt("hTt%d" % i, [128, KC, 128], BF16)) for i in range(1)]
        w["hTtB"] = [Buf() for _ in range(1)]
        w["i"] = 0
        return w

    def rstd(self, w, src, srcB, n):
        k = self.k
        i = w["i"] % 2
        w["i"] += 1
        ss = w["ss"][i]
        sB = w["ssB"][i]
        self.E(k.pool, "memset", ss[:, 0:1], 0.0, W=[sB])
        self.E(k.act, "activation", w["junk"][:, 0:n], src, AF.Square, accum_out=ss[:, 0:1], R=[srcB], W=[w["junkB"], sB])
        self.E(k.dve, "tensor_scalar", ss[:, 1:2], ss[:, 0:1], 1.0 / n, EPS, ALU.mult, ALU.add, R=[sB], W=[sB])
        self.E(k.act, "activation", ss[:, 2:3], ss[:, 1:2], AF.Sqrt, R=[sB], W=[sB])
        self.E(k.dve, "reciprocal", ss[:, 3:4], ss[:, 2:3], R=[sB], W=[sB])
        return ss[:, 3:4], sB

    def prenorm_tile(self, w, xt, xB, va, vb, hTi, tt, pbanks=(0, 1)):
        k = self.k
        r, rB = self.rstd(w, xt, xB, D)
        j = tt % len(w["hb"])
        self.E(k.dve, "scalar_tensor_tensor", w["hn"][:, :], xt, r, self.vec[va][:, :], ALU.mult, ALU.mult,
               R=[xB, rB, self.vecB[va]], W=[w["hnB"]])
        self.E(k.pool, "tensor_tensor", w["hb"][j][:, :], w["hn"][:, :], self.vec[vb][:, :], ALU.add,
               R=[w["hnB"], self.vecB[vb]], W=[w["hbB"][j]])
        self.transpose_to_hT(w, w["hb"][j], w["hbB"][j], hTi, tt, pbanks)

    def transpose_to_hT(self, w, src, srcB, hTi, tt, pbanks=(0, 1), nch=KC, ch0=0):
        k = self.k
        j = tt % len(w["hTt"])
        hTt = w["hTt"][j]
        hB = w["hTtB"][j]
        for half in range((nch + 7) // 8):
            pi = pbanks[half % len(pbanks)]
            n8 = min(8, nch - half * 8)
            pb = self.psbf(pi)
            for c in range(n8):
                cc = half * 8 + c
                self.E(k.pe, "transpose", pb[:, c * 128:(c + 1) * 128], src[:, cc * 128:(cc + 1) * 128], self.ident(),
                       R=[srcB, self.cstB], W=[self.psB[pi]])
            eng = k.act if half % 2 == 0 else k.dve
            meth = "copy" if eng is k.act else "tensor_copy"
            self.E(eng, meth, hTt[:, half * 8: half * 8 + n8, :], pb[:, 0:n8 * 128].rearrange("p (k n) -> p k n", k=n8),
                   R=[self.psB[pi]], W=[hB])
        k.dma(k.sp, self.hT[hTi][:, ch0:ch0 + nch, tt * 128:(tt + 1) * 128], hTt[:, 0:nch, :], R=[hB], W=[self.db("hT%d" % hTi, tt)])

    def norm_phase(self, xsrc, xname, va, vb, hTi):
        k = self.k
        nc = self.nc
        with ExitStack() as es:
            w = self.alloc_norm_tiles(es)
            xt = [es.enter_context(self.sbt("xt%d" % i, [128, D], F32)) for i in range(2)]
            xB = [Buf() for _ in range(2)]
            for tt in range(NT):
                j = tt % 2
                k.dma(k.sp, xt[j][:, :], xsrc[tt * 128:(tt + 1) * 128, :], R=[self.db(xname, tt)], W=[xB[j]])
                self.prenorm_tile(w, xt[j][:, :], xB[j], va, vb, hTi, tt)
            k.fence()

    def residual_tile(self, w, y, yB, xt, xB, xsrc, xsname, xdst, xdname, vc, tt):
        k = self.k
        k.dma(k.sp, xt, xsrc[tt * 128:(tt + 1) * 128, :], R=[self.db(xsname, tt)], W=[xB])
        r, rB = self.rstd(w, y, yB, D)
        self.E(k.dve, "scalar_tensor_tensor", w["hn"][:, :], y, r, self.vec[vc][:, :], ALU.mult, ALU.mult,
               R=[yB, rB, self.vecB[vc]], W=[w["hnB"]])
        self.E(k.dve, "tensor_tensor", xt, xt, w["hn"][:, :], ALU.add, R=[xB, w["hnB"]], W=[xB])
        k.dma(k.sp, xdst[tt * 128:(tt + 1) * 128, :], xt, R=[xB], W=[self.db(xdname, tt)])

    def proj(self, hTi, wsrc, blocks, fF=None, fT=None, pF=(0, 1, 6), pT=(2, 3, 7)):
        k = self.k
        nc = self.nc
        with ExitStack() as es:
            hb = [es.enter_context(self.sbt("phT%d" % i, [128, KC, 512], BF16)) for i in range(2)]
            hB = [Buf() for _ in range(2)]
            cnt = 0
            for tb in range(T // 512):
                j = tb % 2
                k.dma(k.sp, hb[j][:, :, :], self.hT[hTi][:, :, tb * 512:(tb + 1) * 512],
                      R=[self.db("hT%d" % hTi, 4 * tb + s) for s in range(4)], W=[hB[j]])
                for (col0, ncols, mode, tag) in blocks:
                    wt, wB = self.wload_bf(wsrc[0], wsrc[1], col0 // 512, ncols=ncols)
                    if mode == "F":
                        for jj in range(ncols // 128):
                            pi = pF[cnt % len(pF)]
                            cnt += 1
                            for kk in range(KC):
                                self.E(k.pe, "matmul", self.ps[pi][:, :], wt[:, kk, jj * 128:(jj + 1) * 128], hb[j][:, kk, :],
                                       start=(kk == 0), stop=(kk == KC - 1), R=[wB, hB[j]], W=[self.psB[pi]])
                            fF(tag, jj, tb, pi)
                    else:
                        for s in range(4):
                            pi = pT[cnt % len(pT)]
                            cnt += 1
                            for kk in range(KC):
                                self.E(k.pe, "matmul", self.ps[pi][:, 0:ncols], hb[j][:, kk, s * 128:(s + 1) * 128], wt[:, kk, 0:ncols],
                                       start=(kk == 0), stop=(kk == KC - 1), R=[wB, hB[j]], W=[self.psB[pi]])
                            fT(tag, s, tb, pi)
            k.fence()

    def nsa_proj(self, l):
        k = self.k
        nc = self.nc
        with ExitStack() as es:
            cs = [es.enter_context(self.sbt("cs%d" % i, [128, 2, 512], F32)) for i in range(2)]
            csB = [Buf() for _ in range(2)]
            raw = [es.enter_context(self.sbt("raw%d" % i, [128, 512], BF16)) for i in range(2)]
            rawB = [Buf() for _ in range(2)]
            t1 = [es.enter_context(self.sbt("t1_%d" % i, [128, 512], F32)) for i in range(2)]
            t1B = [Buf() for _ in range(2)]
            t2 = [es.enter_context(self.sbt("t2_%d" % i, [128, 512], F32)) for i in range(2)]
            t2B = [Buf() for _ in range(2)]
            ob = [es.enter_context(self.sbt("ob%d" % i, [128, 512], BF16)) for i in range(3)]
            obB = [Buf() for _ in range(3)]
            gb = es.enter_context(self.sbt("gb", [128, 48], F32))
            gbB = Buf()
            gt = es.enter_context(self.sbt("gt", [128, 48], F32))
            gtB = Buf()
            k.dma(k.sp, gb[:, :], self.bcast_row(self.a_gate_b, l * 48, 48), W=[gbB])
            st = {"n": 0, "tb": -1}
            featdst = {4: self.kcT, 5: self.vcT, 6: self.ksT, 8: self.kwT}
            featname = {4: "kcT", 5: "vcT", 6: "ksT", 8: "kwT"}

            def fF(tag, jj, tb, pi):
                n = st["n"]
                st["n"] += 1
                i2 = n % 2
                i3 = n % 3
                if st["tb"] != tb:
                    st["tb"] = tb
                    k.dma(k.sp, cs[tb % 2][:, :, :], self.c_cossin[:, :, tb * 512:(tb + 1) * 512], W=[csB[tb % 2]])
                c_ = cs[tb % 2]
                cB = csB[tb % 2]
                P = self.ps[pi]
                PB = self.psB[pi]
                if tag == 5:
                    self.E(k.act, "copy", ob[i3][:, :], P[:, :], R=[PB], W=[obB[i3]])
                else:
                    self.E(k.act, "copy", raw[i2][:, :], P[:, :], R=[PB], W=[rawB[i2]])
                    p2 = 4 + i2
                    self.E(k.pe, "matmul", self.ps[p2][:, :], self.cst[:, 1, :], raw[i2][:, :], start=True, stop=True,
                           R=[self.cstB, rawB[i2]], W=[self.psB[p2]])
                    self.E(k.dve, "tensor_tensor", t1[i2][:, :], P[:, :], c_[:, 0, :], ALU.mult, R=[PB, cB], W=[t1B[i2]])
                    self.E(k.dve, "tensor_tensor", t2[i2][:, :], self.ps[p2][:, :], c_[:, 1, :], ALU.mult, R=[self.psB[p2], cB], W=[t2B[i2]])
                    self.E(k.pool, "tensor_tensor", ob[i3][:, :], t1[i2][:, :], t2[i2][:, :], ALU.add, R=[t1B[i2], t2B[i2]], W=[obB[i3]])
                if tag < 4:
                    dst = self.qT[tag, :, 4 * tb:4 * tb + 4, jj, :]
                    k.dma(k.sp, dst, ob[i3][:, :].rearrange("p (a t) -> p a t", a=4), R=[obB[i3]], W=[self.db("qT", (tag, tb))])
                else:
                    k.dma(k.sp, featdst[tag][jj, :, tb * 512:(tb + 1) * 512], ob[i3][:, :], R=[obB[i3]], W=[self.db(featname[tag], (jj, tb))])

            def fT(tag, s, tb, pi):
                n = st["n"]
                st["n"] += 1
                i3 = n % 3
                P = self.ps[pi]
                PB = self.psB[pi]
                tt = 4 * tb + s
                if tag == 10:
                    self.E(k.dve, "tensor_tensor", gt[:, :], P[:, 0:48], gb[:, :], ALU.add, R=[PB, gbB], W=[gtB])
                    self.E(k.act, "activation", self.gate_sb[:, tt, :], gt[:, :], AF.Sigmoid, R=[gtB], W=[self.gateB])
                else:
                    dst = self.vs if tag == 7 else self.vw
                    self.E(k.act, "copy", ob[i3][:, :], P[:, :], R=[PB], W=[obB[i3]])
                    k.dma(k.sp, dst[tt * 128:(tt + 1) * 128, :], ob[i3][:, :], R=[obB[i3]], W=[self.db("vs" if tag == 7 else "vw", tt)])

            blocks = [(wb * 512, 512, "F", wb) for wb in (0, 1, 2, 3, 4, 5, 6, 8)]
            blocks += [(7 * 512, 512, "T", 7), (9 * 512, 512, "T", 9), (5120, 48, "T", 10)]
            import os
            if os.environ.get("TB"):
                keep = [int(x) for x in os.environ["TB"].split(",")]
                blocks = [b for b in blocks if b[3] in keep]
            self.proj(0, (self.wsc_p, "wsc_p"), blocks, fF, fT)

    def nsa_compress(self, l, es_outer):
        k = self.k
        nc = self.nc
        kcmpT = es_outer.enter_context(self.sbt("kcmpT", [128, G, 256], BF16))
        kcB = Buf()
        vext = es_outer.enter_context(self.sbt("vext", [128, G, 2, 193], BF16))
        vxB = Buf()
        self.E(k.pool, "memset", kcmpT[:, :, :], 0.0, W=[kcB])
        self.E(k.pool, "memset", vext[:, :, :, :], 0.0, W=[vxB])
        for g in range(G):
            k.dma(k.sp, vext[:, g, :, 128:193], self.c_vext[:, :, :], W=[vxB])
        with ExitStack() as es:
            src = [es.enter_context(self.sbt("csrc%d" % i, [128, T], BF16)) for i in range(2)]
            srcB = [Buf() for _ in range(2)]
            w1 = es.enter_context(self.sbt("cw1", [128, 32, 256], BF16))
            w1B = Buf()
            w2 = es.enter_context(self.sbt("cw2", [128, 2, 128], BF16))
            w2B = Buf()
            pef = es.enter_context(self.sbt("pef", [128, 32], F32))
            pefB = Buf()
            perep = es.enter_context(self.sbt("perep", [128, 32, 128], BF16))
            peB = Buf()
            xs = es.enter_context(self.sbt("gxs", [128, 256], F32))
            xsB = Buf()
            u = es.enter_context(self.sbt("gu", [128, 256], F32))
            uB = Buf()
            sg = es.enter_context(self.sbt("gsg", [128, 256], F32))
            sgB = Buf()
            hid = es.enter_context(self.sbt("ghid", [128, 256], BF16))
            hidB = Buf()
            hidT = es.enter_context(self.sbt("ghidT", [128, 2, 128], BF16))
            hidTB = Buf()
            n = 0
            for kv in range(2):
                k.dma(k.pool, w1[:, :, :], self.a_cmp_w1.ap()[l, kv].rearrange("(c p) n -> p c n", p=128), W=[w1B])
                k.dma(k.pool, w2[:, :, :], self.a_cmp_w2.ap()[l, kv].rearrange("(c p) n -> p c n", p=128), W=[w2B])
                k.dma(k.sp, pef[:, :], self.a_cmp_peT.ap()[l, kv], W=[pefB])
                self.E(k.dve, "tensor_copy", perep[:, :, :], pef[:, :].unsqueeze(2).to_broadcast([128, 32, 128]), R=[pefB], W=[peB])
                srcd = self.kcT if kv == 0 else self.vcT
                sname = "kcT" if kv == 0 else "vcT"
                for g in range(G):
                    sj = n % 2
                    n += 1
                    k.dma(k.sp, src[sj][:, :], srcd[g, :, :], R=[self.db(sname, (g, tb)) for tb in range(8)], W=[srcB[sj]])
                    for half in range(2):
                        nr = 128 if half == 0 else 127
                        n0 = half * 128
                        pi = 0
                        P = self.ps[pi]
                        PB = self.psB[pi]
                        for li in range(32):
                            self.E(k.pe, "matmul", P[0:nr, 0:256], perep[:, li, 0:nr], w1[:, li, :], start=(li == 0), stop=False,
                                   R=[peB, w1B], W=[PB])
                        for li in range(32):
                            a0 = li + 16 * n0
                            self.E(k.pe, "matmul", P[0:nr, 0:256], src[sj][:, a0:a0 + 16 * (nr - 1) + 1:16], w1[:, li, :], start=False, stop=(li == 31),
                                   R=[srcB[sj], w1B], W=[PB])
                        self.E(k.act, "copy", xs[0:nr, :], P[0:nr, 0:256], R=[PB], W=[xsB])
                        self.E(k.dve, "tensor_tensor", u[0:nr, :], xs[0:nr, :], xs[0:nr, :], ALU.mult, R=[xsB], W=[uB])
                        self.E(k.dve, "tensor_scalar", u[0:nr, :], u[0:nr, :], 0.044715, 1.0, ALU.mult, ALU.add, R=[uB], W=[uB])
                        self.E(k.dve, "tensor_tensor", u[0:nr, :], u[0:nr, :], xs[0:nr, :], ALU.mult, R=[uB, xsB], W=[uB])
                        self.E(k.act, "activation", sg[0:nr, :], u[0:nr, :], AF.Sigmoid, scale=1.5957691216057308, R=[uB], W=[sgB])
                        if nr < 128:
                            self.E(k.pool, "memset", hid[:, :], 0.0, W=[hidB])
                        self.E(k.dve, "tensor_tensor", hid[0:nr, :], xs[0:nr, :], sg[0:nr, :], ALU.mult, R=[xsB, sgB], W=[hidB])
                        pb = self.psbf(1)
                        for c in range(2):
                            self.E(k.pe, "transpose", pb[:, c * 128:(c + 1) * 128], hid[:, c * 128:(c + 1) * 128], self.ident(),
                                   R=[hidB, self.cstB], W=[self.psB[1]])
                        self.E(k.act, "copy", hidT[:, :, :], pb[:, 0:256].rearrange("p (c n) -> p c n", c=2), R=[self.psB[1]], W=[hidTB])
                        P2 = self.ps[2]
                        if kv == 0:
                            for c in range(2):
                                self.E(k.pe, "matmul", P2[:, 0:nr], w2[:, c, :], hidT[:, c, 0:nr], start=(c == 0), stop=(c == 1),
                                       R=[w2B, hidTB], W=[self.psB[2]])
                            self.E(k.act, "copy", kcmpT[:, g, n0:n0 + nr], P2[:, 0:nr], R=[self.psB[2]], W=[kcB])
                        else:
                            for c in range(2):
                                self.E(k.pe, "matmul", P2[0:nr, 0:128], hidT[:, c, 0:nr], w2[:, c, :], start=(c == 0), stop=(c == 1),
                                       R=[w2B, hidTB], W=[self.psB[2]])
                            self.E(k.act, "copy", vext[0:nr, g, half, 0:128], P2[0:nr, 0:128], R=[self.psB[2]], W=[vxB])
            k.fence()
        return kcmpT, kcB, vext, vxB

    def attn_pair(self, kT, kB, q, qB, vx, vxB_, nv, mask, mB, acc, first, last, pt, ptB, spi, nrows=128):
        k = self.k
        P = self.ps[spi]
        PB = self.psB[spi]
        self.E(k.pe, "matmul", P[0:nrows, :], kT, q, start=True, stop=True, R=[kB, qB], W=[PB])

        def rest():
            self.E(k.act, "activation", pt[0:nrows, :], P[0:nrows, :], AF.Exp, scale=SCALE, R=[PB], W=[ptB])
            if mask is not None:
                self.E(k.dve, "tensor_tensor", pt[0:nrows, :].rearrange("p (r t) -> p r t", r=4), pt[0:nrows, :].rearrange("p (r t) -> p r t", r=4),
                       mask.unsqueeze(1).to_broadcast([nrows, 4, 128]), ALU.mult, R=[ptB, mB], W=[ptB])
            for r in range(4):
                self.E(k.pe, "matmul", self.ps[acc[r]][:, 0:nv], pt[0:nrows, r * 128:(r + 1) * 128], vx, start=first, stop=last,
                       R=[ptB, vxB_], W=[self.psB[acc[r]]])

        prev = getattr(self, "_pend", None)
        self._pend = rest
        if prev is not None:
            prev()

    def attn_flush(self):
        prev = getattr(self, "_pend", None)
        self._pend = None
        if prev is not None:
            prev()

    def nsa_attn(self, l):
        k = self.k
        nc = self.nc
        with ExitStack() as es0:
            kcmpT, kcB, vext, vxB = self.nsa_compress(l, es0)
            with ExitStack() as es:
                sb = lambda name, shape, dtp: es.enter_context(self.sbt(name, shape, dtp))
                maskc_r = [sb("maskc%d" % i, [128, 2, 128], BF16) for i in range(2)]
                mcB_r = [Buf() for _ in range(2)]
                selm_r = [sb("selm%d" % i, [128, 2, 64], F32) for i in range(2)]
                smB_r = [Buf() for _ in range(2)]
                qg = sb("qg", [128, NT, 512], BF16)
                qgB = Buf()
                ksg = sb("ksg", [128, T], BF16)
                ksB = Buf()
                kwg = sb("kwg", [128, T], BF16)
                kwB = Buf()
                vsg = sb("vsg", [128, NT, 129], BF16)
                vsB = Buf()
                vwg = sb("vwg", [128, NT, 129], BF16)
                vwB = Buf()
                self.E(k.pool, "memset", vsg[:, :, 128:129], 1.0, W=[vsB])
                self.E(k.pool, "memset", vwg[:, :, 128:129], 1.0, W=[vwB])
                pt = [sb("pt%d" % i, [128, 512], BF16) for i in range(3)]
                ptB = [Buf() for _ in range(3)]
                msk = [sb("msk%d" % i, [128, NT, 128], BF16) for i in range(2)]
                mskB = [Buf() for _ in range(2)]
                ot = [sb("ot%d" % i, [128, 512], F32) for i in range(2)]
                otB = [Buf() for _ in range(2)]
                otb = [sb("otb%d" % i, [128, 512], BF16) for i in range(2)]
                otbB = [Buf() for _ in range(2)]
                st = [sb("ast%d" % i, [128, 16], F32) for i in range(2)]
                stB = [Buf() for _ in range(2)]
                imp = sb("imp", [128, 64], F32)
                impB = Buf()
                vv = sb("vv", [128, 64], F32)
                vvB = Buf()
                cmpt = sb("cmpt", [128, 64, 64], BF16)
                cmpB = Buf()
                cnt = sb("cnt", [128, 64], F32)
                cntB = Buf()
                selb = sb("selb", [128, 64], BF16)
                selbB = Buf()
                selT_t = [sb("selTt%d" % i, [64, 128], BF16) for i in range(2)]
                selTB = [Buf() for _ in range(2)]
                w = {"hTt": [sb("ohTt%d" % i, [128, 4, 128], BF16) for i in range(2)], "hTtB": [Buf() for _ in range(2)]}
                acc = (4, 5, 6, 7)
                npair = 0

                def finish_branch(tt, g, gi, nv, stt, sB_, first_out, o, oB, prev=None, prevB=None):
                    for r in range(4):
                        c0 = r * 3
                        self.E(k.dve, "tensor_scalar", stt[:, c0:c0 + 1], self.ps[acc[r]][:, 128:129], 1e-30, None, ALU.max,
                               R=[self.psB[acc[r]]], W=[sB_])
                    for r in range(4):
                        c0 = r * 3
                        self.E(k.dve, "reciprocal", stt[:, c0 + 1:c0 + 2], stt[:, c0:c0 + 1], R=[sB_], W=[sB_])
                    for r in range(4):
                        c0 = r * 3
                        self.E(k.dve, "tensor_tensor", stt[:, c0 + 2:c0 + 3], stt[:, c0 + 1:c0 + 2],
                               self.gate_sb[:, tt, (g * 4 + r) * 3 + gi:(g * 4 + r) * 3 + gi + 1], ALU.mult, R=[sB_, self.gateB], W=[sB_])
                    for r in range(4):
                        c0 = r * 3
                        A = self.ps[acc[r]]
                        AB = self.psB[acc[r]]
                        if prev is None:
                            self.E(k.dve, "tensor_scalar", o[:, r * 128:(r + 1) * 128], A[:, 0:128], stt[:, c0 + 2:c0 + 3], None, ALU.mult,
                                   R=[AB, sB_], W=[oB])
                        else:
                            self.E(k.dve, "scalar_tensor_tensor", o[:, r * 128:(r + 1) * 128], A[:, 0:128], stt[:, c0 + 2:c0 + 3],
                                   prev[:, r * 128:(r + 1) * 128], ALU.mult, ALU.add, R=[AB, sB_, prevB], W=[oB])

                for g in range(G):
                    k.dma(k.sp, qg[:, :, :], self.qT[g].rearrange("p a r t -> p a (r t)"), R=[self.db("qT", (g, tb)) for tb in range(8)], W=[qgB])
                    k.dma(k.sp, ksg[:, :], self.ksT[g, :, :], R=[self.db("ksT", (g, tb)) for tb in range(8)], W=[ksB])
                    k.dma(k.sp, kwg[:, :], self.kwT[g, :, :], R=[self.db("kwT", (g, tb)) for tb in range(8)], W=[kwB])
                    k.dma(k.sp, vsg[:, :, 0:128], self.vs[:, g * 128:(g + 1) * 128].rearrange("(j p) c -> p j c", p=128),
                          R=[self.db("vs", tt) for tt in range(NT)], W=[vsB])
                    k.dma(k.sp, vwg[:, :, 0:128], self.vw[:, g * 128:(g + 1) * 128].rearrange("(j p) c -> p j c", p=128),
                          R=[self.db("vw", tt) for tt in range(NT)], W=[vwB])
                    for tt in range(NT):
                        q = qg[:, tt, :]
                        maskc = maskc_r[tt % 2]
                        mcB = mcB_r[tt % 2]
                        selm = selm_r[tt % 2]
                        smB = smB_r[tt % 2]
                        k.dma(k.sp, maskc[:, :, :], self.c_maskc[:, :, tt * 128:(tt + 1) * 128], W=[mcB])
                        k.dma(k.sp, selm[:, :, :], self.c_selm[:, :, tt, :], W=[smB])
                        halves = [0] + ([1] if tt >= 16 else [])
                        for hi, half in enumerate(halves):
                            need_mask = not (half == 0 and tt >= 17)
                            m = maskc[:, half, :] if need_mask else None
                            pi = npair % 3
                            self.attn_pair(kcmpT[:, g, half * 128:(half + 1) * 128], kcB, q, qgB, vext[:, g, half, :], vxB, 193,
                                           m, mcB, acc, hi == 0, hi == len(halves) - 1, pt[pi], ptB[pi], npair % 3)
                            npair += 1
                        j = tt % 2
                        stt = st[j]
                        sB_ = stB[j]
                        self.attn_flush()
                        finish_branch(tt, g, 0, 193, stt, sB_, True, ot[j], otB[j])
                        for r in range(4):
                            A = self.ps[acc[r]]
                            AB = self.psB[acc[r]]
                            if r == 0:
                                self.E(k.dve, "tensor_scalar", imp[:, :], A[:, 129:193], stt[:, 1:2], None, ALU.mult, R=[AB, sB_], W=[impB])
                            else:
                                self.E(k.dve, "scalar_tensor_tensor", imp[:, :], A[:, 129:193], stt[:, r * 3 + 1:r * 3 + 2], imp[:, :],
                                       ALU.mult, ALU.add, R=[AB, sB_, impB], W=[impB])
                        k.dma(k.sp, self.opart[tt * 128:(tt + 1) * 128, g * 512:(g + 1) * 512], ot[j][:, :], R=[otB[j]], W=[self.db("opart", (g, tt))])
                        self.E(k.dve, "tensor_tensor", vv[:, :], imp[:, :], selm[:, 0, :], ALU.mult, R=[impB, smB], W=[vvB])
                        self.E(k.dve, "tensor_tensor", vv[:, :], vv[:, :], selm[:, 1, :], ALU.add, R=[vvB, smB], W=[vvB])
                        self.E(k.dve, "tensor_tensor", cmpt[:, :, :], vv[:, :].unsqueeze(1).to_broadcast([128, 64, 64]),
                               vv[:, :].unsqueeze(2).to_broadcast([128, 64, 64]), ALU.is_gt, R=[vvB], W=[cmpB])
                        self.E(k.dve, "reduce_sum", cnt[:, :], cmpt[:, :, :], AX.X, R=[cmpB], W=[cntB])
                        self.E(k.dve, "tensor_scalar", selb[:, :], cnt[:, :], 15.5, None, ALU.is_lt, R=[cntB], W=[selbB])
                        pb = self.psbf(3)
                        self.E(k.pe, "transpose", pb[0:64, 0:128], selb[:, :], self.ident(), R=[selbB, self.cstB], W=[self.psB[3]])
                        self.E(k.act, "copy", selT_t[j][:, :], pb[0:64, 0:128], R=[self.psB[3]], W=[selTB[j]])
                        k.dma(k.sp, self.selT[g, :, tt * 128:(tt + 1) * 128], selT_t[j][:, :], R=[selTB[j]], W=[self.db("selT", (g, tt))])
                    for tt in range(NT):
                        q = qg[:, tt, :]
                        j = tt % 2
                        mk = msk[j]
                        mkB = mskB[j]
                        for b in range(2):
                            srcap = bass.AP(self.selT, g * 64 * T + b * T + tt * 128, [[0, 64], [2 * T, tt + 1], [1, 128]])
                            k.dma(k.sp, mk[b * 64:(b + 1) * 64, 0:tt + 1, :], srcap, R=[self.db("selT", (g, tt))], W=[mkB])
                        self.E(k.pool, "tensor_tensor", mk[:, tt, :], mk[:, tt, :], self.cst[:, 4, :], ALU.mult, R=[mkB, self.cstB], W=[mkB])
                        k.dma(k.sp, ot[j][:, :], self.opart[tt * 128:(tt + 1) * 128, g * 512:(g + 1) * 512], R=[self.db("opart", (g, tt))], W=[otB[j]])
                        stt = st[j]
                        sB_ = stB[j]
                        js = list(range(max(0, tt - 4), tt + 1))
                        for ji, jk in enumerate(js):
                            m = None
                            if jk == tt:
                                m = self.cst[:, 4, :]
                            elif jk == tt - 4:
                                m = self.cst[:, 5, :]
                            pi = npair % 3
                            self.attn_pair(kwg[:, jk * 128:(jk + 1) * 128], kwB, q, qgB, vwg[:, jk, :], vwB, 129,
                                           m, self.cstB, acc, ji == 0, ji == len(js) - 1, pt[pi], ptB[pi], npair % 3)
                            npair += 1
                        self.attn_flush()
                        finish_branch(tt, g, 2, 129, stt, sB_, False, ot[j], otB[j], ot[j], otB[j])
                        for jk in range(tt + 1):
                            pi = npair % 3
                            self.attn_pair(ksg[:, jk * 128:(jk + 1) * 128], ksB, q, qgB, vsg[:, jk, :], vsB, 129,
                                           mk[:, jk, :], mkB, acc, jk == 0, jk == tt, pt[pi], ptB[pi], npair % 3)
                            npair += 1
                        self.attn_flush()
                        finish_branch(tt, g, 1, 129, stt, sB_, False, ot[j], otB[j], ot[j], otB[j])
                        self.E(k.act, "copy", otb[j][:, :], ot[j][:, :], R=[otB[j]], W=[otbB[j]])
                        self.transpose_to_hT(w, otb[j], otbB[j], 2, tt, pbanks=(3,), nch=4, ch0=g * 4)
                k.fence()

    def outproj(self, xsrc, xsname, xdst, xdname, vc, va, vb, hTdst):
        k = self.k
        nc = self.nc
        with ExitStack() as es:
            w = self.alloc_norm_tiles(es)
            ysb = [es.enter_context(self.sbt("ysb%d" % i, [128, D], F32)) for i in range(4)]
            yB = [Buf() for _ in range(4)]
            xt = [es.enter_context(self.sbt("xt%d" % i, [128, D], F32)) for i in range(2)]
            xB = [Buf() for _ in range(2)]
            hb = [es.enter_context(self.sbt("phT%d" % i, [128, KC, 512], BF16)) for i in range(2)]
            hB = [Buf() for _ in range(2)]
            cnt = 0
            for tb in range(T // 512):
                j = tb % 2
                k.dma(k.sp, hb[j][:, :, :], self.hT[2][:, :, tb * 512:(tb + 1) * 512],
                      R=[self.db("hT2", 4 * tb + s) for s in range(4)], W=[hB[j]])
                for cb in range(4):
                    wt, wB = self.wload_bf(self.wsc_o, "wsc_o", cb)
                    for s in range(4):
                        pi = 4 + cnt % 4
                        cnt += 1
                        for kk in range(KC):
                            self.E(k.pe, "matmul", self.ps[pi][:, :], hb[j][:, kk, s * 128:(s + 1) * 128], wt[:, kk, :],
                                   start=(kk == 0), stop=(kk == KC - 1), R=[wB, hB[j]], W=[self.psB[pi]])
                        self.E(k.act, "copy", ysb[s][:, cb * 512:(cb + 1) * 512], self.ps[pi][:, :], R=[self.psB[pi]], W=[yB[s]])
                for s in range(4):
                    tt = 4 * tb + s
                    self.residual_tile(w, ysb[s][:, :], yB[s], xt[s % 2][:, :], xB[s % 2], xsrc, xsname, xdst, xdname, vc, tt)
                    if hTdst is not None:
                        self.prenorm_tile(w, xt[s % 2][:, :], xB[s % 2], va, vb, hTdst, tt)
            k.fence()

    def ffn(self, l, xsrc, xsname, xdst, xdname, vc, va, vb, hTdst):
        k = self.k
        nc = self.nc
        Win = self.ffn_w_in.ap()[l]
        Wout = self.ffn_w_out.ap()[l]
        NF = DFF // 128
        with ExitStack() as es:
            w = self.alloc_norm_tiles(es)
            ysb = [es.enter_context(self.sbt("ysb%d" % i, [128, D], F32)) for i in range(4)]
            yB = [Buf() for _ in range(4)]
            xt = [es.enter_context(self.sbt("xt%d" % i, [128, D], F32)) for i in range(1)]
            xB = [Buf() for _ in range(1)]
            hb = [es.enter_context(self.sbt("phT%d" % i, [128, KC, 512], BF16)) for i in range(1)]
            hB = [Buf() for _ in range(1)]
            actT = es.enter_context(self.sbt("actT", [128, NF, 512], BF16))
            aB = [Buf() for _ in range(11)]
            sg = [es.enter_context(self.sbt("fsg%d" % i, [128, 512], F32)) for i in range(2)]
            sgB = [Buf() for _ in range(2)]
            n = 0
            for tb in range(T // 512):
                j = 0
                k.dma(k.sp, hb[j][:, :, :], self.hT[1][:, :, tb * 512:(tb + 1) * 512],
                      R=[self.db("hT1", 4 * tb + s) for s in range(4)], W=[hB[j]])
                for fg in range(11):
                    wg, wgB = self.wload_bf(self.wsc_in, "wsc_in", 2 * fg)
                    wu, wuB = self.wload_bf(self.wsc_in, "wsc_in", 2 * fg + 1)
                    for jj in range(4):
                        i2 = n % 2
                        n += 1
                        pg = i2
                        pu = 2 + i2
                        for kk in range(KC):
                            self.E(k.pe, "matmul", self.ps[pg][:, :], wg[:, kk, jj * 128:(jj + 1) * 128], hb[j][:, kk, :],
                                   start=(kk == 0), stop=(kk == KC - 1), R=[wgB, hB[j]], W=[self.psB[pg]])
                        for kk in range(KC):
                            self.E(k.pe, "matmul", self.ps[pu][:, :], wu[:, kk, jj * 128:(jj + 1) * 128], hb[j][:, kk, :],
                                   start=(kk == 0), stop=(kk == KC - 1), R=[wuB, hB[j]], W=[self.psB[pu]])
                        self.E(k.act, "activation", sg[i2][:, :], self.ps[pg][:, :], AF.Silu, R=[self.psB[pg]], W=[sgB[i2]])
                        self.E(k.dve, "tensor_tensor", actT[:, fg * 4 + jj, :], sg[i2][:, :], self.ps[pu][:, :], ALU.mult,
                               R=[sgB[i2], self.psB[pu]], W=[aB[fg]])
                for cb in range(4):
                    for kg in range(4):
                        wt, wB = self.wload_bf(self.wsc_out, "wsc_out", cb * 4 + kg, kc=11)
                        for s in range(4):
                            pi = 4 + s
                            for kk in range(11):
                                ch = kg * 11 + kk
                                self.E(k.pe, "matmul", self.ps[pi][:, :], actT[:, ch, s * 128:(s + 1) * 128], wt[:, kk, :],
                                       start=(ch == 0), stop=(ch == NF - 1), R=[wB, aB[ch // 4]], W=[self.psB[pi]])
                    for s in range(4):
                        self.E(k.act, "copy", ysb[s][:, cb * 512:(cb + 1) * 512], self.ps[4 + s][:, :], R=[self.psB[4 + s]], W=[yB[s]])
                for s in range(4):
                    tt = 4 * tb + s
                    self.residual_tile(w, ysb[s][:, :], yB[s], xt[0][:, :], xB[0], xsrc, xsname, xdst, xdname, vc, tt)
                    if hTdst is not None:
                        self.prenorm_tile(w, xt[0][:, :], xB[0], va, vb, hTdst, tt, pbanks=(0, 1))
            k.fence()

    def sb_kv(self, xsrc, xname):
        k = self.k
        self.modvec(1, self.kv_mod_w.ap(), 0, self.kv_mod_b, 0)
        self.modvec(0, self.kv_mod_w.ap(), D, self.kv_mod_b, D)
        self.load_gain(2, self.kv_norm_g, 0)
        self.E(k.dve, "scalar_tensor_tensor", self.vec[0][:, :], self.vec[0][:, :], 1.0, self.vec[2][:, :], ALU.add, ALU.mult,
               R=[self.vecB[0], self.vecB[2]], W=[self.vecB[0]])
        self.norm_phase(xsrc, xname, 0, 1, 2)
        nc = self.nc
        with ExitStack() as es:
            ob = [es.enter_context(self.sbt("ob%d" % i, [128, 512], BF16)) for i in range(3)]
            obB = [Buf() for _ in range(3)]
            st = {"n": 0}

            def fF(tag, jj, tb, pi):
                i3 = st["n"] % 3
                st["n"] += 1
                self.E(k.act, "copy", ob[i3][:, :], self.ps[pi][:, :], R=[self.psB[pi]], W=[obB[i3]])
                k.dma(k.sp, self.kTsh[tag * 4 + jj, :, tb * 512:(tb + 1) * 512], ob[i3][:, :], R=[obB[i3]], W=[self.db("kTsh", (tag * 4 + jj, tb))])

            def fT(tag, s, tb, pi):
                i3 = st["n"] % 3
                st["n"] += 1
                tt = 4 * tb + s
                self.E(k.dve, "tensor_copy", ob[i3][:, :], self.ps[pi][:, :], R=[self.psB[pi]], W=[obB[i3]])
                k.dma(k.sp, self.vsh[tt * 128:(tt + 1) * 128, tag * 512:(tag + 1) * 512], ob[i3][:, :], R=[obB[i3]], W=[self.db("vsh", (tag, tt))])

            blocks = [(c * 512, 512, "F", c) for c in range(4)] + [(D + c * 512, 512, "T", c) for c in range(4)]
            self.proj(2, (self.wsc_kv, "wsc_kv"), blocks, fF, fT)

    def sb_q(self, l2):
        k = self.k
        nc = self.nc
        with ExitStack() as es:
            ob = [es.enter_context(self.sbt("ob%d" % i, [128, 512], BF16)) for i in range(3)]
            obB = [Buf() for _ in range(3)]
            st = {"n": 0}

            def fF(tag, jj, tb, pi):
                i3 = st["n"] % 3
                st["n"] += 1
                self.E(k.act, "copy", ob[i3][:, :], self.ps[pi][:, :], R=[self.psB[pi]], W=[obB[i3]])
                k.dma(k.sp, self.qsb[tag * 4 + jj, :, tb * 512:(tb + 1) * 512], ob[i3][:, :], R=[obB[i3]], W=[self.db("qsb", (tag * 4 + jj, tb))])

            blocks = [(c * 512, 512, "F", c) for c in range(4)]
            self.proj(0, (self.wsc_p, "wsc_p"), blocks, fF, None)

    def sb_attn(self):
        k = self.k
        nc = self.nc
        with ExitStack() as es:
            sb = lambda name, shape, dtp: es.enter_context(self.sbt(name, shape, dtp))
            mb = sb("msbb", [128, 4, 512], BF16)
            mbB = Buf()
            k.dma(k.sp, mb[:, :, :], self.c_msb_b[:, :, :], W=[mbB])
            kT = [sb("skT%d" % i, [128, T], BF16) for i in range(2)]
            kTB = [Buf() for _ in range(2)]
            qT = [sb("sqT%d" % i, [128, T], BF16) for i in range(2)]
            qTB = [Buf() for _ in range(2)]
            vh = [sb("svh%d" % i, [128, NT, 128], BF16) for i in range(2)]
            vhB = [Buf() for _ in range(2)]
            NR = 3
            ee = [sb("see%d" % i, [128, 512], F32) for i in range(NR)]
            eeB = [Buf() for _ in range(NR)]
            spb = [sb("sspb%d" % i, [128, 512], BF16) for i in range(NR)]
            spbB = [Buf() for _ in range(NR)]
            t3 = [sb("st3_%d" % i, [128, 512], F32) for i in range(NR)]
            t3B = [Buf() for _ in range(NR)]
            pt = [sb("spt%d" % i, [128, 512], BF16) for i in range(NR)]
            ptB = [Buf() for _ in range(NR)]
            carry = sb("scarry", [128, 512], F32)
            cB = Buf()
            ob = [sb("sob%d" % i, [128, 512], BF16) for i in range(2)]
            obB = [Buf() for _ in range(2)]
            st = {"n": 0, "nq": 0}

            def stageA(p):
                Pz = self.ps[p["zi"]]
                self.E(k.pe, "matmul", Pz[:, :], p["kT"], p["q"], start=True, stop=True, R=[p["kB"], p["qB"]], W=[self.psB[p["zi"]]])

            def stageB(p):
                i = p["i"]
                Pz = self.ps[p["zi"]]
                PzB = self.psB[p["zi"]]
                self.E(k.act, "activation", ee[i][:, :], Pz[:, :], AF.Exp, scale=SCALE, R=[PzB], W=[eeB[i]])
                self.E(k.act, "activation", spb[i][:, :], ee[i][:, :], AF.Ln, bias=1.0, R=[eeB[i]], W=[spbB[i]])
                if p["a"] >= 0:
                    self.E(k.pool, "tensor_tensor", spb[i][:, :], spb[i][:, :], mb[:, p["a"], :], ALU.mult, R=[spbB[i], mbB], W=[spbB[i]])
                self.E(k.pe, "matmul", Pz[:, :], self.cst[:, 6, :], spb[i][:, :], start=False, stop=True, skip_group_check=True,
                       R=[self.cstB, spbB[i]], W=[PzB])
                if not p["last"]:
                    ti = p["ti"]
                    self.E(k.pe, "matmul", self.ps[ti][:, :], self.cst[:, 2, :], spb[i][:, :], start=True, stop=True,
                           R=[self.cstB, spbB[i]], W=[self.psB[ti]])

            def stageC(p):
                i = p["i"]
                Pz = self.ps[p["zi"]]
                PzB = self.psB[p["zi"]]
                if p["first"]:
                    self.E(k.act, "activation", pt[i][:, :], Pz[:, :], AF.Exp, scale=SCALE, R=[PzB], W=[ptB[i]])
                else:
                    self.E(k.dve, "scalar_tensor_tensor", t3[i][:, :], Pz[:, :], SCALE, carry[:, :], ALU.mult, ALU.subtract,
                           R=[PzB, cB], W=[t3B[i]])
                    self.E(k.act, "activation", pt[i][:, :], t3[i][:, :], AF.Exp, R=[t3B[i]], W=[ptB[i]])
                if p["a"] >= 0:
                    self.E(k.pool, "tensor_tensor", pt[i][:, :], pt[i][:, :], mb[:, p["a"], :], ALU.mult, R=[ptB[i], mbB], W=[ptB[i]])
                self.E(k.pe, "matmul", self.ps[p["acc"]][:, :], p["v"], pt[i][:, :], start=p["first"], stop=p["last"],
                       R=[p["vB"], ptB[i]], W=[self.psB[p["acc"]]])
                if not p["last"]:
                    ti = p["ti"]
                    if p["first"]:
                        self.E(k.dve, "tensor_copy", carry[:, :], self.ps[ti][:, :], R=[self.psB[ti]], W=[cB])
                    else:
                        self.E(k.dve, "tensor_tensor", carry[:, :], carry[:, :], self.ps[ti][:, :], ALU.add, R=[cB, self.psB[ti]], W=[cB])
                if p["last"]:
                    oj = st["nq"] % 2
                    st["nq"] += 1
                    self.E(k.act, "copy", ob[oj][:, :], self.ps[p["acc"]][:, :], R=[self.psB[p["acc"]]], W=[obB[oj]])
                    k.dma(k.sp, self.hT[2][:, p["h"], p["qb"] * 512:(p["qb"] + 1) * 512], ob[oj][:, :], R=[obB[oj]],
                          W=[self.db("hT2", 4 * p["qb"] + s) for s in range(4)])

            pairs = []
            for h in range(NH):
                hj = h % 2
                for qb in range(T // 512):
                    jtop = 4 * qb + 3
                    for jk in range(jtop, -1, -1):
                        pairs.append(dict(h=h, hj=hj, qb=qb, jk=jk, a=jk - 4 * qb, first=(jk == jtop), last=(jk == 0)))
            loaded = set()
            nacc = 0
            for n, p in enumerate(pairs):
                p["i"] = n % NR
                p["zi"] = n % 3
                p["ti"] = 3 + n % 2
                if p["first"]:
                    nacc += 1
                p["acc"] = 6 + nacc % 2
            def ensure_loaded(h):
                if h in loaded or h >= NH:
                    return
                loaded.add(h)
                hj = h % 2
                k.dma(k.sp, kT[hj][:, :], self.kTsh[h, :, :], R=[self.db("kTsh", (h, tb)) for tb in range(8)], W=[kTB[hj]])
                k.dma(k.sp, qT[hj][:, :], self.qsb[h, :, :], R=[self.db("qsb", (h, tb)) for tb in range(8)], W=[qTB[hj]])
                k.dma(k.sp, vh[hj][:, :, :], self.vsh[:, h * 128:(h + 1) * 128].rearrange("(j p) c -> p j c", p=128),
                      R=[self.db("vsh", (h // 4, tt)) for tt in range(NT)], W=[vhB[hj]])
            def prep(p):
                ensure_loaded(p["h"])
                hj = p["hj"]
                jk = p["jk"]
                p["kT"] = kT[hj][:, jk * 128:(jk + 1) * 128]
                p["kB"] = kTB[hj]
                p["q"] = qT[hj][:, p["qb"] * 512:(p["qb"] + 1) * 512]
                p["qB"] = qTB[hj]
                p["v"] = vh[hj][:, jk, :]
                p["vB"] = vhB[hj]
            N = len(pairs)
            for n in range(N + 2):
                if n < N:
                    prep(pairs[n])
                    stageA(pairs[n])
                if 0 <= n - 1 < N:
                    stageB(pairs[n - 1])
                if 0 <= n - 2 < N:
                    stageC(pairs[n - 2])
            k.fence()

    def build(self):
        k = self.k
        self.setup()
        xin, xin_name = self.x.ap(), "x"
        self.precast_sq(self.wsc_p, "wsc_p", self.a_w_in.ap()[0], NAIN)
        for l in range(self.nlayers):
            last = (l == self.nlayers - 1)
            if l == 0:
                self.mod_AB(l, 0, 0, 0, 1, 2)
                self.norm_phase(xin, xin_name, 0, 1, 0)
                if self.stop == "norm0":
                    return self.nc
            if l == 2:
                self.sb_kv(xin, xin_name)
            self.precast_sq(self.wsc_o, "wsc_o", (self.a_w_out.ap()[l] if l < 2 else self.b_w_out.ap()[l - 2]), D)
            self.precast_ffn(l)
            if l < 2:
                with ExitStack() as esg:
                    self.gate_sb = esg.enter_context(self.sbt("gate_sb", [128, NT, 48], F32))
                    self.nsa_proj(l)
                    if self.stop == "nsa_proj":
                        return self.nc
                    self.nsa_attn(l)
                    k.fence()
                if self.stop == "nsa_attn":
                    return self.nc
                wout = self.a_w_out.ap()[l]
            else:
                self.sb_q(l - 2)
                self.sb_attn()
                wout = self.b_w_out.ap()[l - 2]
            self.mod_C(l, 0, 1, 2, 3)
            self.mod_AB(l, 1, 2, 0, 1, 3)
            self.outproj(xin, xin_name, self.xs1.ap(), "xs1", 2, 0, 1, 1)
            if self.stop == "outproj%d" % l:
                return self.nc
            self.mod_C(l, 1, 3, 2, 3)
            if not last:
                self.mod_AB(l + 1, 0, 0, 0, 1, 3)
            xdst, xdname = (self.out.ap(), "out") if last else (self.xs2.ap(), "xs2")
            if not last:
                if l + 1 < 2:
                    self.precast_sq(self.wsc_p, "wsc_p", self.a_w_in.ap()[l + 1], NAIN)
                else:
                    self.precast_sq(self.wsc_p, "wsc_p", self.b_w_q.ap()[l + 1 - 2], D)
                if l + 1 == 2:
                    self.precast_sq(self.wsc_kv, "wsc_kv", self.kv_w.ap(), 2 * D)
            self.ffn(l, self.xs1.ap(), "xs1", xdst, xdname, 2, 0, 1, None if last else 0)
            xin, xin_name = xdst, xdname
        k.fence()
        return self.nc


_CONSTS = None


def make_in_maps(inputs, ncores=8):
    global _CONSTS
    if _CONSTS is None:
        _CONSTS = make_consts()
    f = lambda a: np.ascontiguousarray(np.asarray(a, dtype=np.float32))
    shared = {
        "mod_w": f(inputs["mod_w"]), "mod_b": f(inputs["mod_b"]), "norm_g": f(inputs["norm_g"]),
        "ffn_w_in": f(inputs["ffn_w_in"]), "ffn_w_out": f(inputs["ffn_w_out"]), "a_w_in": f(inputs["a_w_in"]),
        "a_gate_b": f(inputs["a_gate_b"]),
        "a_cmp_peT": np.ascontiguousarray(np.asarray(inputs["a_cmp_pe"], dtype=np.float32).transpose(0, 1, 3, 2)),
        "a_cmp_w1": f(inputs["a_cmp_w1"]), "a_cmp_w2": f(inputs["a_cmp_w2"]), "a_w_out": f(inputs["a_w_out"]),
        "b_w_q": f(inputs["b_w_q"]), "b_w_out": f(inputs["b_w_out"]),
        "kv_norm_g": f(inputs["kv_norm_g"]).reshape(1, D), "kv_mod_w": f(inputs["kv_mod_w"]),
        "kv_mod_b": f(inputs["kv_mod_b"]).reshape(1, 2 * D), "kv_w": f(inputs["kv_w"]),
    }
    shared.update(_CONSTS)
    x = np.asarray(inputs["x"], dtype=np.float32)
    c = np.asarray(inputs["c"], dtype=np.float32)
    maps = []
    for core in range(ncores):
        b = core % 4
        m = dict(shared)
        m["x"] = np.ascontiguousarray(x[b])
        m["cT"] = np.ascontiguousarray(c[b].reshape(KC, 128).T)
        maps.append(m)
    return maps


def kernel(**inputs):
    prog = Prog()
    nc = prog.build()
    maps = make_in_maps(inputs)
    res = run_bass_kernel_spmd(nc, maps, core_ids=list(range(8)))
    out = np.stack([np.asarray(res.results[b]["out"], dtype=np.float32) for b in range(4)], axis=0)
    return out
```

```python
from contextlib import ExitStack
import numpy as np
import ml_dtypes
import concourse.bass as bass
import concourse.mybir as mybir
from concourse.bass_utils import run_bass_kernel_spmd

F32 = mybir.dt.float32
BF16 = mybir.dt.bfloat16
AF = mybir.ActivationFunctionType
ALU = mybir.AluOpType
AX = mybir.AxisListType

SAME_ENGINE_SYNC = True

D = 2048
T = 4096
NT = T // 128
KC = 16
DFF = 5632
NH = 16
G = 4
EPS = 1e-6
SCALE = 128 ** -0.5
NCMP = 255
NAIN = 5168


class Buf:
    __slots__ = ("name", "w", "r", "ps")

    def __init__(self, name="", ps=False):
        self.name = name
        self.w = None
        self.r = []
        self.ps = ps


class Eng:
    def __init__(self, nc, e, name, pe=False, ndma=0):
        self.e = e
        self.name = name
        self.sem = nc.alloc_semaphore("s_" + name)
        self.cnt = 0
        self.seen = {}
        self.pe = pe
        self.dsems = [[nc.alloc_semaphore("d_%s%d" % (name, i)), 0] for i in range(ndma)]
        self.dnext = 0


class K:
    def __init__(self):
        nc = bass.Bass("TRN2", target_bir_lowering=False)
        self.nc = nc
        self.pe = Eng(nc, nc.tensor, "pe", pe=True)
        self.act = Eng(nc, nc.scalar, "act")
        self.dve = Eng(nc, nc.vector, "dve")
        self.pool = Eng(nc, nc.gpsimd, "pool", ndma=8)
        self.sp = Eng(nc, nc.sync, "sp", ndma=16)
        self.engs = [self.pe, self.act, self.dve, self.pool, self.sp]
        self.nins = 0
        self.nwait = 0

    def _wait(self, eng, tok):
        sem, val = tok
        if sem is eng.sem and (eng.pe or not SAME_ENGINE_SYNC):
            return
        key = id(sem)
        if eng.seen.get(key, 0) >= val:
            return
        eng.e.wait_ge(sem, val)
        eng.seen[key] = val
        self.nwait += 1

    def _deps(self, eng, R, W):
        for b in R:
            if b.w is not None:
                self._wait(eng, b.w)
        for b in W:
            if b.w is not None:
                self._wait(eng, b.w)
            for t in b.r:
                self._wait(eng, t)

    def _commit(self, tok, R, W):
        for b in W:
            b.w = tok
            b.r = []
        for b in R:
            b.r.append(tok)
            if len(b.r) > 16:
                d = {}
                for s, v in b.r:
                    kk = id(s)
                    if kk not in d or d[kk][1] < v:
                        d[kk] = (s, v)
                b.r = list(d.values())

    def op(self, eng, fn, R=(), W=()):
        px = [b for b in R if b.ps and b not in W]
        if px:
            W = list(W) + px
            R = [b for b in R if not b.ps]
        self._deps(eng, R, W)
        ins = fn(eng.e)
        eng.cnt += 1
        ins.then_inc(eng.sem, 1)
        self._commit((eng.sem, eng.cnt), R, W)
        self.nins += 1
        return ins

    def dma(self, q, out, in_, R=(), W=(), **kw):
        self._deps(q, R, W)
        slot = q.dsems[q.dnext]
        q.dnext = (q.dnext + 1) % len(q.dsems)
        if slot[1] > 0:
            self._wait(q, (slot[0], slot[1]))
        ins = q.e.dma_start(out=out, in_=in_, **kw)
        slot[1] += 16
        ins.then_inc(slot[0], 16)
        self._commit((slot[0], slot[1]), R, W)
        self.nins += 1
        return ins

    def fence(self):
        for e in self.engs:
            for o in self.engs:
                if o is not e and o.cnt > 0:
                    self._wait(e, (o.sem, o.cnt))
                for s in o.dsems:
                    if s[1] > 0:
                        self._wait(e, (s[0], s[1]))


def _bf(a):
    return np.ascontiguousarray(a).astype(ml_dtypes.bfloat16)


def make_consts():
    c = {}
    p = np.arange(128)
    ident = np.eye(128, dtype=np.float32)
    rot = np.zeros((128, 128), np.float32)
    for dp in range(64):
        rot[dp + 64, dp] = -1.0
        rot[dp, dp + 64] = 1.0
    ones = np.ones((128, 128), np.float32)
    ltri = (p[:, None] > p[None, :]).astype(np.float32)
    causal = (p[:, None] <= p[None, :]).astype(np.float32)
    anti = (p[:, None] > p[None, :]).astype(np.float32)
    trin = -(ltri + ident) / np.float32(SCALE)
    c["cst_bf"] = _bf(np.stack([ident, rot, ones, ltri, causal, anti, trin], axis=1))
    t = np.arange(T, dtype=np.float64)
    inv = 10000.0 ** (-np.arange(64, dtype=np.float64) / 64)
    ang = (t[None, :].astype(np.float32) * inv.astype(np.float32)[:, None]).astype(np.float32)
    cs = np.stack([np.concatenate([np.cos(ang), np.cos(ang)], 0), np.concatenate([np.sin(ang), np.sin(ang)], 0)], axis=1)
    c["cossin"] = np.ascontiguousarray(cs.astype(np.float32))
    n = np.arange(256)
    cmp_end = n * 16 + 31
    mc = ((cmp_end[:, None] <= np.arange(T)[None, :]) & (n[:, None] < NCMP)).astype(np.float32)
    c["maskc"] = _bf(mc.reshape(2, 128, T).transpose(1, 0, 2))
    c_start = n[:, None] * 16
    s_start = np.arange(64)[None, :] * 64
    ov = ((c_start < s_start + 64) & (c_start + 32 > s_start) & (n[:, None] < NCMP)).astype(np.float32)
    ext = np.zeros((256, 65), np.float32)
    ext[:NCMP, 0] = 1.0
    ext[:, 1:] = ov
    c["vext"] = _bf(ext.reshape(2, 128, 65).transpose(1, 0, 2))
    tt = np.arange(T)
    blk = np.arange(64)
    cur = tt // 64
    forced = (blk[None, :] == 0) | (blk[None, :] == cur[:, None]) | (blk[None, :] == cur[:, None] - 1)
    avail = blk[None, :] * 64 <= tt[:, None]
    availm = (avail & ~forced).astype(np.float32)
    forcem = np.where(forced, 1e9 * (1.0 + (blk[None, :] == 0) + 2.0 * (blk[None, :] == cur[:, None])), np.where(avail, 0.0, -1e9)).astype(np.float32)
    c["selm"] = np.ascontiguousarray(np.stack([availm, forcem], 0).reshape(2, NT, 128, 64).transpose(2, 0, 1, 3))
    kk = np.arange(128)[:, None]
    qq = np.arange(512)[None, :]
    msb = np.stack([((a * 128 + kk) < qq) for a in range(4)], axis=1).astype(np.float32)
    c["msb_f"] = np.ascontiguousarray(msb)
    c["msb_b"] = _bf(msb)
    return c


class Prog:
    def __init__(self, nlayers=4, dbg=(), stop=None):
        self.stop = stop
        self.k = K()
        k = self.k
        nc = k.nc
        self.nc = nc
        self.dbg = set(dbg)
        self.nlayers = nlayers
        dt = nc.dram_tensor
        self.inp = {}

        def din(name, shape, dtype=F32):
            self.inp[name] = dt(name, list(shape), dtype, kind="ExternalInput")
            return self.inp[name]

        self.x = din("x", [T, D])
        self.cT = din("cT", [128, KC])
        self.mod_w = din("mod_w", [4, D, 6 * D])
        self.mod_b = din("mod_b", [4, 6 * D])
        self.norm_g = din("norm_g", [4, 4, D])
        self.ffn_w_in = din("ffn_w_in", [4, D, 2 * DFF])
        self.ffn_w_out = din("ffn_w_out", [4, DFF, D])
        self.a_w_in = din("a_w_in", [2, D, NAIN])
        self.a_gate_b = din("a_gate_b", [2, 48])
        self.a_cmp_peT = din("a_cmp_peT", [2, 2, 128, 32])
        self.a_cmp_w1 = din("a_cmp_w1", [2, 2, 4096, 256])
        self.a_cmp_w2 = din("a_cmp_w2", [2, 2, 256, 128])
        self.a_w_out = din("a_w_out", [2, D, D])
        self.b_w_q = din("b_w_q", [2, D, D])
        self.b_w_out = din("b_w_out", [2, D, D])
        self.kv_norm_g = din("kv_norm_g", [1, D])
        self.kv_mod_w = din("kv_mod_w", [D, 2 * D])
        self.kv_mod_b = din("kv_mod_b", [1, 2 * D])
        self.kv_w = din("kv_w", [D, 2 * D])
        self.c_cst = din("cst_bf", [128, 7, 128], BF16)
        self.c_cossin = din("cossin", [128, 2, T])
        self.c_maskc = din("maskc", [128, 2, T], BF16)
        self.c_vext = din("vext", [128, 2, 65], BF16)
        self.c_selm = din("selm", [128, 2, NT, 64])
        self.c_msb_f = din("msb_f", [128, 4, 512])
        self.c_msb_b = din("msb_b", [128, 4, 512], BF16)
        self.out = dt("out", [T, D], F32, kind="ExternalOutput")

        def scr(name, shape, dtype):
            kind = "ExternalOutput" if name in self.dbg else "Internal"
            return dt(name, list(shape), dtype, kind=kind)

        self.xs1 = scr("xs1", [T, D], F32)
        self.xs2 = scr("xs2", [T, D], F32)
        self.hT = [scr("hT%d" % i, [128, KC, T], BF16) for i in range(3)]
        self.qT = scr("qT", [G, 128, NT, 4, 128], BF16)
        self.kcT = scr("kcT", [G, 128, T], BF16)
        self.vcT = scr("vcT", [G, 128, T], BF16)
        self.ksT = scr("ksT", [G, 128, T], BF16)
        self.kwT = scr("kwT", [G, 128, T], BF16)
        self.vs = scr("vs", [T, 512], BF16)
        self.vw = scr("vw", [T, 512], BF16)
        self.opart = scr("opart", [T, D], F32)
        self.selT = scr("selT", [G, 64, T], BF16)
        self.kTsh = scr("kTsh", [NH, 128, T], BF16)
        self.vsh = scr("vsh", [T, D], BF16)
        self.qsb = scr("qsb", [NH, 128, T], BF16)
        self.wsc_in = scr("wsc_in", [22, 128, KC * 512], BF16)
        self.wsc_out = scr("wsc_out", [16, 128, 11 * 512], BF16)
        self.wsc_p = scr("wsc_p", [11, 128, KC * 512], BF16)
        self.wsc_o = scr("wsc_o", [4, 128, KC * 512], BF16)
        self.wsc_kv = scr("wsc_kv", [8, 128, KC * 512], BF16)
        self.dbufs = {}

        self.ps = [nc.alloc_psum_tensor("ps%d" % i, [128, 512], F32) for i in range(8)]
        self.psB = [Buf("ps%d" % i, ps=True) for i in range(8)]
        self.cst = nc.alloc_sbuf_tensor("cst", [128, 7, 128], BF16)
        self.cstB = Buf("cst")
        self.crep = nc.alloc_sbuf_tensor("crep", [128, KC, 128], BF16)
        self.crepB = Buf("crep")
        self.NW = 3
        self.wt = [nc.alloc_sbuf_tensor("wt%d" % i, [128, KC, 512], BF16) for i in range(self.NW)]
        self.wB = [Buf("wt%d" % i) for i in range(self.NW)]
        self.wi = 0
        self.vec = [nc.alloc_sbuf_tensor("vec%d" % i, [128, D], F32) for i in range(4)]
        self.vecB = [Buf("vec%d" % i) for i in range(4)]
        self.gate_sb = None
        self.gateB = Buf("gate")
        self.uid = 0

    def sbt(self, name, shape, dtype):
        self.uid += 1
        return self.nc.sbuf_tensor("%s_u%d" % (name, self.uid), shape, dtype)

    def db(self, name, i=0):
        key = (name, i)
        b = self.dbufs.get(key)
        if b is None:
            b = self.dbufs[key] = Buf("%s_%s" % key)
        return b

    def E(self, eng, meth, *a, R=(), W=(), **kw):
        return self.k.op(eng, lambda e: getattr(e, meth)(*a, **kw), R, W)

    def ident(self):
        return self.cst[:, 0, :]

    def bcast_row(self, handle, off, n):
        return bass.AP(handle, off, [[0, 128], [1, n]])

    def wload(self, src2d, kc=16, ncols=512):
        i = self.wi
        self.wi = (i + 1) % self.NW
        self.k.dma(self.k.pool, self.wt[i][:, 0:kc, 0:ncols], src2d.rearrange("(k p) n -> p k n", p=128), W=[self.wB[i]])
        return self.wt[i], self.wB[i]

    def precast(self, scr_t, name, tile, src2d, kc=KC, ncols=512):
        self.k.dma(self.k.pool, scr_t[tile].rearrange("p (k n) -> p k n", k=kc)[:, :, 0:ncols],
                   src2d.rearrange("(k p) n -> p k n", p=128), W=[self.db(name, tile)])

    def wload_bf(self, scr_t, name, tile, kc=KC, ncols=512):
        i = self.wi
        self.wi = (i + 1) % self.NW
        self.k.dma(self.k.sp, self.wt[i][:, 0:kc, 0:ncols], scr_t[tile].rearrange("p (k n) -> p k n", k=kc)[:, :, 0:ncols],
                   R=[self.db(name, tile)], W=[self.wB[i]])
        return self.wt[i], self.wB[i]

    def precast_ffn(self, l):
        Win = self.ffn_w_in.ap()[l]
        Wout = self.ffn_w_out.ap()[l]
        for fg in range(11):
            self.precast(self.wsc_in, "wsc_in", 2 * fg, Win[:, fg * 512:(fg + 1) * 512])
            self.precast(self.wsc_in, "wsc_in", 2 * fg + 1, Win[:, DFF + fg * 512: DFF + (fg + 1) * 512])
        for cb in range(4):
            for kg in range(4):
                self.precast(self.wsc_out, "wsc_out", cb * 4 + kg, Wout[kg * 11 * 128:(kg + 1) * 11 * 128, cb * 512:(cb + 1) * 512], kc=11)

    def precast_sq(self, scr_t, name, w2d, ncols_total):
        nb = (ncols_total + 511) // 512
        for b in range(nb):
            nc_ = min(512, ncols_total - b * 512)
            self.precast(scr_t, name, b, w2d[:, b * 512:b * 512 + nc_], ncols=nc_)

    def psbf(self, i):
        return self.ps[i][:, :].bitcast(BF16)

    def setup(self):
        k = self.k
        nc = self.nc
        k.dma(k.sp, self.cst[:, :, :], self.c_cst[:, :, :], W=[self.cstB])
        with ExitStack() as es:
            cf = es.enter_context(self.sbt("cf", [128, KC], F32))
            ca = es.enter_context(self.sbt("ca", [128, KC], F32))
            cfB = Buf()
            caB = Buf()
            k.dma(k.sp, cf[:, :], self.cT[:, :], W=[cfB])
            self.E(k.act, "activation", ca[:, :], cf[:, :], AF.Silu, R=[cfB], W=[caB])
            self.E(k.dve, "tensor_copy", self.crep[:, :, :], ca[:, :].unsqueeze(2).to_broadcast([128, KC, 128]), R=[caB], W=[self.crepB])
            k.fence()

    def modvec(self, vi, w2d, col0, bias_handle, bias_off):
        k = self.k
        dst = self.vec[vi]
        dB = self.vecB[vi]
        k.dma(k.sp, dst[:, :], self.bcast_row(bias_handle, bias_off, D), W=[dB])
        for cb in range(4):
            wt, wB = self.wload(w2d[:, col0 + cb * 512: col0 + (cb + 1) * 512])
            pi = cb % 2
            for kk in range(KC):
                self.E(k.pe, "matmul", self.ps[pi][:, :], self.crep[:, kk, :], wt[:, kk, :], start=(kk == 0), stop=(kk == KC - 1),
                       R=[self.crepB, wB], W=[self.psB[pi]])
            self.E(k.dve, "tensor_tensor", dst[:, cb * 512:(cb + 1) * 512], self.ps[pi][:, :], dst[:, cb * 512:(cb + 1) * 512], ALU.add,
                   R=[self.psB[pi], dB], W=[dB])

    def load_gain(self, vi, handle, off):
        self.k.dma(self.k.sp, self.vec[vi][:, :], self.bcast_row(handle, off, D), W=[self.vecB[vi]])

    def mod_AB(self, l, which, gi, va, vb, vtmp):
        k = self.k
        w2d = self.mod_w.ap()[l]
        base = 0 if which == 0 else 3 * D
        self.modvec(vb, w2d, base, self.mod_b, l * 6 * D + base)
        self.modvec(va, w2d, base + D, self.mod_b, l * 6 * D + base + D)
        self.load_gain(vtmp, self.norm_g, (l * 4 + gi) * D)
        self.E(k.dve, "scalar_tensor_tensor", self.vec[va][:, :], self.vec[va][:, :], 1.0, self.vec[vtmp][:, :], ALU.add, ALU.mult,
               R=[self.vecB[va], self.vecB[vtmp]], W=[self.vecB[va]])

    def mod_C(self, l, which, gi, vc, vtmp):
        k = self.k
        w2d = self.mod_w.ap()[l]
        base = 2 * D if which == 0 else 5 * D
        self.modvec(vc, w2d, base, self.mod_b, l * 6 * D + base)
        self.load_gain(vtmp, self.norm_g, (l * 4 + gi) * D)
        self.E(k.dve, "tensor_tensor", self.vec[vc][:, :], self.vec[vc][:, :], self.vec[vtmp][:, :], ALU.mult,
               R=[self.vecB[vc], self.vecB[vtmp]], W=[self.vecB[vc]])

    def alloc_norm_tiles(self, es):
        nc = self.nc
        w = {}
        w["ss"] = [es.enter_context(self.sbt("ss%d" % i, [128, 8], F32)) for i in range(2)]
        w["ssB"] = [Buf() for _ in range(2)]
        w["hn"] = es.enter_context(self.sbt("hn", [128, D], F32))
        w["hnB"] = Buf()
        w["junk"] = w["hn"]
        w["junkB"] = w["hnB"]
        w["hb"] = [es.enter_context(self.sbt("hb%d" % i, [128, D], BF16)) for i in range(1)]
        w["hbB"] = [Buf() for _ in range(1)]
        w["hTt"] = [es.enter_context(self.sbt("hTt%d" % i, [128, KC, 128], BF16)) for i in range(1)]
        w["hTtB"] = [Buf() for _ in range(1)]
        w["i"] = 0
        return w

    def rstd(self, w, src, srcB, n):
        k = self.k
        i = w["i"] % 2
        w["i"] += 1
        ss = w["ss"][i]
        sB = w["ssB"][i]
        self.E(k.pool, "memset", ss[:, 0:1], 0.0, W=[sB])
        self.E(k.act, "activation", w["junk"][:, 0:n], src, AF.Square, accum_out=ss[:, 0:1], R=[srcB], W=[w["junkB"], sB])
        self.E(k.dve, "tensor_scalar", ss[:, 1:2], ss[:, 0:1], 1.0 / n, EPS, ALU.mult, ALU.add, R=[sB], W=[sB])
        self.E(k.act, "activation", ss[:, 2:3], ss[:, 1:2], AF.Sqrt, R=[sB], W=[sB])
        self.E(k.dve, "reciprocal", ss[:, 3:4], ss[:, 2:3], R=[sB], W=[sB])
        return ss[:, 3:4], sB

    def prenorm_tile(self, w, xt, xB, va, vb, hTi, tt, pbanks=(0, 1)):
        k = self.k
        r, rB = self.rstd(w, xt, xB, D)
        j = tt % len(w["hb"])
        self.E(k.dve, "scalar_tensor_tensor", w["hn"][:, :], xt, r, self.vec[va][:, :], ALU.mult, ALU.mult,
               R=[xB, rB, self.vecB[va]], W=[w["hnB"]])
        self.E(k.pool, "tensor_tensor", w["hb"][j][:, :], w["hn"][:, :], self.vec[vb][:, :], ALU.add,
               R=[w["hnB"], self.vecB[vb]], W=[w["hbB"][j]])
        self.transpose_to_hT(w, w["hb"][j], w["hbB"][j], hTi, tt, pbanks)

    def transpose_to_hT(self, w, src, srcB, hTi, tt, pbanks=(0, 1), nch=KC, ch0=0):
        k = self.k
        j = tt % len(w["hTt"])
        hTt = w["hTt"][j]
        hB = w["hTtB"][j]
        for half in range((nch + 7) // 8):
            pi = pbanks[half % len(pbanks)]
            n8 = min(8, nch - half * 8)
            pb = self.psbf(pi)
            for c in range(n8):
                cc = half * 8 + c
                self.E(k.pe, "transpose", pb[:, c * 128:(c + 1) * 128], src[:, cc * 128:(cc + 1) * 128], self.ident(),
                       R=[srcB, self.cstB], W=[self.psB[pi]])
            eng = k.act if half % 2 == 0 else k.dve
            meth = "copy" if eng is k.act else "tensor_copy"
            self.E(eng, meth, hTt[:, half * 8: half * 8 + n8, :], pb[:, 0:n8 * 128].rearrange("p (k n) -> p k n", k=n8),
                   R=[self.psB[pi]], W=[hB])
        k.dma(k.sp, self.hT[hTi][:, ch0:ch0 + nch, tt * 128:(tt + 1) * 128], hTt[:, 0:nch, :], R=[hB], W=[self.db("hT%d" % hTi, tt)])

    def norm_phase(self, xsrc, xname, va, vb, hTi):
        k = self.k
        nc = self.nc
        with ExitStack() as es:
            w = self.alloc_norm_tiles(es)
            xt = [es.enter_context(self.sbt("xt%d" % i, [128, D], F32)) for i in range(2)]
            xB = [Buf() for _ in range(2)]
            for tt in range(NT):
                j = tt % 2
                k.dma(k.sp, xt[j][:, :], xsrc[tt * 128:(tt + 1) * 128, :], R=[self.db(xname, tt)], W=[xB[j]])
                self.prenorm_tile(w, xt[j][:, :], xB[j], va, vb, hTi, tt)
            k.fence()

    def residual_tile(self, w, y, yB, xt, xB, xsrc, xsname, xdst, xdname, vc, tt):
        k = self.k
        k.dma(k.sp, xt, xsrc[tt * 128:(tt + 1) * 128, :], R=[self.db(xsname, tt)], W=[xB])
        r, rB = self.rstd(w, y, yB, D)
        self.E(k.dve, "scalar_tensor_tensor", w["hn"][:, :], y, r, self.vec[vc][:, :], ALU.mult, ALU.mult,
               R=[yB, rB, self.vecB[vc]], W=[w["hnB"]])
        self.E(k.dve, "tensor_tensor", xt, xt, w["hn"][:, :], ALU.add, R=[xB, w["hnB"]], W=[xB])
        k.dma(k.sp, xdst[tt * 128:(tt + 1) * 128, :], xt, R=[xB], W=[self.db(xdname, tt)])

    def proj(self, hTi, wsrc, blocks, fF=None, fT=None, pF=(0, 1, 6), pT=(2, 3, 7)):
        k = self.k
        nc = self.nc
        with ExitStack() as es:
            hb = [es.enter_context(self.sbt("phT%d" % i, [128, KC, 512], BF16)) for i in range(2)]
            hB = [Buf() for _ in range(2)]
            cnt = 0
            for tb in range(T // 512):
                j = tb % 2
                k.dma(k.sp, hb[j][:, :, :], self.hT[hTi][:, :, tb * 512:(tb + 1) * 512],
                      R=[self.db("hT%d" % hTi, 4 * tb + s) for s in range(4)], W=[hB[j]])
                for (col0, ncols, mode, tag) in blocks:
                    wt, wB = self.wload_bf(wsrc[0], wsrc[1], col0 // 512, ncols=ncols)
                    if mode == "F":
                        for jj in range(ncols // 128):
                            pi = pF[cnt % len(pF)]
                            cnt += 1
                            for kk in range(KC):
                                self.E(k.pe, "matmul", self.ps[pi][:, :], wt[:, kk, jj * 128:(jj + 1) * 128], hb[j][:, kk, :],
                                       start=(kk == 0), stop=(kk == KC - 1), R=[wB, hB[j]], W=[self.psB[pi]])
                            fF(tag, jj, tb, pi)
                    else:
                        for s in range(4):
                            pi = pT[cnt % len(pT)]
                            cnt += 1
                            for kk in range(KC):
                                self.E(k.pe, "matmul", self.ps[pi][:, 0:ncols], hb[j][:, kk, s * 128:(s + 1) * 128], wt[:, kk, 0:ncols],
                                       start=(kk == 0), stop=(kk == KC - 1), R=[wB, hB[j]], W=[self.psB[pi]])
                            fT(tag, s, tb, pi)
            k.fence()

    def nsa_proj(self, l):
        k = self.k
        nc = self.nc
        with ExitStack() as es:
            cs = [es.enter_context(self.sbt("cs%d" % i, [128, 2, 512], F32)) for i in range(2)]
            csB = [Buf() for _ in range(2)]
            raw = [es.enter_context(self.sbt("raw%d" % i, [128, 512], BF16)) for i in range(2)]
            rawB = [Buf() for _ in range(2)]
            t1 = [es.enter_context(self.sbt("t1_%d" % i, [128, 512], F32)) for i in range(2)]
            t1B = [Buf() for _ in range(2)]
            t2 = [es.enter_context(self.sbt("t2_%d" % i, [128, 512], F32)) for i in range(2)]
            t2B = [Buf() for _ in range(2)]
            ob = [es.enter_context(self.sbt("ob%d" % i, [128, 512], BF16)) for i in range(3)]
            obB = [Buf() for _ in range(3)]
            gb = es.enter_context(self.sbt("gb", [128, 48], F32))
            gbB = Buf()
            gt = es.enter_context(self.sbt("gt", [128, 48], F32))
            gtB = Buf()
            k.dma(k.sp, gb[:, :], self.bcast_row(self.a_gate_b, l * 48, 48), W=[gbB])
            st = {"n": 0, "tb": -1}
            featdst = {4: self.kcT, 5: self.vcT, 6: self.ksT, 8: self.kwT}
            featname = {4: "kcT", 5: "vcT", 6: "ksT", 8: "kwT"}

            def fF(tag, jj, tb, pi):
                n = st["n"]
                st["n"] += 1
                i2 = n % 2
                i3 = n % 3
                if st["tb"] != tb:
                    st["tb"] = tb
                    k.dma(k.sp, cs[tb % 2][:, :, :], self.c_cossin[:, :, tb * 512:(tb + 1) * 512], W=[csB[tb % 2]])
                c_ = cs[tb % 2]
                cB = csB[tb % 2]
                P = self.ps[pi]
                PB = self.psB[pi]
                if tag == 5:
                    self.E(k.act, "copy", ob[i3][:, :], P[:, :], R=[PB], W=[obB[i3]])
                else:
                    self.E(k.act, "copy", raw[i2][:, :], P[:, :], R=[PB], W=[rawB[i2]])
                    p2 = 4 + i2
                    self.E(k.pe, "matmul", self.ps[p2][:, :], self.cst[:, 1, :], raw[i2][:, :], start=True, stop=True,
                           R=[self.cstB, rawB[i2]], W=[self.psB[p2]])
                    self.E(k.dve, "tensor_tensor", t1[i2][:, :], P[:, :], c_[:, 0, :], ALU.mult, R=[PB, cB], W=[t1B[i2]])
                    self.E(k.dve, "tensor_tensor", t2[i2][:, :], self.ps[p2][:, :], c_[:, 1, :], ALU.mult, R=[self.psB[p2], cB], W=[t2B[i2]])
                    self.E(k.pool, "tensor_tensor", ob[i3][:, :], t1[i2][:, :], t2[i2][:, :], ALU.add, R=[t1B[i2], t2B[i2]], W=[obB[i3]])
                if tag < 4:
                    dst = self.qT[tag, :, 4 * tb:4 * tb + 4, jj, :]
                    k.dma(k.sp, dst, ob[i3][:, :].rearrange("p (a t) -> p a t", a=4), R=[obB[i3]], W=[self.db("qT", (tag, tb))])
                else:
                    k.dma(k.sp, featdst[tag][jj, :, tb * 512:(tb + 1) * 512], ob[i3][:, :], R=[obB[i3]], W=[self.db(featname[tag], (jj, tb))])

            def fT(tag, s, tb, pi):
                n = st["n"]
                st["n"] += 1
                i3 = n % 3
                P = self.ps[pi]
                PB = self.psB[pi]
                tt = 4 * tb + s
                if tag == 10:
                    self.E(k.dve, "tensor_tensor", gt[:, :], P[:, 0:48], gb[:, :], ALU.add, R=[PB, gbB], W=[gtB])
                    self.E(k.act, "activation", self.gate_sb[:, tt, :], gt[:, :], AF.Sigmoid, R=[gtB], W=[self.gateB])
                else:
                    dst = self.vs if tag == 7 else self.vw
                    self.E(k.act, "copy", ob[i3][:, :], P[:, :], R=[PB], W=[obB[i3]])
                    k.dma(k.sp, dst[tt * 128:(tt + 1) * 128, :], ob[i3][:, :], R=[obB[i3]], W=[self.db("vs" if tag == 7 else "vw", tt)])

            blocks = [(wb * 512, 512, "F", wb) for wb in (0, 1, 2, 3, 4, 5, 6, 8)]
            blocks += [(7 * 512, 512, "T", 7), (9 * 512, 512, "T", 9), (5120, 48, "T", 10)]
            import os
            if os.environ.get("TB"):
                keep = [int(x) for x in os.environ["TB"].split(",")]
                blocks = [b for b in blocks if b[3] in keep]
            self.proj(0, (self.wsc_p, "wsc_p"), blocks, fF, fT)

    def nsa_compress(self, l, es_outer):
        k = self.k
        nc = self.nc
        kcmpT = es_outer.enter_context(self.sbt("kcmpT", [128, G, 256], BF16))
        kcB = Buf()
        vext = es_outer.enter_context(self.sbt("vext", [128, G, 2, 193], BF16))
        vxB = Buf()
        self.E(k.pool, "memset", kcmpT[:, :, :], 0.0, W=[kcB])
        self.E(k.pool, "memset", vext[:, :, :, :], 0.0, W=[vxB])
        for g in range(G):
            k.dma(k.sp, vext[:, g, :, 128:193], self.c_vext[:, :, :], W=[vxB])
        with ExitStack() as es:
            src = [es.enter_context(self.sbt("csrc%d" % i, [128, T], BF16)) for i in range(2)]
            srcB = [Buf() for _ in range(2)]
            w1 = es.enter_context(self.sbt("cw1", [128, 32, 256], BF16))
            w1B = Buf()
            w2 = es.enter_context(self.sbt("cw2", [128, 2, 128], BF16))
            w2B = Buf()
            pef = es.enter_context(self.sbt("pef", [128, 32], F32))
            pefB = Buf()
            perep = es.enter_context(self.sbt("perep", [128, 32, 128], BF16))
            peB = Buf()
            xs = es.enter_context(self.sbt("gxs", [128, 256], F32))
            xsB = Buf()
            u = es.enter_context(self.sbt("gu", [128, 256], F32))
            uB = Buf()
            sg = es.enter_context(self.sbt("gsg", [128, 256], F32))
            sgB = Buf()
            hid = es.enter_context(self.sbt("ghid", [128, 256], BF16))
            hidB = Buf()
            hidT = es.enter_context(self.sbt("ghidT", [128, 2, 128], BF16))
            hidTB = Buf()
            n = 0
            for kv in range(2):
                k.dma(k.pool, w1[:, :, :], self.a_cmp_w1.ap()[l, kv].rearrange("(c p) n -> p c n", p=128), W=[w1B])
                k.dma(k.pool, w2[:, :, :], self.a_cmp_w2.ap()[l, kv].rearrange("(c p) n -> p c n", p=128), W=[w2B])
                k.dma(k.sp, pef[:, :], self.a_cmp_peT.ap()[l, kv], W=[pefB])
                self.E(k.dve, "tensor_copy", perep[:, :, :], pef[:, :].unsqueeze(2).to_broadcast([128, 32, 128]), R=[pefB], W=[peB])
                srcd = self.kcT if kv == 0 else self.vcT
                sname = "kcT" if kv == 0 else "vcT"
                for g in range(G):
                    sj = n % 2
                    n += 1
                    k.dma(k.sp, src[sj][:, :], srcd[g, :, :], R=[self.db(sname, (g, tb)) for tb in range(8)], W=[srcB[sj]])
                    for half in range(2):
                        nr = 128 if half == 0 else 127
                        n0 = half * 128
                        pi = 0
                        P = self.ps[pi]
                        PB = self.psB[pi]
                        for li in range(32):
                            self.E(k.pe, "matmul", P[0:nr, 0:256], perep[:, li, 0:nr], w1[:, li, :], start=(li == 0), stop=False,
                                   R=[peB, w1B], W=[PB])
                        for li in range(32):
                            a0 = li + 16 * n0
                            self.E(k.pe, "matmul", P[0:nr, 0:256], src[sj][:, a0:a0 + 16 * (nr - 1) + 1:16], w1[:, li, :], start=False, stop=(li == 31),
                                   R=[srcB[sj], w1B], W=[PB])
                        self.E(k.act, "copy", xs[0:nr, :], P[0:nr, 0:256], R=[PB], W=[xsB])
                        self.E(k.dve, "tensor_tensor", u[0:nr, :], xs[0:nr, :], xs[0:nr, :], ALU.mult, R=[xsB], W=[uB])
                        self.E(k.dve, "tensor_scalar", u[0:nr, :], u[0:nr, :], 0.044715, 1.0, ALU.mult, ALU.add, R=[uB], W=[uB])
                        self.E(k.dve, "tensor_tensor", u[0:nr, :], u[0:nr, :], xs[0:nr, :], ALU.mult, R=[uB, xsB], W=[uB])
                        self.E(k.act, "activation", sg[0:nr, :], u[0:nr, :], AF.Sigmoid, scale=1.5957691216057308, R=[uB], W=[sgB])
                        if nr < 128:
                            self.E(k.pool, "memset", hid[:, :], 0.0, W=[hidB])
                        self.E(k.dve, "tensor_tensor", hid[0:nr, :], xs[0:nr, :], sg[0:nr, :], ALU.mult, R=[xsB, sgB], W=[hidB])
                        pb = self.psbf(1)
                        for c in range(2):
                            self.E(k.pe, "transpose", pb[:, c * 128:(c + 1) * 128], hid[:, c * 128:(c + 1) * 128], self.ident(),
                                   R=[hidB, self.cstB], W=[self.psB[1]])
                        self.E(k.act, "copy", hidT[:, :, :], pb[:, 0:256].rearrange("p (c n) -> p c n", c=2), R=[self.psB[1]], W=[hidTB])
                        P2 = self.ps[2]
                        if kv == 0:
                            for c in range(2):
                                self.E(k.pe, "matmul", P2[:, 0:nr], w2[:, c, :], hidT[:, c, 0:nr], start=(c == 0), stop=(c == 1),
                                       R=[w2B, hidTB], W=[self.psB[2]])
                            self.E(k.act, "copy", kcmpT[:, g, n0:n0 + nr], P2[:, 0:nr], R=[self.psB[2]], W=[kcB])
                        else:
                            for c in range(2):
                                self.E(k.pe, "matmul", P2[0:nr, 0:128], hidT[:, c, 0:nr], w2[:, c, :], start=(c == 0), stop=(c == 1),
                                       R=[w2B, hidTB], W=[self.psB[2]])
                            self.E(k.act, "copy", vext[0:nr, g, half, 0:128], P2[0:nr, 0:128], R=[self.psB[2]], W=[vxB])
            k.fence()
        return kcmpT, kcB, vext, vxB

    def attn_pair(self, kT, kB, q, qB, vx, vxB_, nv, mask, mB, acc, first, last, pt, ptB, spi, nrows=128):
        k = self.k
        P = self.ps[spi]
        PB = self.psB[spi]
        self.E(k.pe, "matmul", P[0:nrows, :], kT, q, start=True, stop=True, R=[kB, qB], W=[PB])

        def rest():
            self.E(k.act, "activation", pt[0:nrows, :], P[0:nrows, :], AF.Exp, scale=SCALE, R=[PB], W=[ptB])
            if mask is not None:
                self.E(k.dve, "tensor_tensor", pt[0:nrows, :].rearrange("p (r t) -> p r t", r=4), pt[0:nrows, :].rearrange("p (r t) -> p r t", r=4),
                       mask.unsqueeze(1).to_broadcast([nrows, 4, 128]), ALU.mult, R=[ptB, mB], W=[ptB])
            for r in range(4):
                bk = acc[r // 2]
                off = (r % 2) * 256
                self.E(k.pe, "matmul", self.ps[bk][:, off:off + nv], pt[0:nrows, r * 128:(r + 1) * 128], vx,
                       start=(first and r % 2 == 0), stop=last, skip_group_check=True,
                       R=[ptB, vxB_], W=[self.psB[bk]])

        prev = getattr(self, "_pend", None)
        self._pend = rest
        if prev is not None:
            prev()

    def attn_flush(self):
        prev = getattr(self, "_pend", None)
        self._pend = None
        if prev is not None:
            prev()

    def nsa_attn(self, l):
        k = self.k
        nc = self.nc
        with ExitStack() as es0:
            kcmpT, kcB, vext, vxB = self.nsa_compress(l, es0)
            with ExitStack() as es:
                sb = lambda name, shape, dtp: es.enter_context(self.sbt(name, shape, dtp))
                maskc_r = [sb("maskc%d" % i, [128, 2, 128], BF16) for i in range(2)]
                mcB_r = [Buf() for _ in range(2)]
                selm_r = [sb("selm%d" % i, [128, 2, 64], F32) for i in range(2)]
                smB_r = [Buf() for _ in range(2)]
                qg = sb("qg", [128, NT, 512], BF16)
                qgB = Buf()
                ksg = sb("ksg", [128, T], BF16)
                ksB = Buf()
                kwg = sb("kwg", [128, T], BF16)
                kwB = Buf()
                vsg = sb("vsg", [128, NT, 129], BF16)
                vsB = Buf()
                vwg = sb("vwg", [128, NT, 129], BF16)
                vwB = Buf()
                self.E(k.pool, "memset", vsg[:, :, 128:129], 1.0, W=[vsB])
                self.E(k.pool, "memset", vwg[:, :, 128:129], 1.0, W=[vwB])
                pt = [sb("pt%d" % i, [128, 512], BF16) for i in range(3)]
                ptB = [Buf() for _ in range(3)]
                msk = [sb("msk%d" % i, [128, NT, 128], BF16) for i in range(2)]
                mskB = [Buf() for _ in range(2)]
                ot = [sb("ot%d" % i, [128, 512], F32) for i in range(2)]
                otB = [Buf() for _ in range(2)]
                otb = [sb("otb%d" % i, [128, 512], BF16) for i in range(2)]
                otbB = [Buf() for _ in range(2)]
                st = [sb("ast%d" % i, [128, 16], F32) for i in range(2)]
                stB = [Buf() for _ in range(2)]
                imp = sb("imp", [128, 64], F32)
                impB = Buf()
                vv = sb("vv", [128, 64], F32)
                vvB = Buf()
                cmpt = sb("cmpt", [128, 64, 64], BF16)
                cmpB = Buf()
                cnt = sb("cnt", [128, 64], F32)
                cntB = Buf()
                selb = sb("selb", [128, 64], BF16)
                selbB = Buf()
                selT_t = [sb("selTt%d" % i, [64, 128], BF16) for i in range(2)]
                selTB = [Buf() for _ in range(2)]
                w = {"hTt": [sb("ohTt%d" % i, [128, 4, 128], BF16) for i in range(2)], "hTtB": [Buf() for _ in range(2)]}
                accsets = ((4, 5), (6, 7))
                npair = 0
                nbr = [0]

                def next_acc():
                    a = accsets[nbr[0] % 2]
                    nbr[0] += 1
                    return a

                def accv(acc, r, c0, c1):
                    return self.ps[acc[r // 2]][:, (r % 2) * 256 + c0:(r % 2) * 256 + c1], self.psB[acc[r // 2]]

                def finish_branch(tt, g, gi, acc, stt, sB_, o, oB, prev=None, prevB=None):
                    for r in range(4):
                        A, AB = accv(acc, r, 128, 129)
                        if gi == 0:
                            self.E(k.dve, "tensor_scalar", stt[:, r:r + 1], A, 1e-30, None, ALU.max, R=[AB], W=[sB_])
                        else:
                            self.E(k.dve, "reciprocal", stt[:, 4 + r:5 + r], A, R=[AB], W=[sB_])
                    if gi == 0:
                        self.E(k.dve, "reciprocal", stt[:, 4:8], stt[:, 0:4], R=[sB_], W=[sB_])
                    g0 = (g * 4) * 3 + gi
                    self.E(k.dve, "tensor_tensor", stt[:, 8:12], stt[:, 4:8], self.gate_sb[:, tt, g0:g0 + 10:3], ALU.mult,
                           R=[sB_, self.gateB], W=[sB_])
                    for r in range(4):
                        A, AB = accv(acc, r, 0, 128)
                        if prev is None:
                            self.E(k.dve, "tensor_scalar", o[:, r * 128:(r + 1) * 128], A, stt[:, 8 + r:9 + r], None, ALU.mult,
                                   R=[AB, sB_], W=[oB])
                        else:
                            self.E(k.dve, "scalar_tensor_tensor", o[:, r * 128:(r + 1) * 128], A, stt[:, 8 + r:9 + r],
                                   prev[:, r * 128:(r + 1) * 128], ALU.mult, ALU.add, R=[AB, sB_, prevB], W=[oB])

                for g in range(G):
                    k.dma(k.sp, qg[:, :, :], self.qT[g].rearrange("p a r t -> p a (r t)"), R=[self.db("qT", (g, tb)) for tb in range(8)], W=[qgB])
                    k.dma(k.sp, ksg[:, :], self.ksT[g, :, :], R=[self.db("ksT", (g, tb)) for tb in range(8)], W=[ksB])
                    k.dma(k.sp, kwg[:, :], self.kwT[g, :, :], R=[self.db("kwT", (g, tb)) for tb in range(8)], W=[kwB])
                    k.dma(k.sp, vsg[:, :, 0:128], self.vs[:, g * 128:(g + 1) * 128].rearrange("(j p) c -> p j c", p=128),
                          R=[self.db("vs", tt) for tt in range(NT)], W=[vsB])
                    k.dma(k.sp, vwg[:, :, 0:128], self.vw[:, g * 128:(g + 1) * 128].rearrange("(j p) c -> p j c", p=128),
                          R=[self.db("vw", tt) for tt in range(NT)], W=[vwB])
                    for tt in range(NT):
                        q = qg[:, tt, :]
                        maskc = maskc_r[tt % 2]
                        mcB = mcB_r[tt % 2]
                        selm = selm_r[tt % 2]
                        smB = smB_r[tt % 2]
                        k.dma(k.sp, maskc[:, :, :], self.c_maskc[:, :, tt * 128:(tt + 1) * 128], W=[mcB])
                        k.dma(k.sp, selm[:, :, :], self.c_selm[:, :, tt, :], W=[smB])
                        halves = [0] + ([1] if tt >= 16 else [])
                        acc = next_acc()
                        for hi, half in enumerate(halves):
                            need_mask = not (half == 0 and tt >= 17)
                            m = maskc[:, half, :] if need_mask else None
                            pi = npair % 3
                            self.attn_pair(kcmpT[:, g, half * 128:(half + 1) * 128], kcB, q, qgB, vext[:, g, half, :], vxB, 193,
                                           m, mcB, acc, hi == 0, hi == len(halves) - 1, pt[pi], ptB[pi], npair % 3)
                            npair += 1
                        j = tt % 2
                        stt = st[j]
                        sB_ = stB[j]
                        self.attn_flush()
                        finish_branch(tt, g, 0, acc, stt, sB_, ot[j], otB[j])
                        for r in range(4):
                            A, AB = accv(acc, r, 129, 193)
                            if r == 0:
                                self.E(k.dve, "tensor_scalar", imp[:, :], A, stt[:, 4:5], None, ALU.mult, R=[AB, sB_], W=[impB])
                            else:
                                self.E(k.dve, "scalar_tensor_tensor", imp[:, :], A, stt[:, 4 + r:5 + r], imp[:, :],
                                       ALU.mult, ALU.add, R=[AB, sB_, impB], W=[impB])
                        k.dma(k.sp, self.opart[tt * 128:(tt + 1) * 128, g * 512:(g + 1) * 512], ot[j][:, :], R=[otB[j]], W=[self.db("opart", (g, tt))])
                        self.E(k.dve, "tensor_tensor", vv[:, :], imp[:, :], selm[:, 0, :], ALU.mult, R=[impB, smB], W=[vvB])
                        self.E(k.dve, "tensor_tensor", vv[:, :], vv[:, :], selm[:, 1, :], ALU.add, R=[vvB, smB], W=[vvB])
                        self.E(k.dve, "tensor_tensor", cmpt[:, :, :], vv[:, :].unsqueeze(1).to_broadcast([128, 64, 64]),
                               vv[:, :].unsqueeze(2).to_broadcast([128, 64, 64]), ALU.is_gt, R=[vvB], W=[cmpB])
                        self.E(k.dve, "reduce_sum", cnt[:, :], cmpt[:, :, :], AX.X, R=[cmpB], W=[cntB])
                        self.E(k.dve, "tensor_scalar", selb[:, :], cnt[:, :], 15.5, None, ALU.is_lt, R=[cntB], W=[selbB])
                        pb = self.psbf(3)
                        self.E(k.pe, "transpose", pb[0:64, 0:128], selb[:, :], self.ident(), R=[selbB, self.cstB], W=[self.psB[3]])
                        self.E(k.act, "copy", selT_t[j][:, :], pb[0:64, 0:128], R=[self.psB[3]], W=[selTB[j]])
                        k.dma(k.sp, self.selT[g, :, tt * 128:(tt + 1) * 128], selT_t[j][:, :], R=[selTB[j]], W=[self.db("selT", (g, tt))])

                    def issue_loads(tt):
                        j = tt % 2
                        mk = msk[j]
                        mkB = mskB[j]
                        for b in range(2):
                            srcap = bass.AP(self.selT, g * 64 * T + b * T + tt * 128, [[0, 64], [2 * T, tt + 1], [1, 128]])
                            k.dma(k.sp, mk[b * 64:(b + 1) * 64, 0:tt + 1, :], srcap, R=[self.db("selT", (g, tt))], W=[mkB])
                        self.E(k.pool, "tensor_tensor", mk[:, tt, :], mk[:, tt, :], self.cst[:, 4, :], ALU.mult, R=[mkB, self.cstB], W=[mkB])
                        k.dma(k.sp, ot[j][:, :], self.opart[tt * 128:(tt + 1) * 128, g * 512:(g + 1) * 512], R=[self.db("opart", (g, tt))], W=[otB[j]])

                    issue_loads(0)
                    for tt in range(NT):
                        q = qg[:, tt, :]
                        j = tt % 2
                        mk = msk[j]
                        mkB = mskB[j]
                        stt = st[j]
                        sB_ = stB[j]
                        js = list(range(max(0, tt - 4), tt + 1))
                        accw = next_acc()
                        for ji, jk in enumerate(js):
                            m = None
                            if jk == tt:
                                m = self.cst[:, 4, :]
                            elif jk == tt - 4:
                                m = self.cst[:, 5, :]
                            pi = npair % 3
                            self.attn_pair(kwg[:, jk * 128:(jk + 1) * 128], kwB, q, qgB, vwg[:, jk, :], vwB, 129,
                                           m, self.cstB, accw, ji == 0, ji == len(js) - 1, pt[pi], ptB[pi], npair % 3)
                            npair += 1
                        accs = next_acc()
                        for jk in range(tt + 1):
                            pi = npair % 3
                            self.attn_pair(ksg[:, jk * 128:(jk + 1) * 128], ksB, q, qgB, vsg[:, jk, :], vsB, 129,
                                           mk[:, jk, :], mkB, accs, jk == 0, jk == tt, pt[pi], ptB[pi], npair % 3)
                            npair += 1
                            if jk == 0:
                                finish_branch(tt, g, 2, accw, stt, sB_, ot[j], otB[j], ot[j], otB[j])
                                if tt + 1 < NT:
                                    issue_loads(tt + 1)
                        self.attn_flush()
                        finish_branch(tt, g, 1, accs, stt, sB_, ot[j], otB[j], ot[j], otB[j])
                        self.E(k.act, "copy", otb[j][:, :], ot[j][:, :], R=[otB[j]], W=[otbB[j]])
                        self.transpose_to_hT(w, otb[j], otbB[j], 2, tt, pbanks=(3,), nch=4, ch0=g * 4)
                k.fence()

    def outproj(self, xsrc, xsname, xdst, xdname, vc, va, vb, hTdst):
        k = self.k
        nc = self.nc
        with ExitStack() as es:
            w = self.alloc_norm_tiles(es)
            ysb = [es.enter_context(self.sbt("ysb%d" % i, [128, D], F32)) for i in range(4)]
            yB = [Buf() for _ in range(4)]
            xt = [es.enter_context(self.sbt("xt%d" % i, [128, D], F32)) for i in range(2)]
            xB = [Buf() for _ in range(2)]
            hb = [es.enter_context(self.sbt("phT%d" % i, [128, KC, 512], BF16)) for i in range(2)]
            hB = [Buf() for _ in range(2)]
            cnt = 0
            for tb in range(T // 512):
                j = tb % 2
                k.dma(k.sp, hb[j][:, :, :], self.hT[2][:, :, tb * 512:(tb + 1) * 512],
                      R=[self.db("hT2", 4 * tb + s) for s in range(4)], W=[hB[j]])
                for cb in range(4):
                    wt, wB = self.wload_bf(self.wsc_o, "wsc_o", cb)
                    for s in range(4):
                        pi = 4 + cnt % 4
                        cnt += 1
                        for kk in range(KC):
                            self.E(k.pe, "matmul", self.ps[pi][:, :], hb[j][:, kk, s * 128:(s + 1) * 128], wt[:, kk, :],
                                   start=(kk == 0), stop=(kk == KC - 1), R=[wB, hB[j]], W=[self.psB[pi]])
                        self.E(k.act, "copy", ysb[s][:, cb * 512:(cb + 1) * 512], self.ps[pi][:, :], R=[self.psB[pi]], W=[yB[s]])
                for s in range(4):
                    tt = 4 * tb + s
                    self.residual_tile(w, ysb[s][:, :], yB[s], xt[s % 2][:, :], xB[s % 2], xsrc, xsname, xdst, xdname, vc, tt)
                    if hTdst is not None:
                        self.prenorm_tile(w, xt[s % 2][:, :], xB[s % 2], va, vb, hTdst, tt)
            k.fence()

    def ffn(self, l, xsrc, xsname, xdst, xdname, vc, va, vb, hTdst):
        k = self.k
        nc = self.nc
        Win = self.ffn_w_in.ap()[l]
        Wout = self.ffn_w_out.ap()[l]
        NF = DFF // 128
        with ExitStack() as es:
            w = self.alloc_norm_tiles(es)
            ysb = [es.enter_context(self.sbt("ysb%d" % i, [128, D], F32)) for i in range(4)]
            yB = [Buf() for _ in range(4)]
            xt = [es.enter_context(self.sbt("xt%d" % i, [128, D], F32)) for i in range(1)]
            xB = [Buf() for _ in range(1)]
            hb = [es.enter_context(self.sbt("phT%d" % i, [128, KC, 512], BF16)) for i in range(1)]
            hB = [Buf() for _ in range(1)]
            actT = es.enter_context(self.sbt("actT", [128, NF, 512], BF16))
            aB = [Buf() for _ in range(11)]
            sg = [es.enter_context(self.sbt("fsg%d" % i, [128, 512], F32)) for i in range(2)]
            sgB = [Buf() for _ in range(2)]
            n = 0
            for tb in range(T // 512):
                j = 0
                k.dma(k.sp, hb[j][:, :, :], self.hT[1][:, :, tb * 512:(tb + 1) * 512],
                      R=[self.db("hT1", 4 * tb + s) for s in range(4)], W=[hB[j]])
                for fg in range(11):
                    wg, wgB = self.wload_bf(self.wsc_in, "wsc_in", 2 * fg)
                    wu, wuB = self.wload_bf(self.wsc_in, "wsc_in", 2 * fg + 1)
                    for jj in range(4):
                        i2 = n % 2
                        n += 1
                        pg = i2
                        pu = 2 + i2
                        for kk in range(KC):
                            self.E(k.pe, "matmul", self.ps[pg][:, :], wg[:, kk, jj * 128:(jj + 1) * 128], hb[j][:, kk, :],
                                   start=(kk == 0), stop=(kk == KC - 1), R=[wgB, hB[j]], W=[self.psB[pg]])
                        for kk in range(KC):
                            self.E(k.pe, "matmul", self.ps[pu][:, :], wu[:, kk, jj * 128:(jj + 1) * 128], hb[j][:, kk, :],
                                   start=(kk == 0), stop=(kk == KC - 1), R=[wuB, hB[j]], W=[self.psB[pu]])
                        self.E(k.act, "activation", sg[i2][:, :], self.ps[pg][:, :], AF.Silu, R=[self.psB[pg]], W=[sgB[i2]])
                        self.E(k.dve, "tensor_tensor", actT[:, fg * 4 + jj, :], sg[i2][:, :], self.ps[pu][:, :], ALU.mult,
                               R=[sgB[i2], self.psB[pu]], W=[aB[fg]])
                for cb in range(4):
                    for kg in range(4):
                        wt, wB = self.wload_bf(self.wsc_out, "wsc_out", cb * 4 + kg, kc=11)
                        for s in range(4):
                            pi = 4 + s
                            for kk in range(11):
                                ch = kg * 11 + kk
                                self.E(k.pe, "matmul", self.ps[pi][:, :], actT[:, ch, s * 128:(s + 1) * 128], wt[:, kk, :],
                                       start=(ch == 0), stop=(ch == NF - 1), R=[wB, aB[ch // 4]], W=[self.psB[pi]])
                    for s in range(4):
                        self.E(k.act, "copy", ysb[s][:, cb * 512:(cb + 1) * 512], self.ps[4 + s][:, :], R=[self.psB[4 + s]], W=[yB[s]])
                for s in range(4):
                    tt = 4 * tb + s
                    self.residual_tile(w, ysb[s][:, :], yB[s], xt[0][:, :], xB[0], xsrc, xsname, xdst, xdname, vc, tt)
                    if hTdst is not None:
                        self.prenorm_tile(w, xt[0][:, :], xB[0], va, vb, hTdst, tt, pbanks=(0, 1))
            k.fence()

    def sb_kv(self, xsrc, xname):
        k = self.k
        self.modvec(1, self.kv_mod_w.ap(), 0, self.kv_mod_b, 0)
        self.modvec(0, self.kv_mod_w.ap(), D, self.kv_mod_b, D)
        self.load_gain(2, self.kv_norm_g, 0)
        self.E(k.dve, "scalar_tensor_tensor", self.vec[0][:, :], self.vec[0][:, :], 1.0, self.vec[2][:, :], ALU.add, ALU.mult,
               R=[self.vecB[0], self.vecB[2]], W=[self.vecB[0]])
        self.norm_phase(xsrc, xname, 0, 1, 2)
        nc = self.nc
        with ExitStack() as es:
            ob = [es.enter_context(self.sbt("ob%d" % i, [128, 512], BF16)) for i in range(3)]
            obB = [Buf() for _ in range(3)]
            st = {"n": 0}

            def fF(tag, jj, tb, pi):
                i3 = st["n"] % 3
                st["n"] += 1
                self.E(k.act, "copy", ob[i3][:, :], self.ps[pi][:, :], R=[self.psB[pi]], W=[obB[i3]])
                k.dma(k.sp, self.kTsh[tag * 4 + jj, :, tb * 512:(tb + 1) * 512], ob[i3][:, :], R=[obB[i3]], W=[self.db("kTsh", (tag * 4 + jj, tb))])

            def fT(tag, s, tb, pi):
                i3 = st["n"] % 3
                st["n"] += 1
                tt = 4 * tb + s
                self.E(k.dve, "tensor_copy", ob[i3][:, :], self.ps[pi][:, :], R=[self.psB[pi]], W=[obB[i3]])
                k.dma(k.sp, self.vsh[tt * 128:(tt + 1) * 128, tag * 512:(tag + 1) * 512], ob[i3][:, :], R=[obB[i3]], W=[self.db("vsh", (tag, tt))])

            blocks = [(c * 512, 512, "F", c) for c in range(4)] + [(D + c * 512, 512, "T", c) for c in range(4)]
            self.proj(2, (self.wsc_kv, "wsc_kv"), blocks, fF, fT)

    def sb_q(self, l2):
        k = self.k
        nc = self.nc
        with ExitStack() as es:
            ob = [es.enter_context(self.sbt("ob%d" % i, [128, 512], BF16)) for i in range(3)]
            obB = [Buf() for _ in range(3)]
            st = {"n": 0}

            def fF(tag, jj, tb, pi):
                i3 = st["n"] % 3
                st["n"] += 1
                self.E(k.act, "copy", ob[i3][:, :], self.ps[pi][:, :], R=[self.psB[pi]], W=[obB[i3]])
                k.dma(k.sp, self.qsb[tag * 4 + jj, :, tb * 512:(tb + 1) * 512], ob[i3][:, :], R=[obB[i3]], W=[self.db("qsb", (tag * 4 + jj, tb))])

            blocks = [(c * 512, 512, "F", c) for c in range(4)]
            self.proj(0, (self.wsc_p, "wsc_p"), blocks, fF, None)

    def sb_attn(self):
        k = self.k
        nc = self.nc
        with ExitStack() as es:
            sb = lambda name, shape, dtp: es.enter_context(self.sbt(name, shape, dtp))
            mb = sb("msbb", [128, 4, 512], BF16)
            mbB = Buf()
            k.dma(k.sp, mb[:, :, :], self.c_msb_b[:, :, :], W=[mbB])
            kT = [sb("skT%d" % i, [128, T], BF16) for i in range(2)]
            kTB = [Buf() for _ in range(2)]
            qT = [sb("sqT%d" % i, [128, T], BF16) for i in range(2)]
            qTB = [Buf() for _ in range(2)]
            vh = [sb("svh%d" % i, [128, NT, 128], BF16) for i in range(2)]
            vhB = [Buf() for _ in range(2)]
            NR = 3
            ee = [sb("see%d" % i, [128, 512], F32) for i in range(NR)]
            eeB = [Buf() for _ in range(NR)]
            spb = [sb("sspb%d" % i, [128, 512], BF16) for i in range(NR)]
            spbB = [Buf() for _ in range(NR)]
            t3 = [sb("st3_%d" % i, [128, 512], F32) for i in range(NR)]
            t3B = [Buf() for _ in range(NR)]
            pt = [sb("spt%d" % i, [128, 512], BF16) for i in range(NR)]
            ptB = [Buf() for _ in range(NR)]
            carry = sb("scarry", [128, 512], F32)
            cB = Buf()
            ob = [sb("sob%d" % i, [128, 512], BF16) for i in range(2)]
            obB = [Buf() for _ in range(2)]
            st = {"n": 0, "nq": 0}

            def stageA(p):
                Pz = self.ps[p["zi"]]
                self.E(k.pe, "matmul", Pz[:, :], p["kT"], p["q"], start=True, stop=True, R=[p["kB"], p["qB"]], W=[self.psB[p["zi"]]])

            def stageB(p):
                i = p["i"]
                Pz = self.ps[p["zi"]]
                PzB = self.psB[p["zi"]]
                self.E(k.act, "activation", ee[i][:, :], Pz[:, :], AF.Exp, scale=SCALE, R=[PzB], W=[eeB[i]])
                self.E(k.act, "activation", spb[i][:, :], ee[i][:, :], AF.Ln, bias=1.0, R=[eeB[i]], W=[spbB[i]])
                if p["a"] >= 0:
                    self.E(k.pool, "tensor_tensor", spb[i][:, :], spb[i][:, :], mb[:, p["a"], :], ALU.mult, R=[spbB[i], mbB], W=[spbB[i]])
                self.E(k.pe, "matmul", Pz[:, :], self.cst[:, 6, :], spb[i][:, :], start=False, stop=True, skip_group_check=True,
                       R=[self.cstB, spbB[i]], W=[PzB])
                if not p["last"]:
                    ti = p["ti"]
                    self.E(k.pe, "matmul", self.ps[ti][:, :], self.cst[:, 2, :], spb[i][:, :], start=True, stop=True,
                           R=[self.cstB, spbB[i]], W=[self.psB[ti]])

            def stageC(p):
                i = p["i"]
                Pz = self.ps[p["zi"]]
                PzB = self.psB[p["zi"]]
                if p["first"]:
                    self.E(k.act, "activation", pt[i][:, :], Pz[:, :], AF.Exp, scale=SCALE, R=[PzB], W=[ptB[i]])
                else:
                    self.E(k.dve, "scalar_tensor_tensor", t3[i][:, :], Pz[:, :], SCALE, carry[:, :], ALU.mult, ALU.subtract,
                           R=[PzB, cB], W=[t3B[i]])
                    self.E(k.act, "activation", pt[i][:, :], t3[i][:, :], AF.Exp, R=[t3B[i]], W=[ptB[i]])
                if p["a"] >= 0:
                    self.E(k.pool, "tensor_tensor", pt[i][:, :], pt[i][:, :], mb[:, p["a"], :], ALU.mult, R=[ptB[i], mbB], W=[ptB[i]])
                self.E(k.pe, "matmul", self.ps[p["acc"]][:, :], p["v"], pt[i][:, :], start=p["first"], stop=p["last"],
                       R=[p["vB"], ptB[i]], W=[self.psB[p["acc"]]])
                if not p["last"]:
                    ti = p["ti"]
                    if p["first"]:
                        self.E(k.dve, "tensor_copy", carry[:, :], self.ps[ti][:, :], R=[self.psB[ti]], W=[cB])
                    else:
                        self.E(k.dve, "tensor_tensor", carry[:, :], carry[:, :], self.ps[ti][:, :], ALU.add, R=[cB, self.psB[ti]], W=[cB])
                if p["last"]:
                    oj = st["nq"] % 2
                    st["nq"] += 1
                    self.E(k.act, "copy", ob[oj][:, :], self.ps[p["acc"]][:, :], R=[self.psB[p["acc"]]], W=[obB[oj]])
                    k.dma(k.sp, self.hT[2][:, p["h"], p["qb"] * 512:(p["qb"] + 1) * 512], ob[oj][:, :], R=[obB[oj]],
                          W=[self.db("hT2", 4 * p["qb"] + s) for s in range(4)])

            pairs = []
            for h in range(NH):
                hj = h % 2
                for qb in range(T // 512):
                    jtop = 4 * qb + 3
                    for jk in range(jtop, -1, -1):
                        pairs.append(dict(h=h, hj=hj, qb=qb, jk=jk, a=jk - 4 * qb, first=(jk == jtop), last=(jk == 0)))
            loaded = set()
            nacc = 0
            for n, p in enumerate(pairs):
                p["i"] = n % NR
                p["zi"] = n % 3
                p["ti"] = 3 + n % 2
                if p["first"]:
                    nacc += 1
                p["acc"] = 6 + nacc % 2
            def ensure_loaded(h):
                if h in loaded or h >= NH:
                    return
                loaded.add(h)
                hj = h % 2
                k.dma(k.sp, kT[hj][:, :], self.kTsh[h, :, :], R=[self.db("kTsh", (h, tb)) for tb in range(8)], W=[kTB[hj]])
                k.dma(k.sp, qT[hj][:, :], self.qsb[h, :, :], R=[self.db("qsb", (h, tb)) for tb in range(8)], W=[qTB[hj]])
                k.dma(k.sp, vh[hj][:, :, :], self.vsh[:, h * 128:(h + 1) * 128].rearrange("(j p) c -> p j c", p=128),
                      R=[self.db("vsh", (h // 4, tt)) for tt in range(NT)], W=[vhB[hj]])
            def prep(p):
                ensure_loaded(p["h"])
                hj = p["hj"]
                jk = p["jk"]
                p["kT"] = kT[hj][:, jk * 128:(jk + 1) * 128]
                p["kB"] = kTB[hj]
                p["q"] = qT[hj][:, p["qb"] * 512:(p["qb"] + 1) * 512]
                p["qB"] = qTB[hj]
                p["v"] = vh[hj][:, jk, :]
                p["vB"] = vhB[hj]
            N = len(pairs)
            for n in range(N + 2):
                if n < N:
                    prep(pairs[n])
                    stageA(pairs[n])
                if 0 <= n - 1 < N:
                    stageB(pairs[n - 1])
                if 0 <= n - 2 < N:
                    stageC(pairs[n - 2])
            k.fence()

    def build(self):
        k = self.k
        self.setup()
        xin, xin_name = self.x.ap(), "x"
        self.precast_sq(self.wsc_p, "wsc_p", self.a_w_in.ap()[0], NAIN)
        for l in range(self.nlayers):
            last = (l == self.nlayers - 1)
            if l == 0:
                self.mod_AB(l, 0, 0, 0, 1, 2)
                self.norm_phase(xin, xin_name, 0, 1, 0)
                if self.stop == "norm0":
                    return self.nc
            if l == 2:
                self.sb_kv(xin, xin_name)
            self.precast_sq(self.wsc_o, "wsc_o", (self.a_w_out.ap()[l] if l < 2 else self.b_w_out.ap()[l - 2]), D)
            self.precast_ffn(l)
            if l < 2:
                with ExitStack() as esg:
                    self.gate_sb = esg.enter_context(self.sbt("gate_sb", [128, NT, 48], F32))
                    self.nsa_proj(l)
                    if self.stop == "nsa_proj":
                        return self.nc
                    self.nsa_attn(l)
                    k.fence()
                if self.stop == "nsa_attn":
                    return self.nc
                wout = self.a_w_out.ap()[l]
            else:
                self.sb_q(l - 2)
                self.sb_attn()
                wout = self.b_w_out.ap()[l - 2]
            self.mod_C(l, 0, 1, 2, 3)
            self.mod_AB(l, 1, 2, 0, 1, 3)
            self.outproj(xin, xin_name, self.xs1.ap(), "xs1", 2, 0, 1, 1)
            if self.stop == "outproj%d" % l:
                return self.nc
            self.mod_C(l, 1, 3, 2, 3)
            if not last:
                self.mod_AB(l + 1, 0, 0, 0, 1, 3)
            xdst, xdname = (self.out.ap(), "out") if last else (self.xs2.ap(), "xs2")
            if not last:
                if l + 1 < 2:
                    self.precast_sq(self.wsc_p, "wsc_p", self.a_w_in.ap()[l + 1], NAIN)
                else:
                    self.precast_sq(self.wsc_p, "wsc_p", self.b_w_q.ap()[l + 1 - 2], D)
                if l + 1 == 2:
                    self.precast_sq(self.wsc_kv, "wsc_kv", self.kv_w.ap(), 2 * D)
            self.ffn(l, self.xs1.ap(), "xs1", xdst, xdname, 2, 0, 1, None if last else 0)
            xin, xin_name = xdst, xdname
        k.fence()
        return self.nc


_CONSTS = None


def make_in_maps(inputs, ncores=8):
    global _CONSTS
    if _CONSTS is None:
        _CONSTS = make_consts()
    f = lambda a: np.ascontiguousarray(np.asarray(a, dtype=np.float32))
    shared = {
        "mod_w": f(inputs["mod_w"]), "mod_b": f(inputs["mod_b"]), "norm_g": f(inputs["norm_g"]),
        "ffn_w_in": f(inputs["ffn_w_in"]), "ffn_w_out": f(inputs["ffn_w_out"]), "a_w_in": f(inputs["a_w_in"]),
        "a_gate_b": f(inputs["a_gate_b"]),
        "a_cmp_peT": np.ascontiguousarray(np.asarray(inputs["a_cmp_pe"], dtype=np.float32).transpose(0, 1, 3, 2)),
        "a_cmp_w1": f(inputs["a_cmp_w1"]), "a_cmp_w2": f(inputs["a_cmp_w2"]), "a_w_out": f(inputs["a_w_out"]),
        "b_w_q": f(inputs["b_w_q"]), "b_w_out": f(inputs["b_w_out"]),
        "kv_norm_g": f(inputs["kv_norm_g"]).reshape(1, D), "kv_mod_w": f(inputs["kv_mod_w"]),
        "kv_mod_b": f(inputs["kv_mod_b"]).reshape(1, 2 * D), "kv_w": f(inputs["kv_w"]),
    }
    shared.update(_CONSTS)
    x = np.asarray(inputs["x"], dtype=np.float32)
    c = np.asarray(inputs["c"], dtype=np.float32)
    maps = []
    for core in range(ncores):
        b = core % 4
        m = dict(shared)
        m["x"] = np.ascontiguousarray(x[b])
        m["cT"] = np.ascontiguousarray(c[b].reshape(KC, 128).T)
        maps.append(m)
    return maps


def kernel(**inputs):
    prog = Prog()
    nc = prog.build()
    maps = make_in_maps(inputs)
    res = run_bass_kernel_spmd(nc, maps, core_ids=list(range(8)))
    out = np.stack([np.asarray(res.results[b]["out"], dtype=np.float32) for b in range(4)], axis=0)
    return out
```

```python
from contextlib import ExitStack
import numpy as np
import ml_dtypes
import concourse.bass as bass
import concourse.mybir as mybir
from concourse.bass_utils import run_bass_kernel_spmd

F32 = mybir.dt.float32
BF16 = mybir.dt.bfloat16
AF = mybir.ActivationFunctionType
ALU = mybir.AluOpType
AX = mybir.AxisListType

SAME_ENGINE_SYNC = True

D = 2048
T = 4096
NT = T // 128
KC = 16
DFF = 5632
NH = 16
G = 4
EPS = 1e-6
SCALE = 128 ** -0.5
NCMP = 255
NAIN = 5168


class Buf:
    __slots__ = ("name", "w", "r", "ps")

    def __init__(self, name="", ps=False):
        self.name = name
        self.w = None
        self.r = []
        self.ps = ps


class Eng:
    def __init__(self, nc, e, name, pe=False, ndma=0):
        self.e = e
        self.name = name
        self.sem = nc.alloc_semaphore("s_" + name)
        self.cnt = 0
        self.seen = {}
        self.pe = pe
        self.dsems = [[nc.alloc_semaphore("d_%s%d" % (name, i)), 0] for i in range(ndma)]
        self.dnext = 0


class K:
    def __init__(self):
        nc = bass.Bass("TRN2", target_bir_lowering=False)
        self.nc = nc
        self.pe = Eng(nc, nc.tensor, "pe", pe=True)
        self.act = Eng(nc, nc.scalar, "act")
        self.dve = Eng(nc, nc.vector, "dve")
        self.pool = Eng(nc, nc.gpsimd, "pool", ndma=8)
        self.sp = Eng(nc, nc.sync, "sp", ndma=16)
        self.engs = [self.pe, self.act, self.dve, self.pool, self.sp]
        self.nins = 0
        self.nwait = 0

    def _wait(self, eng, tok):
        sem, val = tok
        if sem is eng.sem and (eng.pe or not SAME_ENGINE_SYNC):
            return
        key = id(sem)
        if eng.seen.get(key, 0) >= val:
            return
        eng.e.wait_ge(sem, val)
        eng.seen[key] = val
        self.nwait += 1

    def _deps(self, eng, R, W):
        for b in R:
            if b.w is not None:
                self._wait(eng, b.w)
        for b in W:
            if b.w is not None:
                self._wait(eng, b.w)
            for t in b.r:
                self._wait(eng, t)

    def _commit(self, tok, R, W):
        for b in W:
            b.w = tok
            b.r = []
        for b in R:
            b.r.append(tok)
            if len(b.r) > 16:
                d = {}
                for s, v in b.r:
                    kk = id(s)
                    if kk not in d or d[kk][1] < v:
                        d[kk] = (s, v)
                b.r = list(d.values())

    def op(self, eng, fn, R=(), W=()):
        px = [b for b in R if b.ps and b not in W]
        if px:
            W = list(W) + px
            R = [b for b in R if not b.ps]
        self._deps(eng, R, W)
        ins = fn(eng.e)
        eng.cnt += 1
        ins.then_inc(eng.sem, 1)
        self._commit((eng.sem, eng.cnt), R, W)
        self.nins += 1
        return ins

    def dma(self, q, out, in_, R=(), W=(), **kw):
        self._deps(q, R, W)
        slot = q.dsems[q.dnext]
        q.dnext = (q.dnext + 1) % len(q.dsems)
        if slot[1] > 0:
            self._wait(q, (slot[0], slot[1]))
        ins = q.e.dma_start(out=out, in_=in_, **kw)
        slot[1] += 16
        ins.then_inc(slot[0], 16)
        self._commit((slot[0], slot[1]), R, W)
        self.nins += 1
        return ins

    def fence(self):
        for e in self.engs:
            for o in self.engs:
                if o is not e and o.cnt > 0:
                    self._wait(e, (o.sem, o.cnt))
                for s in o.dsems:
                    if s[1] > 0:
                        self._wait(e, (s[0], s[1]))


def _bf(a):
    return np.ascontiguousarray(a).astype(ml_dtypes.bfloat16)


def make_consts():
    c = {}
    p = np.arange(128)
    ident = np.eye(128, dtype=np.float32)
    rot = np.zeros((128, 128), np.float32)
    for dp in range(64):
        rot[dp + 64, dp] = -1.0
        rot[dp, dp + 64] = 1.0
    ones = np.ones((128, 128), np.float32)
    ltri = (p[:, None] > p[None, :]).astype(np.float32)
    causal = (p[:, None] <= p[None, :]).astype(np.float32)
    anti = (p[:, None] > p[None, :]).astype(np.float32)
    trin = -(ltri + ident) / np.float32(SCALE)
    c["cst_bf"] = _bf(np.stack([ident, rot, ones, ltri, causal, anti, trin], axis=1))
    t = np.arange(T, dtype=np.float64)
    inv = 10000.0 ** (-np.arange(64, dtype=np.float64) / 64)
    ang = (t[None, :].astype(np.float32) * inv.astype(np.float32)[:, None]).astype(np.float32)
    cs = np.stack([np.concatenate([np.cos(ang), np.cos(ang)], 0), np.concatenate([np.sin(ang), np.sin(ang)], 0)], axis=1)
    c["cossin"] = np.ascontiguousarray(cs.astype(np.float32))
    n = np.arange(256)
    cmp_end = n * 16 + 31
    mc = ((cmp_end[:, None] <= np.arange(T)[None, :]) & (n[:, None] < NCMP)).astype(np.float32)
    c["maskc"] = _bf(mc.reshape(2, 128, T).transpose(1, 0, 2))
    c_start = n[:, None] * 16
    s_start = np.arange(64)[None, :] * 64
    ov = ((c_start < s_start + 64) & (c_start + 32 > s_start) & (n[:, None] < NCMP)).astype(np.float32)
    ext = np.zeros((256, 65), np.float32)
    ext[:NCMP, 0] = 1.0
    ext[:, 1:] = ov
    c["vext"] = _bf(ext.reshape(2, 128, 65).transpose(1, 0, 2))
    tt = np.arange(T)
    blk = np.arange(64)
    cur = tt // 64
    forced = (blk[None, :] == 0) | (blk[None, :] == cur[:, None]) | (blk[None, :] == cur[:, None] - 1)
    avail = blk[None, :] * 64 <= tt[:, None]
    availm = (avail & ~forced).astype(np.float32)
    forcem = np.where(forced, 1e9 * (1.0 + (blk[None, :] == 0) + 2.0 * (blk[None, :] == cur[:, None])), np.where(avail, 0.0, -1e9)).astype(np.float32)
    c["selm"] = np.ascontiguousarray(np.stack([availm, forcem], 0).reshape(2, NT, 128, 64).transpose(2, 0, 1, 3))
    kk = np.arange(128)[:, None]
    qq = np.arange(512)[None, :]
    msb = np.stack([((a * 128 + kk) < qq) for a in range(4)], axis=1).astype(np.float32)
    c["msb_f"] = np.ascontiguousarray(msb)
    c["msb_b"] = _bf(msb)
    return c


class Prog:
    def __init__(self, nlayers=4, dbg=(), stop=None):
        self.stop = stop
        self.k = K()
        k = self.k
        nc = k.nc
        self.nc = nc
        self.dbg = set(dbg)
        self.nlayers = nlayers
        dt = nc.dram_tensor
        self.inp = {}

        def din(name, shape, dtype=F32):
            self.inp[name] = dt(name, list(shape), dtype, kind="ExternalInput")
            return self.inp[name]

        self.x = din("x", [T, D])
        self.cT = din("cT", [128, KC])
        self.mod_w = din("mod_w", [4, D, 6 * D])
        self.mod_b = din("mod_b", [4, 6 * D])
        self.norm_g = din("norm_g", [4, 4, D])
        self.ffn_w_in = din("ffn_w_in", [4, D, 2 * DFF])
        self.ffn_w_out = din("ffn_w_out", [4, DFF, D])
        self.a_w_in = din("a_w_in", [2, D, NAIN])
        self.a_gate_b = din("a_gate_b", [2, 48])
        self.a_cmp_peT = din("a_cmp_peT", [2, 2, 128, 32])
        self.a_cmp_w1 = din("a_cmp_w1", [2, 2, 4096, 256])
        self.a_cmp_w2 = din("a_cmp_w2", [2, 2, 256, 128])
        self.a_w_out = din("a_w_out", [2, D, D])
        self.b_w_q = din("b_w_q", [2, D, D])
        self.b_w_out = din("b_w_out", [2, D, D])
        self.kv_norm_g = din("kv_norm_g", [1, D])
        self.kv_mod_w = din("kv_mod_w", [D, 2 * D])
        self.kv_mod_b = din("kv_mod_b", [1, 2 * D])
        self.kv_w = din("kv_w", [D, 2 * D])
        self.c_cst = din("cst_bf", [128, 7, 128], BF16)
        self.c_cossin = din("cossin", [128, 2, T])
        self.c_maskc = din("maskc", [128, 2, T], BF16)
        self.c_vext = din("vext", [128, 2, 65], BF16)
        self.c_selm = din("selm", [128, 2, NT, 64])
        self.c_msb_f = din("msb_f", [128, 4, 512])
        self.c_msb_b = din("msb_b", [128, 4, 512], BF16)
        self.out = dt("out", [T, D], F32, kind="ExternalOutput")

        def scr(name, shape, dtype):
            kind = "ExternalOutput" if name in self.dbg else "Internal"
            return dt(name, list(shape), dtype, kind=kind)

        self.xs1 = scr("xs1", [T, D], F32)
        self.xs2 = scr("xs2", [T, D], F32)
        self.hT = [scr("hT%d" % i, [128, KC, T], BF16) for i in range(3)]
        self.qT = scr("qT", [G, 128, NT, 4, 128], BF16)
        self.kcT = scr("kcT", [G, 128, T], BF16)
        self.vcT = scr("vcT", [G, 128, T], BF16)
        self.ksT = scr("ksT", [G, 128, T], BF16)
        self.kwT = scr("kwT", [G, 128, T], BF16)
        self.vs = scr("vs", [T, 512], BF16)
        self.vw = scr("vw", [T, 512], BF16)
        self.opart = scr("opart", [T, D], F32)
        self.selT = scr("selT", [G, 64, T], BF16)
        self.kTsh = scr("kTsh", [NH, 128, T], BF16)
        self.vsh = scr("vsh", [T, D], BF16)
        self.qsb = scr("qsb", [NH, 128, T], BF16)
        self.wsc_in = scr("wsc_in", [22, 128, KC * 512], BF16)
        self.wsc_out = scr("wsc_out", [16, 128, 11 * 512], BF16)
        self.wsc_p = scr("wsc_p", [11, 128, KC * 512], BF16)
        self.wsc_o = scr("wsc_o", [4, 128, KC * 512], BF16)
        self.wsc_kv = scr("wsc_kv", [8, 128, KC * 512], BF16)
        self.dbufs = {}

        self.ps = [nc.alloc_psum_tensor("ps%d" % i, [128, 512], F32) for i in range(8)]
        self.psB = [Buf("ps%d" % i, ps=True) for i in range(8)]
        self.cst = nc.alloc_sbuf_tensor("cst", [128, 7, 128], BF16)
        self.cstB = Buf("cst")
        self.crep = nc.alloc_sbuf_tensor("crep", [128, KC, 128], BF16)
        self.crepB = Buf("crep")
        self.NW = 3
        self.wt = [nc.alloc_sbuf_tensor("wt%d" % i, [128, KC, 512], BF16) for i in range(self.NW)]
        self.wB = [Buf("wt%d" % i) for i in range(self.NW)]
        self.wi = 0
        self.vec = [nc.alloc_sbuf_tensor("vec%d" % i, [128, D], F32) for i in range(4)]
        self.vecB = [Buf("vec%d" % i) for i in range(4)]
        self.gate_sb = None
        self.gateB = Buf("gate")
        self.uid = 0

    def sbt(self, name, shape, dtype):
        self.uid += 1
        return self.nc.sbuf_tensor("%s_u%d" % (name, self.uid), shape, dtype)

    def db(self, name, i=0):
        key = (name, i)
        b = self.dbufs.get(key)
        if b is None:
            b = self.dbufs[key] = Buf("%s_%s" % key)
        return b

    def E(self, eng, meth, *a, R=(), W=(), **kw):
        return self.k.op(eng, lambda e: getattr(e, meth)(*a, **kw), R, W)

    def ident(self):
        return self.cst[:, 0, :]

    def bcast_row(self, handle, off, n):
        return bass.AP(handle, off, [[0, 128], [1, n]])

    def wload(self, src2d, kc=16, ncols=512):
        i = self.wi
        self.wi = (i + 1) % self.NW
        self.k.dma(self.k.pool, self.wt[i][:, 0:kc, 0:ncols], src2d.rearrange("(k p) n -> p k n", p=128), W=[self.wB[i]])
        return self.wt[i], self.wB[i]

    def precast(self, scr_t, name, tile, src2d, kc=KC, ncols=512):
        self.k.dma(self.k.pool, scr_t[tile].rearrange("p (k n) -> p k n", k=kc)[:, :, 0:ncols],
                   src2d.rearrange("(k p) n -> p k n", p=128), W=[self.db(name, tile)])

    def wload_bf(self, scr_t, name, tile, kc=KC, ncols=512):
        i = self.wi
        self.wi = (i + 1) % self.NW
        self.k.dma(self.k.sp, self.wt[i][:, 0:kc, 0:ncols], scr_t[tile].rearrange("p (k n) -> p k n", k=kc)[:, :, 0:ncols],
                   R=[self.db(name, tile)], W=[self.wB[i]])
        return self.wt[i], self.wB[i]

    def precast_ffn(self, l):
        Win = self.ffn_w_in.ap()[l]
        Wout = self.ffn_w_out.ap()[l]
        for fg in range(11):
            self.precast(self.wsc_in, "wsc_in", 2 * fg, Win[:, fg * 512:(fg + 1) * 512])
            self.precast(self.wsc_in, "wsc_in", 2 * fg + 1, Win[:, DFF + fg * 512: DFF + (fg + 1) * 512])
        for cb in range(4):
            for kg in range(4):
                self.precast(self.wsc_out, "wsc_out", cb * 4 + kg, Wout[kg * 11 * 128:(kg + 1) * 11 * 128, cb * 512:(cb + 1) * 512], kc=11)

    def precast_sq(self, scr_t, name, w2d, ncols_total):
        nb = (ncols_total + 511) // 512
        for b in range(nb):
            nc_ = min(512, ncols_total - b * 512)
            self.precast(scr_t, name, b, w2d[:, b * 512:b * 512 + nc_], ncols=nc_)

    def psbf(self, i):
        return self.ps[i][:, :].bitcast(BF16)

    def setup(self):
        k = self.k
        nc = self.nc
        k.dma(k.sp, self.cst[:, :, :], self.c_cst[:, :, :], W=[self.cstB])
        with ExitStack() as es:
            cf = es.enter_context(self.sbt("cf", [128, KC], F32))
            ca = es.enter_context(self.sbt("ca", [128, KC], F32))
            cfB = Buf()
            caB = Buf()
            k.dma(k.sp, cf[:, :], self.cT[:, :], W=[cfB])
            self.E(k.act, "activation", ca[:, :], cf[:, :], AF.Silu, R=[cfB], W=[caB])
            self.E(k.dve, "tensor_copy", self.crep[:, :, :], ca[:, :].unsqueeze(2).to_broadcast([128, KC, 128]), R=[caB], W=[self.crepB])
            k.fence()

    def modvec(self, vi, w2d, col0, bias_handle, bias_off):
        k = self.k
        dst = self.vec[vi]
        dB = self.vecB[vi]
        k.dma(k.sp, dst[:, :], self.bcast_row(bias_handle, bias_off, D), W=[dB])
        for cb in range(4):
            wt, wB = self.wload(w2d[:, col0 + cb * 512: col0 + (cb + 1) * 512])
            pi = cb % 2
            for kk in range(KC):
                self.E(k.pe, "matmul", self.ps[pi][:, :], self.crep[:, kk, :], wt[:, kk, :], start=(kk == 0), stop=(kk == KC - 1),
                       R=[self.crepB, wB], W=[self.psB[pi]])
            self.E(k.dve, "tensor_tensor", dst[:, cb * 512:(cb + 1) * 512], self.ps[pi][:, :], dst[:, cb * 512:(cb + 1) * 512], ALU.add,
                   R=[self.psB[pi], dB], W=[dB])

    def load_gain(self, vi, handle, off):
        self.k.dma(self.k.sp, self.vec[vi][:, :], self.bcast_row(handle, off, D), W=[self.vecB[vi]])

    def mod_AB(self, l, which, gi, va, vb, vtmp):
        k = self.k
        w2d = self.mod_w.ap()[l]
        base = 0 if which == 0 else 3 * D
        self.modvec(vb, w2d, base, self.mod_b, l * 6 * D + base)
        self.modvec(va, w2d, base + D, self.mod_b, l * 6 * D + base + D)
        self.load_gain(vtmp, self.norm_g, (l * 4 + gi) * D)
        self.E(k.dve, "scalar_tensor_tensor", self.vec[va][:, :], self.vec[va][:, :], 1.0, self.vec[vtmp][:, :], ALU.add, ALU.mult,
               R=[self.vecB[va], self.vecB[vtmp]], W=[self.vecB[va]])

    def mod_C(self, l, which, gi, vc, vtmp):
        k = self.k
        w2d = self.mod_w.ap()[l]
        base = 2 * D if which == 0 else 5 * D
        self.modvec(vc, w2d, base, self.mod_b, l * 6 * D + base)
        self.load_gain(vtmp, self.norm_g, (l * 4 + gi) * D)
        self.E(k.dve, "tensor_tensor", self.vec[vc][:, :], self.vec[vc][:, :], self.vec[vtmp][:, :], ALU.mult,
               R=[self.vecB[vc], self.vecB[vtmp]], W=[self.vecB[vc]])

    def alloc_norm_tiles(self, es):
        nc = self.nc
        w = {}
        w["ss"] = [es.enter_context(self.sbt("ss%d" % i, [128, 8], F32)) for i in range(2)]
        w["ssB"] = [Buf() for _ in range(2)]
        w["hn"] = es.enter_context(self.sbt("hn", [128, D], F32))
        w["hnB"] = Buf()
        w["junk"] = w["hn"]
        w["junkB"] = w["hnB"]
        w["hb"] = [es.enter_context(self.sbt("hb%d" % i, [128, D], BF16)) for i in range(1)]
        w["hbB"] = [Buf() for _ in range(1)]
        w["hTt"] = [es.enter_context(self.sbt("hTt%d" % i, [128, KC, 128], BF16)) for i in range(1)]
        w["hTtB"] = [Buf() for _ in range(1)]
        w["i"] = 0
        return w

    def rstd(self, w, src, srcB, n):
        k = self.k
        i = w["i"] % 2
        w["i"] += 1
        ss = w["ss"][i]
        sB = w["ssB"][i]
        self.E(k.pool, "memset", ss[:, 0:1], 0.0, W=[sB])
        self.E(k.act, "activation", w["junk"][:, 0:n], src, AF.Square, accum_out=ss[:, 0:1], R=[srcB], W=[w["junkB"], sB])
        self.E(k.dve, "tensor_scalar", ss[:, 1:2], ss[:, 0:1], 1.0 / n, EPS, ALU.mult, ALU.add, R=[sB], W=[sB])
        self.E(k.act, "activation", ss[:, 2:3], ss[:, 1:2], AF.Sqrt, R=[sB], W=[sB])
        self.E(k.dve, "reciprocal", ss[:, 3:4], ss[:, 2:3], R=[sB], W=[sB])
        return ss[:, 3:4], sB

    def prenorm_tile(self, w, xt, xB, va, vb, hTi, tt, pbanks=(0, 1)):
        k = self.k
        r, rB = self.rstd(w, xt, xB, D)
        j = tt % len(w["hb"])
        self.E(k.dve, "scalar_tensor_tensor", w["hn"][:, :], xt, r, self.vec[va][:, :], ALU.mult, ALU.mult,
               R=[xB, rB, self.vecB[va]], W=[w["hnB"]])
        self.E(k.pool, "tensor_tensor", w["hb"][j][:, :], w["hn"][:, :], self.vec[vb][:, :], ALU.add,
               R=[w["hnB"], self.vecB[vb]], W=[w["hbB"][j]])
        self.transpose_to_hT(w, w["hb"][j], w["hbB"][j], hTi, tt, pbanks)

    def transpose_to_hT(self, w, src, srcB, hTi, tt, pbanks=(0, 1), nch=KC, ch0=0):
        k = self.k
        j = tt % len(w["hTt"])
        hTt = w["hTt"][j]
        hB = w["hTtB"][j]
        for half in range((nch + 7) // 8):
            pi = pbanks[half % len(pbanks)]
            n8 = min(8, nch - half * 8)
            pb = self.psbf(pi)
            for c in range(n8):
                cc = half * 8 + c
                self.E(k.pe, "transpose", pb[:, c * 128:(c + 1) * 128], src[:, cc * 128:(cc + 1) * 128], self.ident(),
                       R=[srcB, self.cstB], W=[self.psB[pi]])
            eng = k.act if half % 2 == 0 else k.dve
            meth = "copy" if eng is k.act else "tensor_copy"
            self.E(eng, meth, hTt[:, half * 8: half * 8 + n8, :], pb[:, 0:n8 * 128].rearrange("p (k n) -> p k n", k=n8),
                   R=[self.psB[pi]], W=[hB])
        k.dma(k.sp, self.hT[hTi][:, ch0:ch0 + nch, tt * 128:(tt + 1) * 128], hTt[:, 0:nch, :], R=[hB], W=[self.db("hT%d" % hTi, tt)])

    def norm_phase(self, xsrc, xname, va, vb, hTi):
        k = self.k
        nc = self.nc
        with ExitStack() as es:
            w = self.alloc_norm_tiles(es)
            xt = [es.enter_context(self.sbt("xt%d" % i, [128, D], F32)) for i in range(2)]
            xB = [Buf() for _ in range(2)]
            for tt in range(NT):
                j = tt % 2
                k.dma(k.sp, xt[j][:, :], xsrc[tt * 128:(tt + 1) * 128, :], R=[self.db(xname, tt)], W=[xB[j]])
                self.prenorm_tile(w, xt[j][:, :], xB[j], va, vb, hTi, tt)
            k.fence()

    def residual_tile(self, w, y, yB, xt, xB, xsrc, xsname, xdst, xdname, vc, tt):
        k = self.k
        k.dma(k.sp, xt, xsrc[tt * 128:(tt + 1) * 128, :], R=[self.db(xsname, tt)], W=[xB])
        r, rB = self.rstd(w, y, yB, D)
        self.E(k.dve, "scalar_tensor_tensor", w["hn"][:, :], y, r, self.vec[vc][:, :], ALU.mult, ALU.mult,
               R=[yB, rB, self.vecB[vc]], W=[w["hnB"]])
        self.E(k.dve, "tensor_tensor", xt, xt, w["hn"][:, :], ALU.add, R=[xB, w["hnB"]], W=[xB])
        k.dma(k.sp, xdst[tt * 128:(tt + 1) * 128, :], xt, R=[xB], W=[self.db(xdname, tt)])

    def proj(self, hTi, wsrc, blocks, fF=None, fT=None, pF=(0, 1, 6), pT=(2, 3, 7)):
        k = self.k
        nc = self.nc
        with ExitStack() as es:
            hb = [es.enter_context(self.sbt("phT%d" % i, [128, KC, 512], BF16)) for i in range(2)]
            hB = [Buf() for _ in range(2)]
            cnt = 0
            for tb in range(T // 512):
                j = tb % 2
                k.dma(k.sp, hb[j][:, :, :], self.hT[hTi][:, :, tb * 512:(tb + 1) * 512],
                      R=[self.db("hT%d" % hTi, 4 * tb + s) for s in range(4)], W=[hB[j]])
                for (col0, ncols, mode, tag) in blocks:
                    wt, wB = self.wload_bf(wsrc[0], wsrc[1], col0 // 512, ncols=ncols)
                    if mode == "F":
                        for jj in range(ncols // 128):
                            pi = pF[cnt % len(pF)]
                            cnt += 1
                            for kk in range(KC):
                                self.E(k.pe, "matmul", self.ps[pi][:, :], wt[:, kk, jj * 128:(jj + 1) * 128], hb[j][:, kk, :],
                                       start=(kk == 0), stop=(kk == KC - 1), R=[wB, hB[j]], W=[self.psB[pi]])
                            fF(tag, jj, tb, pi)
                    else:
                        for s in range(4):
                            pi = pT[cnt % len(pT)]
                            cnt += 1
                            for kk in range(KC):
                                self.E(k.pe, "matmul", self.ps[pi][:, 0:ncols], hb[j][:, kk, s * 128:(s + 1) * 128], wt[:, kk, 0:ncols],
                                       start=(kk == 0), stop=(kk == KC - 1), R=[wB, hB[j]], W=[self.psB[pi]])
                            fT(tag, s, tb, pi)
            k.fence()

    def nsa_proj(self, l):
        k = self.k
        nc = self.nc
        with ExitStack() as es:
            cs = [es.enter_context(self.sbt("cs%d" % i, [128, 2, 512], F32)) for i in range(2)]
            csB = [Buf() for _ in range(2)]
            raw = [es.enter_context(self.sbt("raw%d" % i, [128, 512], BF16)) for i in range(2)]
            rawB = [Buf() for _ in range(2)]
            t1 = [es.enter_context(self.sbt("t1_%d" % i, [128, 512], F32)) for i in range(2)]
            t1B = [Buf() for _ in range(2)]
            t2 = [es.enter_context(self.sbt("t2_%d" % i, [128, 512], F32)) for i in range(2)]
            t2B = [Buf() for _ in range(2)]
            ob = [es.enter_context(self.sbt("ob%d" % i, [128, 512], BF16)) for i in range(3)]
            obB = [Buf() for _ in range(3)]
            gb = es.enter_context(self.sbt("gb", [128, 48], F32))
            gbB = Buf()
            gt = es.enter_context(self.sbt("gt", [128, 48], F32))
            gtB = Buf()
            k.dma(k.sp, gb[:, :], self.bcast_row(self.a_gate_b, l * 48, 48), W=[gbB])
            st = {"n": 0, "tb": -1}
            featdst = {4: self.kcT, 5: self.vcT, 6: self.ksT, 8: self.kwT}
            featname = {4: "kcT", 5: "vcT", 6: "ksT", 8: "kwT"}

            def fF(tag, jj, tb, pi):
                n = st["n"]
                st["n"] += 1
                i2 = n % 2
                i3 = n % 3
                if st["tb"] != tb:
                    st["tb"] = tb
                    k.dma(k.sp, cs[tb % 2][:, :, :], self.c_cossin[:, :, tb * 512:(tb + 1) * 512], W=[csB[tb % 2]])
                c_ = cs[tb % 2]
                cB = csB[tb % 2]
                P = self.ps[pi]
                PB = self.psB[pi]
                if tag == 5:
                    self.E(k.act, "copy", ob[i3][:, :], P[:, :], R=[PB], W=[obB[i3]])
                else:
                    self.E(k.act, "copy", raw[i2][:, :], P[:, :], R=[PB], W=[rawB[i2]])
                    p2 = 4 + i2
                    self.E(k.pe, "matmul", self.ps[p2][:, :], self.cst[:, 1, :], raw[i2][:, :], start=True, stop=True,
                           R=[self.cstB, rawB[i2]], W=[self.psB[p2]])
                    self.E(k.dve, "tensor_tensor", t1[i2][:, :], P[:, :], c_[:, 0, :], ALU.mult, R=[PB, cB], W=[t1B[i2]])
                    self.E(k.dve, "tensor_tensor", t2[i2][:, :], self.ps[p2][:, :], c_[:, 1, :], ALU.mult, R=[self.psB[p2], cB], W=[t2B[i2]])
                    self.E(k.pool, "tensor_tensor", ob[i3][:, :], t1[i2][:, :], t2[i2][:, :], ALU.add, R=[t1B[i2], t2B[i2]], W=[obB[i3]])
                if tag < 4:
                    dst = self.qT[tag, :, 4 * tb:4 * tb + 4, jj, :]
                    k.dma(k.sp, dst, ob[i3][:, :].rearrange("p (a t) -> p a t", a=4), R=[obB[i3]], W=[self.db("qT", (tag, tb))])
                else:
                    k.dma(k.sp, featdst[tag][jj, :, tb * 512:(tb + 1) * 512], ob[i3][:, :], R=[obB[i3]], W=[self.db(featname[tag], (jj, tb))])

            def fT(tag, s, tb, pi):
                n = st["n"]
                st["n"] += 1
                i3 = n % 3
                P = self.ps[pi]
                PB = self.psB[pi]
                tt = 4 * tb + s
                if tag == 10:
                    self.E(k.dve, "tensor_tensor", gt[:, :], P[:, 0:48], gb[:, :], ALU.add, R=[PB, gbB], W=[gtB])
                    self.E(k.act, "activation", self.gate_sb[:, tt, :], gt[:, :], AF.Sigmoid, R=[gtB], W=[self.gateB])
                else:
                    dst = self.vs if tag == 7 else self.vw
                    self.E(k.act, "copy", ob[i3][:, :], P[:, :], R=[PB], W=[obB[i3]])
                    k.dma(k.sp, dst[tt * 128:(tt + 1) * 128, :], ob[i3][:, :], R=[obB[i3]], W=[self.db("vs" if tag == 7 else "vw", tt)])

            blocks = [(wb * 512, 512, "F", wb) for wb in (0, 1, 2, 3, 4, 5, 6, 8)]
            blocks += [(7 * 512, 512, "T", 7), (9 * 512, 512, "T", 9), (5120, 48, "T", 10)]
            import os
            if os.environ.get("TB"):
                keep = [int(x) for x in os.environ["TB"].split(",")]
                blocks = [b for b in blocks if b[3] in keep]
            self.proj(0, (self.wsc_p, "wsc_p"), blocks, fF, fT)

    def nsa_compress(self, l, es_outer):
        k = self.k
        nc = self.nc
        kcmpT = es_outer.enter_context(self.sbt("kcmpT", [128, G, 256], BF16))
        kcB = Buf()
        vext = es_outer.enter_context(self.sbt("vext", [128, G, 2, 193], BF16))
        vxB = Buf()
        self.E(k.pool, "memset", kcmpT[:, :, :], 0.0, W=[kcB])
        self.E(k.pool, "memset", vext[:, :, :, :], 0.0, W=[vxB])
        for g in range(G):
            k.dma(k.sp, vext[:, g, :, 128:193], self.c_vext[:, :, :], W=[vxB])
        with ExitStack() as es:
            src = [es.enter_context(self.sbt("csrc%d" % i, [128, T], BF16)) for i in range(2)]
            srcB = [Buf() for _ in range(2)]
            w1 = es.enter_context(self.sbt("cw1", [128, 32, 256], BF16))
            w1B = Buf()
            w2 = es.enter_context(self.sbt("cw2", [128, 2, 128], BF16))
            w2B = Buf()
            pef = es.enter_context(self.sbt("pef", [128, 32], F32))
            pefB = Buf()
            perep = es.enter_context(self.sbt("perep", [128, 32, 128], BF16))
            peB = Buf()
            xs = es.enter_context(self.sbt("gxs", [128, 256], F32))
            xsB = Buf()
            u = es.enter_context(self.sbt("gu", [128, 256], F32))
            uB = Buf()
            sg = es.enter_context(self.sbt("gsg", [128, 256], F32))
            sgB = Buf()
            hid = es.enter_context(self.sbt("ghid", [128, 256], BF16))
            hidB = Buf()
            hidT = es.enter_context(self.sbt("ghidT", [128, 2, 128], BF16))
            hidTB = Buf()
            n = 0
            for kv in range(2):
                k.dma(k.pool, w1[:, :, :], self.a_cmp_w1.ap()[l, kv].rearrange("(c p) n -> p c n", p=128), W=[w1B])
                k.dma(k.pool, w2[:, :, :], self.a_cmp_w2.ap()[l, kv].rearrange("(c p) n -> p c n", p=128), W=[w2B])
                k.dma(k.sp, pef[:, :], self.a_cmp_peT.ap()[l, kv], W=[pefB])
                self.E(k.dve, "tensor_copy", perep[:, :, :], pef[:, :].unsqueeze(2).to_broadcast([128, 32, 128]), R=[pefB], W=[peB])
                srcd = self.kcT if kv == 0 else self.vcT
                sname = "kcT" if kv == 0 else "vcT"
                for g in range(G):
                    sj = n % 2
                    n += 1
                    k.dma(k.sp, src[sj][:, :], srcd[g, :, :], R=[self.db(sname, (g, tb)) for tb in range(8)], W=[srcB[sj]])
                    for half in range(2):
                        nr = 128 if half == 0 else 127
                        n0 = half * 128
                        pi = 0
                        P = self.ps[pi]
                        PB = self.psB[pi]
                        for li in range(32):
                            self.E(k.pe, "matmul", P[0:nr, 0:256], perep[:, li, 0:nr], w1[:, li, :], start=(li == 0), stop=False,
                                   R=[peB, w1B], W=[PB])
                        for li in range(32):
                            a0 = li + 16 * n0
                            self.E(k.pe, "matmul", P[0:nr, 0:256], src[sj][:, a0:a0 + 16 * (nr - 1) + 1:16], w1[:, li, :], start=False, stop=(li == 31),
                                   R=[srcB[sj], w1B], W=[PB])
                        self.E(k.act, "copy", xs[0:nr, :], P[0:nr, 0:256], R=[PB], W=[xsB])
                        self.E(k.dve, "tensor_tensor", u[0:nr, :], xs[0:nr, :], xs[0:nr, :], ALU.mult, R=[xsB], W=[uB])
                        self.E(k.dve, "tensor_scalar", u[0:nr, :], u[0:nr, :], 0.044715, 1.0, ALU.mult, ALU.add, R=[uB], W=[uB])
                        self.E(k.dve, "tensor_tensor", u[0:nr, :], u[0:nr, :], xs[0:nr, :], ALU.mult, R=[uB, xsB], W=[uB])
                        self.E(k.act, "activation", sg[0:nr, :], u[0:nr, :], AF.Sigmoid, scale=1.5957691216057308, R=[uB], W=[sgB])
                        if nr < 128:
                            self.E(k.pool, "memset", hid[:, :], 0.0, W=[hidB])
                        self.E(k.dve, "tensor_tensor", hid[0:nr, :], xs[0:nr, :], sg[0:nr, :], ALU.mult, R=[xsB, sgB], W=[hidB])
                        pb = self.psbf(1)
                        for c in range(2):
                            self.E(k.pe, "transpose", pb[:, c * 128:(c + 1) * 128], hid[:, c * 128:(c + 1) * 128], self.ident(),
                                   R=[hidB, self.cstB], W=[self.psB[1]])
                        self.E(k.act, "copy", hidT[:, :, :], pb[:, 0:256].rearrange("p (c n) -> p c n", c=2), R=[self.psB[1]], W=[hidTB])
                        P2 = self.ps[2]
                        if kv == 0:
                            for c in range(2):
                                self.E(k.pe, "matmul", P2[:, 0:nr], w2[:, c, :], hidT[:, c, 0:nr], start=(c == 0), stop=(c == 1),
                                       R=[w2B, hidTB], W=[self.psB[2]])
                            self.E(k.act, "copy", kcmpT[:, g, n0:n0 + nr], P2[:, 0:nr], R=[self.psB[2]], W=[kcB])
                        else:
                            for c in range(2):
                                self.E(k.pe, "matmul", P2[0:nr, 0:128], hidT[:, c, 0:nr], w2[:, c, :], start=(c == 0), stop=(c == 1),
                                       R=[w2B, hidTB], W=[self.psB[2]])
                            self.E(k.act, "copy", vext[0:nr, g, half, 0:128], P2[0:nr, 0:128], R=[self.psB[2]], W=[vxB])
            k.fence()
        return kcmpT, kcB, vext, vxB

    def attn_pair(self, kT, kB, q, qB, vx, vxB_, nv, mask, mB, acc, first, last, pt, ptB, spi, nrows=128, mask_eng=None):
        k = self.k
        P = self.ps[spi]
        PB = self.psB[spi]
        self.E(k.pe, "matmul", P[0:nrows, :], kT, q, start=True, stop=True, R=[kB, qB], W=[PB])

        def rest():
            self.E(k.act, "activation", pt[0:nrows, :], P[0:nrows, :], AF.Exp, scale=SCALE, R=[PB], W=[ptB])
            if mask is not None:
                self.E(mask_eng or k.dve, "tensor_tensor", pt[0:nrows, :].rearrange("p (r t) -> p r t", r=4), pt[0:nrows, :].rearrange("p (r t) -> p r t", r=4),
                       mask.unsqueeze(1).to_broadcast([nrows, 4, 128]), ALU.mult, R=[ptB, mB], W=[ptB])
            for r in range(4):
                bk = acc[r // 2]
                off = (r % 2) * 256
                self.E(k.pe, "matmul", self.ps[bk][:, off:off + nv], pt[0:nrows, r * 128:(r + 1) * 128], vx,
                       start=(first and r % 2 == 0), stop=last, skip_group_check=True,
                       R=[ptB, vxB_], W=[self.psB[bk]])

        prev = getattr(self, "_pend", None)
        self._pend = rest
        if prev is not None:
            prev()

    def attn_flush(self):
        prev = getattr(self, "_pend", None)
        self._pend = None
        if prev is not None:
            prev()

    def nsa_attn(self, l):
        k = self.k
        nc = self.nc
        with ExitStack() as es0:
            kcmpT, kcB, vext, vxB = self.nsa_compress(l, es0)
            with ExitStack() as es:
                sb = lambda name, shape, dtp: es.enter_context(self.sbt(name, shape, dtp))
                maskc_r = [sb("maskc%d" % i, [128, 2, 128], BF16) for i in range(2)]
                mcB_r = [Buf() for _ in range(2)]
                selm_r = [sb("selm%d" % i, [128, 2, 64], F32) for i in range(2)]
                smB_r = [Buf() for _ in range(2)]
                qg = sb("qg", [128, NT, 512], BF16)
                qgB = Buf()
                ksg = sb("ksg", [128, T], BF16)
                ksB = Buf()
                kwg = sb("kwg", [128, T], BF16)
                kwB = Buf()
                vsg = sb("vsg", [128, NT, 129], BF16)
                vsB = Buf()
                vwg = sb("vwg", [128, NT, 129], BF16)
                vwB = Buf()
                self.E(k.pool, "memset", vsg[:, :, 128:129], 1.0, W=[vsB])
                self.E(k.pool, "memset", vwg[:, :, 128:129], 1.0, W=[vwB])
                pt = [sb("pt%d" % i, [128, 512], BF16) for i in range(3)]
                ptB = [Buf() for _ in range(3)]
                msk = [sb("msk%d" % i, [128, NT, 128], BF16) for i in range(2)]
                mskB = [Buf() for _ in range(2)]
                ot = [sb("ot%d" % i, [128, 512], F32) for i in range(2)]
                otB = [Buf() for _ in range(2)]
                otb = [sb("otb%d" % i, [128, 512], BF16) for i in range(2)]
                otbB = [Buf() for _ in range(2)]
                st = [sb("ast%d" % i, [128, 16], F32) for i in range(2)]
                stB = [Buf() for _ in range(2)]
                imp = sb("imp", [128, 64], F32)
                impB = Buf()
                vv = sb("vv", [128, 64], F32)
                vvB = Buf()
                m8 = sb("m8", [128, 16], F32)
                m8B = Buf()
                cnt = sb("cnt", [128, 64], F32)
                cntB = Buf()
                selb = sb("selb", [128, 64], BF16)
                selbB = Buf()
                selT_t = [sb("selTt%d" % i, [64, 128], BF16) for i in range(2)]
                selTB = [Buf() for _ in range(2)]
                w = {"hTt": [sb("ohTt%d" % i, [128, 4, 128], BF16) for i in range(2)], "hTtB": [Buf() for _ in range(2)]}
                accsets = ((4, 5), (6, 7))
                npair = 0
                nbr = [0]

                def next_acc():
                    a = accsets[nbr[0] % 2]
                    nbr[0] += 1
                    return a

                def accv(acc, r, c0, c1):
                    return self.ps[acc[r // 2]][:, (r % 2) * 256 + c0:(r % 2) * 256 + c1], self.psB[acc[r // 2]]

                def finish_branch(tt, g, gi, acc, stt, sB_, o, oB, prev=None, prevB=None):
                    for r in range(4):
                        A, AB = accv(acc, r, 128, 129)
                        if gi == 0:
                            self.E(k.dve, "tensor_scalar", stt[:, r:r + 1], A, 1e-30, None, ALU.max, R=[AB], W=[sB_])
                        else:
                            self.E(k.dve, "reciprocal", stt[:, 4 + r:5 + r], A, R=[AB], W=[sB_])
                    if gi == 0:
                        self.E(k.dve, "reciprocal", stt[:, 4:8], stt[:, 0:4], R=[sB_], W=[sB_])
                    g0 = (g * 4) * 3 + gi
                    self.E(k.dve, "tensor_tensor", stt[:, 8:12], stt[:, 4:8], self.gate_sb[:, tt, g0:g0 + 10:3], ALU.mult,
                           R=[sB_, self.gateB], W=[sB_])
                    for r in range(4):
                        A, AB = accv(acc, r, 0, 128)
                        if prev is None:
                            self.E(k.dve, "tensor_scalar", o[:, r * 128:(r + 1) * 128], A, stt[:, 8 + r:9 + r], None, ALU.mult,
                                   R=[AB, sB_], W=[oB])
                        else:
                            self.E(k.dve, "scalar_tensor_tensor", o[:, r * 128:(r + 1) * 128], A, stt[:, 8 + r:9 + r],
                                   prev[:, r * 128:(r + 1) * 128], ALU.mult, ALU.add, R=[AB, sB_, prevB], W=[oB])

                for g in range(G):
                    k.dma(k.sp, qg[:, :, :], self.qT[g].rearrange("p a r t -> p a (r t)"), R=[self.db("qT", (g, tb)) for tb in range(8)], W=[qgB])
                    k.dma(k.sp, ksg[:, :], self.ksT[g, :, :], R=[self.db("ksT", (g, tb)) for tb in range(8)], W=[ksB])
                    k.dma(k.sp, kwg[:, :], self.kwT[g, :, :], R=[self.db("kwT", (g, tb)) for tb in range(8)], W=[kwB])
                    k.dma(k.sp, vsg[:, :, 0:128], self.vs[:, g * 128:(g + 1) * 128].rearrange("(j p) c -> p j c", p=128),
                          R=[self.db("vs", tt) for tt in range(NT)], W=[vsB])
                    k.dma(k.sp, vwg[:, :, 0:128], self.vw[:, g * 128:(g + 1) * 128].rearrange("(j p) c -> p j c", p=128),
                          R=[self.db("vw", tt) for tt in range(NT)], W=[vwB])
                    for tt in range(NT):
                        q = qg[:, tt, :]
                        maskc = maskc_r[tt % 2]
                        mcB = mcB_r[tt % 2]
                        selm = selm_r[tt % 2]
                        smB = smB_r[tt % 2]
                        k.dma(k.sp, maskc[:, :, :], self.c_maskc[:, :, tt * 128:(tt + 1) * 128], W=[mcB])
                        k.dma(k.sp, selm[:, :, :], self.c_selm[:, :, tt, :], W=[smB])
                        halves = [0] + ([1] if tt >= 16 else [])
                        acc = next_acc()
                        for hi, half in enumerate(halves):
                            need_mask = not (half == 0 and tt >= 17)
                            m = maskc[:, half, :] if need_mask else None
                            pi = npair % 3
                            self.attn_pair(kcmpT[:, g, half * 128:(half + 1) * 128], kcB, q, qgB, vext[:, g, half, :], vxB, 193,
                                           m, mcB, acc, hi == 0, hi == len(halves) - 1, pt[pi], ptB[pi], npair % 3, mask_eng=k.pool)
                            npair += 1
                        j = tt % 2
                        stt = st[j]
                        sB_ = stB[j]
                        self.attn_flush()
                        finish_branch(tt, g, 0, acc, stt, sB_, ot[j], otB[j])
                        for r in range(4):
                            A, AB = accv(acc, r, 129, 193)
                            if r == 0:
                                self.E(k.dve, "tensor_scalar", imp[:, :], A, stt[:, 4:5], None, ALU.mult, R=[AB, sB_], W=[impB])
                            else:
                                self.E(k.dve, "scalar_tensor_tensor", imp[:, :], A, stt[:, 4 + r:5 + r], imp[:, :],
                                       ALU.mult, ALU.add, R=[AB, sB_, impB], W=[impB])
                        k.dma(k.sp, self.opart[tt * 128:(tt + 1) * 128, g * 512:(g + 1) * 512], ot[j][:, :], R=[otB[j]], W=[self.db("opart", (g, tt))])
                        self.E(k.dve, "tensor_tensor", vv[:, :], imp[:, :], selm[:, 0, :], ALU.mult, R=[impB, smB], W=[vvB])
                        self.E(k.dve, "tensor_tensor", vv[:, :], vv[:, :], selm[:, 1, :], ALU.add, R=[vvB, smB], W=[vvB])
                        self.E(k.dve, "max", m8[:, 0:8], vv[:, :], R=[vvB], W=[m8B])
                        self.E(k.dve, "match_replace", cnt[:, :], m8[:, 0:8], vv[:, :], -3.0e38, R=[vvB, m8B], W=[cntB])
                        self.E(k.dve, "max", m8[:, 8:16], cnt[:, :], R=[cntB], W=[m8B])
                        self.E(k.dve, "tensor_scalar", selb[:, :], vv[:, :], m8[:, 15:16], None, ALU.is_ge, R=[vvB, m8B], W=[selbB])
                        pb = self.psbf(3)
                        self.E(k.pe, "transpose", pb[0:64, 0:128], selb[:, :], self.ident(), R=[selbB, self.cstB], W=[self.psB[3]])
                        self.E(k.act, "copy", selT_t[j][:, :], pb[0:64, 0:128], R=[self.psB[3]], W=[selTB[j]])
                        k.dma(k.sp, self.selT[g, :, tt * 128:(tt + 1) * 128], selT_t[j][:, :], R=[selTB[j]], W=[self.db("selT", (g, tt))])

                    def issue_loads(tt):
                        j = tt % 2
                        mk = msk[j]
                        mkB = mskB[j]
                        for b in range(2):
                            srcap = bass.AP(self.selT, g * 64 * T + b * T + tt * 128, [[0, 64], [2 * T, tt + 1], [1, 128]])
                            k.dma(k.sp, mk[b * 64:(b + 1) * 64, 0:tt + 1, :], srcap, R=[self.db("selT", (g, tt))], W=[mkB])
                        self.E(k.pool, "tensor_tensor", mk[:, tt, :], mk[:, tt, :], self.cst[:, 4, :], ALU.mult, R=[mkB, self.cstB], W=[mkB])
                        k.dma(k.sp, ot[j][:, :], self.opart[tt * 128:(tt + 1) * 128, g * 512:(g + 1) * 512], R=[self.db("opart", (g, tt))], W=[otB[j]])

                    issue_loads(0)
                    for tt in range(NT):
                        q = qg[:, tt, :]
                        j = tt % 2
                        mk = msk[j]
                        mkB = mskB[j]
                        stt = st[j]
                        sB_ = stB[j]
                        js = list(range(max(0, tt - 4), tt + 1))
                        accw = next_acc()
                        for ji, jk in enumerate(js):
                            m = None
                            if jk == tt:
                                m = self.cst[:, 4, :]
                            elif jk == tt - 4:
                                m = self.cst[:, 5, :]
                            pi = npair % 3
                            self.attn_pair(kwg[:, jk * 128:(jk + 1) * 128], kwB, q, qgB, vwg[:, jk, :], vwB, 129,
                                           m, self.cstB, accw, ji == 0, ji == len(js) - 1, pt[pi], ptB[pi], npair % 3)
                            npair += 1
                        accs = next_acc()
                        for jk in range(tt + 1):
                            pi = npair % 3
                            self.attn_pair(ksg[:, jk * 128:(jk + 1) * 128], ksB, q, qgB, vsg[:, jk, :], vsB, 129,
                                           mk[:, jk, :], mkB, accs, jk == 0, jk == tt, pt[pi], ptB[pi], npair % 3)
                            npair += 1
                            if jk == 0:
                                finish_branch(tt, g, 2, accw, stt, sB_, ot[j], otB[j], ot[j], otB[j])
                                if tt + 1 < NT:
                                    issue_loads(tt + 1)
                        self.attn_flush()
                        finish_branch(tt, g, 1, accs, stt, sB_, ot[j], otB[j], ot[j], otB[j])
                        self.E(k.act, "copy", otb[j][:, :], ot[j][:, :], R=[otB[j]], W=[otbB[j]])
                        self.transpose_to_hT(w, otb[j], otbB[j], 2, tt, pbanks=(3,), nch=4, ch0=g * 4)
                k.fence()

    def outproj(self, xsrc, xsname, xdst, xdname, vc, va, vb, hTdst):
        k = self.k
        nc = self.nc
        with ExitStack() as es:
            w = self.alloc_norm_tiles(es)
            ysb = [es.enter_context(self.sbt("ysb%d" % i, [128, D], F32)) for i in range(4)]
            yB = [Buf() for _ in range(4)]
            xt = [es.enter_context(self.sbt("xt%d" % i, [128, D], F32)) for i in range(2)]
            xB = [Buf() for _ in range(2)]
            hb = [es.enter_context(self.sbt("phT%d" % i, [128, KC, 512], BF16)) for i in range(2)]
            hB = [Buf() for _ in range(2)]
            cnt = 0
            for tb in range(T // 512):
                j = tb % 2
                k.dma(k.sp, hb[j][:, :, :], self.hT[2][:, :, tb * 512:(tb + 1) * 512],
                      R=[self.db("hT2", 4 * tb + s) for s in range(4)], W=[hB[j]])
                for cb in range(4):
                    wt, wB = self.wload_bf(self.wsc_o, "wsc_o", cb)
                    for s in range(4):
                        pi = 4 + cnt % 4
                        cnt += 1
                        for kk in range(KC):
                            self.E(k.pe, "matmul", self.ps[pi][:, :], hb[j][:, kk, s * 128:(s + 1) * 128], wt[:, kk, :],
                                   start=(kk == 0), stop=(kk == KC - 1), R=[wB, hB[j]], W=[self.psB[pi]])
                        self.E(k.act, "copy", ysb[s][:, cb * 512:(cb + 1) * 512], self.ps[pi][:, :], R=[self.psB[pi]], W=[yB[s]])
                for s in range(4):
                    tt = 4 * tb + s
                    self.residual_tile(w, ysb[s][:, :], yB[s], xt[s % 2][:, :], xB[s % 2], xsrc, xsname, xdst, xdname, vc, tt)
                    if hTdst is not None:
                        self.prenorm_tile(w, xt[s % 2][:, :], xB[s % 2], va, vb, hTdst, tt)
            k.fence()

    def ffn(self, l, xsrc, xsname, xdst, xdname, vc, va, vb, hTdst):
        k = self.k
        nc = self.nc
        Win = self.ffn_w_in.ap()[l]
        Wout = self.ffn_w_out.ap()[l]
        NF = DFF // 128
        with ExitStack() as es:
            w = self.alloc_norm_tiles(es)
            ysb = [es.enter_context(self.sbt("ysb%d" % i, [128, D], F32)) for i in range(4)]
            yB = [Buf() for _ in range(4)]
            xt = [es.enter_context(self.sbt("xt%d" % i, [128, D], F32)) for i in range(1)]
            xB = [Buf() for _ in range(1)]
            hb = [es.enter_context(self.sbt("phT%d" % i, [128, KC, 512], BF16)) for i in range(1)]
            hB = [Buf() for _ in range(1)]
            actT = es.enter_context(self.sbt("actT", [128, NF, 512], BF16))
            aB = [Buf() for _ in range(11)]
            sg = [es.enter_context(self.sbt("fsg%d" % i, [128, 512], F32)) for i in range(2)]
            sgB = [Buf() for _ in range(2)]
            n = 0
            for tb in range(T // 512):
                j = 0
                k.dma(k.sp, hb[j][:, :, :], self.hT[1][:, :, tb * 512:(tb + 1) * 512],
                      R=[self.db("hT1", 4 * tb + s) for s in range(4)], W=[hB[j]])
                for fg in range(11):
                    wg, wgB = self.wload_bf(self.wsc_in, "wsc_in", 2 * fg)
                    wu, wuB = self.wload_bf(self.wsc_in, "wsc_in", 2 * fg + 1)
                    for jj in range(4):
                        i2 = n % 2
                        n += 1
                        pg = i2
                        pu = 2 + i2
                        for kk in range(KC):
                            self.E(k.pe, "matmul", self.ps[pg][:, :], wg[:, kk, jj * 128:(jj + 1) * 128], hb[j][:, kk, :],
                                   start=(kk == 0), stop=(kk == KC - 1), R=[wgB, hB[j]], W=[self.psB[pg]])
                        for kk in range(KC):
                            self.E(k.pe, "matmul", self.ps[pu][:, :], wu[:, kk, jj * 128:(jj + 1) * 128], hb[j][:, kk, :],
                                   start=(kk == 0), stop=(kk == KC - 1), R=[wuB, hB[j]], W=[self.psB[pu]])
                        self.E(k.act, "activation", sg[i2][:, :], self.ps[pg][:, :], AF.Silu, R=[self.psB[pg]], W=[sgB[i2]])
                        self.E(k.dve, "tensor_tensor", actT[:, fg * 4 + jj, :], sg[i2][:, :], self.ps[pu][:, :], ALU.mult,
                               R=[sgB[i2], self.psB[pu]], W=[aB[fg]])
                for cb in range(4):
                    for kg in range(4):
                        wt, wB = self.wload_bf(self.wsc_out, "wsc_out", cb * 4 + kg, kc=11)
                        for s in range(4):
                            pi = 4 + s
                            for kk in range(11):
                                ch = kg * 11 + kk
                                self.E(k.pe, "matmul", self.ps[pi][:, :], actT[:, ch, s * 128:(s + 1) * 128], wt[:, kk, :],
                                       start=(ch == 0), stop=(ch == NF - 1), R=[wB, aB[ch // 4]], W=[self.psB[pi]])
                    for s in range(4):
                        self.E(k.act, "copy", ysb[s][:, cb * 512:(cb + 1) * 512], self.ps[4 + s][:, :], R=[self.psB[4 + s]], W=[yB[s]])
                for s in range(4):
                    tt = 4 * tb + s
                    self.residual_tile(w, ysb[s][:, :], yB[s], xt[0][:, :], xB[0], xsrc, xsname, xdst, xdname, vc, tt)
                    if hTdst is not None:
                        self.prenorm_tile(w, xt[0][:, :], xB[0], va, vb, hTdst, tt, pbanks=(0, 1))
            k.fence()

    def sb_kv(self, xsrc, xname):
        k = self.k
        self.modvec(1, self.kv_mod_w.ap(), 0, self.kv_mod_b, 0)
        self.modvec(0, self.kv_mod_w.ap(), D, self.kv_mod_b, D)
        self.load_gain(2, self.kv_norm_g, 0)
        self.E(k.dve, "scalar_tensor_tensor", self.vec[0][:, :], self.vec[0][:, :], 1.0, self.vec[2][:, :], ALU.add, ALU.mult,
               R=[self.vecB[0], self.vecB[2]], W=[self.vecB[0]])
        self.norm_phase(xsrc, xname, 0, 1, 2)
        nc = self.nc
        with ExitStack() as es:
            ob = [es.enter_context(self.sbt("ob%d" % i, [128, 512], BF16)) for i in range(3)]
            obB = [Buf() for _ in range(3)]
            st = {"n": 0}

            def fF(tag, jj, tb, pi):
                i3 = st["n"] % 3
                st["n"] += 1
                self.E(k.act, "copy", ob[i3][:, :], self.ps[pi][:, :], R=[self.psB[pi]], W=[obB[i3]])
                k.dma(k.sp, self.kTsh[tag * 4 + jj, :, tb * 512:(tb + 1) * 512], ob[i3][:, :], R=[obB[i3]], W=[self.db("kTsh", (tag * 4 + jj, tb))])

            def fT(tag, s, tb, pi):
                i3 = st["n"] % 3
                st["n"] += 1
                tt = 4 * tb + s
                self.E(k.dve, "tensor_copy", ob[i3][:, :], self.ps[pi][:, :], R=[self.psB[pi]], W=[obB[i3]])
                k.dma(k.sp, self.vsh[tt * 128:(tt + 1) * 128, tag * 512:(tag + 1) * 512], ob[i3][:, :], R=[obB[i3]], W=[self.db("vsh", (tag, tt))])

            blocks = [(c * 512, 512, "F", c) for c in range(4)] + [(D + c * 512, 512, "T", c) for c in range(4)]
            self.proj(2, (self.wsc_kv, "wsc_kv"), blocks, fF, fT)

    def sb_q(self, l2):
        k = self.k
        nc = self.nc
        with ExitStack() as es:
            ob = [es.enter_context(self.sbt("ob%d" % i, [128, 512], BF16)) for i in range(3)]
            obB = [Buf() for _ in range(3)]
            st = {"n": 0}

            def fF(tag, jj, tb, pi):
                i3 = st["n"] % 3
                st["n"] += 1
                self.E(k.act, "copy", ob[i3][:, :], self.ps[pi][:, :], R=[self.psB[pi]], W=[obB[i3]])
                k.dma(k.sp, self.qsb[tag * 4 + jj, :, tb * 512:(tb + 1) * 512], ob[i3][:, :], R=[obB[i3]], W=[self.db("qsb", (tag * 4 + jj, tb))])

            blocks = [(c * 512, 512, "F", c) for c in range(4)]
            self.proj(0, (self.wsc_p, "wsc_p"), blocks, fF, None)

    def sb_attn(self):
        k = self.k
        nc = self.nc
        with ExitStack() as es:
            sb = lambda name, shape, dtp: es.enter_context(self.sbt(name, shape, dtp))
            mb = sb("msbb", [128, 4, 512], BF16)
            mbB = Buf()
            k.dma(k.sp, mb[:, :, :], self.c_msb_b[:, :, :], W=[mbB])
            kT = [sb("skT%d" % i, [128, T], BF16) for i in range(2)]
            kTB = [Buf() for _ in range(2)]
            qT = [sb("sqT%d" % i, [128, T], BF16) for i in range(2)]
            qTB = [Buf() for _ in range(2)]
            vh = [sb("svh%d" % i, [128, NT, 128], BF16) for i in range(2)]
            vhB = [Buf() for _ in range(2)]
            NR = 3
            ee = [sb("see%d" % i, [128, 512], F32) for i in range(NR)]
            eeB = [Buf() for _ in range(NR)]
            spb = [sb("sspb%d" % i, [128, 512], BF16) for i in range(NR)]
            spbB = [Buf() for _ in range(NR)]
            t3 = [sb("st3_%d" % i, [128, 512], F32) for i in range(NR)]
            t3B = [Buf() for _ in range(NR)]
            pt = [sb("spt%d" % i, [128, 512], BF16) for i in range(NR)]
            ptB = [Buf() for _ in range(NR)]
            carry = sb("scarry", [128, 512], F32)
            cB = Buf()
            ob = [sb("sob%d" % i, [128, 512], BF16) for i in range(2)]
            obB = [Buf() for _ in range(2)]
            st = {"n": 0, "nq": 0}

            def stageA(p):
                Pz = self.ps[p["zi"]]
                self.E(k.pe, "matmul", Pz[:, :], p["kT"], p["q"], start=True, stop=True, R=[p["kB"], p["qB"]], W=[self.psB[p["zi"]]])

            def stageB(p):
                i = p["i"]
                Pz = self.ps[p["zi"]]
                PzB = self.psB[p["zi"]]
                self.E(k.act, "activation", ee[i][:, :], Pz[:, :], AF.Exp, scale=SCALE, R=[PzB], W=[eeB[i]])
                self.E(k.act, "activation", spb[i][:, :], ee[i][:, :], AF.Ln, bias=1.0, R=[eeB[i]], W=[spbB[i]])
                if p["a"] >= 0:
                    self.E(k.pool, "tensor_tensor", spb[i][:, :], spb[i][:, :], mb[:, p["a"], :], ALU.mult, R=[spbB[i], mbB], W=[spbB[i]])
                self.E(k.pe, "matmul", Pz[:, :], self.cst[:, 6, :], spb[i][:, :], start=False, stop=True, skip_group_check=True,
                       R=[self.cstB, spbB[i]], W=[PzB])
                if not p["last"]:
                    ti = p["ti"]
                    self.E(k.pe, "matmul", self.ps[ti][:, :], self.cst[:, 2, :], spb[i][:, :], start=True, stop=True,
                           R=[self.cstB, spbB[i]], W=[self.psB[ti]])

            def stageC(p):
                i = p["i"]
                Pz = self.ps[p["zi"]]
                PzB = self.psB[p["zi"]]
                if p["first"]:
                    self.E(k.act, "activation", pt[i][:, :], Pz[:, :], AF.Exp, scale=SCALE, R=[PzB], W=[ptB[i]])
                else:
                    self.E(k.dve, "scalar_tensor_tensor", t3[i][:, :], Pz[:, :], SCALE, carry[:, :], ALU.mult, ALU.subtract,
                           R=[PzB, cB], W=[t3B[i]])
                    self.E(k.act, "activation", pt[i][:, :], t3[i][:, :], AF.Exp, R=[t3B[i]], W=[ptB[i]])
                if p["a"] >= 0:
                    self.E(k.pool, "tensor_tensor", pt[i][:, :], pt[i][:, :], mb[:, p["a"], :], ALU.mult, R=[ptB[i], mbB], W=[ptB[i]])
                self.E(k.pe, "matmul", self.ps[p["acc"]][:, :], p["v"], pt[i][:, :], start=p["first"], stop=p["last"],
                       R=[p["vB"], ptB[i]], W=[self.psB[p["acc"]]])
                if not p["last"]:
                    ti = p["ti"]
                    if p["first"]:
                        self.E(k.dve, "tensor_copy", carry[:, :], self.ps[ti][:, :], R=[self.psB[ti]], W=[cB])
                    else:
                        self.E(k.dve, "tensor_tensor", carry[:, :], carry[:, :], self.ps[ti][:, :], ALU.add, R=[cB, self.psB[ti]], W=[cB])
                if p["last"]:
                    oj = st["nq"] % 2
                    st["nq"] += 1
                    self.E(k.act, "copy", ob[oj][:, :], self.ps[p["acc"]][:, :], R=[self.psB[p["acc"]]], W=[obB[oj]])
                    k.dma(k.sp, self.hT[2][:, p["h"], p["qb"] * 512:(p["qb"] + 1) * 512], ob[oj][:, :], R=[obB[oj]],
                          W=[self.db("hT2", 4 * p["qb"] + s) for s in range(4)])

            pairs = []
            for h in range(NH):
                hj = h % 2
                for qb in range(T // 512):
                    jtop = 4 * qb + 3
                    for jk in range(jtop, -1, -1):
                        pairs.append(dict(h=h, hj=hj, qb=qb, jk=jk, a=jk - 4 * qb, first=(jk == jtop), last=(jk == 0)))
            loaded = set()
            nacc = 0
            for n, p in enumerate(pairs):
                p["i"] = n % NR
                p["zi"] = n % 3
                p["ti"] = 3 + n % 2
                if p["first"]:
                    nacc += 1
                p["acc"] = 6 + nacc % 2
            def ensure_loaded(h):
                if h in loaded or h >= NH:
                    return
                loaded.add(h)
                hj = h % 2
                k.dma(k.sp, kT[hj][:, :], self.kTsh[h, :, :], R=[self.db("kTsh", (h, tb)) for tb in range(8)], W=[kTB[hj]])
                k.dma(k.sp, qT[hj][:, :], self.qsb[h, :, :], R=[self.db("qsb", (h, tb)) for tb in range(8)], W=[qTB[hj]])
                k.dma(k.sp, vh[hj][:, :, :], self.vsh[:, h * 128:(h + 1) * 128].rearrange("(j p) c -> p j c", p=128),
                      R=[self.db("vsh", (h // 4, tt)) for tt in range(NT)], W=[vhB[hj]])
            def prep(p):
                ensure_loaded(p["h"])
                hj = p["hj"]
                jk = p["jk"]
                p["kT"] = kT[hj][:, jk * 128:(jk + 1) * 128]
                p["kB"] = kTB[hj]
                p["q"] = qT[hj][:, p["qb"] * 512:(p["qb"] + 1) * 512]
                p["qB"] = qTB[hj]
                p["v"] = vh[hj][:, jk, :]
                p["vB"] = vhB[hj]
            N = len(pairs)
            for n in range(N + 2):
                if n < N:
                    prep(pairs[n])
                    stageA(pairs[n])
                if 0 <= n - 1 < N:
                    stageB(pairs[n - 1])
                if 0 <= n - 2 < N:
                    stageC(pairs[n - 2])
            k.fence()

    def build(self):
        k = self.k
        self.setup()
        xin, xin_name = self.x.ap(), "x"
        self.precast_sq(self.wsc_p, "wsc_p", self.a_w_in.ap()[0], NAIN)
        for l in range(self.nlayers):
            last = (l == self.nlayers - 1)
            if l == 0:
                self.mod_AB(l, 0, 0, 0, 1, 2)
                self.norm_phase(xin, xin_name, 0, 1, 0)
                if self.stop == "norm0":
                    return self.nc
            if l == 2:
                self.sb_kv(xin, xin_name)
            self.precast_sq(self.wsc_o, "wsc_o", (self.a_w_out.ap()[l] if l < 2 else self.b_w_out.ap()[l - 2]), D)
            self.precast_ffn(l)
            if l < 2:
                with ExitStack() as esg:
                    self.gate_sb = esg.enter_context(self.sbt("gate_sb", [128, NT, 48], F32))
                    self.nsa_proj(l)
                    if self.stop == "nsa_proj":
                        return self.nc
                    self.nsa_attn(l)
                    k.fence()
                if self.stop == "nsa_attn":
                    return self.nc
                wout = self.a_w_out.ap()[l]
            else:
                self.sb_q(l - 2)
                self.sb_attn()
                wout = self.b_w_out.ap()[l - 2]
            self.mod_C(l, 0, 1, 2, 3)
            self.mod_AB(l, 1, 2, 0, 1, 3)
            self.outproj(xin, xin_name, self.xs1.ap(), "xs1", 2, 0, 1, 1)
            if self.stop == "outproj%d" % l:
                return self.nc
            self.mod_C(l, 1, 3, 2, 3)
            if not last:
                self.mod_AB(l + 1, 0, 0, 0, 1, 3)
            xdst, xdname = (self.out.ap(), "out") if last else (self.xs2.ap(), "xs2")
            if not last:
                if l + 1 < 2:
                    self.precast_sq(self.wsc_p, "wsc_p", self.a_w_in.ap()[l + 1], NAIN)
                else:
                    self.precast_sq(self.wsc_p, "wsc_p", self.b_w_q.ap()[l + 1 - 2], D)
                if l + 1 == 2:
                    self.precast_sq(self.wsc_kv, "wsc_kv", self.kv_w.ap(), 2 * D)
            self.ffn(l, self.xs1.ap(), "xs1", xdst, xdname, 2, 0, 1, None if last else 0)
            xin, xin_name = xdst, xdname
        k.fence()
        return self.nc


_CONSTS = None


def make_in_maps(inputs, ncores=8):
    global _CONSTS
    if _CONSTS is None:
        _CONSTS = make_consts()
    f = lambda a: np.ascontiguousarray(np.asarray(a, dtype=np.float32))
    shared = {
        "mod_w": f(inputs["mod_w"]), "mod_b": f(inputs["mod_b"]), "norm_g": f(inputs["norm_g"]),
        "ffn_w_in": f(inputs["ffn_w_in"]), "ffn_w_out": f(inputs["ffn_w_out"]), "a_w_in": f(inputs["a_w_in"]),
        "a_gate_b": f(inputs["a_gate_b"]),
        "a_cmp_peT": np.ascontiguousarray(np.asarray(inputs["a_cmp_pe"], dtype=np.float32).transpose(0, 1, 3, 2)),
        "a_cmp_w1": f(inputs["a_cmp_w1"]), "a_cmp_w2": f(inputs["a_cmp_w2"]), "a_w_out": f(inputs["a_w_out"]),
        "b_w_q": f(inputs["b_w_q"]), "b_w_out": f(inputs["b_w_out"]),
        "kv_norm_g": f(inputs["kv_norm_g"]).reshape(1, D), "kv_mod_w": f(inputs["kv_mod_w"]),
        "kv_mod_b": f(inputs["kv_mod_b"]).reshape(1, 2 * D), "kv_w": f(inputs["kv_w"]),
    }
    shared.update(_CONSTS)
    x = np.asarray(inputs["x"], dtype=np.float32)
    c = np.asarray(inputs["c"], dtype=np.float32)
    maps = []
    for core in range(ncores):
        b = core % 4
        m = dict(shared)
        m["x"] = np.ascontiguousarray(x[b])
        m["cT"] = np.ascontiguousarray(c[b].reshape(KC, 128).T)
        maps.append(m)
    return maps


def kernel(**inputs):
    prog = Prog()
    nc = prog.build()
    maps = make_in_maps(inputs)
    res = run_bass_kernel_spmd(nc, maps, core_ids=list(range(8)))
    out = np.stack([np.asarray(res.results[b]["out"], dtype=np.float32) for b in range(4)], axis=0)
    return out
```

```python
from contextlib import ExitStack
import numpy as np
import ml_dtypes
import concourse.bass as bass
import concourse.mybir as mybir
from concourse.bass_utils import run_bass_kernel_spmd

F32 = mybir.dt.float32
BF16 = mybir.dt.bfloat16
AF = mybir.ActivationFunctionType
ALU = mybir.AluOpType
AX = mybir.AxisListType

SAME_ENGINE_SYNC = True

D = 2048
T = 4096
NT = T // 128
KC = 16
DFF = 5632
NH = 16
G = 4
EPS = 1e-6
SCALE = 128 ** -0.5
NCMP = 255
NAIN = 5168


class Buf:
    __slots__ = ("name", "w", "r", "ps")

    def __init__(self, name="", ps=False):
        self.name = name
        self.w = None
        self.r = []
        self.ps = ps


class Eng:
    def __init__(self, nc, e, name, pe=False, ndma=0):
        self.e = e
        self.name = name
        self.sem = nc.alloc_semaphore("s_" + name)
        self.cnt = 0
        self.seen = {}
        self.pe = pe
        self.dsems = [[nc.alloc_semaphore("d_%s%d" % (name, i)), 0] for i in range(ndma)]
        self.dnext = 0


class K:
    def __init__(self):
        nc = bass.Bass("TRN2", target_bir_lowering=False)
        self.nc = nc
        self.pe = Eng(nc, nc.tensor, "pe", pe=True)
        self.act = Eng(nc, nc.scalar, "act")
        self.dve = Eng(nc, nc.vector, "dve")
        self.pool = Eng(nc, nc.gpsimd, "pool", ndma=8)
        self.sp = Eng(nc, nc.sync, "sp", ndma=16)
        self.engs = [self.pe, self.act, self.dve, self.pool, self.sp]
        self.nins = 0
        self.nwait = 0

    def _wait(self, eng, tok):
        sem, val = tok
        if sem is eng.sem and (eng.pe or not SAME_ENGINE_SYNC):
            return
        key = id(sem)
        if eng.seen.get(key, 0) >= val:
            return
        eng.e.wait_ge(sem, val)
        eng.seen[key] = val
        self.nwait += 1

    def _deps(self, eng, R, W):
        for b in R:
            if b.w is not None:
                self._wait(eng, b.w)
        for b in W:
            if b.w is not None:
                self._wait(eng, b.w)
            for t in b.r:
                self._wait(eng, t)

    def _commit(self, tok, R, W):
        for b in W:
            b.w = tok
            b.r = []
        for b in R:
            b.r.append(tok)
            if len(b.r) > 16:
                d = {}
                for s, v in b.r:
                    kk = id(s)
                    if kk not in d or d[kk][1] < v:
                        d[kk] = (s, v)
                b.r = list(d.values())

    def op(self, eng, fn, R=(), W=()):
        px = [b for b in R if b.ps and b not in W]
        if px:
            W = list(W) + px
            R = [b for b in R if not b.ps]
        self._deps(eng, R, W)
        ins = fn(eng.e)
        eng.cnt += 1
        ins.then_inc(eng.sem, 1)
        self._commit((eng.sem, eng.cnt), R, W)
        self.nins += 1
        return ins

    def dma(self, q, out, in_, R=(), W=(), **kw):
        self._deps(q, R, W)
        slot = q.dsems[q.dnext]
        q.dnext = (q.dnext + 1) % len(q.dsems)
        if slot[1] > 0:
            self._wait(q, (slot[0], slot[1]))
        ins = q.e.dma_start(out=out, in_=in_, **kw)
        slot[1] += 16
        ins.then_inc(slot[0], 16)
        self._commit((slot[0], slot[1]), R, W)
        self.nins += 1
        return ins

    def fence(self):
        for e in self.engs:
            for o in self.engs:
                if o is not e and o.cnt > 0:
                    self._wait(e, (o.sem, o.cnt))
                for s in o.dsems:
                    if s[1] > 0:
                        self._wait(e, (s[0], s[1]))


def _bf(a):
    return np.ascontiguousarray(a).astype(ml_dtypes.bfloat16)


def make_consts():
    c = {}
    p = np.arange(128)
    ident = np.eye(128, dtype=np.float32)
    rot = np.zeros((128, 128), np.float32)
    for dp in range(64):
        rot[dp + 64, dp] = -1.0
        rot[dp, dp + 64] = 1.0
    ones = np.ones((128, 128), np.float32)
    ltri = (p[:, None] > p[None, :]).astype(np.float32)
    causal = (p[:, None] <= p[None, :]).astype(np.float32)
    anti = (p[:, None] > p[None, :]).astype(np.float32)
    trin = -(ltri + ident) / np.float32(SCALE)
    c["cst_bf"] = _bf(np.stack([ident, rot, ones, ltri, causal, anti, trin], axis=1))
    t = np.arange(T, dtype=np.float64)
    inv = 10000.0 ** (-np.arange(64, dtype=np.float64) / 64)
    ang = (t[None, :].astype(np.float32) * inv.astype(np.float32)[:, None]).astype(np.float32)
    cs = np.stack([np.concatenate([np.cos(ang), np.cos(ang)], 0), np.concatenate([np.sin(ang), np.sin(ang)], 0)], axis=1)
    c["cossin"] = np.ascontiguousarray(cs.astype(np.float32))
    n = np.arange(256)
    cmp_end = n * 16 + 31
    mc = ((cmp_end[:, None] <= np.arange(T)[None, :]) & (n[:, None] < NCMP)).astype(np.float32)
    c["maskc"] = _bf(mc.reshape(2, 128, T).transpose(1, 0, 2))
    c_start = n[:, None] * 16
    s_start = np.arange(64)[None, :] * 64
    ov = ((c_start < s_start + 64) & (c_start + 32 > s_start) & (n[:, None] < NCMP)).astype(np.float32)
    ext = np.zeros((256, 65), np.float32)
    ext[:NCMP, 0] = 1.0
    ext[:, 1:] = ov
    c["vext"] = _bf(ext.reshape(2, 128, 65).transpose(1, 0, 2))
    tt = np.arange(T)
    blk = np.arange(64)
    cur = tt // 64
    forced = (blk[None, :] == 0) | (blk[None, :] == cur[:, None]) | (blk[None, :] == cur[:, None] - 1)
    avail = blk[None, :] * 64 <= tt[:, None]
    availm = (avail & ~forced).astype(np.float32)
    forcem = np.where(forced, 1e9 * (1.0 + (blk[None, :] == 0) + 2.0 * (blk[None, :] == cur[:, None])), np.where(avail, 0.0, -1e9)).astype(np.float32)
    c["selm"] = np.ascontiguousarray(np.stack([availm, forcem], 0).reshape(2, NT, 128, 64).transpose(2, 0, 1, 3))
    kk = np.arange(128)[:, None]
    qq = np.arange(512)[None, :]
    msb = np.stack([((a * 128 + kk) < qq) for a in range(4)], axis=1).astype(np.float32)
    c["msb_f"] = np.ascontiguousarray(msb)
    c["msb_b"] = _bf(msb)
    return c


class Prog:
    def __init__(self, nlayers=4, dbg=(), stop=None):
        self.stop = stop
        self.k = K()
        k = self.k
        nc = k.nc
        self.nc = nc
        self.dbg = set(dbg)
        self.nlayers = nlayers
        dt = nc.dram_tensor
        self.inp = {}

        def din(name, shape, dtype=F32):
            self.inp[name] = dt(name, list(shape), dtype, kind="ExternalInput")
            return self.inp[name]

        self.x = din("x", [T, D])
        self.cT = din("cT", [128, KC])
        self.mod_w = din("mod_w", [4, D, 6 * D])
        self.mod_b = din("mod_b", [4, 6 * D])
        self.norm_g = din("norm_g", [4, 4, D])
        self.ffn_w_in = din("ffn_w_in", [4, D, 2 * DFF])
        self.ffn_w_out = din("ffn_w_out", [4, DFF, D])
        self.a_w_in = din("a_w_in", [2, D, NAIN])
        self.a_gate_b = din("a_gate_b", [2, 48])
        self.a_cmp_peT = din("a_cmp_peT", [2, 2, 128, 32])
        self.a_cmp_w1 = din("a_cmp_w1", [2, 2, 4096, 256])
        self.a_cmp_w2 = din("a_cmp_w2", [2, 2, 256, 128])
        self.a_w_out = din("a_w_out", [2, D, D])
        self.b_w_q = din("b_w_q", [2, D, D])
        self.b_w_out = din("b_w_out", [2, D, D])
        self.kv_norm_g = din("kv_norm_g", [1, D])
        self.kv_mod_w = din("kv_mod_w", [D, 2 * D])
        self.kv_mod_b = din("kv_mod_b", [1, 2 * D])
        self.kv_w = din("kv_w", [D, 2 * D])
        self.c_cst = din("cst_bf", [128, 7, 128], BF16)
        self.c_cossin = din("cossin", [128, 2, T])
        self.c_maskc = din("maskc", [128, 2, T], BF16)
        self.c_vext = din("vext", [128, 2, 65], BF16)
        self.c_selm = din("selm", [128, 2, NT, 64])
        self.c_msb_f = din("msb_f", [128, 4, 512])
        self.c_msb_b = din("msb_b", [128, 4, 512], BF16)
        self.out = dt("out", [T, D], F32, kind="ExternalOutput")

        def scr(name, shape, dtype):
            kind = "ExternalOutput" if name in self.dbg else "Internal"
            return dt(name, list(shape), dtype, kind=kind)

        self.xs1 = scr("xs1", [T, D], F32)
        self.xs2 = scr("xs2", [T, D], F32)
        self.hT = [scr("hT%d" % i, [128, KC, T], BF16) for i in range(3)]
        self.qT = scr("qT", [G, 128, NT, 4, 128], BF16)
        self.kcT = scr("kcT", [G, 128, T], BF16)
        self.vcT = scr("vcT", [G, 128, T], BF16)
        self.ksT = scr("ksT", [G, 128, T], BF16)
        self.kwT = scr("kwT", [G, 128, T], BF16)
        self.vs = scr("vs", [T, 512], BF16)
        self.vw = scr("vw", [T, 512], BF16)
        self.opart = scr("opart", [T, D], F32)
        self.selT = scr("selT", [G, 64, T], BF16)
        self.kTsh = scr("kTsh", [NH, 128, T], BF16)
        self.vsh = scr("vsh", [T, D], BF16)
        self.qsb = scr("qsb", [NH, 128, T], BF16)
        self.wsc_in = scr("wsc_in", [22, 128, KC * 512], BF16)
        self.wsc_out = scr("wsc_out", [16, 128, 11 * 512], BF16)
        self.wsc_p = scr("wsc_p", [11, 128, KC * 512], BF16)
        self.wsc_o = scr("wsc_o", [4, 128, KC * 512], BF16)
        self.wsc_kv = scr("wsc_kv", [8, 128, KC * 512], BF16)
        self.dbufs = {}

        self.ps = [nc.alloc_psum_tensor("ps%d" % i, [128, 512], F32) for i in range(8)]
        self.psB = [Buf("ps%d" % i, ps=True) for i in range(8)]
        self.cst = nc.alloc_sbuf_tensor("cst", [128, 7, 128], BF16)
        self.cstB = Buf("cst")
        self.crep = nc.alloc_sbuf_tensor("crep", [128, KC, 128], BF16)
        self.crepB = Buf("crep")
        self.NW = 3
        self.wt = [nc.alloc_sbuf_tensor("wt%d" % i, [128, KC, 512], BF16) for i in range(self.NW)]
        self.wB = [Buf("wt%d" % i) for i in range(self.NW)]
        self.wi = 0
        self.vec = [nc.alloc_sbuf_tensor("vec%d" % i, [128, D], F32) for i in range(4)]
        self.vecB = [Buf("vec%d" % i) for i in range(4)]
        self.gate_sb = None
        self.gateB = Buf("gate")
        self.uid = 0

    def sbt(self, name, shape, dtype):
        self.uid += 1
        return self.nc.sbuf_tensor("%s_u%d" % (name, self.uid), shape, dtype)

    def db(self, name, i=0):
        key = (name, i)
        b = self.dbufs.get(key)
        if b is None:
            b = self.dbufs[key] = Buf("%s_%s" % key)
        return b

    def E(self, eng, meth, *a, R=(), W=(), **kw):
        return self.k.op(eng, lambda e: getattr(e, meth)(*a, **kw), R, W)

    def ident(self):
        return self.cst[:, 0, :]

    def bcast_row(self, handle, off, n):
        return bass.AP(handle, off, [[0, 128], [1, n]])

    def wload(self, src2d, kc=16, ncols=512):
        i = self.wi
        self.wi = (i + 1) % self.NW
        self.k.dma(self.k.pool, self.wt[i][:, 0:kc, 0:ncols], src2d.rearrange("(k p) n -> p k n", p=128), W=[self.wB[i]])
        return self.wt[i], self.wB[i]

    def precast(self, scr_t, name, tile, src2d, kc=KC, ncols=512):
        self.k.dma(self.k.pool, scr_t[tile].rearrange("p (k n) -> p k n", k=kc)[:, :, 0:ncols],
                   src2d.rearrange("(k p) n -> p k n", p=128), W=[self.db(name, tile)])

    def wload_bf(self, scr_t, name, tile, kc=KC, ncols=512):
        i = self.wi
        self.wi = (i + 1) % self.NW
        self.k.dma(self.k.sp, self.wt[i][:, 0:kc, 0:ncols], scr_t[tile].rearrange("p (k n) -> p k n", k=kc)[:, :, 0:ncols],
                   R=[self.db(name, tile)], W=[self.wB[i]])
        return self.wt[i], self.wB[i]

    def precast_ffn(self, l):
        Win = self.ffn_w_in.ap()[l]
        Wout = self.ffn_w_out.ap()[l]
        for fg in range(11):
            self.precast(self.wsc_in, "wsc_in", 2 * fg, Win[:, fg * 512:(fg + 1) * 512])
            self.precast(self.wsc_in, "wsc_in", 2 * fg + 1, Win[:, DFF + fg * 512: DFF + (fg + 1) * 512])
        for cb in range(4):
            for kg in range(4):
                self.precast(self.wsc_out, "wsc_out", cb * 4 + kg, Wout[kg * 11 * 128:(kg + 1) * 11 * 128, cb * 512:(cb + 1) * 512], kc=11)

    def precast_sq(self, scr_t, name, w2d, ncols_total):
        nb = (ncols_total + 511) // 512
        for b in range(nb):
            nc_ = min(512, ncols_total - b * 512)
            self.precast(scr_t, name, b, w2d[:, b * 512:b * 512 + nc_], ncols=nc_)

    def psbf(self, i):
        return self.ps[i][:, :].bitcast(BF16)

    def setup(self):
        k = self.k
        nc = self.nc
        k.dma(k.sp, self.cst[:, :, :], self.c_cst[:, :, :], W=[self.cstB])
        with ExitStack() as es:
            cf = es.enter_context(self.sbt("cf", [128, KC], F32))
            ca = es.enter_context(self.sbt("ca", [128, KC], F32))
            cfB = Buf()
            caB = Buf()
            k.dma(k.sp, cf[:, :], self.cT[:, :], W=[cfB])
            self.E(k.act, "activation", ca[:, :], cf[:, :], AF.Silu, R=[cfB], W=[caB])
            self.E(k.dve, "tensor_copy", self.crep[:, :, :], ca[:, :].unsqueeze(2).to_broadcast([128, KC, 128]), R=[caB], W=[self.crepB])
            k.fence()

    def modvec(self, vi, w2d, col0, bias_handle, bias_off):
        k = self.k
        dst = self.vec[vi]
        dB = self.vecB[vi]
        k.dma(k.sp, dst[:, :], self.bcast_row(bias_handle, bias_off, D), W=[dB])
        for cb in range(4):
            wt, wB = self.wload(w2d[:, col0 + cb * 512: col0 + (cb + 1) * 512])
            pi = cb % 2
            for kk in range(KC):
                self.E(k.pe, "matmul", self.ps[pi][:, :], self.crep[:, kk, :], wt[:, kk, :], start=(kk == 0), stop=(kk == KC - 1),
                       R=[self.crepB, wB], W=[self.psB[pi]])
            self.E(k.dve, "tensor_tensor", dst[:, cb * 512:(cb + 1) * 512], self.ps[pi][:, :], dst[:, cb * 512:(cb + 1) * 512], ALU.add,
                   R=[self.psB[pi], dB], W=[dB])

    def load_gain(self, vi, handle, off):
        self.k.dma(self.k.sp, self.vec[vi][:, :], self.bcast_row(handle, off, D), W=[self.vecB[vi]])

    def mod_AB(self, l, which, gi, va, vb, vtmp):
        k = self.k
        w2d = self.mod_w.ap()[l]
        base = 0 if which == 0 else 3 * D
        self.modvec(vb, w2d, base, self.mod_b, l * 6 * D + base)
        self.modvec(va, w2d, base + D, self.mod_b, l * 6 * D + base + D)
        self.load_gain(vtmp, self.norm_g, (l * 4 + gi) * D)
        self.E(k.dve, "scalar_tensor_tensor", self.vec[va][:, :], self.vec[va][:, :], 1.0, self.vec[vtmp][:, :], ALU.add, ALU.mult,
               R=[self.vecB[va], self.vecB[vtmp]], W=[self.vecB[va]])

    def mod_C(self, l, which, gi, vc, vtmp):
        k = self.k
        w2d = self.mod_w.ap()[l]
        base = 2 * D if which == 0 else 5 * D
        self.modvec(vc, w2d, base, self.mod_b, l * 6 * D + base)
        self.load_gain(vtmp, self.norm_g, (l * 4 + gi) * D)
        self.E(k.dve, "tensor_tensor", self.vec[vc][:, :], self.vec[vc][:, :], self.vec[vtmp][:, :], ALU.mult,
               R=[self.vecB[vc], self.vecB[vtmp]], W=[self.vecB[vc]])

    def alloc_norm_tiles(self, es):
        nc = self.nc
        w = {}
        w["ss"] = [es.enter_context(self.sbt("ss%d" % i, [128, 8], F32)) for i in range(2)]
        w["ssB"] = [Buf() for _ in range(2)]
        w["hn"] = es.enter_context(self.sbt("hn", [128, D], F32))
        w["hnB"] = Buf()
        w["junk"] = w["hn"]
        w["junkB"] = w["hnB"]
        w["hb"] = [es.enter_context(self.sbt("hb%d" % i, [128, D], BF16)) for i in range(1)]
        w["hbB"] = [Buf() for _ in range(1)]
        w["hTt"] = [es.enter_context(self.sbt("hTt%d" % i, [128, KC, 128], BF16)) for i in range(1)]
        w["hTtB"] = [Buf() for _ in range(1)]
        w["i"] = 0
        return w

    def rstd(self, w, src, srcB, n):
        k = self.k
        i = w["i"] % 2
        w["i"] += 1
        ss = w["ss"][i]
        sB = w["ssB"][i]
        self.E(k.pool, "memset", ss[:, 0:1], 0.0, W=[sB])
        self.E(k.act, "activation", w["junk"][:, 0:n], src, AF.Square, accum_out=ss[:, 0:1], R=[srcB], W=[w["junkB"], sB])
        self.E(k.dve, "tensor_scalar", ss[:, 1:2], ss[:, 0:1], 1.0 / n, EPS, ALU.mult, ALU.add, R=[sB], W=[sB])
        self.E(k.act, "activation", ss[:, 2:3], ss[:, 1:2], AF.Sqrt, R=[sB], W=[sB])
        self.E(k.dve, "reciprocal", ss[:, 3:4], ss[:, 2:3], R=[sB], W=[sB])
        return ss[:, 3:4], sB

    def prenorm_tile(self, w, xt, xB, va, vb, hTi, tt, pbanks=(0, 1)):
        k = self.k
        r, rB = self.rstd(w, xt, xB, D)
        j = tt % len(w["hb"])
        self.E(k.dve, "scalar_tensor_tensor", w["hn"][:, :], xt, r, self.vec[va][:, :], ALU.mult, ALU.mult,
               R=[xB, rB, self.vecB[va]], W=[w["hnB"]])
        self.E(k.pool, "tensor_tensor", w["hb"][j][:, :], w["hn"][:, :], self.vec[vb][:, :], ALU.add,
               R=[w["hnB"], self.vecB[vb]], W=[w["hbB"][j]])
        self.transpose_to_hT(w, w["hb"][j], w["hbB"][j], hTi, tt, pbanks)

    def transpose_to_hT(self, w, src, srcB, hTi, tt, pbanks=(0, 1), nch=KC, ch0=0):
        k = self.k
        j = tt % len(w["hTt"])
        hTt = w["hTt"][j]
        hB = w["hTtB"][j]
        for half in range((nch + 7) // 8):
            pi = pbanks[half % len(pbanks)]
            n8 = min(8, nch - half * 8)
            pb = self.psbf(pi)
            for c in range(n8):
                cc = half * 8 + c
                self.E(k.pe, "transpose", pb[:, c * 128:(c + 1) * 128], src[:, cc * 128:(cc + 1) * 128], self.ident(),
                       R=[srcB, self.cstB], W=[self.psB[pi]])
            eng = k.act if half % 2 == 0 else k.dve
            meth = "copy" if eng is k.act else "tensor_copy"
            self.E(eng, meth, hTt[:, half * 8: half * 8 + n8, :], pb[:, 0:n8 * 128].rearrange("p (k n) -> p k n", k=n8),
                   R=[self.psB[pi]], W=[hB])
        k.dma(k.pool, self.hT[hTi][:, ch0:ch0 + nch, tt * 128:(tt + 1) * 128], hTt[:, 0:nch, :], R=[hB], W=[self.db("hT%d" % hTi, tt)])

    def norm_phase(self, xsrc, xname, va, vb, hTi):
        k = self.k
        nc = self.nc
        with ExitStack() as es:
            w = self.alloc_norm_tiles(es)
            xt = [es.enter_context(self.sbt("xt%d" % i, [128, D], F32)) for i in range(2)]
            xB = [Buf() for _ in range(2)]
            for tt in range(NT):
                j = tt % 2
                k.dma(k.sp, xt[j][:, :], xsrc[tt * 128:(tt + 1) * 128, :], R=[self.db(xname, tt)], W=[xB[j]])
                self.prenorm_tile(w, xt[j][:, :], xB[j], va, vb, hTi, tt)
            k.fence()

    def residual_tile(self, w, y, yB, xt, xB, xsrc, xsname, xdst, xdname, vc, tt):
        k = self.k
        k.dma(k.pool, xt, xsrc[tt * 128:(tt + 1) * 128, :], R=[self.db(xsname, tt)], W=[xB])
        r, rB = self.rstd(w, y, yB, D)
        self.E(k.dve, "scalar_tensor_tensor", w["hn"][:, :], y, r, self.vec[vc][:, :], ALU.mult, ALU.mult,
               R=[yB, rB, self.vecB[vc]], W=[w["hnB"]])
        self.E(k.dve, "tensor_tensor", xt, xt, w["hn"][:, :], ALU.add, R=[xB, w["hnB"]], W=[xB])
        k.dma(k.pool, xdst[tt * 128:(tt + 1) * 128, :], xt, R=[xB], W=[self.db(xdname, tt)])

    def proj(self, hTi, wsrc, blocks, fF=None, fT=None, pF=(0, 1, 6), pT=(2, 3, 7)):
        k = self.k
        nc = self.nc
        with ExitStack() as es:
            hb = [es.enter_context(self.sbt("phT%d" % i, [128, KC, 512], BF16)) for i in range(2)]
            hB = [Buf() for _ in range(2)]
            cnt = 0
            for tb in range(T // 512):
                j = tb % 2
                k.dma(k.sp, hb[j][:, :, :], self.hT[hTi][:, :, tb * 512:(tb + 1) * 512],
                      R=[self.db("hT%d" % hTi, 4 * tb + s) for s in range(4)], W=[hB[j]])
                for (col0, ncols, mode, tag) in blocks:
                    wt, wB = self.wload_bf(wsrc[0], wsrc[1], col0 // 512, ncols=ncols)
                    if mode == "F":
                        for jj in range(ncols // 128):
                            pi = pF[cnt % len(pF)]
                            cnt += 1
                            for kk in range(KC):
                                self.E(k.pe, "matmul", self.ps[pi][:, :], wt[:, kk, jj * 128:(jj + 1) * 128], hb[j][:, kk, :],
                                       start=(kk == 0), stop=(kk == KC - 1), R=[wB, hB[j]], W=[self.psB[pi]])
                            fF(tag, jj, tb, pi)
                    else:
                        for s in range(4):
                            pi = pT[cnt % len(pT)]
                            cnt += 1
                            for kk in range(KC):
                                self.E(k.pe, "matmul", self.ps[pi][:, 0:ncols], hb[j][:, kk, s * 128:(s + 1) * 128], wt[:, kk, 0:ncols],
                                       start=(kk == 0), stop=(kk == KC - 1), R=[wB, hB[j]], W=[self.psB[pi]])
                            fT(tag, s, tb, pi)
            k.fence()

    def nsa_proj(self, l):
        k = self.k
        nc = self.nc
        with ExitStack() as es:
            cs = [es.enter_context(self.sbt("cs%d" % i, [128, 2, 512], F32)) for i in range(2)]
            csB = [Buf() for _ in range(2)]
            raw = [es.enter_context(self.sbt("raw%d" % i, [128, 512], BF16)) for i in range(2)]
            rawB = [Buf() for _ in range(2)]
            t1 = [es.enter_context(self.sbt("t1_%d" % i, [128, 512], F32)) for i in range(2)]
            t1B = [Buf() for _ in range(2)]
            t2 = [es.enter_context(self.sbt("t2_%d" % i, [128, 512], F32)) for i in range(2)]
            t2B = [Buf() for _ in range(2)]
            ob = [es.enter_context(self.sbt("ob%d" % i, [128, 512], BF16)) for i in range(3)]
            obB = [Buf() for _ in range(3)]
            gb = es.enter_context(self.sbt("gb", [128, 48], F32))
            gbB = Buf()
            gt = es.enter_context(self.sbt("gt", [128, 48], F32))
            gtB = Buf()
            k.dma(k.sp, gb[:, :], self.bcast_row(self.a_gate_b, l * 48, 48), W=[gbB])
            st = {"n": 0, "tb": -1}
            featdst = {4: self.kcT, 5: self.vcT, 6: self.ksT, 8: self.kwT}
            featname = {4: "kcT", 5: "vcT", 6: "ksT", 8: "kwT"}

            def fF(tag, jj, tb, pi):
                n = st["n"]
                st["n"] += 1
                i2 = n % 2
                i3 = n % 3
                if st["tb"] != tb:
                    st["tb"] = tb
                    k.dma(k.sp, cs[tb % 2][:, :, :], self.c_cossin[:, :, tb * 512:(tb + 1) * 512], W=[csB[tb % 2]])
                c_ = cs[tb % 2]
                cB = csB[tb % 2]
                P = self.ps[pi]
                PB = self.psB[pi]
                if tag == 5:
                    self.E(k.act, "copy", ob[i3][:, :], P[:, :], R=[PB], W=[obB[i3]])
                else:
                    self.E(k.act, "copy", raw[i2][:, :], P[:, :], R=[PB], W=[rawB[i2]])
                    p2 = 4 + i2
                    self.E(k.pe, "matmul", self.ps[p2][:, :], self.cst[:, 1, :], raw[i2][:, :], start=True, stop=True,
                           R=[self.cstB, rawB[i2]], W=[self.psB[p2]])
                    self.E(k.dve, "tensor_tensor", t1[i2][:, :], P[:, :], c_[:, 0, :], ALU.mult, R=[PB, cB], W=[t1B[i2]])
                    self.E(k.dve, "tensor_tensor", t2[i2][:, :], self.ps[p2][:, :], c_[:, 1, :], ALU.mult, R=[self.psB[p2], cB], W=[t2B[i2]])
                    self.E(k.pool, "tensor_tensor", ob[i3][:, :], t1[i2][:, :], t2[i2][:, :], ALU.add, R=[t1B[i2], t2B[i2]], W=[obB[i3]])
                if tag < 4:
                    dst = self.qT[tag, :, 4 * tb:4 * tb + 4, jj, :]
                    k.dma(k.sp, dst, ob[i3][:, :].rearrange("p (a t) -> p a t", a=4), R=[obB[i3]], W=[self.db("qT", (tag, tb))])
                else:
                    k.dma(k.sp, featdst[tag][jj, :, tb * 512:(tb + 1) * 512], ob[i3][:, :], R=[obB[i3]], W=[self.db(featname[tag], (jj, tb))])

            def fT(tag, s, tb, pi):
                n = st["n"]
                st["n"] += 1
                i3 = n % 3
                P = self.ps[pi]
                PB = self.psB[pi]
                tt = 4 * tb + s
                if tag == 10:
                    self.E(k.dve, "tensor_tensor", gt[:, :], P[:, 0:48], gb[:, :], ALU.add, R=[PB, gbB], W=[gtB])
                    self.E(k.act, "activation", self.gate_sb[:, tt, :], gt[:, :], AF.Sigmoid, R=[gtB], W=[self.gateB])
                else:
                    dst = self.vs if tag == 7 else self.vw
                    self.E(k.act, "copy", ob[i3][:, :], P[:, :], R=[PB], W=[obB[i3]])
                    k.dma(k.sp, dst[tt * 128:(tt + 1) * 128, :], ob[i3][:, :], R=[obB[i3]], W=[self.db("vs" if tag == 7 else "vw", tt)])

            blocks = [(wb * 512, 512, "F", wb) for wb in (0, 1, 2, 3, 4, 5, 6, 8)]
            blocks += [(7 * 512, 512, "T", 7), (9 * 512, 512, "T", 9), (5120, 48, "T", 10)]
            import os
            if os.environ.get("TB"):
                keep = [int(x) for x in os.environ["TB"].split(",")]
                blocks = [b for b in blocks if b[3] in keep]
            self.proj(0, (self.wsc_p, "wsc_p"), blocks, fF, fT)

    def nsa_compress(self, l, es_outer):
        k = self.k
        nc = self.nc
        kcmpT = es_outer.enter_context(self.sbt("kcmpT", [128, G, 256], BF16))
        kcB = Buf()
        vext = es_outer.enter_context(self.sbt("vext", [128, G, 2, 193], BF16))
        vxB = Buf()
        self.E(k.pool, "memset", kcmpT[:, :, :], 0.0, W=[kcB])
        self.E(k.pool, "memset", vext[:, :, :, :], 0.0, W=[vxB])
        for g in range(G):
            k.dma(k.sp, vext[:, g, :, 128:193], self.c_vext[:, :, :], W=[vxB])
        with ExitStack() as es:
            src = [es.enter_context(self.sbt("csrc%d" % i, [128, T], BF16)) for i in range(2)]
            srcB = [Buf() for _ in range(2)]
            w1 = es.enter_context(self.sbt("cw1", [128, 32, 256], BF16))
            w1B = Buf()
            w2 = es.enter_context(self.sbt("cw2", [128, 2, 128], BF16))
            w2B = Buf()
            pef = es.enter_context(self.sbt("pef", [128, 32], F32))
            pefB = Buf()
            perep = es.enter_context(self.sbt("perep", [128, 32, 128], BF16))
            peB = Buf()
            xs = es.enter_context(self.sbt("gxs", [128, 256], F32))
            xsB = Buf()
            u = es.enter_context(self.sbt("gu", [128, 256], F32))
            uB = Buf()
            sg = es.enter_context(self.sbt("gsg", [128, 256], F32))
            sgB = Buf()
            hid = es.enter_context(self.sbt("ghid", [128, 256], BF16))
            hidB = Buf()
            hidT = es.enter_context(self.sbt("ghidT", [128, 2, 128], BF16))
            hidTB = Buf()
            n = 0
            for kv in range(2):
                k.dma(k.pool, w1[:, :, :], self.a_cmp_w1.ap()[l, kv].rearrange("(c p) n -> p c n", p=128), W=[w1B])
                k.dma(k.pool, w2[:, :, :], self.a_cmp_w2.ap()[l, kv].rearrange("(c p) n -> p c n", p=128), W=[w2B])
                k.dma(k.sp, pef[:, :], self.a_cmp_peT.ap()[l, kv], W=[pefB])
                self.E(k.dve, "tensor_copy", perep[:, :, :], pef[:, :].unsqueeze(2).to_broadcast([128, 32, 128]), R=[pefB], W=[peB])
                srcd = self.kcT if kv == 0 else self.vcT
                sname = "kcT" if kv == 0 else "vcT"
                for g in range(G):
                    sj = n % 2
                    n += 1
                    k.dma(k.sp, src[sj][:, :], srcd[g, :, :], R=[self.db(sname, (g, tb)) for tb in range(8)], W=[srcB[sj]])
                    for half in range(2):
                        nr = 128 if half == 0 else 127
                        n0 = half * 128
                        pi = 0
                        P = self.ps[pi]
                        PB = self.psB[pi]
                        for li in range(32):
                            self.E(k.pe, "matmul", P[0:nr, 0:256], perep[:, li, 0:nr], w1[:, li, :], start=(li == 0), stop=False,
                                   R=[peB, w1B], W=[PB])
                        for li in range(32):
                            a0 = li + 16 * n0
                            self.E(k.pe, "matmul", P[0:nr, 0:256], src[sj][:, a0:a0 + 16 * (nr - 1) + 1:16], w1[:, li, :], start=False, stop=(li == 31),
                                   R=[srcB[sj], w1B], W=[PB])
                        self.E(k.act, "copy", xs[0:nr, :], P[0:nr, 0:256], R=[PB], W=[xsB])
                        self.E(k.dve, "tensor_tensor", u[0:nr, :], xs[0:nr, :], xs[0:nr, :], ALU.mult, R=[xsB], W=[uB])
                        self.E(k.dve, "tensor_scalar", u[0:nr, :], u[0:nr, :], 0.044715, 1.0, ALU.mult, ALU.add, R=[uB], W=[uB])
                        self.E(k.dve, "tensor_tensor", u[0:nr, :], u[0:nr, :], xs[0:nr, :], ALU.mult, R=[uB, xsB], W=[uB])
                        self.E(k.act, "activation", sg[0:nr, :], u[0:nr, :], AF.Sigmoid, scale=1.5957691216057308, R=[uB], W=[sgB])
                        if nr < 128:
                            self.E(k.pool, "memset", hid[:, :], 0.0, W=[hidB])
                        self.E(k.dve, "tensor_tensor", hid[0:nr, :], xs[0:nr, :], sg[0:nr, :], ALU.mult, R=[xsB, sgB], W=[hidB])
                        pb = self.psbf(1)
                        for c in range(2):
                            self.E(k.pe, "transpose", pb[:, c * 128:(c + 1) * 128], hid[:, c * 128:(c + 1) * 128], self.ident(),
                                   R=[hidB, self.cstB], W=[self.psB[1]])
                        self.E(k.act, "copy", hidT[:, :, :], pb[:, 0:256].rearrange("p (c n) -> p c n", c=2), R=[self.psB[1]], W=[hidTB])
                        P2 = self.ps[2]
                        if kv == 0:
                            for c in range(2):
                                self.E(k.pe, "matmul", P2[:, 0:nr], w2[:, c, :], hidT[:, c, 0:nr], start=(c == 0), stop=(c == 1),
                                       R=[w2B, hidTB], W=[self.psB[2]])
                            self.E(k.act, "copy", kcmpT[:, g, n0:n0 + nr], P2[:, 0:nr], R=[self.psB[2]], W=[kcB])
                        else:
                            for c in range(2):
                                self.E(k.pe, "matmul", P2[0:nr, 0:128], hidT[:, c, 0:nr], w2[:, c, :], start=(c == 0), stop=(c == 1),
                                       R=[w2B, hidTB], W=[self.psB[2]])
                            self.E(k.act, "copy", vext[0:nr, g, half, 0:128], P2[0:nr, 0:128], R=[self.psB[2]], W=[vxB])
            k.fence()
        return kcmpT, kcB, vext, vxB

    def attn_pair(self, kT, kB, q, qB, vx, vxB_, nv, mask, mB, acc, first, last, pt, ptB, spi, nrows=128, mask_eng=None):
        k = self.k
        P = self.ps[spi]
        PB = self.psB[spi]
        self.E(k.pe, "matmul", P[0:nrows, :], kT, q, start=True, stop=True, R=[kB, qB], W=[PB])

        def rest():
            self.E(k.act, "activation", pt[0:nrows, :], P[0:nrows, :], AF.Exp, scale=SCALE, R=[PB], W=[ptB])
            if mask is not None:
                self.E(mask_eng or k.dve, "tensor_tensor", pt[0:nrows, :].rearrange("p (r t) -> p r t", r=4), pt[0:nrows, :].rearrange("p (r t) -> p r t", r=4),
                       mask.unsqueeze(1).to_broadcast([nrows, 4, 128]), ALU.mult, R=[ptB, mB], W=[ptB])
            for r in range(4):
                bk = acc[r // 2]
                off = (r % 2) * 256
                self.E(k.pe, "matmul", self.ps[bk][:, off:off + nv], pt[0:nrows, r * 128:(r + 1) * 128], vx,
                       start=(first and r % 2 == 0), stop=last, skip_group_check=True,
                       R=[ptB, vxB_], W=[self.psB[bk]])

        prev = getattr(self, "_pend", None)
        self._pend = rest
        if prev is not None:
            prev()

    def attn_flush(self):
        prev = getattr(self, "_pend", None)
        self._pend = None
        if prev is not None:
            prev()

    def nsa_attn(self, l):
        k = self.k
        nc = self.nc
        with ExitStack() as es0:
            kcmpT, kcB, vext, vxB = self.nsa_compress(l, es0)
            with ExitStack() as es:
                sb = lambda name, shape, dtp: es.enter_context(self.sbt(name, shape, dtp))
                maskc_r = [sb("maskc%d" % i, [128, 2, 128], BF16) for i in range(2)]
                mcB_r = [Buf() for _ in range(2)]
                selm_r = [sb("selm%d" % i, [128, 2, 64], F32) for i in range(2)]
                smB_r = [Buf() for _ in range(2)]
                qg = sb("qg", [128, NT, 512], BF16)
                qgB = Buf()
                ksg = sb("ksg", [128, T], BF16)
                ksB = Buf()
                kwg = sb("kwg", [128, T], BF16)
                kwB = Buf()
                vsg = sb("vsg", [128, NT, 129], BF16)
                vsB = Buf()
                vwg = sb("vwg", [128, NT, 129], BF16)
                vwB = Buf()
                self.E(k.pool, "memset", vsg[:, :, 128:129], 1.0, W=[vsB])
                self.E(k.pool, "memset", vwg[:, :, 128:129], 1.0, W=[vwB])
                pt = [sb("pt%d" % i, [128, 512], BF16) for i in range(3)]
                ptB = [Buf() for _ in range(3)]
                msk = [sb("msk%d" % i, [128, NT, 128], BF16) for i in range(2)]
                mskB = [Buf() for _ in range(2)]
                ot = [sb("ot%d" % i, [128, 512], F32) for i in range(2)]
                otB = [Buf() for _ in range(2)]
                otb = [sb("otb%d" % i, [128, 512], BF16) for i in range(2)]
                otbB = [Buf() for _ in range(2)]
                st = [sb("ast%d" % i, [128, 16], F32) for i in range(2)]
                stB = [Buf() for _ in range(2)]
                imp = sb("imp", [128, 64], F32)
                impB = Buf()
                vv = sb("vv", [128, 64], F32)
                vvB = Buf()
                m8 = sb("m8", [128, 16], F32)
                m8B = Buf()
                cnt = sb("cnt", [128, 64], F32)
                cntB = Buf()
                selb = sb("selb", [128, 64], BF16)
                selbB = Buf()
                selT_t = [sb("selTt%d" % i, [64, 128], BF16) for i in range(2)]
                selTB = [Buf() for _ in range(2)]
                w = {"hTt": [sb("ohTt%d" % i, [128, 4, 128], BF16) for i in range(2)], "hTtB": [Buf() for _ in range(2)]}
                accsets = ((4, 5), (6, 7))
                npair = 0
                nbr = [0]

                def next_acc():
                    a = accsets[nbr[0] % 2]
                    nbr[0] += 1
                    return a

                def accv(acc, r, c0, c1):
                    return self.ps[acc[r // 2]][:, (r % 2) * 256 + c0:(r % 2) * 256 + c1], self.psB[acc[r // 2]]

                def finish_branch(tt, g, gi, acc, stt, sB_, o, oB, prev=None, prevB=None):
                    for r in range(4):
                        A, AB = accv(acc, r, 128, 129)
                        if gi == 0:
                            self.E(k.dve, "tensor_scalar", stt[:, r:r + 1], A, 1e-30, None, ALU.max, R=[AB], W=[sB_])
                        else:
                            self.E(k.dve, "reciprocal", stt[:, 4 + r:5 + r], A, R=[AB], W=[sB_])
                    if gi == 0:
                        self.E(k.dve, "reciprocal", stt[:, 4:8], stt[:, 0:4], R=[sB_], W=[sB_])
                    g0 = (g * 4) * 3 + gi
                    self.E(k.dve, "tensor_tensor", stt[:, 8:12], stt[:, 4:8], self.gate_sb[:, tt, g0:g0 + 10:3], ALU.mult,
                           R=[sB_, self.gateB], W=[sB_])
                    for r in range(4):
                        A, AB = accv(acc, r, 0, 128)
                        if prev is None:
                            self.E(k.dve, "tensor_scalar", o[:, r * 128:(r + 1) * 128], A, stt[:, 8 + r:9 + r], None, ALU.mult,
                                   R=[AB, sB_], W=[oB])
                        else:
                            self.E(k.dve, "scalar_tensor_tensor", o[:, r * 128:(r + 1) * 128], A, stt[:, 8 + r:9 + r],
                                   prev[:, r * 128:(r + 1) * 128], ALU.mult, ALU.add, R=[AB, sB_, prevB], W=[oB])

                for g in range(G):
                    k.dma(k.sp, qg[:, :, :], self.qT[g].rearrange("p a r t -> p a (r t)"), R=[self.db("qT", (g, tb)) for tb in range(8)], W=[qgB])
                    k.dma(k.sp, ksg[:, :], self.ksT[g, :, :], R=[self.db("ksT", (g, tb)) for tb in range(8)], W=[ksB])
                    k.dma(k.sp, kwg[:, :], self.kwT[g, :, :], R=[self.db("kwT", (g, tb)) for tb in range(8)], W=[kwB])
                    k.dma(k.sp, vsg[:, :, 0:128], self.vs[:, g * 128:(g + 1) * 128].rearrange("(j p) c -> p j c", p=128),
                          R=[self.db("vs", tt) for tt in range(NT)], W=[vsB])
                    k.dma(k.sp, vwg[:, :, 0:128], self.vw[:, g * 128:(g + 1) * 128].rearrange("(j p) c -> p j c", p=128),
                          R=[self.db("vw", tt) for tt in range(NT)], W=[vwB])
                    for tt in range(NT):
                        q = qg[:, tt, :]
                        maskc = maskc_r[tt % 2]
                        mcB = mcB_r[tt % 2]
                        selm = selm_r[tt % 2]
                        smB = smB_r[tt % 2]
                        k.dma(k.sp, maskc[:, :, :], self.c_maskc[:, :, tt * 128:(tt + 1) * 128], W=[mcB])
                        k.dma(k.sp, selm[:, :, :], self.c_selm[:, :, tt, :], W=[smB])
                        halves = [0] + ([1] if tt >= 16 else [])
                        acc = next_acc()
                        for hi, half in enumerate(halves):
                            need_mask = not (half == 0 and tt >= 17)
                            m = maskc[:, half, :] if need_mask else None
                            pi = npair % 3
                            self.attn_pair(kcmpT[:, g, half * 128:(half + 1) * 128], kcB, q, qgB, vext[:, g, half, :], vxB, 193,
                                           m, mcB, acc, hi == 0, hi == len(halves) - 1, pt[pi], ptB[pi], npair % 3, mask_eng=k.pool)
                            npair += 1
                        j = tt % 2
                        stt = st[j]
                        sB_ = stB[j]
                        self.attn_flush()
                        finish_branch(tt, g, 0, acc, stt, sB_, ot[j], otB[j])
                        for r in range(4):
                            A, AB = accv(acc, r, 129, 193)
                            if r == 0:
                                self.E(k.dve, "tensor_scalar", imp[:, :], A, stt[:, 4:5], None, ALU.mult, R=[AB, sB_], W=[impB])
                            else:
                                self.E(k.dve, "scalar_tensor_tensor", imp[:, :], A, stt[:, 4 + r:5 + r], imp[:, :],
                                       ALU.mult, ALU.add, R=[AB, sB_, impB], W=[impB])
                        k.dma(k.sp, self.opart[tt * 128:(tt + 1) * 128, g * 512:(g + 1) * 512], ot[j][:, :], R=[otB[j]], W=[self.db("opart", (g, tt))])
                        self.E(k.dve, "tensor_tensor", vv[:, :], imp[:, :], selm[:, 0, :], ALU.mult, R=[impB, smB], W=[vvB])
                        self.E(k.dve, "tensor_tensor", vv[:, :], vv[:, :], selm[:, 1, :], ALU.add, R=[vvB, smB], W=[vvB])
                        self.E(k.dve, "max", m8[:, 0:8], vv[:, :], R=[vvB], W=[m8B])
                        self.E(k.dve, "match_replace", cnt[:, :], m8[:, 0:8], vv[:, :], -3.0e38, R=[vvB, m8B], W=[cntB])
                        self.E(k.dve, "max", m8[:, 8:16], cnt[:, :], R=[cntB], W=[m8B])
                        self.E(k.dve, "tensor_scalar", selb[:, :], vv[:, :], m8[:, 15:16], None, ALU.is_ge, R=[vvB, m8B], W=[selbB])
                        pb = self.psbf(3)
                        self.E(k.pe, "transpose", pb[0:64, 0:128], selb[:, :], self.ident(), R=[selbB, self.cstB], W=[self.psB[3]])
                        self.E(k.act, "copy", selT_t[j][:, :], pb[0:64, 0:128], R=[self.psB[3]], W=[selTB[j]])
                        k.dma(k.sp, self.selT[g, :, tt * 128:(tt + 1) * 128], selT_t[j][:, :], R=[selTB[j]], W=[self.db("selT", (g, tt))])

                    def issue_loads(tt):
                        j = tt % 2
                        mk = msk[j]
                        mkB = mskB[j]
                        for b in range(2):
                            srcap = bass.AP(self.selT, g * 64 * T + b * T + tt * 128, [[0, 64], [2 * T, tt + 1], [1, 128]])
                            k.dma(k.sp, mk[b * 64:(b + 1) * 64, 0:tt + 1, :], srcap, R=[self.db("selT", (g, tt))], W=[mkB])
                        self.E(k.pool, "tensor_tensor", mk[:, tt, :], mk[:, tt, :], self.cst[:, 4, :], ALU.mult, R=[mkB, self.cstB], W=[mkB])
                        k.dma(k.sp, ot[j][:, :], self.opart[tt * 128:(tt + 1) * 128, g * 512:(g + 1) * 512], R=[self.db("opart", (g, tt))], W=[otB[j]])

                    issue_loads(0)
                    for tt in range(NT):
                        q = qg[:, tt, :]
                        j = tt % 2
                        mk = msk[j]
                        mkB = mskB[j]
                        stt = st[j]
                        sB_ = stB[j]
                        js = list(range(max(0, tt - 4), tt + 1))
                        accw = next_acc()
                        for ji, jk in enumerate(js):
                            m = None
                            if jk == tt:
                                m = self.cst[:, 4, :]
                            elif jk == tt - 4:
                                m = self.cst[:, 5, :]
                            pi = npair % 3
                            self.attn_pair(kwg[:, jk * 128:(jk + 1) * 128], kwB, q, qgB, vwg[:, jk, :], vwB, 129,
                                           m, self.cstB, accw, ji == 0, ji == len(js) - 1, pt[pi], ptB[pi], npair % 3)
                            npair += 1
                        accs = next_acc()
                        for jk in range(tt + 1):
                            pi = npair % 3
                            self.attn_pair(ksg[:, jk * 128:(jk + 1) * 128], ksB, q, qgB, vsg[:, jk, :], vsB, 129,
                                           mk[:, jk, :], mkB, accs, jk == 0, jk == tt, pt[pi], ptB[pi], npair % 3)
                            npair += 1
                            if jk == 0:
                                finish_branch(tt, g, 2, accw, stt, sB_, ot[j], otB[j], ot[j], otB[j])
                                if tt + 1 < NT:
                                    issue_loads(tt + 1)
                        self.attn_flush()
                        finish_branch(tt, g, 1, accs, stt, sB_, ot[j], otB[j], ot[j], otB[j])
                        self.E(k.act, "copy", otb[j][:, :], ot[j][:, :], R=[otB[j]], W=[otbB[j]])
                        self.transpose_to_hT(w, otb[j], otbB[j], 2, tt, pbanks=(3,), nch=4, ch0=g * 4)
                k.fence()

    def outproj(self, xsrc, xsname, xdst, xdname, vc, va, vb, hTdst):
        k = self.k
        nc = self.nc
        with ExitStack() as es:
            w = self.alloc_norm_tiles(es)
            ysb = [es.enter_context(self.sbt("ysb%d" % i, [128, D], F32)) for i in range(8)]
            yB = [Buf() for _ in range(8)]
            xt = [es.enter_context(self.sbt("xt%d" % i, [128, D], F32)) for i in range(1)]
            xB = [Buf() for _ in range(1)]
            hb = [es.enter_context(self.sbt("phT%d" % i, [128, KC, 512], BF16)) for i in range(1)]
            hB = [Buf() for _ in range(1)]
            cnt = 0
            for tb in range(T // 512):
                j = 0
                yo = (tb % 2) * 4
                k.dma(k.sp, hb[j][:, :, :], self.hT[2][:, :, tb * 512:(tb + 1) * 512],
                      R=[self.db("hT2", 4 * tb + s) for s in range(4)], W=[hB[j]])
                for cb in range(4):
                    wt, wB = self.wload_bf(self.wsc_o, "wsc_o", cb)
                    for s in range(4):
                        pi = 4 + cnt % 4
                        cnt += 1
                        for kk in range(KC):
                            self.E(k.pe, "matmul", self.ps[pi][:, :], hb[j][:, kk, s * 128:(s + 1) * 128], wt[:, kk, :],
                                   start=(kk == 0), stop=(kk == KC - 1), R=[wB, hB[j]], W=[self.psB[pi]])
                        self.E(k.act, "copy", ysb[yo + s][:, cb * 512:(cb + 1) * 512], self.ps[pi][:, :], R=[self.psB[pi]], W=[yB[yo + s]])
                for s in range(4):
                    tt = 4 * tb + s
                    self.residual_tile(w, ysb[yo + s][:, :], yB[yo + s], xt[0][:, :], xB[0], xsrc, xsname, xdst, xdname, vc, tt)
                    if hTdst is not None:
                        self.prenorm_tile(w, xt[0][:, :], xB[0], va, vb, hTdst, tt)
            k.fence()

    def ffn(self, l, xsrc, xsname, xdst, xdname, vc, va, vb, hTdst):
        k = self.k
        nc = self.nc
        Win = self.ffn_w_in.ap()[l]
        Wout = self.ffn_w_out.ap()[l]
        NF = DFF // 128
        with ExitStack() as es:
            w = self.alloc_norm_tiles(es)
            ysb = [es.enter_context(self.sbt("ysb%d" % i, [128, D], F32)) for i in range(4)]
            yB = [Buf() for _ in range(4)]
            xt = [es.enter_context(self.sbt("xt%d" % i, [128, D], F32)) for i in range(1)]
            xB = [Buf() for _ in range(1)]
            hb = [es.enter_context(self.sbt("phT%d" % i, [128, KC, 512], BF16)) for i in range(1)]
            hB = [Buf() for _ in range(1)]
            actT = es.enter_context(self.sbt("actT", [128, NF, 512], BF16))
            aB = [Buf() for _ in range(11)]
            sg = [es.enter_context(self.sbt("fsg%d" % i, [128, 512], F32)) for i in range(2)]
            sgB = [Buf() for _ in range(2)]
            n = 0
            for tb in range(T // 512):
                j = 0
                k.dma(k.sp, hb[j][:, :, :], self.hT[1][:, :, tb * 512:(tb + 1) * 512],
                      R=[self.db("hT1", 4 * tb + s) for s in range(4)], W=[hB[j]])
                for fg in range(11):
                    wg, wgB = self.wload_bf(self.wsc_in, "wsc_in", 2 * fg)
                    wu, wuB = self.wload_bf(self.wsc_in, "wsc_in", 2 * fg + 1)
                    for jj in range(4):
                        i2 = n % 2
                        n += 1
                        pg = i2
                        pu = 2 + i2
                        for kk in range(KC):
                            self.E(k.pe, "matmul", self.ps[pg][:, :], wg[:, kk, jj * 128:(jj + 1) * 128], hb[j][:, kk, :],
                                   start=(kk == 0), stop=(kk == KC - 1), R=[wgB, hB[j]], W=[self.psB[pg]])
                        for kk in range(KC):
                            self.E(k.pe, "matmul", self.ps[pu][:, :], wu[:, kk, jj * 128:(jj + 1) * 128], hb[j][:, kk, :],
                                   start=(kk == 0), stop=(kk == KC - 1), R=[wuB, hB[j]], W=[self.psB[pu]])
                        self.E(k.act, "activation", sg[i2][:, :], self.ps[pg][:, :], AF.Silu, R=[self.psB[pg]], W=[sgB[i2]])
                        self.E(k.dve, "tensor_tensor", actT[:, fg * 4 + jj, :], sg[i2][:, :], self.ps[pu][:, :], ALU.mult,
                               R=[sgB[i2], self.psB[pu]], W=[aB[fg]])
                for cb in range(4):
                    for kg in range(4):
                        wt, wB = self.wload_bf(self.wsc_out, "wsc_out", cb * 4 + kg, kc=11)
                        for s in range(4):
                            pi = 4 + s
                            for kk in range(11):
                                ch = kg * 11 + kk
                                self.E(k.pe, "matmul", self.ps[pi][:, :], actT[:, ch, s * 128:(s + 1) * 128], wt[:, kk, :],
                                       start=(ch == 0), stop=(ch == NF - 1), R=[wB, aB[ch // 4]], W=[self.psB[pi]])
                    for s in range(4):
                        self.E(k.act, "copy", ysb[s][:, cb * 512:(cb + 1) * 512], self.ps[4 + s][:, :], R=[self.psB[4 + s]], W=[yB[s]])
                for s in range(4):
                    tt = 4 * tb + s
                    self.residual_tile(w, ysb[s][:, :], yB[s], xt[0][:, :], xB[0], xsrc, xsname, xdst, xdname, vc, tt)
                    if hTdst is not None:
                        self.prenorm_tile(w, xt[0][:, :], xB[0], va, vb, hTdst, tt, pbanks=(0, 1))
            k.fence()

    def sb_kv(self, xsrc, xname):
        k = self.k
        self.modvec(1, self.kv_mod_w.ap(), 0, self.kv_mod_b, 0)
        self.modvec(0, self.kv_mod_w.ap(), D, self.kv_mod_b, D)
        self.load_gain(2, self.kv_norm_g, 0)
        self.E(k.dve, "scalar_tensor_tensor", self.vec[0][:, :], self.vec[0][:, :], 1.0, self.vec[2][:, :], ALU.add, ALU.mult,
               R=[self.vecB[0], self.vecB[2]], W=[self.vecB[0]])
        self.norm_phase(xsrc, xname, 0, 1, 2)
        nc = self.nc
        with ExitStack() as es:
            ob = [es.enter_context(self.sbt("ob%d" % i, [128, 512], BF16)) for i in range(3)]
            obB = [Buf() for _ in range(3)]
            st = {"n": 0}

            def fF(tag, jj, tb, pi):
                i3 = st["n"] % 3
                st["n"] += 1
                self.E(k.act, "copy", ob[i3][:, :], self.ps[pi][:, :], R=[self.psB[pi]], W=[obB[i3]])
                k.dma(k.sp, self.kTsh[tag * 4 + jj, :, tb * 512:(tb + 1) * 512], ob[i3][:, :], R=[obB[i3]], W=[self.db("kTsh", (tag * 4 + jj, tb))])

            def fT(tag, s, tb, pi):
                i3 = st["n"] % 3
                st["n"] += 1
                tt = 4 * tb + s
                self.E(k.dve, "tensor_copy", ob[i3][:, :], self.ps[pi][:, :], R=[self.psB[pi]], W=[obB[i3]])
                k.dma(k.sp, self.vsh[tt * 128:(tt + 1) * 128, tag * 512:(tag + 1) * 512], ob[i3][:, :], R=[obB[i3]], W=[self.db("vsh", (tag, tt))])

            blocks = [(c * 512, 512, "F", c) for c in range(4)] + [(D + c * 512, 512, "T", c) for c in range(4)]
            self.proj(2, (self.wsc_kv, "wsc_kv"), blocks, fF, fT)

    def sb_q(self, l2):
        k = self.k
        nc = self.nc
        with ExitStack() as es:
            ob = [es.enter_context(self.sbt("ob%d" % i, [128, 512], BF16)) for i in range(3)]
            obB = [Buf() for _ in range(3)]
            st = {"n": 0}

            def fF(tag, jj, tb, pi):
                i3 = st["n"] % 3
                st["n"] += 1
                self.E(k.act, "copy", ob[i3][:, :], self.ps[pi][:, :], R=[self.psB[pi]], W=[obB[i3]])
                k.dma(k.sp, self.qsb[tag * 4 + jj, :, tb * 512:(tb + 1) * 512], ob[i3][:, :], R=[obB[i3]], W=[self.db("qsb", (tag * 4 + jj, tb))])

            blocks = [(c * 512, 512, "F", c) for c in range(4)]
            self.proj(0, (self.wsc_p, "wsc_p"), blocks, fF, None)

    def sb_attn(self):
        k = self.k
        nc = self.nc
        with ExitStack() as es:
            sb = lambda name, shape, dtp: es.enter_context(self.sbt(name, shape, dtp))
            mb = sb("msbb", [128, 4, 512], BF16)
            mbB = Buf()
            k.dma(k.sp, mb[:, :, :], self.c_msb_b[:, :, :], W=[mbB])
            kT = [sb("skT%d" % i, [128, T], BF16) for i in range(2)]
            kTB = [Buf() for _ in range(2)]
            qT = [sb("sqT%d" % i, [128, T], BF16) for i in range(2)]
            qTB = [Buf() for _ in range(2)]
            vh = [sb("svh%d" % i, [128, NT, 128], BF16) for i in range(2)]
            vhB = [Buf() for _ in range(2)]
            NR = 3
            ee = [sb("see%d" % i, [128, 512], F32) for i in range(NR)]
            eeB = [Buf() for _ in range(NR)]
            spb = [sb("sspb%d" % i, [128, 512], BF16) for i in range(NR)]
            spbB = [Buf() for _ in range(NR)]
            t3 = [sb("st3_%d" % i, [128, 512], F32) for i in range(NR)]
            t3B = [Buf() for _ in range(NR)]
            pt = [sb("spt%d" % i, [128, 512], BF16) for i in range(NR)]
            ptB = [Buf() for _ in range(NR)]
            carry = sb("scarry", [128, 512], F32)
            cB = Buf()
            ob = [sb("sob%d" % i, [128, 512], BF16) for i in range(2)]
            obB = [Buf() for _ in range(2)]
            st = {"n": 0, "nq": 0}

            def stageA(p):
                Pz = self.ps[p["zi"]]
                self.E(k.pe, "matmul", Pz[:, :], p["kT"], p["q"], start=True, stop=True, R=[p["kB"], p["qB"]], W=[self.psB[p["zi"]]])

            def stageB(p):
                i = p["i"]
                Pz = self.ps[p["zi"]]
                PzB = self.psB[p["zi"]]
                self.E(k.act, "activation", ee[i][:, :], Pz[:, :], AF.Exp, scale=SCALE, R=[PzB], W=[eeB[i]])
                self.E(k.act, "activation", spb[i][:, :], ee[i][:, :], AF.Ln, bias=1.0, R=[eeB[i]], W=[spbB[i]])
                if p["a"] >= 0:
                    self.E(k.pool, "tensor_tensor", spb[i][:, :], spb[i][:, :], mb[:, p["a"], :], ALU.mult, R=[spbB[i], mbB], W=[spbB[i]])
                self.E(k.pe, "matmul", Pz[:, :], self.cst[:, 6, :], spb[i][:, :], start=False, stop=True, skip_group_check=True,
                       R=[self.cstB, spbB[i]], W=[PzB])
                if not p["last"]:
                    ti = p["ti"]
                    self.E(k.pe, "matmul", self.ps[ti][:, :], self.cst[:, 2, :], spb[i][:, :], start=True, stop=True,
                           R=[self.cstB, spbB[i]], W=[self.psB[ti]])

            def stageC(p):
                i = p["i"]
                Pz = self.ps[p["zi"]]
                PzB = self.psB[p["zi"]]
                if p["first"]:
                    self.E(k.act, "activation", pt[i][:, :], Pz[:, :], AF.Exp, scale=SCALE, R=[PzB], W=[ptB[i]])
                else:
                    self.E(k.dve, "scalar_tensor_tensor", t3[i][:, :], Pz[:, :], SCALE, carry[:, :], ALU.mult, ALU.subtract,
                           R=[PzB, cB], W=[t3B[i]])
                    self.E(k.act, "activation", pt[i][:, :], t3[i][:, :], AF.Exp, R=[t3B[i]], W=[ptB[i]])
                if p["a"] >= 0:
                    self.E(k.pool, "tensor_tensor", pt[i][:, :], pt[i][:, :], mb[:, p["a"], :], ALU.mult, R=[ptB[i], mbB], W=[ptB[i]])
                self.E(k.pe, "matmul", self.ps[p["acc"]][:, :], p["v"], pt[i][:, :], start=p["first"], stop=p["last"],
                       R=[p["vB"], ptB[i]], W=[self.psB[p["acc"]]])
                if not p["last"]:
                    ti = p["ti"]
                    if p["first"]:
                        self.E(k.dve, "tensor_copy", carry[:, :], self.ps[ti][:, :], R=[self.psB[ti]], W=[cB])
                    else:
                        self.E(k.dve, "tensor_tensor", carry[:, :], carry[:, :], self.ps[ti][:, :], ALU.add, R=[cB, self.psB[ti]], W=[cB])
                if p["last"]:
                    oj = st["nq"] % 2
                    st["nq"] += 1
                    self.E(k.act, "copy", ob[oj][:, :], self.ps[p["acc"]][:, :], R=[self.psB[p["acc"]]], W=[obB[oj]])
                    k.dma(k.sp, self.hT[2][:, p["h"], p["qb"] * 512:(p["qb"] + 1) * 512], ob[oj][:, :], R=[obB[oj]],
                          W=[self.db("hT2", 4 * p["qb"] + s) for s in range(4)])

            pairs = []
            for h in range(NH):
                hj = h % 2
                for qb in range(T // 512):
                    jtop = 4 * qb + 3
                    for jk in range(jtop, -1, -1):
                        pairs.append(dict(h=h, hj=hj, qb=qb, jk=jk, a=jk - 4 * qb, first=(jk == jtop), last=(jk == 0)))
            loaded = set()
            nacc = 0
            for n, p in enumerate(pairs):
                p["i"] = n % NR
                p["zi"] = n % 3
                p["ti"] = 3 + n % 2
                if p["first"]:
                    nacc += 1
                p["acc"] = 6 + nacc % 2
            def ensure_loaded(h):
                if h in loaded or h >= NH:
                    return
                loaded.add(h)
                hj = h % 2
                k.dma(k.sp, kT[hj][:, :], self.kTsh[h, :, :], R=[self.db("kTsh", (h, tb)) for tb in range(8)], W=[kTB[hj]])
                k.dma(k.sp, qT[hj][:, :], self.qsb[h, :, :], R=[self.db("qsb", (h, tb)) for tb in range(8)], W=[qTB[hj]])
                k.dma(k.sp, vh[hj][:, :, :], self.vsh[:, h * 128:(h + 1) * 128].rearrange("(j p) c -> p j c", p=128),
                      R=[self.db("vsh", (h // 4, tt)) for tt in range(NT)], W=[vhB[hj]])
            def prep(p):
                ensure_loaded(p["h"])
                hj = p["hj"]
                jk = p["jk"]
                p["kT"] = kT[hj][:, jk * 128:(jk + 1) * 128]
                p["kB"] = kTB[hj]
                p["q"] = qT[hj][:, p["qb"] * 512:(p["qb"] + 1) * 512]
                p["qB"] = qTB[hj]
                p["v"] = vh[hj][:, jk, :]
                p["vB"] = vhB[hj]
            N = len(pairs)
            for n in range(N + 2):
                if n < N:
                    prep(pairs[n])
                    stageA(pairs[n])
                if 0 <= n - 1 < N:
                    stageB(pairs[n - 1])
                if 0 <= n - 2 < N:
                    stageC(pairs[n - 2])
            k.fence()

    def build(self):
        k = self.k
        self.setup()
        xin, xin_name = self.x.ap(), "x"
        self.precast_sq(self.wsc_p, "wsc_p", self.a_w_in.ap()[0], NAIN)
        for l in range(self.nlayers):
            last = (l == self.nlayers - 1)
            if l == 0:
                self.mod_AB(l, 0, 0, 0, 1, 2)
                self.norm_phase(xin, xin_name, 0, 1, 0)
                if self.stop == "norm0":
                    return self.nc
            if l == 2:
                self.sb_kv(xin, xin_name)
            self.precast_sq(self.wsc_o, "wsc_o", (self.a_w_out.ap()[l] if l < 2 else self.b_w_out.ap()[l - 2]), D)
            self.precast_ffn(l)
            if l < 2:
                with ExitStack() as esg:
                    self.gate_sb = esg.enter_context(self.sbt("gate_sb", [128, NT, 48], F32))
                    self.nsa_proj(l)
                    if self.stop == "nsa_proj":
                        return self.nc
                    self.nsa_attn(l)
                    k.fence()
                if self.stop == "nsa_attn":
                    return self.nc
                wout = self.a_w_out.ap()[l]
            else:
                self.sb_q(l - 2)
                self.sb_attn()
                wout = self.b_w_out.ap()[l - 2]
            self.mod_C(l, 0, 1, 2, 3)
            self.mod_AB(l, 1, 2, 0, 1, 3)
            self.outproj(xin, xin_name, self.xs1.ap(), "xs1", 2, 0, 1, 1)
            if self.stop == "outproj%d" % l:
                return self.nc
            self.mod_C(l, 1, 3, 2, 3)
            if not last:
                self.mod_AB(l + 1, 0, 0, 0, 1, 3)
            xdst, xdname = (self.out.ap(), "out") if last else (self.xs2.ap(), "xs2")
            if not last:
                if l + 1 < 2:
                    self.precast_sq(self.wsc_p, "wsc_p", self.a_w_in.ap()[l + 1], NAIN)
                else:
                    self.precast_sq(self.wsc_p, "wsc_p", self.b_w_q.ap()[l + 1 - 2], D)
                if l + 1 == 2:
                    self.precast_sq(self.wsc_kv, "wsc_kv", self.kv_w.ap(), 2 * D)
            self.ffn(l, self.xs1.ap(), "xs1", xdst, xdname, 2, 0, 1, None if last else 0)
            xin, xin_name = xdst, xdname
        k.fence()
        return self.nc


_CONSTS = None


def make_in_maps(inputs, ncores=8):
    global _CONSTS
    if _CONSTS is None:
        _CONSTS = make_consts()
    f = lambda a: np.ascontiguousarray(np.asarray(a, dtype=np.float32))
    shared = {
        "mod_w": f(inputs["mod_w"]), "mod_b": f(inputs["mod_b"]), "norm_g": f(inputs["norm_g"]),
        "ffn_w_in": f(inputs["ffn_w_in"]), "ffn_w_out": f(inputs["ffn_w_out"]), "a_w_in": f(inputs["a_w_in"]),
        "a_gate_b": f(inputs["a_gate_b"]),
        "a_cmp_peT": np.ascontiguousarray(np.asarray(inputs["a_cmp_pe"], dtype=np.float32).transpose(0, 1, 3, 2)),
        "a_cmp_w1": f(inputs["a_cmp_w1"]), "a_cmp_w2": f(inputs["a_cmp_w2"]), "a_w_out": f(inputs["a_w_out"]),
        "b_w_q": f(inputs["b_w_q"]), "b_w_out": f(inputs["b_w_out"]),
        "kv_norm_g": f(inputs["kv_norm_g"]).reshape(1, D), "kv_mod_w": f(inputs["kv_mod_w"]),
        "kv_mod_b": f(inputs["kv_mod_b"]).reshape(1, 2 * D), "kv_w": f(inputs["kv_w"]),
    }
    shared.update(_CONSTS)
    x = np.asarray(inputs["x"], dtype=np.float32)
    c = np.asarray(inputs["c"], dtype=np.float32)
    maps = []
    for core in range(ncores):
        b = core % 4
        m = dict(shared)
        m["x"] = np.ascontiguousarray(x[b])
        m["cT"] = np.ascontiguousarray(c[b].reshape(KC, 128).T)
        maps.append(m)
    return maps


def kernel(**inputs):
    prog = Prog()
    nc = prog.build()
    maps = make_in_maps(inputs)
    res = run_bass_kernel_spmd(nc, maps, core_ids=list(range(8)))
    out = np.stack([np.asarray(res.results[b]["out"], dtype=np.float32) for b in range(4)], axis=0)
    return out
```
